# Optimizing a Trainium2 kernel written in Bass

```python
import jax, jax.numpy as jnp
from jax import lax
import numpy as np

D_MODEL = 1024
BATCH = 4
SEQ = 4096
DEPTH = 2
DEC_BATCH = 128
DEC_SEQ = 4
PAST_LEN = 16384
PAGE_SIZE = 128

ATT_HEADS = 8
ATT_KV_HEADS = 2
HEAD_DIM = 64
GROUP = ATT_HEADS // ATT_KV_HEADS
ATT_WIDTH = ATT_HEADS * HEAD_DIM
KV_WIDTH = ATT_KV_HEADS * HEAD_DIM
WINDOW = 128
Q_BLOCK = 128
ROT_DIM = HEAD_DIM // 4
ROPE_THETA = 500000.0
HG_HEADS = 4
HG_DK = 128
HG_DV = 128
HG_KEY_WIDTH = HG_HEADS * HG_DK
HG_VAL_WIDTH = HG_HEADS * HG_DV
HG_CHUNK = 32
MIX_WIDTH = ATT_WIDTH + HG_VAL_WIDTH
IN_WIDTH = ATT_WIDTH + 2 * KV_WIDTH + 2 * HG_KEY_WIDTH + 2 * HG_VAL_WIDTH
D_FF = 4 * D_MODEL
EPS = 1e-6

kernel_name = 'hymba_swa_sink_hgrn2_decoder_step'


def rms_norm(x, g):
    xf = x.astype(jnp.float32)
    y = xf * lax.rsqrt(jnp.mean(xf * xf, axis=-1, keepdims=True) + EPS)
    return (y * g.astype(jnp.float32)).astype(x.dtype)


def partial_rotary(x, pos):
    half = ROT_DIM // 2
    inv_freq = jnp.power(ROPE_THETA, -jnp.arange(half, dtype=jnp.float32) * (2.0 / ROT_DIM))
    ang = pos.astype(jnp.float32)[:, None] * inv_freq[None, :]
    cos = jnp.cos(ang)[None, :, None, :]
    sin = jnp.sin(ang)[None, :, None, :]
    xf = x.astype(jnp.float32)
    x1 = xf[..., :half]
    x2 = xf[..., half:ROT_DIM]
    out = jnp.concatenate([x1 * cos - x2 * sin, x2 * cos + x1 * sin, xf[..., ROT_DIM:]], axis=-1)
    return out.astype(x.dtype)


def window_attention(q, k_full, v_full, n_prefix_valid, sinks):
    B, L = q.shape[0], q.shape[1]
    qb = min(Q_BLOCK, L)
    n_blk = L // qb
    span = qb + WINDOW
    key_idx = jnp.arange(n_blk)[:, None] * qb + jnp.arange(span)[None, :]
    kb = k_full[:, key_idx]
    vb = v_full[:, key_idx]
    qg = q.reshape(B, n_blk, qb, ATT_KV_HEADS, GROUP, HEAD_DIM)
    scores = jnp.einsum('bnqkgd,bnskd->bnkgqs', qg, kb,
                        preferred_element_type=jnp.float32) * (HEAD_DIM ** -0.5)
    q_idx = WINDOW + jnp.arange(n_blk)[:, None] * qb + jnp.arange(qb)[None, :]
    dist = q_idx[:, :, None] - key_idx[:, None, :]
    allowed = (dist >= 0) & (dist <= WINDOW) & (key_idx[:, None, :] >= WINDOW - n_prefix_valid)
    scores = jnp.where(allowed[None, :, None, None, :, :], scores, -jnp.inf)
    sink = sinks.astype(jnp.float32).reshape(ATT_KV_HEADS, GROUP)[None, None, :, :, None, None]
    m = jnp.maximum(jnp.max(scores, axis=-1, keepdims=True), sink)
    p = jnp.exp(scores - m)
    denom = jnp.sum(p, axis=-1, keepdims=True) + jnp.exp(sink - m)
    out = jnp.einsum('bnkgqs,bnskd->bnqkgd', (p / denom).astype(v_full.dtype), vb)
    return out.reshape(B, L, ATT_WIDTH)


def hgrn2_recurrence(q, k, v, log_f, s0):
    B, L = q.shape[0], q.shape[1]
    c = min(HG_CHUNK, L)
    n = -(-L // c)
    pad = n * c - L

    def prep(a):
        a = jnp.pad(a.astype(jnp.float32), ((0, 0), (0, pad), (0, 0), (0, 0)))
        return a.reshape(B, n, c, HG_HEADS, a.shape[-1]).transpose(1, 0, 3, 2, 4)

    qc, kc, vc, gc = prep(q), prep(k), prep(v), prep(log_f)
    causal = jnp.tril(jnp.ones((c, c), dtype=bool))

    def step(S, inp):
        qi, ki, vi, gi = inp
        b = jnp.cumsum(gi, axis=2)
        o_inter = jnp.einsum('bhtd,bhde->bhte', qi * jnp.exp(b), S)
        diff = b[:, :, :, None, :] - b[:, :, None, :, :]
        decay = jnp.exp(jnp.where(causal[None, None, :, :, None], diff, -jnp.inf))
        a = jnp.einsum('bhtd,bhsd,bhtsd->bhts', qi, ki, decay)
        o = o_inter + jnp.einsum('bhts,bhse->bhte', a, vi)
        b_last = b[:, :, -1:, :]
        S_new = jnp.exp(b_last[:, :, 0, :, None]) * S + \
            jnp.einsum('bhsd,bhse->bhde', ki * jnp.exp(b_last - b), vi)
        return S_new, o

    S_fin, o = lax.scan(step, s0.astype(jnp.float32), (qc, kc, vc, gc))
    o = o.transpose(1, 0, 3, 2, 4).reshape(B, n * c, HG_HEADS, HG_DV)[:, :L]
    return o, S_fin


def mixer(h, k_prefix, v_prefix, s0, pos0, n_prefix_valid, w_in_l, sinks_l, lb_l, out_norm_l):
    B, L, _ = h.shape
    proj = jnp.einsum('bld,de->ble', h, w_in_l)
    widths = [ATT_WIDTH, KV_WIDTH, KV_WIDTH, HG_KEY_WIDTH, HG_KEY_WIDTH, HG_VAL_WIDTH]
    splits = []
    acc = 0
    for wd in widths:
        acc += wd
        splits.append(acc)
    q_a, k_a, v_a, q_h, f_h, i_h, g_h = jnp.split(proj, splits, axis=-1)

    pos = pos0 + jnp.arange(L, dtype=jnp.int32)
    q_a = partial_rotary(q_a.reshape(B, L, ATT_HEADS, HEAD_DIM), pos)
    k_a = partial_rotary(k_a.reshape(B, L, ATT_KV_HEADS, HEAD_DIM), pos)
    v_a = v_a.reshape(B, L, ATT_KV_HEADS, HEAD_DIM)
    k_full = jnp.concatenate([k_prefix.astype(k_a.dtype), k_a], axis=1)
    v_full = jnp.concatenate([v_prefix.astype(v_a.dtype), v_a], axis=1)
    att_out = window_attention(q_a, k_full, v_full, n_prefix_valid, sinks_l)
    new_k = k_full[:, -WINDOW:]
    new_v = v_full[:, -WINDOW:]

    z = f_h.astype(jnp.float32).reshape(B, L, HG_HEADS, HG_DK)
    lb = lb_l.reshape(HG_HEADS, HG_DK)
    log_f = jnp.logaddexp(jnp.log(lb), jnp.log1p(-lb) + jax.nn.log_sigmoid(z))
    k_in = (1.0 - lb) * jax.nn.sigmoid(-z)
    q_in = jax.nn.silu(q_h.astype(jnp.float32)).reshape(B, L, HG_HEADS, HG_DK)
    v_in = i_h.reshape(B, L, HG_HEADS, HG_DV)
    o, S_fin = hgrn2_recurrence(q_in, k_in, v_in, log_f, s0)
    gate = jax.nn.silu(g_h.astype(jnp.float32)).reshape(B, L, HG_HEADS, HG_DV)
    hg_out = (rms_norm(o, out_norm_l) * gate).reshape(B, L, HG_VAL_WIDTH)

    mix = jnp.concatenate([att_out.astype(h.dtype), hg_out.astype(h.dtype)], axis=-1)
    return mix, new_k, new_v, S_fin.astype(s0.dtype)


def trunk(x, k_bufs, v_bufs, states, pos0, n_prefix_valid, attn_norm, w_in, att_sinks,
          lower_bounds, hgrn_out_norm, w_o, mlp_norm, w_up, w_down, final_norm):
    new_ks, new_vs, new_ss = [], [], []
    for l in range(DEPTH):
        h = rms_norm(x, attn_norm[l])
        mix, nk, nv, ns = mixer(h, k_bufs[l], v_bufs[l], states[l], pos0, n_prefix_valid,
                                w_in[l], att_sinks[l], lower_bounds[l], hgrn_out_norm[l])
        x = x + jnp.einsum('ble,ed->bld', mix, w_o[l])
        h = rms_norm(x, mlp_norm[l])
        u = jnp.square(jax.nn.relu(jnp.einsum('bld,df->blf', h, w_up[l])))
        x = x + jnp.einsum('blf,fd->bld', u, w_down[l])
        new_ks.append(nk)
        new_vs.append(nv)
        new_ss.append(ns)
    return rms_norm(x, final_norm), jnp.stack(new_ks), jnp.stack(new_vs), jnp.stack(new_ss)


def setup_inputs(seed: int = 0) -> dict:
    key = jax.random.key(seed)
    ks = jax.random.split(key, 16)
    f32 = jnp.float32
    nrm = lambda k, shape: jax.random.normal(k, shape, dtype=f32)
    return {
        'x_prompt': nrm(ks[0], (BATCH, SEQ, D_MODEL)),
        'x_sample': nrm(ks[1], (DEC_BATCH, DEC_SEQ, D_MODEL)),
        'cache_k': nrm(ks[2], (DEPTH, DEC_BATCH, WINDOW, ATT_KV_HEADS, HEAD_DIM)),
        'cache_v': nrm(ks[3], (DEPTH, DEC_BATCH, WINDOW, ATT_KV_HEADS, HEAD_DIM)),
        'state_hgrn': 0.3 * nrm(ks[4], (DEPTH, DEC_BATCH, HG_HEADS, HG_DK, HG_DV)),
        'attn_norm': 1.0 + 0.02 * nrm(ks[5], (DEPTH, D_MODEL)),
        'w_in': nrm(ks[6], (DEPTH, D_MODEL, IN_WIDTH)) * D_MODEL ** -0.5,
        'att_sinks': 0.5 * nrm(ks[7], (DEPTH, ATT_HEADS)),
        'hgrn_lower_bounds': 0.1 * nrm(ks[8], (DEPTH, HG_KEY_WIDTH)),
        'hgrn_out_norm': 1.0 + 0.02 * nrm(ks[9], (DEPTH, HG_DV)),
        'w_o': nrm(ks[10], (DEPTH, MIX_WIDTH, D_MODEL)) * MIX_WIDTH ** -0.5,
        'mlp_norm': 1.0 + 0.02 * nrm(ks[11], (DEPTH, D_MODEL)),
        'w_up': nrm(ks[12], (DEPTH, D_MODEL, D_FF)) * D_MODEL ** -0.5,
        'w_down': nrm(ks[13], (DEPTH, D_FF, D_MODEL)) * D_FF ** -0.5,
        'final_norm': 1.0 + 0.02 * nrm(ks[14], (D_MODEL,)),
    }


def reference(x_prompt, x_sample, cache_k, cache_v, state_hgrn, attn_norm, w_in, att_sinks,
              hgrn_lower_bounds, hgrn_out_norm, w_o, mlp_norm, w_up, w_down, final_norm):
    p = jax.nn.softmax(hgrn_lower_bounds.astype(jnp.float32), axis=0)
    lower_bounds = jnp.maximum(jnp.cumsum(p, axis=0) - p[0:1], 0.0)
    weights = (attn_norm, w_in, att_sinks, lower_bounds, hgrn_out_norm, w_o, mlp_norm, w_up, w_down, final_norm)

    zero_kv = jnp.zeros((DEPTH, BATCH, WINDOW, ATT_KV_HEADS, HEAD_DIM), dtype=x_prompt.dtype)
    zero_s = jnp.zeros((DEPTH, BATCH, HG_HEADS, HG_DK, HG_DV), dtype=state_hgrn.dtype)
    y_prompt, nk_p, nv_p, ns_p = trunk(x_prompt, zero_kv, zero_kv, zero_s, 0, 0, *weights)

    y_sample, nk_s, nv_s, ns_s = trunk(x_sample, cache_k, cache_v, state_hgrn, PAST_LEN,
                                       min(WINDOW, PAST_LEN), *weights)
    return (y_prompt, y_sample, nk_p, nv_p, ns_p, nk_s, nv_s, ns_s)
```

```python
import numpy as np
from contextlib import ExitStack
import concourse.bass as bass
import concourse.mybir as mybir
from concourse.bass_utils import run_bass_kernel_spmd

F32 = mybir.dt.float32
BF16 = mybir.dt.bfloat16
AF = mybir.ActivationFunctionType
ALU = mybir.AluOpType
AX = mybir.AxisListType

D = 1024
SEQ = 4096
NSEG = 2
TPS = 16
NSMP = 16
PAST = 16384
EPS = 1e-6
INW = 2816
DFF = 4096
NPASS = 8
GRP = 256


class Prog:
    ENGS = ('pe', 'act', 'dve', 'pool', 'sp')

    def __init__(self, nc, es):
        self.nc = nc
        self.es = es
        self.eng = {'pe': nc.tensor, 'act': nc.scalar, 'dve': nc.vector, 'pool': nc.gpsimd, 'sp': nc.sync}
        self.cnt = {e: 0 for e in self.ENGS}
        self.semh = {k: es.enter_context(nc.semaphore('s_' + k)) for k in self.ENGS}
        self.dma_cnt = {}
        self.last_write = {}
        self.reads_since = {}
        self.seen = {e: {} for e in self.ENGS}
        self.nins = 0

    def _wait(self, e, tok):
        key, val = tok
        if self.seen[e].get(key, 0) >= val:
            return
        self.seen[e][key] = val
        self.eng[e].wait_ge(self.semh[key], val)

    def op(self, e, fn, reads=(), writes=(), dma=None):
        deps = []
        for r in reads:
            if r in self.last_write:
                deps.append(self.last_write[r])
        for w in writes:
            rs = self.reads_since.get(w, ())
            if rs:
                deps.extend(rs)
            elif w in self.last_write:
                deps.append(self.last_write[w])
        mx = {}
        for key, val in deps:
            if key == 'pe' and e == 'pe' and dma is None:
                continue
            if val > mx.get(key, 0):
                mx[key] = val
        for key, val in mx.items():
            self._wait(e, (key, val))
        self.nins += 1
        if dma is None:
            self.cnt[e] += 1
            tok = (e, self.cnt[e])
            fn(self.eng[e]).then_inc(self.semh[e], 1)
        else:
            k = 'dma:' + dma
            if k not in self.semh:
                self.semh[k] = self.es.enter_context(self.nc.semaphore('d%d' % len(self.semh)))
            self.dma_cnt[k] = self.dma_cnt.get(k, 0) + 16
            tok = (k, self.dma_cnt[k])
            fn(self.eng[e]).then_inc(self.semh[k], 16)
        for r in reads:
            self.reads_since.setdefault(r, []).append(tok)
        for w in writes:
            self.last_write[w] = tok
            self.reads_since[w] = []
        return tok

    def barrier(self, skip=()):
        toks = [(k, v) for k, v in self.dma_cnt.items() if not any(k.startswith('dma:' + s) for s in skip)] + \
               [(k, self.cnt[k]) for k in self.ENGS if self.cnt[k]]
        for e in self.ENGS:
            for t in toks:
                self._wait(e, t)

    def emit(self):
        self.barrier()


def bc(ap, axis, shape):
    return ap.unsqueeze(axis).broadcast_to(shape)


def build_program():
    nc = bass.Bass("TRN2", target_bir_lowering=False)

    def din(name, shape, dt=F32):
        return nc.dram_tensor(name, list(shape), dt, kind="ExternalInput").ap()

    def dout(name, shape):
        return nc.dram_tensor(name, list(shape), F32, kind="ExternalOutput").ap()

    x_seq = din("x_seq", [SEQ, D])
    x_smp = din("x_smp", [64, D])
    ck = din("ck", [2, NSMP, 128, 128])
    cv = din("cv", [2, NSMP, 128, 128])
    st_in = din("st_in", [2, NSMP, 4, 128, 128])
    gains = din("gains", [40, 128])
    fin_g = din("fin_g", [1, D])
    w_in = din("w_in", [2, D, INW])
    sinks = din("sinks", [2, 8])
    lbraw = din("lbraw", [2, 512])
    onorm = din("onorm", [2, 128])
    w_o = din("w_o", [2, D, D])
    w_up = din("w_up", [2, D, DFF])
    w_dn = din("w_dn", [2, DFF, D])
    c_identf = din("c_identf", [128, 128])
    c_identb = din("c_identb", [128, 128])
    c_cosp = din("c_cosp", [128, 32, 8])
    c_sinp = din("c_sinp", [128, 32, 8])
    c_coss = din("c_coss", [128, 8])
    c_sins = din("c_sins", [128, 8])
    c_negm = din("c_negm", [128, 2, 512])
    c_mrel = din("c_mrel", [2, 128, 128])
    c_mkd = din("c_mkd", [2, 128, 128])
    c_ma = din("c_ma", [2, 128, 128])
    c_indp = din("c_indp", [128, 8])
    c_inds = din("c_inds", [128, 16])
    c_msel = din("c_msel", [128, 16])
    c_mc0 = din("c_mc0", [128, 128])
    c_mns = din("c_mns", [128, 128])
    c_selT = din("c_selT", [128, 16, 64])

    y_seq = dout("y_seq", [SEQ, D])
    y_smp = dout("y_smp", [64, D])
    nk_p = dout("nk_p", [2, 128, 128])
    nv_p = dout("nv_p", [2, 128, 128])
    ns_p = dout("ns_p", [2, 4, 128, 128])
    nk_s = dout("nk_s", [2, NSMP, 128, 128])
    nv_s = dout("nv_s", [2, NSMP, 128, 128])
    ns_s = dout("ns_s", [2, NSMP, 4, 128, 128])

    with ExitStack() as es:
        def sb(name, shape, dt=F32):
            return es.enter_context(nc.sbuf_tensor(name, list(shape), dt))

        def ps(name, shape, dt=F32):
            return es.enter_context(nc.psum_tensor(name, list(shape), dt))

        NT = TPS + 1
        xres = sb("xres", [128, NT, D])
        hT_elems = 8 * NT * 128
        ring_slot = 8 * 512 + 4 * D
        uT_elems = 4 * GRP
        rg_b = max(8 * INW + 8 * D + ring_slot, hT_elems + ring_slot + 2 * uT_elems)
        rgn32 = sb("rgn", [128, rg_b // 2])
        rgnb = rgn32[:].bitcast(BF16)
        w_in_sb = rgnb[:, 0:8 * INW].rearrange("p (k n) -> p k n", k=8)
        w_o_sb = rgnb[:, 8 * INW:8 * INW + 8 * D].rearrange("p (k n) -> p k n", k=8)
        hT_all = rgnb[:, 0:hT_elems].rearrange("p (k t) -> p k t", k=8)
        ring = []
        for o0 in (8 * INW + 8 * D, hT_elems):
            wu = rgnb[:, o0:o0 + 4096].rearrange("p (k n) -> p k n", k=8)
            wd = rgnb[:, o0 + 4096:o0 + 8192].rearrange("p (k n) -> p k n", k=4)
            ring.append((wu, wd))
        o0 = hT_elems + ring_slot
        assert o0 + 2 * uT_elems <= 8 * INW + 8 * D
        uTs = []
        for s in range(2):
            uTs.append(rgnb[:, o0:o0 + uT_elems].rearrange("p (k n) -> p k n", k=4))
            o0 += uT_elems

        identf = sb("identf", [128, 128]); identb = sb("identb", [128, 128], BF16)
        cosp = sb("cosp", [128, TPS, 8]); sinp = sb("sinp", [128, TPS, 8])
        coss = sb("coss", [128, 8]); sins = sb("sins", [128, 8])
        negm = sb("negm", [128, 2, 512], BF16)
        mrel = sb("mrel", [128, 2, 128]); mkd = sb("mkd", [128, 2, 128]); ma = sb("ma", [128, 2, 128], BF16)
        indp = sb("indp", [128, 8]); inds = sb("inds", [128, 16]); msel = sb("msel", [128, 16])
        mc0 = sb("mc0", [128, 128], BF16); mns = sb("mns", [128, 128], BF16)
        selT = sb("selT", [128, 16, 64], BF16)
        gT = sb("gT", [128, 40])
        esink = sb("esink", [128, 2, 8])
        lb1 = sb("lb1", [128, 512])
        onb = sb("onb", [128, 2, 128])
        EPS_AP = sb("eps_ap", [128, 1])

        ss = sb("ss", [128, 1]); rstd = sb("rstd", [128, 1]); nrstd = sb("nrstd", [128, 1])
        rr2 = sb("rr2", [128, NT])
        dmy = sb("dmy", [128, 1])
        hT = sb("hT", [128, 8, 128], BF16)
        junk = hT[:].rearrange("p k t -> p (k t)")
        rl = junk[:, 0:512]
        ft = sb("ft", [128, 4096])
        fing = ft[:, 0:D]
        ysb = [ft[:, D:2 * D], ft[:, 2 * D:3 * D]]
        qa = ft[:, 0:512].rearrange("p (h d) -> p h d", h=8)
        ka = sb("ka", [128, 2, 64]); va = sb("va", [128, 128])
        qab = sb("qab", [128, 8, 64], BF16); kab = sb("kab", [128, 2, 64], BF16)
        qT = [sb("qT%d" % i, [64, 8, 128], BF16) for i in range(2)]
        kT = [[sb("kT%d_%d" % (l, i), [64, 2, 128], BF16) for i in range(2)] for l in range(2)]
        vaug = [[sb("vaug%d_%d" % (l, i), [128, 2, 65], BF16) for i in range(2)] for l in range(2)]
        pT = [sb("pT%d" % i, [128, 4, 128], BF16) for i in range(2)]
        den = sb("den", [128, 4]); rden = sb("rden", [128, 4])
        mix = sb("mix", [128, D], BF16)
        e1 = ft[:, 512:1024]
        qh = ft[:, 1024:1536]; fg = ft[:, 1536:2048]; gl = ft[:, 2048:2560]; kk = ft[:, 2560:3072]
        cl = ft[:, 3072:3584]; eq = ft[:, 3584:4096]; ek = e1
        rt = [cl[:, 0:64].rearrange("p (h d) -> p h d", h=8), cl[:, 64:128].rearrange("p (h d) -> p h d", h=8),
              eq[:, 0:64].rearrange("p (h d) -> p h d", h=8), eq[:, 64:128].rearrange("p (h d) -> p h d", h=8)]
        RTK = ["cl", "cl", "eq", "eq"]
        vh = [sb("vh%d" % i, [128, 512], BF16) for i in range(2)]
        gate = [sb("gate%d" % i, [128, 512], BF16) for i in range(2)]
        qe = [sb("qe%d" % i, [128, 4, 128], BF16) for i in range(2)]
        ke = [sb("ke%d" % i, [128, 4, 128], BF16) for i in range(2)]
        kd = [sb("kd%d" % i, [128, 512], BF16) for i in range(2)]
        bg = [sb("bg%d" % i, [128, 4, 16]) for i in range(2)]
        tq = sb("tq", [128, 8, 128], BF16)
        qeT = tq[:, 0:4, :]; keT = tq[:, 4:8, :]; mixT = tq[:]
        atm = sb("atm", [128, 4, 128], BF16)
        S = [sb("S%d" % l, [128, 4, 128]) for l in range(2)]
        Sp = [sb("Sp%d" % c, [128, 4, 128], BF16) for c in range(2)]
        sso = sb("sso", [128, 4]); rso = sb("rso", [128, 4])
        kcT2 = [Sp[1][0:64, 0:2, :], Sp[1][0:64, 2:4, :]]
        kcsA = ft[:, 0:1024].bitcast(BF16).rearrange("p (j d) -> p j d", j=NSMP)
        vcsA = ft[:, 1024:2064].bitcast(BF16).rearrange("p (j g d) -> p j g d", j=NSMP, g=2)
        sstA = [ft[:, 2064:2576].rearrange("p (h d) -> p h d", h=4), ft[:, 2576:3088].rearrange("p (h d) -> p h d", h=4)]
        stmp = ft[:, 3088:3600].rearrange("p (h d) -> p h d", h=4)
        sbf = ft[:, 3600:3856].bitcast(BF16).rearrange("p (h d) -> p h d", h=4)

        PT = ps("PT", [128, 512])
        PB = ps("PB", [128, 1024], BF16)
        PJ = [ps("PJ0", [128, 512]), ps("PJ1", [128, 512])]
        PS = [ps("PS0", [128, 512]), ps("PS1", [128, 512])]
        PV = ps("PV", [128, 512])
        PO = ps("PO", [128, 512])

        P = Prog(nc, es)
        cnt = {'pj': 0, 'ps': 0, 'y': 0, 'pt': 0, 'u': 0, 'pb': 0}

        def nxt(k, n=2):
            v = cnt[k] % n
            cnt[k] += 1
            return v


        WGRP = [(0, 512), (512, 768), (768, 1280), (1280, 1792), (1792, 2304), (2304, 2816)]

        def load_pass_l(l, p_):
            slot = p_ % 2
            wu, wd = ring[slot]
            for kc in range(8):
                P.op('pool', lambda e: e.dma_start(out=wu[:, kc, :], in_=w_up[l, kc * 128:(kc + 1) * 128, p_ * 512:(p_ + 1) * 512]), writes=['ringu%d_%d' % (slot, kc)], dma='lru%d' % slot)
            for fc in range(4):
                P.op('pool', lambda e: e.dma_start(out=wd[:, fc, :], in_=w_dn[l, p_ * 512 + fc * 128: p_ * 512 + (fc + 1) * 128, :]), writes=['ringd%d_%d' % (slot, fc)], dma='lrd%d' % slot)

        def load_phase_a(l):
            for gi, (c0, c1) in enumerate(WGRP):
                P.op('pool', lambda e: e.dma_start(out=w_in_sb[:, :, c0:c1], in_=w_in[l, :, c0:c1].rearrange("(k p) n -> p k n", p=128)), writes=['w_in_g%d' % gi], dma='lw_in%d' % gi)

        def load_phase_a2(l):
            for kc in range(8):
                P.op('pool', lambda e: e.dma_start(out=w_o_sb[:, kc, :], in_=w_o[l, kc * 128:(kc + 1) * 128, :]), writes=['w_o%d' % kc], dma='lw_o')
            load_pass_l(l, 0)

        P.op('sp', lambda e: e.dma_start(out=xres[:, 0, :], in_=x_seq[0:128, :]), writes=['x0'], dma='lx0')
        load_phase_a(0)

        cl_list = [(identf, c_identf), (coss, c_coss), (sins, c_sins),
                   (indp, c_indp), (inds, c_inds), (msel, c_msel)]
        for i, (t, d) in enumerate(cl_list):
            P.op('sp', lambda e: e.dma_start(out=t[:], in_=d), writes=[t.name], dma='c%d' % i)
        P.op('sp', lambda e: e.dma_start(out=mrel[:], in_=c_mrel.rearrange("a p n -> p a n")), writes=['mrel'], dma='c20')
        P.op('sp', lambda e: e.dma_start(out=mkd[:], in_=c_mkd.rearrange("a p n -> p a n")), writes=['mkd'], dma='c21')
        cb_list = [(identb, c_identb), (negm, c_negm), (mc0, c_mc0), (mns, c_mns), (selT, c_selT)]
        for i, (t, d) in enumerate(cb_list):
            P.op('pool', lambda e: e.dma_start(out=t[:], in_=d), writes=[t.name], dma='cb%d' % i)
        P.op('pool', lambda e: e.dma_start(out=ma[:], in_=c_ma.rearrange("a p n -> p a n")), writes=['ma'], dma='cb9')
        P.op('sp', lambda e: e.dma_start(out=esink[:].rearrange("p l h -> p (l h)"), in_=sinks.rearrange("l h -> (l h)").unsqueeze(0).partition_broadcast(128)), writes=['esink'], dma='c11')
        gstage = ft[0:40, 1024:1152]
        lbr = ft[:, 0:1024].rearrange("p (l c) -> p l c", l=2)
        P.op('sp', lambda e: e.dma_start(out=ft[:, 0:1024], in_=lbraw.rearrange("l c -> (l c)").unsqueeze(0).partition_broadcast(128)), writes=['lbr'], dma='c12')
        P.op('sp', lambda e: e.dma_start(out=onb[:].rearrange("p l h -> p (l h)"), in_=onorm.rearrange("l c -> (l c)").unsqueeze(0).partition_broadcast(128)), writes=['onb'], dma='c13')
        P.op('sp', lambda e: e.dma_start(out=gstage, in_=gains), writes=['gstage'], dma='c19')
        P.op('act', lambda e: e.activation(out=esink[:], in_=esink[:], func=AF.Exp), reads=['esink'], writes=['esink'])
        P.op('dve', lambda e: e.tensor_tensor(out=lb1[:], in0=lbr[:, 0, :], in1=lbr[:, 1, :], op=ALU.subtract), reads=['lbr'], writes=['lb1'])
        P.op('act', lambda e: e.activation(out=lb1[:], in_=lb1[:], func=AF.Exp), reads=['lb1'], writes=['lb1'])
        P.op('dve', lambda e: e.tensor_scalar_add(out=lb1[:], in0=lb1[:], scalar1=1.0), reads=['lb1'], writes=['lb1'])
        P.op('dve', lambda e: e.reciprocal(out=lb1[:], in_=lb1[:]), reads=['lb1'], writes=['lb1'])
        P.op('pe', lambda e: e.transpose(PT[:, 0:40], gstage, identf[0:40, 0:40]), reads=['gstage', 'identf'], writes=['PT'])
        P.op('dve', lambda e: e.tensor_copy(out=gT[:], in_=PT[:, 0:40]), reads=['PT'], writes=['gT'])
        P.op('pool', lambda e: e.memset(dmy[:], 0.0), writes=['dmy'])
        P.op('pool', lambda e: e.memset(EPS_AP[:], EPS), writes=['eps_ap'])
        for l in range(2):
            for i in range(2):
                P.op('pool', lambda e: e.memset(vaug[l][i][:], 1.0), writes=['vaug%d_%d' % (l, i)])
        P.op('pool', lambda e: e.memset(xres[:, TPS, :], 0.0), writes=['x%d' % TPS])
        for l in range(2):
            P.op('pool', lambda e: e.memset(S[l][:], 0.0), writes=['S%d' % l])
        P.barrier(skip=('lw_in', 'lw_o', 'lru', 'lrd', 'lx'))

        def rms_stats(xt_ap, xkey, out_r, out_key, also_neg=None, sq=None):
            def f(e):
                e.activation(out=junk, in_=xt_ap, func=AF.Square, accum_out=ss[:])
                return e.activation(out=dmy[:], in_=dmy[:], func=AF.Copy)
            P.op('act', f, reads=[xkey], writes=['hTa', 'hTb', 'ss', 'dmy'])
            P.op('act', lambda e: e.activation(out=ss[:], in_=ss[:], func=AF.Ln, scale=1.0 / D, bias=EPS_AP[:]), reads=['ss', 'eps_ap'], writes=['ss'])
            P.op('act', lambda e: e.activation(out=out_r, in_=ss[:], func=AF.Exp, scale=-0.5), reads=['ss'], writes=[out_key])
            if also_neg is not None:
                P.op('dve', lambda e: e.tensor_scalar_mul(out=also_neg, in0=out_r, scalar1=-1.0), reads=[out_key], writes=['nrstd'])
            if sq is not None:
                P.op('act', lambda e: e.activation(out=sq, in_=ss[:], func=AF.Exp, scale=-1.0), reads=['ss'], writes=['rr2'])

        def make_hT(ti, gcol, dst, dst_key):
            for half in range(2):
                if half == 0:
                    bank, bkey = PT, 'PT'
                else:
                    pj_ = nxt('pj')
                    bank, bkey = PJ[pj_], 'PJ%d' % pj_
                def f(e):
                    ins = None
                    for j in range(4):
                        kc = half * 4 + j
                        ins = e.transpose(bank[:, j * 128:(j + 1) * 128], xres[:, ti, kc * 128:(kc + 1) * 128], identf[:])
                    return ins
                P.op('pe', f, reads=['x%d' % ti, 'identf'], writes=[bkey])
                P.op('dve', lambda e: e.tensor_tensor(
                    out=dst[:, half * 4:half * 4 + 4, :], in0=bank[:].rearrange("p (k t) -> p k t", k=4),
                    in1=bc(gT[:, gcol * 8 + half * 4:gcol * 8 + half * 4 + 4], 2, [128, 4, 128]), op=ALU.mult),
                    reads=[bkey, 'gT'], writes=[(dst_key + 'ab'[half]) if dst_key == 'hT' else dst_key])
                yield

        WO_KEYS = ['w_o%d' % k_ for k_ in range(8)]

        def proj_group(c0, c1):
            pj = nxt('pj')
            for hf in range(2):
                def f(e):
                    ins = None
                    for kc in range(hf * 4, hf * 4 + 4):
                        ins = e.matmul(PJ[pj][:, 0:c1 - c0], lhsT=hT[:, kc, :], rhs=w_in_sb[:, kc, c0:c1], start=(kc == 0), stop=(kc == 7))
                    return ins
                P.op('pe', f, reads=['hT' + 'ab'[hf], 'w_in_g%d' % WGRP.index((c0, c1))], writes=['PJ%d' % pj])
            return pj

        def silu_from_psum(pj, dst, dst_key, eng='pool'):
            k = 'PJ%d' % pj
            P.op('act', lambda e: e.activation(out=dst, in_=PJ[pj][:], func=AF.Copy, scale=rstd[:]), reads=[k, 'rstd'], writes=[dst_key])
            P.op('act', lambda e: e.activation(out=e1[:], in_=dst, func=AF.Exp, scale=-1.0), reads=[dst_key], writes=['e1'])
            P.op('act', lambda e: e.activation(out=e1[:], in_=e1[:], func=AF.Ln, bias=1.0), reads=['e1'], writes=['e1'])
            P.op('act', lambda e: e.activation(out=e1[:], in_=e1[:], func=AF.Exp, scale=-1.0), reads=['e1'], writes=['e1'])
            P.op(eng, lambda e: e.tensor_tensor(out=dst, in0=dst, in1=e1[:], op=ALU.mult), reads=[dst_key, 'e1'], writes=[dst_key])

        def rotary(src, nh, cos_ap, sin_ap, dstb, skey, dkey):
            x1 = src[:, :, 0:8]; x2 = src[:, :, 8:16]
            cb = bc(cos_ap, 1, [128, nh, 8]); sn = bc(sin_ap, 1, [128, nh, 8])
            r0, r1, r2, r3 = [t[:, 0:nh, :] for t in rt]
            ck_ = ['cosp', 'sinp']
            P.op('pool', lambda e: e.tensor_tensor(out=r0, in0=x1, in1=cb, op=ALU.mult), reads=[skey] + ck_, writes=[RTK[0]])
            P.op('pool', lambda e: e.tensor_tensor(out=r1, in0=x2, in1=sn, op=ALU.mult), reads=[skey] + ck_, writes=[RTK[1]])
            P.op('pool', lambda e: e.tensor_tensor(out=r2, in0=x2, in1=cb, op=ALU.mult), reads=[skey] + ck_, writes=[RTK[2]])
            P.op('pool', lambda e: e.tensor_tensor(out=r3, in0=x1, in1=sn, op=ALU.mult), reads=[skey] + ck_, writes=[RTK[3]])
            P.op('pool', lambda e: e.tensor_tensor(out=x1, in0=r0, in1=r1, op=ALU.subtract), reads=['cl', skey], writes=[skey])
            P.op('pool', lambda e: e.tensor_tensor(out=x2, in0=r2, in1=r3, op=ALU.add), reads=['eq', skey], writes=[skey])
            P.op('pool', lambda e: e.tensor_copy(out=dstb, in_=src), reads=[skey], writes=[dkey])

        def transposes_bf(srcs, src_keys, dst, dst_key, rows, evac='act'):
            n = len(srcs)
            dkeys = dst_key if isinstance(dst_key, list) else [dst_key]
            def f(e):
                ins = None
                for i, s_ in enumerate(srcs):
                    ins = e.transpose(PB[0:rows, i * 128:i * 128 + 128], s_, identb[:])
                return ins
            P.op('pe', f, reads=list(src_keys) + ['identb'], writes=['PB'])
            src_v = PB[0:rows, 0:n * 128].rearrange("p (k t) -> p k t", k=n)
            if evac == 'act':
                P.op('act', lambda e: e.activation(out=dst, in_=src_v, func=AF.Copy), reads=['PB'], writes=dkeys)
            else:
                P.op('dve', lambda e: e.tensor_copy(out=dst, in_=src_v), reads=['PB'], writes=dkeys)

        def F_tile(l, ti, seg, is_smp, par):
            xk = 'x%d' % ti
            rms_stats(xres[:, ti, :], xk, rstd[:], 'rstd', also_neg=nrstd[:])
            for _ in make_hT(ti, l, hT, 'hT'):
                yield
            kTc, vac = kT[l][par], vaug[l][par]
            kTck, vack = 'kT%d_%d' % (l, par), 'vaug%d_%d' % (l, par)
            if is_smp:
                cos_ap, sin_ap = coss[:], sins[:]
            else:
                cos_ap, sin_ap = cosp[:, ti, :], sinp[:, ti, :]
            pj = proj_group(0, 512)
            P.op('act', lambda e: e.activation(out=qa[:].rearrange("p h d -> p (h d)"), in_=PJ[pj][:], func=AF.Copy, scale=rstd[:]),
                 reads=['PJ%d' % pj, 'rstd'], writes=['qa'])
            yield
            rotary(qa[:], 8, cos_ap, sin_ap, qab[:], 'qa', 'qab')
            pj = proj_group(512, 768)
            P.op('act', lambda e: e.activation(out=ka[:].rearrange("p h d -> p (h d)"), in_=PJ[pj][:, 0:128], func=AF.Copy, scale=rstd[:]),
                 reads=['PJ%d' % pj, 'rstd'], writes=['ka'])
            P.op('act', lambda e: e.activation(out=va[:], in_=PJ[pj][:, 128:256], func=AF.Copy, scale=rstd[:]),
                 reads=['PJ%d' % pj, 'rstd'], writes=['va'])
            yield
            rotary(ka[:], 2, cos_ap, sin_ap, kab[:], 'ka', 'kab')
            P.op('pool', lambda e: e.tensor_copy(out=vac[:, :, 0:64], in_=va[:].rearrange("p (g d) -> p g d", g=2)), reads=['va'], writes=[vack])
            if is_smp:
                for t in range(4):
                    P.op('sp', lambda e: e.dma_start(out=nk_s[l, :, 124 + t, :], in_=ka[t * 16:(t + 1) * 16].rearrange("p h d -> p (h d)")), reads=['ka'], dma='o_nk%d' % t)
                    P.op('sp', lambda e: e.dma_start(out=nv_s[l, :, 124 + t, :], in_=va[t * 16:(t + 1) * 16, :]), reads=['va'], dma='o_nv%d' % t)
            elif seg == NSEG - 1 and ti == TPS - 1:
                P.op('sp', lambda e: e.dma_start(out=nk_p[l], in_=ka[:].rearrange("p h d -> p (h d)")), reads=['ka'], dma='o_nkp')
                P.op('sp', lambda e: e.dma_start(out=nv_p[l], in_=va[:]), reads=['va'], dma='o_nvp')
            pj = proj_group(768, 1280)
            silu_from_psum(pj, qh[:], 'qh', eng='dve')
            yield
            pj = proj_group(1280, 1792)
            k = 'PJ%d' % pj
            P.op('act', lambda e: e.activation(out=e1[:], in_=PJ[pj][:], func=AF.Exp, scale=nrstd[:]), reads=[k, 'nrstd'], writes=['e1'])
            P.op('act', lambda e: e.activation(out=e1[:], in_=e1[:], func=AF.Ln, bias=1.0), reads=['e1'], writes=['e1'])
            P.op('act', lambda e: e.activation(out=fg[:], in_=e1[:], func=AF.Exp, scale=-1.0), reads=['e1'], writes=['fg'])
            yield
            transposes_bf([qab[:, h, :] for h in range(8)], ['qab'], qT[par][:], 'qT%d' % par, 64)
            yield
            P.op('dve', lambda e: e.tensor_scalar(out=kk[:], in0=fg[:], scalar1=-1.0, scalar2=1.0, op0=ALU.mult, op1=ALU.add), reads=['fg'], writes=['kk'])
            if l == 1:
                P.op('dve', lambda e: e.tensor_tensor(out=fg[:], in0=kk[:], in1=lb1[:], op=ALU.mult), reads=['kk', 'lb1'], writes=['fg'])
                P.op('pool', lambda e: e.tensor_tensor(out=kk[:], in0=kk[:], in1=fg[:], op=ALU.subtract), reads=['kk', 'fg'], writes=['kk'])
                P.op('pool', lambda e: e.tensor_scalar(out=fg[:], in0=kk[:], scalar1=-1.0, scalar2=1.0, op0=ALU.mult, op1=ALU.add), reads=['kk'], writes=['fg'])
            P.op('act', lambda e: e.activation(out=gl[:], in_=fg[:], func=AF.Ln), reads=['fg'], writes=['gl'])
            pj = proj_group(1792, 2304)
            P.op('act', lambda e: e.activation(out=vh[par][:], in_=PJ[pj][:], func=AF.Copy, scale=rstd[:]), reads=['PJ%d' % pj, 'rstd'], writes=['vh%d' % par])
            yield
            pj = proj_group(2304, 2816)
            silu_from_psum(pj, fg[:], 'fg')
            P.op('pool', lambda e: e.tensor_tensor(out=gate[par][:].rearrange("p (h d) -> p h d", h=4), in0=fg[:].rearrange("p (h d) -> p h d", h=4),
                                                   in1=bc(onb[:, l, :], 1, [128, 4, 128]), op=ALU.mult), reads=['fg', 'onb'], writes=['gate%d' % par])
            yield
            mi = 1 if is_smp else 0
            p1 = nxt('pj'); p2 = nxt('pj')
            P.op('pe', lambda e: e.matmul(PJ[p1][:], lhsT=mrel[:, mi, :], rhs=gl[:], start=True, stop=True), reads=['mrel', 'gl'], writes=['PJ%d' % p1])
            P.op('pe', lambda e: e.matmul(PJ[p2][:], lhsT=mkd[:, mi, :], rhs=gl[:], start=True, stop=True), reads=['mkd', 'gl'], writes=['PJ%d' % p2])
            P.op('dve', lambda e: e.tensor_scalar(out=cl[:], in0=PJ[p1][:], scalar1=-40.0, scalar2=40.0, op0=ALU.max, op1=ALU.min), reads=['PJ%d' % p1], writes=['cl'])
            P.op('act', lambda e: e.activation(out=eq[:], in_=cl[:], func=AF.Exp), reads=['cl'], writes=['eq'])
            P.op('act', lambda e: e.activation(out=ek[:], in_=cl[:], func=AF.Exp, scale=-1.0), reads=['cl'], writes=['e1'])
            P.op('act', lambda e: e.activation(out=cl[:], in_=PJ[p2][:], func=AF.Exp), reads=['PJ%d' % p2, 'cl'], writes=['cl'])
            P.op('dve', lambda e: e.tensor_tensor(out=qe[par][:].rearrange("p h d -> p (h d)"), in0=qh[:], in1=eq[:], op=ALU.mult), reads=['qh', 'eq'], writes=['qe%d' % par])
            P.op('dve', lambda e: e.tensor_tensor(out=ke[par][:].rearrange("p h d -> p (h d)"), in0=kk[:], in1=ek[:], op=ALU.mult), reads=['kk', 'e1'], writes=['ke%d' % par])
            P.op('dve', lambda e: e.tensor_tensor(out=kd[par][:], in0=kk[:], in1=cl[:], op=ALU.mult), reads=['kk', 'cl'], writes=['kd%d' % par])
            yield
            nind = 16 if is_smp else 8
            ind_ap = inds[:] if is_smp else indp[:]
            p3 = nxt('pj')
            def fbg(e):
                ins = None
                for h in range(4):
                    ins = e.matmul(PJ[p3][:, h * 16:h * 16 + nind], lhsT=gl[:, h * 128:(h + 1) * 128], rhs=ind_ap, start=True, stop=True)
                return ins
            P.op('pe', fbg, reads=['gl', 'inds', 'indp'], writes=['PJ%d' % p3])
            P.op('act', lambda e: e.activation(out=bg[par][:, :, 0:nind], in_=PJ[p3][:, 0:64].rearrange("p (h c) -> p h c", h=4)[:, :, 0:nind], func=AF.Exp),
                 reads=['PJ%d' % p3], writes=['bg%d' % par])
            yield

            transposes_bf([kab[:, g, :] for g in range(2)], ['kab'], kTc[:], kTck, 64)
            yield

        def B_tile(l, ti, seg, is_smp, par):
            xk = 'x%d' % ti
            prv = 1 - par
            kTc, vac = kT[l][par], vaug[l][par]
            kTck, vack = 'kT%d_%d' % (l, par), 'vaug%d_%d' % (l, par)
            qTc, qTk = qT[par], 'qT%d' % par
            mi = 1 if is_smp else 0
            if is_smp:
                blocks = [('c', j) for j in range(NSMP)] + [('n', 0)]
            else:
                blocks = ([('p', 0)] if not (seg == 0 and ti == 0) else []) + [('u', 0)]
            nb = len(blocks)
            pv_bank = {0: (PV, 'PV'), 1: ((PT, 'PT') if is_smp else (PV, 'PV'))}

            def finish_g(g):
                bank, bkey = pv_bank[g]
                pv4 = bank[:].rearrange("p (h c) -> p h c", h=4)
                P.op('dve', lambda e: e.tensor_tensor(out=den[:], in0=pv4[:, :, 64], in1=esink[:, l, 4 * g:4 * g + 4], op=ALU.add), reads=[bkey, 'esink'], writes=['den'])
                P.op('dve', lambda e: e.reciprocal(out=rden[:], in_=den[:]), reads=['den'], writes=['rden'])
                P.op('dve', lambda e: e.tensor_tensor(out=mix[:, g * 256:(g + 1) * 256].rearrange("p (h d) -> p h d", h=4), in0=pv4[:, :, 0:64],
                                                      in1=bc(rden[:], 2, [128, 4, 64]), op=ALU.mult), reads=[bkey, 'rden'], writes=['mixa'])

            def block_g(g, bi, bk, j, mask_ap, mask_key, lhs, lk, rv, rvk, neg=None):
                bank, bkey = pv_bank[g]
                psi = nxt('ps')
                pk = 'PS%d' % psi
                def fsc(e):
                    ins = e.matmul(PS[psi][:], lhsT=lhs, rhs=qTc[:, 4 * g:4 * g + 4, :], start=True, stop=(neg is None))
                    if neg is not None:
                        ins = e.matmul(PS[psi][:], lhsT=identb[:], rhs=neg, start=False, stop=True)
                    return ins
                P.op('pe', fsc, reads=[lk, qTk, 'identb', 'negm'], writes=[pk])
                pi = nxt('pt')
                P.op('act', lambda e: e.activation(out=pT[pi][:].rearrange("p h t -> p (h t)"), in_=PS[psi][:], func=AF.Exp, scale=0.125),
                     reads=[pk], writes=['pT%d' % pi])
                if neg is None:
                    P.op('pool', lambda e: e.tensor_tensor(out=pT[pi][:], in0=pT[pi][:], in1=bc(mask_ap, 1, [128, 4, 128]), op=ALU.mult),
                         reads=['pT%d' % pi, mask_key], writes=['pT%d' % pi])
                yield 1
                def fpv(e):
                    ins = None
                    for hh in range(4):
                        ins = e.matmul(bank[:, hh * 128:hh * 128 + 65], lhsT=pT[pi][:, hh, :], rhs=rv,
                                       start=(bi == 0 and hh == 0), stop=(bi == nb - 1), skip_group_check=True)
                    return ins
                P.op('pe', fpv, reads=['pT%d' % pi, rvk], writes=[bkey])
                yield 0

            if not is_smp:
                for g in range(2):
                    for bi, (bk, j) in enumerate(blocks):
                        if bk == 'p':
                            yield from block_g(g, bi, bk, j, None, None, kT[l][prv][:, g, :], 'kT%d_%d' % (l, prv), vaug[l][prv][:, g, :], 'vaug%d_%d' % (l, prv), neg=negm[:, 1, :])
                        else:
                            yield from block_g(g, bi, bk, j, None, None, kTc[:, g, :], kTck, vac[:, g, :], vack, neg=negm[:, 0, :])
                    finish_g(g)
            else:
                P.barrier()
                P.op('pool', lambda e: e.memset(ft[:, 1024:2064], 0.0), writes=['vcsA'])
                P.op('dve', lambda e: e.memset(vcsA[:, :, :, 64], 1.0), reads=['vcsA'], writes=['vcsA'])
                for i_ in range(2):
                    P.op('dve', lambda e: e.memset(pT[i_][:], 0.0), writes=['pT%d' % i_])
                P.op('pool', lambda e: e.dma_start(out=kcsA, in_=ck[l].rearrange("j k d -> k j d")), writes=['kcsA'], dma='l_kc')
                for g in range(2):
                    P.op('pool', lambda e: e.dma_start(out=vcsA[:, :, g, 0:64], in_=cv[l, :, :, g * 64:(g + 1) * 64].rearrange("j k d -> k j d")), reads=['vcsA'], writes=['vcsA'], dma='l_vc%d' % g)

            def att_seq(bi, j):
                kcT = kcT2[j % 2]
                kck = 'kcT%d' % (j % 2)
                transposes_bf([kcsA[:, j, g * 64:(g + 1) * 64] for g in range(2)], ['kcsA'], kcT, kck, 64)
                for g in range(2):
                    bank, bkey = pv_bank[g]
                    psi = nxt('ps')
                    pk = 'PS%d' % psi
                    P.op('pe', lambda e: e.matmul(PS[psi][:, 0:16], lhsT=kcT[:, g, :], rhs=qTc[:, 4 * g:4 * g + 4, j:64:16], start=True, stop=True),
                         reads=[kck, qTk], writes=[pk])
                    pi = nxt('pt')
                    pslice = pT[pi][:, :, j:64:16]
                    P.op('act', lambda e: e.activation(out=pslice, in_=PS[psi][:, 0:16].rearrange("p (h t) -> p h t", h=4), func=AF.Exp, scale=0.125),
                         reads=[pk], writes=['pT%d' % pi])
                    P.op('dve', lambda e: e.tensor_tensor(out=pslice, in0=pslice, in1=bc(mc0[:, j:64:16], 1, [128, 4, 4]), op=ALU.mult),
                         reads=['pT%d' % pi, 'mc0'], writes=['pT%d' % pi])
                    def fpv(e):
                        ins = None
                        for hh in range(4):
                            ins = e.matmul(bank[:, hh * 128:hh * 128 + 65], lhsT=pT[pi][:, hh, :], rhs=vcsA[:, j, g, :],
                                           start=(bi == 0 and hh == 0), stop=False, skip_group_check=True)
                        return ins
                    P.op('pe', fpv, reads=['pT%d' % pi, 'vcsA'], writes=[bkey])
                    P.op('dve', lambda e: e.memset(pslice, 0.0), reads=['pT%d' % pi], writes=['pT%d' % pi])

            def att_epi():
                bi, (bk, j) = nb - 1, blocks[-1]
                for g in range(2):
                    for _ in block_g(g, bi, bk, j, mns[:], 'mns', kTc[:, g, :], kTck, vac[:, g, :], vack):
                        pass
                finish_g(0)
                finish_g(1)

            qec, kec, kdc, bgc, vhc = qe[par], ke[par], kd[par], bg[par], vh[par]
            qek, kek, kdk, bgk, vhk = 'qe%d' % par, 'ke%d' % par, 'kd%d' % par, 'bg%d' % par, 'vh%d' % par
            transposes_bf([qec[:, h, :] for h in range(4)], [qek], qeT[:], 'qeT', 128)
            yield 1
            transposes_bf([kec[:, h, :] for h in range(4)], [kek], keT[:], 'keT', 128, evac='dve')
            yield 1
            p4 = nxt('ps')
            def fa(e):
                ins = None
                for h in range(4):
                    ins = e.matmul(PS[p4][:, h * 128:(h + 1) * 128], lhsT=keT[:, h, :], rhs=qeT[:, h, :], start=True, stop=True)
                return ins
            P.op('pe', fa, reads=['keT', 'qeT'], writes=['PS%d' % p4])
            P.op('dve', lambda e: e.tensor_tensor(out=atm[:], in0=PS[p4][:].rearrange("p (h t) -> p h t", h=4), in1=bc(ma[:, mi, :], 1, [128, 4, 128]), op=ALU.mult),
                 reads=['PS%d' % p4, 'ma'], writes=['atm'])
            yield 0
            def fo(e):
                ins = None
                for h in range(4):
                    ins = e.matmul(PO[:, h * 128:(h + 1) * 128], lhsT=atm[:, h, :], rhs=vhc[:, h * 128:(h + 1) * 128],
                                   start=(h == 0), stop=False, skip_group_check=True)
                return ins
            P.op('pe', fo, reads=['atm', vhk], writes=['PO'])
            yield 0
            Sl = S[l]
            Sk = 'S%d' % l
            if not is_smp:
                qeTs = keT
                P.op('dve', lambda e: e.tensor_tensor(out=qeTs.rearrange("p h (c t) -> p h c t", c=4), in0=qeT.rearrange("p h (c t) -> p h c t", c=4),
                                                      in1=bc(bgc[:, :, 4:8], 3, [128, 4, 4, 32]), op=ALU.mult),
                     reads=['qeT', 'keT', bgk], writes=['keT'])
                P.op('dve', lambda e: e.tensor_copy(out=Sp[0][:], in_=Sl[:]), reads=[Sk], writes=['Sp0'])
                for c in range(4):
                    pu = nxt('ps')
                    def fu(e):
                        ins = None
                        for h in range(4):
                            ins = e.matmul(PS[pu][:, h * 128:(h + 1) * 128], lhsT=kdc[32 * c:32 * c + 32, h * 128:(h + 1) * 128],
                                           rhs=vhc[32 * c:32 * c + 32, h * 128:(h + 1) * 128], start=True, stop=True, tile_position=(32 * c, 0))
                        return ins
                    P.op('pe', fu, reads=[kdk, vhk], writes=['PS%d' % pu])
                    def fupd(e):
                        ins = None
                        for h in range(4):
                            ins = e.scalar_tensor_tensor(out=Sl[:, h, :], in0=Sl[:, h, :], scalar=bgc[:, h, c:c + 1], in1=PS[pu][:, h * 128:(h + 1) * 128],
                                                         op0=ALU.mult, op1=ALU.add)
                        return ins
                    if c < 3:
                        def fupb(e):
                            ins = None
                            for h in range(4):
                                ins = e.scalar_tensor_tensor(out=Sp[(c + 1) % 2][:, h, :], in0=Sl[:, h, :], scalar=bgc[:, h, c:c + 1], in1=PS[pu][:, h * 128:(h + 1) * 128],
                                                             op0=ALU.mult, op1=ALU.add)
                            return ins
                        P.op('dve', fupb, reads=[Sk, bgk, 'PS%d' % pu], writes=['Sp%d' % ((c + 1) % 2)])
                    P.op('dve', fupd, reads=[Sk, bgk, 'PS%d' % pu], writes=[Sk])
                    if c == 3:
                        yield 3
                    def foi(e):
                        ins = None
                        for h in range(4):
                            ins = e.matmul(PO[32 * c:32 * c + 32, h * 128:(h + 1) * 128], lhsT=qeTs[:, h, 32 * c:32 * c + 32], rhs=Sp[c % 2][:, h, :],
                                           start=False, stop=(c == 3 and h == 3), skip_group_check=True, tile_position=(0, 32 * c))
                        return ins
                    P.op('pe', foi, reads=['keT', 'Sp%d' % (c % 2)], writes=['PO'])
                    if c < 3:
                        yield 1
                if seg == NSEG - 1 and ti == TPS - 1:
                    P.op('sp', lambda e: e.dma_start(out=ns_p[l].rearrange("h k v -> k h v"), in_=Sl[:]), reads=[Sk], dma='o_nsp')
            else:
                kdm = atm[:].rearrange("p h d -> p (h d)")
                qeTm = Sp[0][:, :, 0:64]
                sbf2 = [sbf, mix[:, 512:1024].rearrange("p (h d) -> p h d", h=4)]
                sbk = ['sbfA', 'mixh']
                def ld_state(jj):
                    s_ = jj % 2
                    P.op('sp', lambda e: e.dma_start(out=sstA[s_], in_=st_in[l, jj].rearrange("h k v -> k h v")), writes=['sstA%d' % s_], dma='l_st%d' % s_)
                    P.op('pool', lambda e: e.dma_start(out=sbf2[s_], in_=st_in[l, jj].rearrange("h k v -> k h v")), writes=[sbk[s_]], dma='l_sb%d' % s_)
                ld_state(0)
                for j in range(NSMP):
                    si = j % 2
                    if j + 1 < NSMP:
                        ld_state(j + 1)
                    att_seq(j, j)
                    P.op('dve', lambda e: e.tensor_tensor(out=qeTm, in0=qeT[:, :, 0:64], in1=bc(selT[:, j, :], 1, [128, 4, 64]), op=ALU.mult),
                         reads=['qeT', 'selT'], writes=['Sp0'])
                    def foi(e):
                        ins = None
                        for h in range(4):
                            ins = e.matmul(PO[0:64, h * 128:(h + 1) * 128], lhsT=qeTm[:, h, :], rhs=sbf2[si][:, h, :],
                                           start=False, stop=(j == NSMP - 1), skip_group_check=True)
                        return ins
                    P.op('pe', foi, reads=['Sp0', sbk[si]], writes=['PO'])
                    P.op('act', lambda e: e.activation(out=kdm, in_=kdc[:], func=AF.Copy, scale=msel[:, j:j + 1]), reads=[kdk, 'msel'], writes=['atm'])
                    pu = nxt('ps')
                    def fu(e):
                        ins = None
                        for h in range(4):
                            ins = e.matmul(PS[pu][:, h * 128:(h + 1) * 128], lhsT=kdm[:, h * 128:(h + 1) * 128], rhs=vhc[:, h * 128:(h + 1) * 128], start=True, stop=True)
                        return ins
                    P.op('pe', fu, reads=['atm', vhk], writes=['PS%d' % pu])
                    P.op('dve', lambda e: e.tensor_tensor(out=stmp, in0=sstA[si], in1=bc(bgc[:, :, j], 2, [128, 4, 128]), op=ALU.mult),
                         reads=['sstA%d' % si, bgk], writes=['stmpA'])
                    P.op('dve', lambda e: e.tensor_tensor(out=stmp.rearrange("p h d -> p (h d)"), in0=stmp.rearrange("p h d -> p (h d)"), in1=PS[pu][:], op=ALU.add),
                         reads=['stmpA', 'PS%d' % pu], writes=['stmpA'])
                    P.op('sp', lambda e: e.dma_start(out=ns_s[l, j].rearrange("h k v -> k h v"), in_=stmp), reads=['stmpA'], dma='o_st')
                    yield 0
                att_epi()
            def fsq(e):
                for h in range(4):
                    e.activation(out=atm[:, h, :], in_=PO[:, h * 128:(h + 1) * 128], func=AF.Square, accum_out=sso[:, h:h + 1])
                return e.activation(out=dmy[:], in_=dmy[:], func=AF.Copy)
            P.op('act', fsq, reads=['PO'], writes=['atm', 'sso', 'dmy'])
            P.op('act', lambda e: e.activation(out=sso[:], in_=sso[:], func=AF.Ln, scale=1.0 / 128.0, bias=EPS_AP[:]), reads=['sso', 'eps_ap'], writes=['sso'])
            P.op('act', lambda e: e.activation(out=rso[:], in_=sso[:], func=AF.Exp, scale=-0.5), reads=['sso'], writes=['rso'])
            def fmx(e):
                ins = None
                for h in range(4):
                    ins = e.scalar_tensor_tensor(out=mix[:, 512 + h * 128:512 + (h + 1) * 128], in0=PO[:, h * 128:(h + 1) * 128], scalar=rso[:, h:h + 1],
                                                 in1=gate[par][:, h * 128:(h + 1) * 128], op0=ALU.mult, op1=ALU.mult)
                return ins
            P.op('dve', fmx, reads=['PO', 'rso', 'gate%d' % par], writes=['mixh'])
            transposes_bf([mix[:, i * 128:(i + 1) * 128] for i in range(4)], ['mixa'], mixT[:, 0:4, :], 'qeT', 128)
            wob = [(PV, 'PV'), (PT, 'PT')]
            for half in range(2):
                wbank, wkey = wob[half]
                def fw0(e):
                    ins = None
                    for kc in range(4):
                        ins = e.matmul(wbank[:], lhsT=mixT[:, kc, :], rhs=w_o_sb[:, kc, half * 512:(half + 1) * 512], start=(kc == 0), stop=False)
                    return ins
                P.op('pe', fw0, reads=['qeT'] + WO_KEYS, writes=[wkey])
            yield 99
            transposes_bf([mix[:, i * 128:(i + 1) * 128] for i in range(4, 8)], ['mixh'], mixT[:, 4:8, :], 'keT', 128, evac='dve')
            for half in range(2):
                wbank, wkey = wob[half]
                def fw1(e):
                    ins = None
                    for kc in range(4, 8):
                        ins = e.matmul(wbank[:], lhsT=mixT[:, kc, :], rhs=w_o_sb[:, kc, half * 512:(half + 1) * 512], start=False, stop=(kc == 7))
                    return ins
                P.op('pe', fw1, reads=['keT'] + WO_KEYS, writes=[wkey])
                P.op('dve', lambda e: e.tensor_tensor(out=xres[:, ti, half * 512:(half + 1) * 512], in0=xres[:, ti, half * 512:(half + 1) * 512], in1=wbank[:], op=ALU.add),
                     reads=[xk, wkey], writes=[xk])
                yield 0

        def drain(g):
            for _ in g:
                pass

        def interleave(gb, gf):
            done_f = False
            for nf in gb:
                for _ in range(nf or 0):
                    if done_f:
                        break
                    try:
                        next(gf)
                    except StopIteration:
                        done_f = True
            if not done_f:
                drain(gf)

        for seg in range(NSEG):
            tiles = list(range(TPS)) + ([TPS] if seg == 0 else [])
            for ti in range(TPS):
                if ti == 0 or seg > 0:
                    continue
                P.op('sp', lambda e: e.dma_start(out=xres[:, ti, :], in_=x_seq[(seg * TPS + ti) * 128:(seg * TPS + ti + 1) * 128, :]),
                     writes=['x%d' % ti], dma='lx%d' % ti)
            P.op('sp', lambda e: e.dma_start(out=cosp[:], in_=c_cosp[:, seg * TPS:(seg + 1) * TPS, :]), writes=['cosp'], dma='l_cos')
            P.op('sp', lambda e: e.dma_start(out=sinp[:], in_=c_sinp[:, seg * TPS:(seg + 1) * TPS, :]), writes=['sinp'], dma='l_sin')
            if seg == 0:
                P.op('sp', lambda e: e.dma_start(out=xres[0:64, TPS, :], in_=x_smp), writes=['x%d' % TPS], dma='lx%d' % TPS)
            for l in range(2):
                if not (seg == 0 and l == 0):
                    P.barrier()
                def load_pass(p_):
                    load_pass_l(l, p_)
                if l != 0:
                    load_phase_a(l)
                if seg == 0:
                    P.op('sp', lambda e: e.dma_start(out=nk_s[l, :, 0:124, :], in_=ck[l, :, 4:128, :]), dma='o_ckc')
                    P.op('sp', lambda e: e.dma_start(out=nv_s[l, :, 0:124, :], in_=cv[l, :, 4:128, :]), dma='o_cvc')
                n = len(tiles)
                drain(F_tile(l, tiles[0], seg, tiles[0] == TPS, 0))
                load_phase_a2(l)
                for tix, ti in enumerate(tiles):
                    gb = B_tile(l, ti, seg, ti == TPS, tix % 2)
                    if tix + 1 < n:
                        gf = F_tile(l, tiles[tix + 1], seg, tiles[tix + 1] == TPS, (tix + 1) % 2)
                        interleave(gb, gf)
                    else:
                        drain(gb)
                P.barrier()
                ntok = n * 128
                for ti in tiles:
                    rms_stats(xres[:, ti, :], 'x%d' % ti, rstd[:], 'rstd', sq=rr2[:, ti:ti + 1])
                    drain(make_hT(ti, 2 + l, hT_all[:, :, ti * 128:(ti + 1) * 128], 'hTall'))
                groups = [(g0, min(GRP, ntok - g0)) for g0 in range(0, ntok, GRP)]

                def up(p_, g0, gn, ui):
                    slot = p_ % 2
                    wu, wd = ring[slot]
                    uT = uTs[ui]
                    for fc in range(4):
                        psi = nxt('ps')
                        def fup(e):
                            ins = None
                            for kc in range(8):
                                ins = e.matmul(PS[psi][:, 0:gn], lhsT=wu[:, kc, fc * 128:(fc + 1) * 128], rhs=hT_all[:, kc, g0:g0 + gn], start=(kc == 0), stop=(kc == 7))
                            return ins
                        P.op('pe', fup, reads=['ringu%d_%d' % (slot, k_) for k_ in range(8)] + ['hTall'], writes=['PS%d' % psi])
                        P.op('act', lambda e: e.activation(out=rl[:, 0:gn], in_=PS[psi][:, 0:gn], func=AF.Relu), reads=['PS%d' % psi], writes=['hTa'])
                        P.op('dve', lambda e: e.tensor_tensor(out=uT[:, fc, 0:gn], in0=rl[:, 0:gn], in1=rl[:, 0:gn], op=ALU.mult), reads=['hTa'], writes=['uT%d' % ui])

                def down(p_, g0, gn, ui):
                    slot = p_ % 2
                    wu, wd = ring[slot]
                    uT = uTs[ui]
                    for tt in range(gn // 128):
                        ti = (g0 // 128) + tt
                        for half in range(2):
                            pj = nxt('pj')
                            def fdn(e):
                                ins = None
                                for fc in range(4):
                                    ins = e.matmul(PJ[pj][:], lhsT=uT[:, fc, tt * 128:(tt + 1) * 128], rhs=wd[:, fc, half * 512:(half + 1) * 512], start=(fc == 0), stop=(fc == 3))
                                return ins
                            P.op('pe', fdn, reads=['uT%d' % ui] + ['ringd%d_%d' % (slot, k_) for k_ in range(4)], writes=['PJ%d' % pj])
                            P.op('dve', lambda e: e.scalar_tensor_tensor(
                                out=xres[:, ti, half * 512:(half + 1) * 512], in0=PJ[pj][:], scalar=rr2[:, ti:ti + 1],
                                in1=xres[:, ti, half * 512:(half + 1) * 512], op0=ALU.mult, op1=ALU.add),
                                reads=['PJ%d' % pj, 'rr2', 'x%d' % ti], writes=['x%d' % ti])

                work = [(p_, g0, gn) for p_ in range(NPASS) for (g0, gn) in groups]
                prev = None
                for wi, (p_, g0, gn) in enumerate(work):
                    ui = nxt('u')
                    up(p_, g0, gn, ui)
                    if prev is not None:
                        down(*prev)
                    if g0 == 0 and p_ + 1 < NPASS:
                        load_pass(p_ + 1)
                    prev = (p_, g0, gn, ui)
                down(*prev)
            P.barrier()
            if seg + 1 < NSEG:
                load_phase_a(0)
            P.op('sp', lambda e: e.dma_start(out=fing, in_=fin_g.partition_broadcast(128)), writes=['fing'], dma='c10')
            for ti in tiles:
                rms_stats(xres[:, ti, :], 'x%d' % ti, rstd[:], 'rstd')
                yi = nxt('y')
                P.op('dve', lambda e: e.scalar_tensor_tensor(out=ysb[yi], in0=xres[:, ti, :], scalar=rstd[:], in1=fing, op0=ALU.mult, op1=ALU.mult),
                     reads=['x%d' % ti, 'rstd', 'fing'], writes=['ysb%d' % yi])
                if ti == TPS:
                    P.op('sp', lambda e: e.dma_start(out=y_smp, in_=ysb[yi][0:64, :]), reads=['ysb%d' % yi], dma='o_y%d' % yi)
                else:
                    P.op('sp', lambda e: e.dma_start(out=y_seq[(seg * TPS + ti) * 128:(seg * TPS + ti + 1) * 128, :], in_=ysb[yi]),
                         reads=['ysb%d' % yi], dma='o_y%d' % yi)
                    if seg + 1 < NSEG:
                        P.op('sp', lambda e: e.dma_start(out=xres[:, ti, :], in_=x_seq[((seg + 1) * TPS + ti) * 128:((seg + 1) * TPS + ti + 1) * 128, :]),
                             writes=['x%d' % ti], dma='lx%d' % ti)
            P.barrier()
        P.emit()
    return nc


def _consts():
    c = {}
    c['c_identf'] = np.eye(128, dtype=np.float32)
    c['c_identb'] = np.eye(128, dtype=np.float32)
    half = 8
    inv_freq = np.power(np.float32(500000.0), -np.arange(half, dtype=np.float32) * np.float32(2.0 / 16)).astype(np.float32)
    pos = np.arange(SEQ, dtype=np.float32)
    ang = (pos[:, None] * inv_freq[None, :]).astype(np.float32)
    c['c_cosp'] = np.cos(ang).astype(np.float32).reshape(32, 128, 8).transpose(1, 0, 2).copy()
    c['c_sinp'] = np.sin(ang).astype(np.float32).reshape(32, 128, 8).transpose(1, 0, 2).copy()
    p = np.arange(128)
    tt = p // 16
    jj = p % 16
    valid = p < 64
    pos_s = (PAST + tt).astype(np.float32)
    ang_s = (pos_s[:, None] * inv_freq[None, :]).astype(np.float32)
    c['c_coss'] = np.where(valid[:, None], np.cos(ang_s), 1.0).astype(np.float32)
    c['c_sins'] = np.where(valid[:, None], np.sin(ang_s), 0.0).astype(np.float32)
    s = p[:, None]; t = p[None, :]
    ncur = np.where(s <= t, 0.0, -30000.0).astype(np.float32)
    nprev = np.where(s >= t, 0.0, -30000.0).astype(np.float32)
    c['c_negm'] = np.stack([np.tile(ncur, (1, 4)), np.tile(nprev, (1, 4))], axis=1).astype(np.float32)
    ch = p // 32
    same = ch[:, None] == ch[None, :]
    ref = ch * 32 + 15
    mrel_p = same * ((s <= t).astype(np.float32) - (s <= ref[None, :]).astype(np.float32))
    mkd_p = same * (s > t)
    ma_p = same * (s <= t)
    same_s = (jj[:, None] == jj[None, :]) & valid[:, None] & valid[None, :]
    mrel_s = same_s * (tt[:, None] <= tt[None, :])
    mkd_s = same_s * (tt[:, None] > tt[None, :])
    ma_s = same_s * (tt[:, None] <= tt[None, :])
    c['c_mrel'] = np.stack([mrel_p, mrel_s]).astype(np.float32)
    c['c_mkd'] = np.stack([mkd_p, mkd_s]).astype(np.float32)
    c['c_ma'] = np.stack([ma_p, ma_s]).astype(np.float32)
    indp = np.zeros((128, 8), np.float32)
    for cc in range(4):
        indp[cc * 32:(cc + 1) * 32, cc] = 1.0
        indp[cc * 32:cc * 32 + 16, 4 + cc] = 1.0
    c['c_indp'] = indp
    inds = np.zeros((128, 16), np.float32)
    inds[p[valid], jj[valid]] = 1.0
    c['c_inds'] = inds
    c['c_msel'] = inds.copy()
    c['c_mc0'] = ((p[:, None] >= tt[None, :]) & valid[None, :]).astype(np.float32)
    mns = (same_s & (tt[:, None] <= tt[None, :])).astype(np.float32)
    c['c_mns'] = mns
    selT = np.zeros((128, 16, 64), np.float32)
    for j in range(16):
        selT[:, j, :] = ((jj == j) & valid)[None, 0:64]
    c['c_selT'] = selT
    return c


_CACHE = {}
PCORES = [0, 1, 4, 5]


def kernel(x_prompt, x_sample, cache_k, cache_v, state_hgrn, attn_norm, w_in, att_sinks,
           hgrn_lower_bounds, hgrn_out_norm, w_o, mlp_norm, w_up, w_down, final_norm):
    f = lambda a: np.ascontiguousarray(np.asarray(a, dtype=np.float32))
    x_prompt, x_sample, cache_k, cache_v, state_hgrn = map(f, (x_prompt, x_sample, cache_k, cache_v, state_hgrn))
    if 'nc' not in _CACHE:
        _CACHE['nc'] = build_program()
        _CACHE['consts'] = _consts()
    nc = _CACHE['nc']
    consts = _CACHE['consts']
    gains = np.concatenate([f(attn_norm).reshape(2, 8, 128), f(mlp_norm).reshape(2, 8, 128), f(final_norm).reshape(1, 8, 128)], axis=0).reshape(40, 128)
    shared = dict(gains=np.ascontiguousarray(gains), fin_g=f(final_norm).reshape(1, D), w_in=f(w_in), sinks=f(att_sinks),
                  lbraw=f(hgrn_lower_bounds), onorm=f(hgrn_out_norm), w_o=f(w_o), w_up=f(w_up), w_dn=f(w_down))
    shared.update(consts)
    zero_seq = np.zeros((SEQ, D), np.float32)
    in_maps = []
    for c in range(8):
        sl = slice(c * NSMP, (c + 1) * NSMP)
        m = dict(shared)
        m['x_seq'] = x_prompt[PCORES.index(c)] if c in PCORES else zero_seq
        m['x_smp'] = np.ascontiguousarray(x_sample[sl].transpose(1, 0, 2).reshape(64, D))
        m['ck'] = np.ascontiguousarray(cache_k[:, sl].reshape(2, NSMP, 128, 128))
        m['cv'] = np.ascontiguousarray(cache_v[:, sl].reshape(2, NSMP, 128, 128))
        m['st_in'] = np.ascontiguousarray(state_hgrn[:, sl])
        in_maps.append(m)
    res = run_bass_kernel_spmd(nc, in_maps, core_ids=list(range(8)))
    R = res.results
    y_prompt = np.stack([R[b]['y_seq'] for b in PCORES]).astype(np.float32)
    y_sample = np.concatenate([R[c]['y_smp'].reshape(4, NSMP, D).transpose(1, 0, 2) for c in range(8)], axis=0).astype(np.float32)
    nk_p = np.stack([R[b]['nk_p'] for b in PCORES], axis=1).reshape(2, 4, 128, 2, 64)
    nv_p = np.stack([R[b]['nv_p'] for b in PCORES], axis=1).reshape(2, 4, 128, 2, 64)
    ns_p = np.stack([R[b]['ns_p'] for b in PCORES], axis=1)
    nk_s = np.concatenate([R[c]['nk_s'] for c in range(8)], axis=1).reshape(2, 128, 128, 2, 64)
    nv_s = np.concatenate([R[c]['nv_s'] for c in range(8)], axis=1).reshape(2, 128, 128, 2, 64)
    ns_s = np.concatenate([R[c]['ns_s'] for c in range(8)], axis=1)
    return (y_prompt, y_sample, np.ascontiguousarray(nk_p), np.ascontiguousarray(nv_p), np.ascontiguousarray(ns_p),
            np.ascontiguousarray(nk_s), np.ascontiguousarray(nv_s), np.ascontiguousarray(ns_s))
```

```python
import numpy as np
from contextlib import ExitStack
import concourse.bass as bass
import concourse.mybir as mybir
from concourse.bass_utils import run_bass_kernel_spmd

F32 = mybir.dt.float32
BF16 = mybir.dt.bfloat16
AF = mybir.ActivationFunctionType
ALU = mybir.AluOpType
AX = mybir.AxisListType

D = 1024
SEQ = 4096
NSEG = 2
TPS = 16
NSMP = 16
PAST = 16384
EPS = 1e-6
INW = 2816
DFF = 4096
NPASS = 8
GRP = 256


class Prog:
    ENGS = ('pe', 'act', 'dve', 'pool', 'sp')

    def __init__(self, nc, es):
        self.nc = nc
        self.es = es
        self.eng = {'pe': nc.tensor, 'act': nc.scalar, 'dve': nc.vector, 'pool': nc.gpsimd, 'sp': nc.sync}
        self.cnt = {e: 0 for e in self.ENGS}
        self.semh = {k: es.enter_context(nc.semaphore('s_' + k)) for k in self.ENGS}
        self.dma_cnt = {}
        self.last_write = {}
        self.reads_since = {}
        self.seen = {e: {} for e in self.ENGS}
        self.nins = 0

    def _wait(self, e, tok):
        key, val = tok
        if self.seen[e].get(key, 0) >= val:
            return
        self.seen[e][key] = val
        self.eng[e].wait_ge(self.semh[key], val)

    def op(self, e, fn, reads=(), writes=(), dma=None):
        deps = []
        for r in reads:
            if r in self.last_write:
                deps.append(self.last_write[r])
        for w in writes:
            rs = self.reads_since.get(w, ())
            if rs:
                deps.extend(rs)
            elif w in self.last_write:
                deps.append(self.last_write[w])
        mx = {}
        for key, val in deps:
            if key == 'pe' and e == 'pe' and dma is None:
                continue
            if val > mx.get(key, 0):
                mx[key] = val
        for key, val in mx.items():
            self._wait(e, (key, val))
        self.nins += 1
        if dma is None:
            self.cnt[e] += 1
            tok = (e, self.cnt[e])
            fn(self.eng[e]).then_inc(self.semh[e], 1)
        else:
            k = 'dma:' + dma
            if k not in self.semh:
                self.semh[k] = self.es.enter_context(self.nc.semaphore('d%d' % len(self.semh)))
            self.dma_cnt[k] = self.dma_cnt.get(k, 0) + 16
            tok = (k, self.dma_cnt[k])
            fn(self.eng[e]).then_inc(self.semh[k], 16)
        for r in reads:
            self.reads_since.setdefault(r, []).append(tok)
        for w in writes:
            self.last_write[w] = tok
            self.reads_since[w] = []
        return tok

    def barrier(self, skip=()):
        toks = [(k, v) for k, v in self.dma_cnt.items() if not any(k.startswith('dma:' + s) for s in skip)] + \
               [(k, self.cnt[k]) for k in self.ENGS if self.cnt[k]]
        for e in self.ENGS:
            for t in toks:
                self._wait(e, t)

    def emit(self):
        self.barrier()


def bc(ap, axis, shape):
    return ap.unsqueeze(axis).broadcast_to(shape)


def build_program():
    nc = bass.Bass("TRN2", target_bir_lowering=False)

    def din(name, shape, dt=F32):
        return nc.dram_tensor(name, list(shape), dt, kind="ExternalInput").ap()

    def dout(name, shape):
        return nc.dram_tensor(name, list(shape), F32, kind="ExternalOutput").ap()

    x_seq = din("x_seq", [SEQ, D])
    x_smp = din("x_smp", [64, D])
    ck = din("ck", [2, NSMP, 128, 128])
    cv = din("cv", [2, NSMP, 128, 128])
    st_in = din("st_in", [2, NSMP, 4, 128, 128])
    gains = din("gains", [40, 128])
    fin_g = din("fin_g", [1, D])
    w_in = din("w_in", [2, D, INW])
    sinks = din("sinks", [2, 8])
    lbraw = din("lbraw", [2, 512])
    onorm = din("onorm", [2, 128])
    w_o = din("w_o", [2, D, D])
    w_up = din("w_up", [2, D, DFF])
    w_dn = din("w_dn", [2, DFF, D])
    c_identf = din("c_identf", [128, 128])
    c_identb = din("c_identb", [128, 128])
    c_cosp = din("c_cosp", [128, 32, 8])
    c_sinp = din("c_sinp", [128, 32, 8])
    c_coss = din("c_coss", [128, 8])
    c_sins = din("c_sins", [128, 8])
    c_negm = din("c_negm", [128, 2, 512])
    c_mrel = din("c_mrel", [2, 128, 128])
    c_mkd = din("c_mkd", [2, 128, 128])
    c_ma = din("c_ma", [2, 128, 128])
    c_indp = din("c_indp", [128, 8])
    c_inds = din("c_inds", [128, 16])
    c_msel = din("c_msel", [128, 16])
    c_mc0 = din("c_mc0", [128, 128])
    c_mns = din("c_mns", [128, 128])
    c_selT = din("c_selT", [128, 16, 64])

    y_seq = dout("y_seq", [SEQ, D])
    y_smp = dout("y_smp", [64, D])
    nk_p = dout("nk_p", [2, 128, 128])
    nv_p = dout("nv_p", [2, 128, 128])
    ns_p = dout("ns_p", [2, 4, 128, 128])
    nk_s = dout("nk_s", [2, NSMP, 128, 128])
    nv_s = dout("nv_s", [2, NSMP, 128, 128])
    ns_s = dout("ns_s", [2, NSMP, 4, 128, 128])

    with ExitStack() as es:
        def sb(name, shape, dt=F32):
            return es.enter_context(nc.sbuf_tensor(name, list(shape), dt))

        def ps(name, shape, dt=F32):
            return es.enter_context(nc.psum_tensor(name, list(shape), dt))

        NT = TPS + 1
        xres = sb("xres", [128, NT, D])
        hT_elems = 8 * NT * 128
        ring_slot = 8 * 512 + 4 * D
        uT_elems = 4 * GRP
        rg_b = max(8 * INW + 8 * D + ring_slot, hT_elems + ring_slot + 2 * uT_elems)
        rgn32 = sb("rgn", [128, rg_b // 2])
        rgnb = rgn32[:].bitcast(BF16)
        w_in_sb = rgnb[:, 0:8 * INW].rearrange("p (k n) -> p k n", k=8)
        w_o_sb = rgnb[:, 8 * INW:8 * INW + 8 * D].rearrange("p (k n) -> p k n", k=8)
        hT_all = rgnb[:, 0:hT_elems].rearrange("p (k t) -> p k t", k=8)
        ring = []
        for o0 in (8 * INW + 8 * D, hT_elems):
            wu = rgnb[:, o0:o0 + 4096].rearrange("p (k n) -> p k n", k=8)
            wd = rgnb[:, o0 + 4096:o0 + 8192].rearrange("p (k n) -> p k n", k=4)
            ring.append((wu, wd))
        o0 = hT_elems + ring_slot
        assert o0 + 2 * uT_elems <= 8 * INW + 8 * D
        uTs = []
        for s in range(2):
            uTs.append(rgnb[:, o0:o0 + uT_elems].rearrange("p (k n) -> p k n", k=4))
            o0 += uT_elems

        identf = sb("identf", [128, 128]); identb = sb("identb", [128, 128], BF16)
        cosp = sb("cosp", [128, TPS, 8]); sinp = sb("sinp", [128, TPS, 8])
        coss = sb("coss", [128, 8]); sins = sb("sins", [128, 8])
        negm = sb("negm", [128, 2, 512], BF16)
        mrel = sb("mrel", [128, 2, 128]); mkd = sb("mkd", [128, 2, 128]); ma = sb("ma", [128, 2, 128], BF16)
        indp = sb("indp", [128, 8]); inds = sb("inds", [128, 16]); msel = sb("msel", [128, 16])
        mc0 = sb("mc0", [128, 128], BF16); mns = sb("mns", [128, 128], BF16)
        selT = sb("selT", [128, 16, 64], BF16)
        gT = sb("gT", [128, 40])
        esink = sb("esink", [128, 2, 8])
        lb1 = sb("lb1", [128, 512])
        onb = sb("onb", [128, 2, 128])
        EPS_AP = sb("eps_ap", [128, 1])

        ss = sb("ss", [128, 1]); rstd = sb("rstd", [128, 1]); nrstd = sb("nrstd", [128, 1])
        rr2 = sb("rr2", [128, NT])
        dmy = sb("dmy", [128, 1])
        hT = sb("hT", [128, 8, 128], BF16)
        junk = hT[:].rearrange("p k t -> p (k t)")
        rl = junk[:, 0:512]
        ft = sb("ft", [128, 4096])
        fing = ft[:, 0:D]
        ysb = [ft[:, D:2 * D], ft[:, 2 * D:3 * D]]
        qa = ft[:, 0:512].rearrange("p (h d) -> p h d", h=8)
        ka = sb("ka", [128, 2, 64]); va = sb("va", [128, 128])
        qab = sb("qab", [128, 8, 64], BF16); kab = sb("kab", [128, 2, 64], BF16)
        qT = [sb("qT%d" % i, [64, 8, 128], BF16) for i in range(2)]
        kT = [[sb("kT%d_%d" % (l, i), [64, 2, 128], BF16) for i in range(2)] for l in range(2)]
        vaug = [[sb("vaug%d_%d" % (l, i), [128, 2, 65], BF16) for i in range(2)] for l in range(2)]
        pT = [sb("pT%d" % i, [128, 4, 128], BF16) for i in range(2)]
        den = sb("den", [128, 4]); rden = sb("rden", [128, 4])
        mix = sb("mix", [128, D], BF16)
        e1 = ft[:, 512:1024]
        qh = ft[:, 1024:1536]; fg = ft[:, 1536:2048]; gl = ft[:, 2048:2560]; kk = ft[:, 2560:3072]
        cl = ft[:, 3072:3584]; eq = ft[:, 3584:4096]; ek = e1
        rt = [cl[:, 0:64].rearrange("p (h d) -> p h d", h=8), cl[:, 64:128].rearrange("p (h d) -> p h d", h=8),
              eq[:, 0:64].rearrange("p (h d) -> p h d", h=8), eq[:, 64:128].rearrange("p (h d) -> p h d", h=8)]
        RTK = ["cl", "cl", "eq", "eq"]
        vh = [sb("vh%d" % i, [128, 512], BF16) for i in range(2)]
        gate = [sb("gate%d" % i, [128, 512], BF16) for i in range(2)]
        qe = [sb("qe%d" % i, [128, 4, 128], BF16) for i in range(2)]
        ke = [sb("ke%d" % i, [128, 4, 128], BF16) for i in range(2)]
        kd = [sb("kd%d" % i, [128, 512], BF16) for i in range(2)]
        bg = [sb("bg%d" % i, [128, 4, 16]) for i in range(2)]
        tq = sb("tq", [128, 8, 128], BF16)
        qeT = tq[:, 0:4, :]; keT = tq[:, 4:8, :]; mixT = tq[:]
        atm = sb("atm", [128, 4, 128], BF16)
        S = [sb("S%d" % l, [128, 4, 128]) for l in range(2)]
        Sp = [sb("Sp%d" % c, [128, 4, 128], BF16) for c in range(2)]
        sso = sb("sso", [128, 4]); rso = sb("rso", [128, 4])
        kcT2 = [Sp[1][0:64, 0:2, :], Sp[1][0:64, 2:4, :]]
        kcsA = ft[:, 0:1024].bitcast(BF16).rearrange("p (j d) -> p j d", j=NSMP)
        vcsA = ft[:, 1024:2064].bitcast(BF16).rearrange("p (j g d) -> p j g d", j=NSMP, g=2)
        sstA = [ft[:, 2064:2576].rearrange("p (h d) -> p h d", h=4), ft[:, 2576:3088].rearrange("p (h d) -> p h d", h=4)]
        stmp = ft[:, 3088:3600].rearrange("p (h d) -> p h d", h=4)
        sbf = ft[:, 3600:3856].bitcast(BF16).rearrange("p (h d) -> p h d", h=4)

        PT = ps("PT", [128, 512])
        PB = ps("PB", [128, 1024], BF16)
        PJ = [ps("PJ0", [128, 512]), ps("PJ1", [128, 512])]
        PS = [ps("PS0", [128, 512]), ps("PS1", [128, 512])]
        PV = ps("PV", [128, 512])
        PO = ps("PO", [128, 512])

        P = Prog(nc, es)
        cnt = {'pj': 0, 'ps': 0, 'y': 0, 'pt': 0, 'u': 0, 'pb': 0}

        def nxt(k, n=2):
            v = cnt[k] % n
            cnt[k] += 1
            return v


        WGRP = [(0, 512), (512, 768), (768, 1280), (1280, 1792), (1792, 2304), (2304, 2816)]

        def load_pass_l(l, p_):
            slot = p_ % 2
            wu, wd = ring[slot]
            for kc in range(8):
                P.op('pool', lambda e: e.dma_start(out=wu[:, kc, :], in_=w_up[l, kc * 128:(kc + 1) * 128, p_ * 512:(p_ + 1) * 512]), writes=['ringu%d_%d' % (slot, kc)], dma='lru%d' % slot)
            for fc in range(4):
                P.op('pool', lambda e: e.dma_start(out=wd[:, fc, :], in_=w_dn[l, p_ * 512 + fc * 128: p_ * 512 + (fc + 1) * 128, :]), writes=['ringd%d_%d' % (slot, fc)], dma='lrd%d' % slot)

        def load_phase_a(l):
            for gi, (c0, c1) in enumerate(WGRP):
                P.op('pool', lambda e: e.dma_start(out=w_in_sb[:, :, c0:c1], in_=w_in[l, :, c0:c1].rearrange("(k p) n -> p k n", p=128)), writes=['w_in_g%d' % gi], dma='lw_in%d' % gi)

        def load_phase_a2(l):
            for kc in range(8):
                P.op('pool', lambda e: e.dma_start(out=w_o_sb[:, kc, :], in_=w_o[l, kc * 128:(kc + 1) * 128, :]), writes=['w_o%d' % kc], dma='lw_o')
            load_pass_l(l, 0)

        P.op('sp', lambda e: e.dma_start(out=xres[:, 0, :], in_=x_seq[0:128, :]), writes=['x0'], dma='lx0')
        load_phase_a(0)

        cl_list = [(identf, c_identf), (coss, c_coss), (sins, c_sins),
                   (indp, c_indp), (inds, c_inds), (msel, c_msel)]
        for i, (t, d) in enumerate(cl_list):
            P.op('sp', lambda e: e.dma_start(out=t[:], in_=d), writes=[t.name], dma='c%d' % i)
        P.op('sp', lambda e: e.dma_start(out=mrel[:], in_=c_mrel.rearrange("a p n -> p a n")), writes=['mrel'], dma='c20')
        P.op('sp', lambda e: e.dma_start(out=mkd[:], in_=c_mkd.rearrange("a p n -> p a n")), writes=['mkd'], dma='c21')
        cb_list = [(identb, c_identb), (negm, c_negm), (mc0, c_mc0), (mns, c_mns), (selT, c_selT)]
        for i, (t, d) in enumerate(cb_list):
            P.op('pool', lambda e: e.dma_start(out=t[:], in_=d), writes=[t.name], dma='cb%d' % i)
        P.op('pool', lambda e: e.dma_start(out=ma[:], in_=c_ma.rearrange("a p n -> p a n")), writes=['ma'], dma='cb9')
        P.op('sp', lambda e: e.dma_start(out=esink[:].rearrange("p l h -> p (l h)"), in_=sinks.rearrange("l h -> (l h)").unsqueeze(0).partition_broadcast(128)), writes=['esink'], dma='c11')
        gstage = ft[0:40, 1024:1152]
        lbr = ft[:, 0:1024].rearrange("p (l c) -> p l c", l=2)
        P.op('sp', lambda e: e.dma_start(out=ft[:, 0:1024], in_=lbraw.rearrange("l c -> (l c)").unsqueeze(0).partition_broadcast(128)), writes=['lbr'], dma='c12')
        P.op('sp', lambda e: e.dma_start(out=onb[:].rearrange("p l h -> p (l h)"), in_=onorm.rearrange("l c -> (l c)").unsqueeze(0).partition_broadcast(128)), writes=['onb'], dma='c13')
        P.op('sp', lambda e: e.dma_start(out=gstage, in_=gains), writes=['gstage'], dma='c19')
        P.op('act', lambda e: e.activation(out=esink[:], in_=esink[:], func=AF.Exp), reads=['esink'], writes=['esink'])
        P.op('dve', lambda e: e.tensor_tensor(out=lb1[:], in0=lbr[:, 0, :], in1=lbr[:, 1, :], op=ALU.subtract), reads=['lbr'], writes=['lb1'])
        P.op('act', lambda e: e.activation(out=lb1[:], in_=lb1[:], func=AF.Exp), reads=['lb1'], writes=['lb1'])
        P.op('dve', lambda e: e.tensor_scalar_add(out=lb1[:], in0=lb1[:], scalar1=1.0), reads=['lb1'], writes=['lb1'])
        P.op('dve', lambda e: e.reciprocal(out=lb1[:], in_=lb1[:]), reads=['lb1'], writes=['lb1'])
        P.op('pe', lambda e: e.transpose(PT[:, 0:40], gstage, identf[0:40, 0:40]), reads=['gstage', 'identf'], writes=['PT'])
        P.op('dve', lambda e: e.tensor_copy(out=gT[:], in_=PT[:, 0:40]), reads=['PT'], writes=['gT'])
        P.op('pool', lambda e: e.memset(dmy[:], 0.0), writes=['dmy'])
        P.op('pool', lambda e: e.memset(EPS_AP[:], EPS), writes=['eps_ap'])
        for l in range(2):
            for i in range(2):
                P.op('pool', lambda e: e.memset(vaug[l][i][:], 1.0), writes=['vaug%d_%d' % (l, i)])
        P.op('pool', lambda e: e.memset(xres[:, TPS, :], 0.0), writes=['x%d' % TPS])
        for l in range(2):
            P.op('pool', lambda e: e.memset(S[l][:], 0.0), writes=['S%d' % l])
        P.barrier(skip=('lw_in', 'lw_o', 'lru', 'lrd', 'lx'))

        def rms_stats(xt_ap, xkey, out_r, out_key, also_neg=None, sq=None):
            def f(e):
                e.activation(out=junk, in_=xt_ap, func=AF.Square, accum_out=ss[:])
                return e.activation(out=dmy[:], in_=dmy[:], func=AF.Copy)
            P.op('act', f, reads=[xkey], writes=['hTa', 'hTb', 'ss', 'dmy'])
            P.op('act', lambda e: e.activation(out=ss[:], in_=ss[:], func=AF.Ln, scale=1.0 / D, bias=EPS_AP[:]), reads=['ss', 'eps_ap'], writes=['ss'])
            P.op('act', lambda e: e.activation(out=out_r, in_=ss[:], func=AF.Exp, scale=-0.5), reads=['ss'], writes=[out_key])
            if also_neg is not None:
                P.op('dve', lambda e: e.tensor_scalar_mul(out=also_neg, in0=out_r, scalar1=-1.0), reads=[out_key], writes=['nrstd'])
            if sq is not None:
                P.op('act', lambda e: e.activation(out=sq, in_=ss[:], func=AF.Exp, scale=-1.0), reads=['ss'], writes=['rr2'])

        def make_hT(ti, gcol, dst, dst_key, pj_only=False):
            for half in range(2):
                if half == 0 and not pj_only:
                    bank, bkey = PT, 'PT'
                else:
                    pj_ = nxt('pj')
                    bank, bkey = PJ[pj_], 'PJ%d' % pj_
                def f(e):
                    ins = None
                    for j in range(4):
                        kc = half * 4 + j
                        ins = e.transpose(bank[:, j * 128:(j + 1) * 128], xres[:, ti, kc * 128:(kc + 1) * 128], identf[:])
                    return ins
                P.op('pe', f, reads=['x%d' % ti, 'identf'], writes=[bkey])
                P.op('dve', lambda e: e.tensor_tensor(
                    out=dst[:, half * 4:half * 4 + 4, :], in0=bank[:].rearrange("p (k t) -> p k t", k=4),
                    in1=bc(gT[:, gcol * 8 + half * 4:gcol * 8 + half * 4 + 4], 2, [128, 4, 128]), op=ALU.mult),
                    reads=[bkey, 'gT'], writes=[(dst_key + 'ab'[half]) if dst_key == 'hT' else dst_key])
                yield

        WO_KEYS = ['w_o%d' % k_ for k_ in range(8)]

        def proj_group(c0, c1):
            pj = nxt('pj')
            for hf in range(2):
                def f(e):
                    ins = None
                    for kc in range(hf * 4, hf * 4 + 4):
                        ins = e.matmul(PJ[pj][:, 0:c1 - c0], lhsT=hT[:, kc, :], rhs=w_in_sb[:, kc, c0:c1], start=(kc == 0), stop=(kc == 7))
                    return ins
                P.op('pe', f, reads=['hT' + 'ab'[hf], 'w_in_g%d' % WGRP.index((c0, c1))], writes=['PJ%d' % pj])
            return pj

        def silu_from_psum(pj, dst, dst_key):
            k = 'PJ%d' % pj
            P.op('act', lambda e: e.activation(out=dst, in_=PJ[pj][:], func=AF.Copy, scale=rstd[:]), reads=[k, 'rstd'], writes=[dst_key])
            P.op('act', lambda e: e.activation(out=e1[:], in_=dst, func=AF.Exp, scale=-1.0), reads=[dst_key], writes=['e1'])
            P.op('act', lambda e: e.activation(out=e1[:], in_=e1[:], func=AF.Ln, bias=1.0), reads=['e1'], writes=['e1'])
            P.op('act', lambda e: e.activation(out=e1[:], in_=e1[:], func=AF.Exp, scale=-1.0), reads=['e1'], writes=['e1'])
            P.op('pool', lambda e: e.tensor_tensor(out=dst, in0=dst, in1=e1[:], op=ALU.mult), reads=[dst_key, 'e1'], writes=[dst_key])

        def rotary(src, nh, cos_ap, sin_ap, dstb, skey, dkey):
            x1 = src[:, :, 0:8]; x2 = src[:, :, 8:16]
            cb = bc(cos_ap, 1, [128, nh, 8]); sn = bc(sin_ap, 1, [128, nh, 8])
            r0, r1, r2, r3 = [t[:, 0:nh, :] for t in rt]
            ck_ = ['cosp', 'sinp']
            P.op('pool', lambda e: e.tensor_tensor(out=r0, in0=x1, in1=cb, op=ALU.mult), reads=[skey] + ck_, writes=[RTK[0]])
            P.op('pool', lambda e: e.tensor_tensor(out=r1, in0=x2, in1=sn, op=ALU.mult), reads=[skey] + ck_, writes=[RTK[1]])
            P.op('pool', lambda e: e.tensor_tensor(out=r2, in0=x2, in1=cb, op=ALU.mult), reads=[skey] + ck_, writes=[RTK[2]])
            P.op('pool', lambda e: e.tensor_tensor(out=r3, in0=x1, in1=sn, op=ALU.mult), reads=[skey] + ck_, writes=[RTK[3]])
            P.op('pool', lambda e: e.tensor_tensor(out=x1, in0=r0, in1=r1, op=ALU.subtract), reads=['cl', skey], writes=[skey])
            P.op('pool', lambda e: e.tensor_tensor(out=x2, in0=r2, in1=r3, op=ALU.add), reads=['eq', skey], writes=[skey])
            P.op('pool', lambda e: e.tensor_copy(out=dstb, in_=src), reads=[skey], writes=[dkey])

        def transposes_bf(srcs, src_keys, dst, dst_key, rows, evac='act'):
            n = len(srcs)
            dkeys = dst_key if isinstance(dst_key, list) else [dst_key]
            def f(e):
                ins = None
                for i, s_ in enumerate(srcs):
                    ins = e.transpose(PB[0:rows, i * 128:i * 128 + 128], s_, identb[:])
                return ins
            P.op('pe', f, reads=list(src_keys) + ['identb'], writes=['PB'])
            src_v = PB[0:rows, 0:n * 128].rearrange("p (k t) -> p k t", k=n)
            if evac == 'act':
                P.op('act', lambda e: e.activation(out=dst, in_=src_v, func=AF.Copy), reads=['PB'], writes=dkeys)
            else:
                P.op('dve', lambda e: e.tensor_copy(out=dst, in_=src_v), reads=['PB'], writes=dkeys)

        def F_tile(l, ti, seg, is_smp, par):
            xk = 'x%d' % ti
            rms_stats(xres[:, ti, :], xk, rstd[:], 'rstd', also_neg=nrstd[:])
            for _ in make_hT(ti, l, hT, 'hT'):
                yield
            kTc, vac = kT[l][par], vaug[l][par]
            kTck, vack = 'kT%d_%d' % (l, par), 'vaug%d_%d' % (l, par)
            if is_smp:
                cos_ap, sin_ap = coss[:], sins[:]
            else:
                cos_ap, sin_ap = cosp[:, ti, :], sinp[:, ti, :]
            pj = proj_group(0, 512)
            P.op('act', lambda e: e.activation(out=qa[:].rearrange("p h d -> p (h d)"), in_=PJ[pj][:], func=AF.Copy, scale=rstd[:]),
                 reads=['PJ%d' % pj, 'rstd'], writes=['qa'])
            yield
            rotary(qa[:], 8, cos_ap, sin_ap, qab[:], 'qa', 'qab')
            pj = proj_group(512, 768)
            P.op('act', lambda e: e.activation(out=ka[:].rearrange("p h d -> p (h d)"), in_=PJ[pj][:, 0:128], func=AF.Copy, scale=rstd[:]),
                 reads=['PJ%d' % pj, 'rstd'], writes=['ka'])
            P.op('act', lambda e: e.activation(out=va[:], in_=PJ[pj][:, 128:256], func=AF.Copy, scale=rstd[:]),
                 reads=['PJ%d' % pj, 'rstd'], writes=['va'])
            yield
            rotary(ka[:], 2, cos_ap, sin_ap, kab[:], 'ka', 'kab')
            P.op('pool', lambda e: e.tensor_copy(out=vac[:, :, 0:64], in_=va[:].rearrange("p (g d) -> p g d", g=2)), reads=['va'], writes=[vack])
            if is_smp:
                for t in range(4):
                    P.op('sp', lambda e: e.dma_start(out=nk_s[l, :, 124 + t, :], in_=ka[t * 16:(t + 1) * 16].rearrange("p h d -> p (h d)")), reads=['ka'], dma='o_nk%d' % t)
                    P.op('sp', lambda e: e.dma_start(out=nv_s[l, :, 124 + t, :], in_=va[t * 16:(t + 1) * 16, :]), reads=['va'], dma='o_nv%d' % t)
            elif seg == NSEG - 1 and ti == TPS - 1:
                P.op('sp', lambda e: e.dma_start(out=nk_p[l], in_=ka[:].rearrange("p h d -> p (h d)")), reads=['ka'], dma='o_nkp')
                P.op('sp', lambda e: e.dma_start(out=nv_p[l], in_=va[:]), reads=['va'], dma='o_nvp')
            pj = proj_group(768, 1280)
            silu_from_psum(pj, qh[:], 'qh')
            yield
            pj = proj_group(1280, 1792)
            k = 'PJ%d' % pj
            P.op('act', lambda e: e.activation(out=e1[:], in_=PJ[pj][:], func=AF.Exp, scale=nrstd[:]), reads=[k, 'nrstd'], writes=['e1'])
            P.op('act', lambda e: e.activation(out=e1[:], in_=e1[:], func=AF.Ln, bias=1.0), reads=['e1'], writes=['e1'])
            P.op('act', lambda e: e.activation(out=fg[:], in_=e1[:], func=AF.Exp, scale=-1.0), reads=['e1'], writes=['fg'])
            yield
            transposes_bf([qab[:, h, :] for h in range(8)], ['qab'], qT[par][:], 'qT%d' % par, 64)
            yield
            P.op('pool', lambda e: e.tensor_scalar(out=kk[:], in0=fg[:], scalar1=-1.0, scalar2=1.0, op0=ALU.mult, op1=ALU.add), reads=['fg'], writes=['kk'])
            if l == 1:
                P.op('dve', lambda e: e.tensor_tensor(out=fg[:], in0=kk[:], in1=lb1[:], op=ALU.mult), reads=['kk', 'lb1'], writes=['fg'])
                P.op('pool', lambda e: e.tensor_tensor(out=kk[:], in0=kk[:], in1=fg[:], op=ALU.subtract), reads=['kk', 'fg'], writes=['kk'])
                P.op('pool', lambda e: e.tensor_scalar(out=fg[:], in0=kk[:], scalar1=-1.0, scalar2=1.0, op0=ALU.mult, op1=ALU.add), reads=['kk'], writes=['fg'])
            P.op('act', lambda e: e.activation(out=gl[:], in_=fg[:], func=AF.Ln), reads=['fg'], writes=['gl'])
            pj = proj_group(1792, 2304)
            P.op('act', lambda e: e.activation(out=vh[par][:], in_=PJ[pj][:], func=AF.Copy, scale=rstd[:]), reads=['PJ%d' % pj, 'rstd'], writes=['vh%d' % par])
            yield
            pj = proj_group(2304, 2816)
            silu_from_psum(pj, fg[:], 'fg')
            P.op('pool', lambda e: e.tensor_tensor(out=gate[par][:].rearrange("p (h d) -> p h d", h=4), in0=fg[:].rearrange("p (h d) -> p h d", h=4),
                                                   in1=bc(onb[:, l, :], 1, [128, 4, 128]), op=ALU.mult), reads=['fg', 'onb'], writes=['gate%d' % par])
            yield
            mi = 1 if is_smp else 0
            p1 = nxt('pj'); p2 = nxt('pj')
            P.op('pe', lambda e: e.matmul(PJ[p1][:], lhsT=mrel[:, mi, :], rhs=gl[:], start=True, stop=True), reads=['mrel', 'gl'], writes=['PJ%d' % p1])
            P.op('pe', lambda e: e.matmul(PJ[p2][:], lhsT=mkd[:, mi, :], rhs=gl[:], start=True, stop=True), reads=['mkd', 'gl'], writes=['PJ%d' % p2])
            P.op('dve', lambda e: e.tensor_scalar(out=cl[:], in0=PJ[p1][:], scalar1=-40.0, scalar2=40.0, op0=ALU.max, op1=ALU.min), reads=['PJ%d' % p1], writes=['cl'])
            P.op('act', lambda e: e.activation(out=eq[:], in_=cl[:], func=AF.Exp), reads=['cl'], writes=['eq'])
            P.op('act', lambda e: e.activation(out=ek[:], in_=cl[:], func=AF.Exp, scale=-1.0), reads=['cl'], writes=['e1'])
            P.op('act', lambda e: e.activation(out=cl[:], in_=PJ[p2][:], func=AF.Exp), reads=['PJ%d' % p2, 'cl'], writes=['cl'])
            P.op('dve', lambda e: e.tensor_tensor(out=qe[par][:].rearrange("p h d -> p (h d)"), in0=qh[:], in1=eq[:], op=ALU.mult), reads=['qh', 'eq'], writes=['qe%d' % par])
            P.op('dve', lambda e: e.tensor_tensor(out=ke[par][:].rearrange("p h d -> p (h d)"), in0=kk[:], in1=ek[:], op=ALU.mult), reads=['kk', 'e1'], writes=['ke%d' % par])
            P.op('dve', lambda e: e.tensor_tensor(out=kd[par][:], in0=kk[:], in1=cl[:], op=ALU.mult), reads=['kk', 'cl'], writes=['kd%d' % par])
            yield
            nind = 16 if is_smp else 8
            ind_ap = inds[:] if is_smp else indp[:]
            p3 = nxt('pj')
            def fbg(e):
                ins = None
                for h in range(4):
                    ins = e.matmul(PJ[p3][:, h * 16:h * 16 + nind], lhsT=gl[:, h * 128:(h + 1) * 128], rhs=ind_ap, start=True, stop=True)
                return ins
            P.op('pe', fbg, reads=['gl', 'inds', 'indp'], writes=['PJ%d' % p3])
            P.op('act', lambda e: e.activation(out=bg[par][:, :, 0:nind], in_=PJ[p3][:, 0:64].rearrange("p (h c) -> p h c", h=4)[:, :, 0:nind], func=AF.Exp),
                 reads=['PJ%d' % p3], writes=['bg%d' % par])
            yield

            transposes_bf([kab[:, g, :] for g in range(2)], ['kab'], kTc[:], kTck, 64)
            yield

        def B_tile(l, ti, seg, is_smp, par):
            xk = 'x%d' % ti
            prv = 1 - par
            kTc, vac = kT[l][par], vaug[l][par]
            kTck, vack = 'kT%d_%d' % (l, par), 'vaug%d_%d' % (l, par)
            qTc, qTk = qT[par], 'qT%d' % par
            mi = 1 if is_smp else 0
            if is_smp:
                blocks = [('c', j) for j in range(NSMP)] + [('n', 0)]
            else:
                blocks = ([('p', 0)] if not (seg == 0 and ti == 0) else []) + [('u', 0)]
            nb = len(blocks)
            pv_bank = {0: (PV, 'PV'), 1: ((PT, 'PT') if is_smp else (PV, 'PV'))}

            def finish_g(g):
                bank, bkey = pv_bank[g]
                pv4 = bank[:].rearrange("p (h c) -> p h c", h=4)
                P.op('dve', lambda e: e.tensor_tensor(out=den[:], in0=pv4[:, :, 64], in1=esink[:, l, 4 * g:4 * g + 4], op=ALU.add), reads=[bkey, 'esink'], writes=['den'])
                P.op('dve', lambda e: e.reciprocal(out=rden[:], in_=den[:]), reads=['den'], writes=['rden'])
                P.op('dve', lambda e: e.tensor_tensor(out=mix[:, g * 256:(g + 1) * 256].rearrange("p (h d) -> p h d", h=4), in0=pv4[:, :, 0:64],
                                                      in1=bc(rden[:], 2, [128, 4, 64]), op=ALU.mult), reads=[bkey, 'rden'], writes=['mixa'])

            def block_g(g, bi, bk, j, mask_ap, mask_key, lhs, lk, rv, rvk, neg=None):
                bank, bkey = pv_bank[g]
                psi = nxt('ps')
                pk = 'PS%d' % psi
                def fsc(e):
                    ins = e.matmul(PS[psi][:], lhsT=lhs, rhs=qTc[:, 4 * g:4 * g + 4, :], start=True, stop=(neg is None))
                    if neg is not None:
                        ins = e.matmul(PS[psi][:], lhsT=identb[:], rhs=neg, start=False, stop=True)
                    return ins
                P.op('pe', fsc, reads=[lk, qTk, 'identb', 'negm'], writes=[pk])
                pi = nxt('pt')
                P.op('act', lambda e: e.activation(out=pT[pi][:].rearrange("p h t -> p (h t)"), in_=PS[psi][:], func=AF.Exp, scale=0.125),
                     reads=[pk], writes=['pT%d' % pi])
                if neg is None:
                    P.op('pool', lambda e: e.tensor_tensor(out=pT[pi][:], in0=pT[pi][:], in1=bc(mask_ap, 1, [128, 4, 128]), op=ALU.mult),
                         reads=['pT%d' % pi, mask_key], writes=['pT%d' % pi])
                yield 1
                def fpv(e):
                    ins = None
                    for hh in range(4):
                        ins = e.matmul(bank[:, hh * 128:hh * 128 + 65], lhsT=pT[pi][:, hh, :], rhs=rv,
                                       start=(bi == 0 and hh == 0), stop=(bi == nb - 1), skip_group_check=True)
                    return ins
                P.op('pe', fpv, reads=['pT%d' % pi, rvk], writes=[bkey])
                yield 0

            if not is_smp:
                for g in range(2):
                    for bi, (bk, j) in enumerate(blocks):
                        if bk == 'p':
                            yield from block_g(g, bi, bk, j, None, None, kT[l][prv][:, g, :], 'kT%d_%d' % (l, prv), vaug[l][prv][:, g, :], 'vaug%d_%d' % (l, prv), neg=negm[:, 1, :])
                        else:
                            yield from block_g(g, bi, bk, j, None, None, kTc[:, g, :], kTck, vac[:, g, :], vack, neg=negm[:, 0, :])
                    finish_g(g)
            else:
                P.barrier()
                P.op('pool', lambda e: e.memset(ft[:, 1024:2064], 0.0), writes=['vcsA'])
                P.op('dve', lambda e: e.memset(vcsA[:, :, :, 64], 1.0), reads=['vcsA'], writes=['vcsA'])
                for i_ in range(2):
                    P.op('dve', lambda e: e.memset(pT[i_][:], 0.0), writes=['pT%d' % i_])
                P.op('pool', lambda e: e.dma_start(out=kcsA, in_=ck[l].rearrange("j k d -> k j d")), writes=['kcsA'], dma='l_kc')
                for g in range(2):
                    P.op('pool', lambda e: e.dma_start(out=vcsA[:, :, g, 0:64], in_=cv[l, :, :, g * 64:(g + 1) * 64].rearrange("j k d -> k j d")), reads=['vcsA'], writes=['vcsA'], dma='l_vc%d' % g)

            def att_seq(bi, j):
                kcT = kcT2[j % 2]
                kck = 'kcT%d' % (j % 2)
                transposes_bf([kcsA[:, j, g * 64:(g + 1) * 64] for g in range(2)], ['kcsA'], kcT, kck, 64)
                for g in range(2):
                    bank, bkey = pv_bank[g]
                    psi = nxt('ps')
                    pk = 'PS%d' % psi
                    P.op('pe', lambda e: e.matmul(PS[psi][:, 0:16], lhsT=kcT[:, g, :], rhs=qTc[:, 4 * g:4 * g + 4, j:64:16], start=True, stop=True),
                         reads=[kck, qTk], writes=[pk])
                    pi = nxt('pt')
                    pslice = pT[pi][:, :, j:64:16]
                    P.op('act', lambda e: e.activation(out=pslice, in_=PS[psi][:, 0:16].rearrange("p (h t) -> p h t", h=4), func=AF.Exp, scale=0.125),
                         reads=[pk], writes=['pT%d' % pi])
                    P.op('dve', lambda e: e.tensor_tensor(out=pslice, in0=pslice, in1=bc(mc0[:, j:64:16], 1, [128, 4, 4]), op=ALU.mult),
                         reads=['pT%d' % pi, 'mc0'], writes=['pT%d' % pi])
                    def fpv(e):
                        ins = None
                        for hh in range(4):
                            ins = e.matmul(bank[:, hh * 128:hh * 128 + 65], lhsT=pT[pi][:, hh, :], rhs=vcsA[:, j, g, :],
                                           start=(bi == 0 and hh == 0), stop=False, skip_group_check=True)
                        return ins
                    P.op('pe', fpv, reads=['pT%d' % pi, 'vcsA'], writes=[bkey])
                    P.op('dve', lambda e: e.memset(pslice, 0.0), reads=['pT%d' % pi], writes=['pT%d' % pi])

            def att_epi():
                bi, (bk, j) = nb - 1, blocks[-1]
                for g in range(2):
                    for _ in block_g(g, bi, bk, j, mns[:], 'mns', kTc[:, g, :], kTck, vac[:, g, :], vack):
                        pass
                finish_g(0)
                finish_g(1)

            qec, kec, kdc, bgc, vhc = qe[par], ke[par], kd[par], bg[par], vh[par]
            qek, kek, kdk, bgk, vhk = 'qe%d' % par, 'ke%d' % par, 'kd%d' % par, 'bg%d' % par, 'vh%d' % par
            transposes_bf([qec[:, h, :] for h in range(4)], [qek], qeT[:], 'qeT', 128)
            yield 1
            transposes_bf([kec[:, h, :] for h in range(4)], [kek], keT[:], 'keT', 128, evac='dve')
            yield 1
            p4 = nxt('ps')
            def fa(e):
                ins = None
                for h in range(4):
                    ins = e.matmul(PS[p4][:, h * 128:(h + 1) * 128], lhsT=keT[:, h, :], rhs=qeT[:, h, :], start=True, stop=True)
                return ins
            P.op('pe', fa, reads=['keT', 'qeT'], writes=['PS%d' % p4])
            P.op('dve', lambda e: e.tensor_tensor(out=atm[:], in0=PS[p4][:].rearrange("p (h t) -> p h t", h=4), in1=bc(ma[:, mi, :], 1, [128, 4, 128]), op=ALU.mult),
                 reads=['PS%d' % p4, 'ma'], writes=['atm'])
            yield 0
            def fo(e):
                ins = None
                for h in range(4):
                    ins = e.matmul(PO[:, h * 128:(h + 1) * 128], lhsT=atm[:, h, :], rhs=vhc[:, h * 128:(h + 1) * 128],
                                   start=(h == 0), stop=False, skip_group_check=True)
                return ins
            P.op('pe', fo, reads=['atm', vhk], writes=['PO'])
            yield 0
            Sl = S[l]
            Sk = 'S%d' % l
            if not is_smp:
                qeTs = keT
                P.op('dve', lambda e: e.tensor_tensor(out=qeTs.rearrange("p h (c t) -> p h c t", c=4), in0=qeT.rearrange("p h (c t) -> p h c t", c=4),
                                                      in1=bc(bgc[:, :, 4:8], 3, [128, 4, 4, 32]), op=ALU.mult),
                     reads=['qeT', 'keT', bgk], writes=['keT'])
                P.op('dve', lambda e: e.tensor_copy(out=Sp[0][:], in_=Sl[:]), reads=[Sk], writes=['Sp0'])
                for c in range(4):
                    pu = nxt('ps')
                    def fu(e):
                        ins = None
                        for h in range(4):
                            ins = e.matmul(PS[pu][:, h * 128:(h + 1) * 128], lhsT=kdc[32 * c:32 * c + 32, h * 128:(h + 1) * 128],
                                           rhs=vhc[32 * c:32 * c + 32, h * 128:(h + 1) * 128], start=True, stop=True, tile_position=(32 * c, 0))
                        return ins
                    P.op('pe', fu, reads=[kdk, vhk], writes=['PS%d' % pu])
                    def fupd(e):
                        ins = None
                        for h in range(4):
                            ins = e.scalar_tensor_tensor(out=Sl[:, h, :], in0=Sl[:, h, :], scalar=bgc[:, h, c:c + 1], in1=PS[pu][:, h * 128:(h + 1) * 128],
                                                         op0=ALU.mult, op1=ALU.add)
                        return ins
                    if c < 3:
                        def fupb(e):
                            ins = None
                            for h in range(4):
                                ins = e.scalar_tensor_tensor(out=Sp[(c + 1) % 2][:, h, :], in0=Sl[:, h, :], scalar=bgc[:, h, c:c + 1], in1=PS[pu][:, h * 128:(h + 1) * 128],
                                                             op0=ALU.mult, op1=ALU.add)
                            return ins
                        P.op('dve', fupb, reads=[Sk, bgk, 'PS%d' % pu], writes=['Sp%d' % ((c + 1) % 2)])
                    P.op('dve', fupd, reads=[Sk, bgk, 'PS%d' % pu], writes=[Sk])
                    if c == 3:
                        yield 3
                    def foi(e):
                        ins = None
                        for h in range(4):
                            ins = e.matmul(PO[32 * c:32 * c + 32, h * 128:(h + 1) * 128], lhsT=qeTs[:, h, 32 * c:32 * c + 32], rhs=Sp[c % 2][:, h, :],
                                           start=False, stop=(c == 3 and h == 3), skip_group_check=True, tile_position=(0, 32 * c))
                        return ins
                    P.op('pe', foi, reads=['keT', 'Sp%d' % (c % 2)], writes=['PO'])
                    if c < 3:
                        yield 1
                if seg == NSEG - 1 and ti == TPS - 1:
                    P.op('sp', lambda e: e.dma_start(out=ns_p[l].rearrange("h k v -> k h v"), in_=Sl[:]), reads=[Sk], dma='o_nsp')
            else:
                kdm = atm[:].rearrange("p h d -> p (h d)")
                qeTm = Sp[0][:, :, 0:64]
                sbf2 = [sbf, mix[:, 512:1024].rearrange("p (h d) -> p h d", h=4)]
                sbk = ['sbfA', 'mixh']
                def ld_state(jj):
                    s_ = jj % 2
                    P.op('sp', lambda e: e.dma_start(out=sstA[s_], in_=st_in[l, jj].rearrange("h k v -> k h v")), writes=['sstA%d' % s_], dma='l_st%d' % s_)
                    P.op('pool', lambda e: e.dma_start(out=sbf2[s_], in_=st_in[l, jj].rearrange("h k v -> k h v")), writes=[sbk[s_]], dma='l_sb%d' % s_)
                ld_state(0)
                for j in range(NSMP):
                    si = j % 2
                    if j + 1 < NSMP:
                        ld_state(j + 1)
                    att_seq(j, j)
                    P.op('dve', lambda e: e.tensor_tensor(out=qeTm, in0=qeT[:, :, 0:64], in1=bc(selT[:, j, :], 1, [128, 4, 64]), op=ALU.mult),
                         reads=['qeT', 'selT'], writes=['Sp0'])
                    def foi(e):
                        ins = None
                        for h in range(4):
                            ins = e.matmul(PO[0:64, h * 128:(h + 1) * 128], lhsT=qeTm[:, h, :], rhs=sbf2[si][:, h, :],
                                           start=False, stop=(j == NSMP - 1), skip_group_check=True)
                        return ins
                    P.op('pe', foi, reads=['Sp0', sbk[si]], writes=['PO'])
                    P.op('act', lambda e: e.activation(out=kdm, in_=kdc[:], func=AF.Copy, scale=msel[:, j:j + 1]), reads=[kdk, 'msel'], writes=['atm'])
                    pu = nxt('ps')
                    def fu(e):
                        ins = None
                        for h in range(4):
                            ins = e.matmul(PS[pu][:, h * 128:(h + 1) * 128], lhsT=kdm[:, h * 128:(h + 1) * 128], rhs=vhc[:, h * 128:(h + 1) * 128], start=True, stop=True)
                        return ins
                    P.op('pe', fu, reads=['atm', vhk], writes=['PS%d' % pu])
                    P.op('dve', lambda e: e.tensor_tensor(out=stmp, in0=sstA[si], in1=bc(bgc[:, :, j], 2, [128, 4, 128]), op=ALU.mult),
                         reads=['sstA%d' % si, bgk], writes=['stmpA'])
                    P.op('dve', lambda e: e.tensor_tensor(out=stmp.rearrange("p h d -> p (h d)"), in0=stmp.rearrange("p h d -> p (h d)"), in1=PS[pu][:], op=ALU.add),
                         reads=['stmpA', 'PS%d' % pu], writes=['stmpA'])
                    P.op('sp', lambda e: e.dma_start(out=ns_s[l, j].rearrange("h k v -> k h v"), in_=stmp), reads=['stmpA'], dma='o_st')
                    yield 1
                att_epi()
            def fsq(e):
                for h in range(4):
                    e.activation(out=atm[:, h, :], in_=PO[:, h * 128:(h + 1) * 128], func=AF.Square, accum_out=sso[:, h:h + 1])
                return e.activation(out=dmy[:], in_=dmy[:], func=AF.Copy)
            P.op('act', fsq, reads=['PO'], writes=['atm', 'sso', 'dmy'])
            P.op('act', lambda e: e.activation(out=sso[:], in_=sso[:], func=AF.Ln, scale=1.0 / 128.0, bias=EPS_AP[:]), reads=['sso', 'eps_ap'], writes=['sso'])
            P.op('act', lambda e: e.activation(out=rso[:], in_=sso[:], func=AF.Exp, scale=-0.5), reads=['sso'], writes=['rso'])
            def fmx(e):
                ins = None
                for h in range(4):
                    ins = e.scalar_tensor_tensor(out=mix[:, 512 + h * 128:512 + (h + 1) * 128], in0=PO[:, h * 128:(h + 1) * 128], scalar=rso[:, h:h + 1],
                                                 in1=gate[par][:, h * 128:(h + 1) * 128], op0=ALU.mult, op1=ALU.mult)
                return ins
            P.op('dve', fmx, reads=['PO', 'rso', 'gate%d' % par], writes=['mixh'])
            transposes_bf([mix[:, i * 128:(i + 1) * 128] for i in range(4)], ['mixa'], mixT[:, 0:4, :], 'qeT', 128)
            wob = [(PV, 'PV'), (PT, 'PT')]
            for half in range(2):
                wbank, wkey = wob[half]
                def fw0(e):
                    ins = None
                    for kc in range(4):
                        ins = e.matmul(wbank[:], lhsT=mixT[:, kc, :], rhs=w_o_sb[:, kc, half * 512:(half + 1) * 512], start=(kc == 0), stop=False)
                    return ins
                P.op('pe', fw0, reads=['qeT'] + WO_KEYS, writes=[wkey])
            yield 99
            transposes_bf([mix[:, i * 128:(i + 1) * 128] for i in range(4, 8)], ['mixh'], mixT[:, 4:8, :], 'keT', 128, evac='dve')
            for half in range(2):
                wbank, wkey = wob[half]
                def fw1(e):
                    ins = None
                    for kc in range(4, 8):
                        ins = e.matmul(wbank[:], lhsT=mixT[:, kc, :], rhs=w_o_sb[:, kc, half * 512:(half + 1) * 512], start=False, stop=(kc == 7))
                    return ins
                P.op('pe', fw1, reads=['keT'] + WO_KEYS, writes=[wkey])
                P.op('dve', lambda e: e.tensor_tensor(out=xres[:, ti, half * 512:(half + 1) * 512], in0=xres[:, ti, half * 512:(half + 1) * 512], in1=wbank[:], op=ALU.add),
                     reads=[xk, wkey], writes=[xk])
                yield 0

        def drain(g):
            for _ in g:
                pass

        def interleave(gb, gf):
            done_f = False
            for nf in gb:
                for _ in range(nf or 0):
                    if done_f:
                        break
                    try:
                        next(gf)
                    except StopIteration:
                        done_f = True
            if not done_f:
                drain(gf)

        for seg in range(NSEG):
            tiles = list(range(TPS)) + ([TPS] if seg == 0 else [])
            for ti in range(TPS):
                if ti == 0 or seg > 0:
                    continue
                P.op('sp', lambda e: e.dma_start(out=xres[:, ti, :], in_=x_seq[(seg * TPS + ti) * 128:(seg * TPS + ti + 1) * 128, :]),
                     writes=['x%d' % ti], dma='lx%d' % ti)
            P.op('sp', lambda e: e.dma_start(out=cosp[:], in_=c_cosp[:, seg * TPS:(seg + 1) * TPS, :]), writes=['cosp'], dma='l_cos')
            P.op('sp', lambda e: e.dma_start(out=sinp[:], in_=c_sinp[:, seg * TPS:(seg + 1) * TPS, :]), writes=['sinp'], dma='l_sin')
            if seg == 0:
                P.op('sp', lambda e: e.dma_start(out=xres[0:64, TPS, :], in_=x_smp), writes=['x%d' % TPS], dma='lx%d' % TPS)
            for l in range(2):
                if not (seg == 0 and l == 0):
                    P.barrier()
                def load_pass(p_):
                    load_pass_l(l, p_)
                if l != 0:
                    load_phase_a(l)
                if seg == 0:
                    P.op('sp', lambda e: e.dma_start(out=nk_s[l, :, 0:124, :], in_=ck[l, :, 4:128, :]), dma='o_ckc')
                    P.op('sp', lambda e: e.dma_start(out=nv_s[l, :, 0:124, :], in_=cv[l, :, 4:128, :]), dma='o_cvc')
                n = len(tiles)
                prepped = set()
                drain(F_tile(l, tiles[0], seg, tiles[0] == TPS, 0))
                load_phase_a2(l)
                for tix, ti in enumerate(tiles):
                    gb = B_tile(l, ti, seg, ti == TPS, tix % 2)
                    if tix + 1 < n:
                        gf = F_tile(l, tiles[tix + 1], seg, tiles[tix + 1] == TPS, (tix + 1) % 2)
                        interleave(gb, gf)
                    elif ti == TPS:
                        def prep_gen():
                            for t_ in range(TPS):
                                rms_stats(xres[:, t_, :], 'x%d' % t_, rstd[:], 'rstd', sq=rr2[:, t_:t_ + 1])
                                for _ in make_hT(t_, 2 + l, hT_all[:, :, t_ * 128:(t_ + 1) * 128], 'hTall', pj_only=True):
                                    pass
                                prepped.add(t_)
                                yield
                        interleave(gb, prep_gen())
                    else:
                        drain(gb)
                P.barrier()
                ntok = n * 128
                for ti in tiles:
                    if ti in prepped:
                        continue
                    rms_stats(xres[:, ti, :], 'x%d' % ti, rstd[:], 'rstd', sq=rr2[:, ti:ti + 1])
                    drain(make_hT(ti, 2 + l, hT_all[:, :, ti * 128:(ti + 1) * 128], 'hTall'))
                groups = [(g0, min(GRP, ntok - g0)) for g0 in range(0, ntok, GRP)]

                def up(p_, g0, gn, ui):
                    slot = p_ % 2
                    wu, wd = ring[slot]
                    uT = uTs[ui]
                    for fc in range(4):
                        psi = nxt('ps')
                        def fup(e):
                            ins = None
                            for kc in range(8):
                                ins = e.matmul(PS[psi][:, 0:gn], lhsT=wu[:, kc, fc * 128:(fc + 1) * 128], rhs=hT_all[:, kc, g0:g0 + gn], start=(kc == 0), stop=(kc == 7))
                            return ins
                        P.op('pe', fup, reads=['ringu%d_%d' % (slot, k_) for k_ in range(8)] + ['hTall'], writes=['PS%d' % psi])
                        P.op('act', lambda e: e.activation(out=rl[:, 0:gn], in_=PS[psi][:, 0:gn], func=AF.Relu), reads=['PS%d' % psi], writes=['hTa'])
                        P.op('dve', lambda e: e.tensor_tensor(out=uT[:, fc, 0:gn], in0=rl[:, 0:gn], in1=rl[:, 0:gn], op=ALU.mult), reads=['hTa'], writes=['uT%d' % ui])

                def down(p_, g0, gn, ui):
                    slot = p_ % 2
                    wu, wd = ring[slot]
                    uT = uTs[ui]
                    for tt in range(gn // 128):
                        ti = (g0 // 128) + tt
                        for half in range(2):
                            pj = nxt('pj')
                            def fdn(e):
                                ins = None
                                for fc in range(4):
                                    ins = e.matmul(PJ[pj][:], lhsT=uT[:, fc, tt * 128:(tt + 1) * 128], rhs=wd[:, fc, half * 512:(half + 1) * 512], start=(fc == 0), stop=(fc == 3))
                                return ins
                            P.op('pe', fdn, reads=['uT%d' % ui] + ['ringd%d_%d' % (slot, k_) for k_ in range(4)], writes=['PJ%d' % pj])
                            P.op('dve', lambda e: e.scalar_tensor_tensor(
                                out=xres[:, ti, half * 512:(half + 1) * 512], in0=PJ[pj][:], scalar=rr2[:, ti:ti + 1],
                                in1=xres[:, ti, half * 512:(half + 1) * 512], op0=ALU.mult, op1=ALU.add),
                                reads=['PJ%d' % pj, 'rr2', 'x%d' % ti], writes=['x%d' % ti])

                work = [(p_, g0, gn) for p_ in range(NPASS) for (g0, gn) in groups]
                prev = None
                for wi, (p_, g0, gn) in enumerate(work):
                    ui = nxt('u')
                    up(p_, g0, gn, ui)
                    if prev is not None:
                        down(*prev)
                    if g0 == 0 and p_ + 1 < NPASS:
                        load_pass(p_ + 1)
                    prev = (p_, g0, gn, ui)
                down(*prev)
            P.barrier()
            if seg + 1 < NSEG:
                load_phase_a(0)
            P.op('sp', lambda e: e.dma_start(out=fing, in_=fin_g.partition_broadcast(128)), writes=['fing'], dma='c10')
            for ti in tiles:
                rms_stats(xres[:, ti, :], 'x%d' % ti, rstd[:], 'rstd')
                yi = nxt('y')
                P.op('dve', lambda e: e.scalar_tensor_tensor(out=ysb[yi], in0=xres[:, ti, :], scalar=rstd[:], in1=fing, op0=ALU.mult, op1=ALU.mult),
                     reads=['x%d' % ti, 'rstd', 'fing'], writes=['ysb%d' % yi])
                if ti == TPS:
                    P.op('sp', lambda e: e.dma_start(out=y_smp, in_=ysb[yi][0:64, :]), reads=['ysb%d' % yi], dma='o_y%d' % yi)
                else:
                    P.op('sp', lambda e: e.dma_start(out=y_seq[(seg * TPS + ti) * 128:(seg * TPS + ti + 1) * 128, :], in_=ysb[yi]),
                         reads=['ysb%d' % yi], dma='o_y%d' % yi)
                    if seg + 1 < NSEG:
                        P.op('sp', lambda e: e.dma_start(out=xres[:, ti, :], in_=x_seq[((seg + 1) * TPS + ti) * 128:((seg + 1) * TPS + ti + 1) * 128, :]),
                             writes=['x%d' % ti], dma='lx%d' % ti)
            P.barrier()
        P.emit()
    return nc


def _consts():
    c = {}
    c['c_identf'] = np.eye(128, dtype=np.float32)
    c['c_identb'] = np.eye(128, dtype=np.float32)
    half = 8
    inv_freq = np.power(np.float32(500000.0), -np.arange(half, dtype=np.float32) * np.float32(2.0 / 16)).astype(np.float32)
    pos = np.arange(SEQ, dtype=np.float32)
    ang = (pos[:, None] * inv_freq[None, :]).astype(np.float32)
    c['c_cosp'] = np.cos(ang).astype(np.float32).reshape(32, 128, 8).transpose(1, 0, 2).copy()
    c['c_sinp'] = np.sin(ang).astype(np.float32).reshape(32, 128, 8).transpose(1, 0, 2).copy()
    p = np.arange(128)
    tt = p // 16
    jj = p % 16
    valid = p < 64
    pos_s = (PAST + tt).astype(np.float32)
    ang_s = (pos_s[:, None] * inv_freq[None, :]).astype(np.float32)
    c['c_coss'] = np.where(valid[:, None], np.cos(ang_s), 1.0).astype(np.float32)
    c['c_sins'] = np.where(valid[:, None], np.sin(ang_s), 0.0).astype(np.float32)
    s = p[:, None]; t = p[None, :]
    ncur = np.where(s <= t, 0.0, -30000.0).astype(np.float32)
    nprev = np.where(s >= t, 0.0, -30000.0).astype(np.float32)
    c['c_negm'] = np.stack([np.tile(ncur, (1, 4)), np.tile(nprev, (1, 4))], axis=1).astype(np.float32)
    ch = p // 32
    same = ch[:, None] == ch[None, :]
    ref = ch * 32 + 15
    mrel_p = same * ((s <= t).astype(np.float32) - (s <= ref[None, :]).astype(np.float32))
    mkd_p = same * (s > t)
    ma_p = same * (s <= t)
    same_s = (jj[:, None] == jj[None, :]) & valid[:, None] & valid[None, :]
    mrel_s = same_s * (tt[:, None] <= tt[None, :])
    mkd_s = same_s * (tt[:, None] > tt[None, :])
    ma_s = same_s * (tt[:, None] <= tt[None, :])
    c['c_mrel'] = np.stack([mrel_p, mrel_s]).astype(np.float32)
    c['c_mkd'] = np.stack([mkd_p, mkd_s]).astype(np.float32)
    c['c_ma'] = np.stack([ma_p, ma_s]).astype(np.float32)
    indp = np.zeros((128, 8), np.float32)
    for cc in range(4):
        indp[cc * 32:(cc + 1) * 32, cc] = 1.0
        indp[cc * 32:cc * 32 + 16, 4 + cc] = 1.0
    c['c_indp'] = indp
    inds = np.zeros((128, 16), np.float32)
    inds[p[valid], jj[valid]] = 1.0
    c['c_inds'] = inds
    c['c_msel'] = inds.copy()
    c['c_mc0'] = ((p[:, None] >= tt[None, :]) & valid[None, :]).astype(np.float32)
    mns = (same_s & (tt[:, None] <= tt[None, :])).astype(np.float32)
    c['c_mns'] = mns
    selT = np.zeros((128, 16, 64), np.float32)
    for j in range(16):
        selT[:, j, :] = ((jj == j) & valid)[None, 0:64]
    c['c_selT'] = selT
    return c


_CACHE = {}
PCORES = [0, 1, 4, 5]


def kernel(x_prompt, x_sample, cache_k, cache_v, state_hgrn, attn_norm, w_in, att_sinks,
           hgrn_lower_bounds, hgrn_out_norm, w_o, mlp_norm, w_up, w_down, final_norm):
    f = lambda a: np.ascontiguousarray(np.asarray(a, dtype=np.float32))
    x_prompt, x_sample, cache_k, cache_v, state_hgrn = map(f, (x_prompt, x_sample, cache_k, cache_v, state_hgrn))
    if 'nc' not in _CACHE:
        _CACHE['nc'] = build_program()
        _CACHE['consts'] = _consts()
    nc = _CACHE['nc']
    consts = _CACHE['consts']
    gains = np.concatenate([f(attn_norm).reshape(2, 8, 128), f(mlp_norm).reshape(2, 8, 128), f(final_norm).reshape(1, 8, 128)], axis=0).reshape(40, 128)
    shared = dict(gains=np.ascontiguousarray(gains), fin_g=f(final_norm).reshape(1, D), w_in=f(w_in), sinks=f(att_sinks),
                  lbraw=f(hgrn_lower_bounds), onorm=f(hgrn_out_norm), w_o=f(w_o), w_up=f(w_up), w_dn=f(w_down))
    shared.update(consts)
    zero_seq = np.zeros((SEQ, D), np.float32)
    in_maps = []
    for c in range(8):
        sl = slice(c * NSMP, (c + 1) * NSMP)
        m = dict(shared)
        m['x_seq'] = x_prompt[PCORES.index(c)] if c in PCORES else zero_seq
        m['x_smp'] = np.ascontiguousarray(x_sample[sl].transpose(1, 0, 2).reshape(64, D))
        m['ck'] = np.ascontiguousarray(cache_k[:, sl].reshape(2, NSMP, 128, 128))
        m['cv'] = np.ascontiguousarray(cache_v[:, sl].reshape(2, NSMP, 128, 128))
        m['st_in'] = np.ascontiguousarray(state_hgrn[:, sl])
        in_maps.append(m)
    res = run_bass_kernel_spmd(nc, in_maps, core_ids=list(range(8)))
    R = res.results
    y_prompt = np.stack([R[b]['y_seq'] for b in PCORES]).astype(np.float32)
    y_sample = np.concatenate([R[c]['y_smp'].reshape(4, NSMP, D).transpose(1, 0, 2) for c in range(8)], axis=0).astype(np.float32)
    nk_p = np.stack([R[b]['nk_p'] for b in PCORES], axis=1).reshape(2, 4, 128, 2, 64)
    nv_p = np.stack([R[b]['nv_p'] for b in PCORES], axis=1).reshape(2, 4, 128, 2, 64)
    ns_p = np.stack([R[b]['ns_p'] for b in PCORES], axis=1)
    nk_s = np.concatenate([R[c]['nk_s'] for c in range(8)], axis=1).reshape(2, 128, 128, 2, 64)
    nv_s = np.concatenate([R[c]['nv_s'] for c in range(8)], axis=1).reshape(2, 128, 128, 2, 64)
    ns_s = np.concatenate([R[c]['ns_s'] for c in range(8)], axis=1)
    return (y_prompt, y_sample, np.ascontiguousarray(nk_p), np.ascontiguousarray(nv_p), np.ascontiguousarray(ns_p),
            np.ascontiguousarray(nk_s), np.ascontiguousarray(nv_s), np.ascontiguousarray(ns_s))
```

```python
import numpy as np
from contextlib import ExitStack
import concourse.bass as bass
import concourse.mybir as mybir
from concourse.bass_utils import run_bass_kernel_spmd

F32 = mybir.dt.float32
BF16 = mybir.dt.bfloat16
AF = mybir.ActivationFunctionType
ALU = mybir.AluOpType
AX = mybir.AxisListType

D = 1024
SEQ = 4096
NSEG = 2
TPS = 16
NSMP = 16
PAST = 16384
EPS = 1e-6
INW = 2816
DFF = 4096
NPASS = 8
GRP = 256


class Prog:
    ENGS = ('pe', 'act', 'dve', 'pool', 'sp')

    def __init__(self, nc, es):
        self.nc = nc
        self.es = es
        self.eng = {'pe': nc.tensor, 'act': nc.scalar, 'dve': nc.vector, 'pool': nc.gpsimd, 'sp': nc.sync}
        self.cnt = {e: 0 for e in self.ENGS}
        self.semh = {k: es.enter_context(nc.semaphore('s_' + k)) for k in self.ENGS}
        self.dma_cnt = {}
        self.last_write = {}
        self.reads_since = {}
        self.seen = {e: {} for e in self.ENGS}
        self.nins = 0

    def _wait(self, e, tok):
        key, val = tok
        if self.seen[e].get(key, 0) >= val:
            return
        self.seen[e][key] = val
        self.eng[e].wait_ge(self.semh[key], val)

    def op(self, e, fn, reads=(), writes=(), dma=None):
        deps = []
        for r in reads:
            if r in self.last_write:
                deps.append(self.last_write[r])
        for w in writes:
            rs = self.reads_since.get(w, ())
            if rs:
                deps.extend(rs)
            elif w in self.last_write:
                deps.append(self.last_write[w])
        mx = {}
        for key, val in deps:
            if key == 'pe' and e == 'pe' and dma is None:
                continue
            if val > mx.get(key, 0):
                mx[key] = val
        for key, val in mx.items():
            self._wait(e, (key, val))
        self.nins += 1
        if dma is None:
            self.cnt[e] += 1
            tok = (e, self.cnt[e])
            fn(self.eng[e]).then_inc(self.semh[e], 1)
        else:
            k = 'dma:' + dma
            if k not in self.semh:
                self.semh[k] = self.es.enter_context(self.nc.semaphore('d%d' % len(self.semh)))
            self.dma_cnt[k] = self.dma_cnt.get(k, 0) + 16
            tok = (k, self.dma_cnt[k])
            fn(self.eng[e]).then_inc(self.semh[k], 16)
        for r in reads:
            self.reads_since.setdefault(r, []).append(tok)
        for w in writes:
            self.last_write[w] = tok
            self.reads_since[w] = []
        return tok

    def barrier(self, skip=()):
        toks = [(k, v) for k, v in self.dma_cnt.items() if not any(k.startswith('dma:' + s) for s in skip)] + \
               [(k, self.cnt[k]) for k in self.ENGS if self.cnt[k]]
        for e in self.ENGS:
            for t in toks:
                self._wait(e, t)

    def emit(self):
        self.barrier()


def bc(ap, axis, shape):
    return ap.unsqueeze(axis).broadcast_to(shape)


def build_program():
    nc = bass.Bass("TRN2", target_bir_lowering=False)

    def din(name, shape, dt=F32):
        return nc.dram_tensor(name, list(shape), dt, kind="ExternalInput").ap()

    def dout(name, shape):
        return nc.dram_tensor(name, list(shape), F32, kind="ExternalOutput").ap()

    x_seq = din("x_seq", [SEQ, D])
    x_smp = din("x_smp", [64, D])
    ck = din("ck", [2, NSMP, 128, 128])
    cv = din("cv", [2, NSMP, 128, 128])
    st_in = din("st_in", [2, NSMP, 4, 128, 128])
    gains = din("gains", [40, 128])
    fin_g = din("fin_g", [1, D])
    w_in = din("w_in", [2, D, INW])
    sinks = din("sinks", [2, 8])
    lbraw = din("lbraw", [2, 512])
    onorm = din("onorm", [2, 128])
    w_o = din("w_o", [2, D, D])
    w_up = din("w_up", [2, D, DFF])
    w_dn = din("w_dn", [2, DFF, D])
    c_identf = din("c_identf", [128, 128])
    c_identb = din("c_identb", [128, 128])
    c_cosp = din("c_cosp", [128, 32, 8])
    c_sinp = din("c_sinp", [128, 32, 8])
    c_coss = din("c_coss", [128, 8])
    c_sins = din("c_sins", [128, 8])
    c_negm = din("c_negm", [128, 2, 512])
    c_mrel = din("c_mrel", [2, 128, 128])
    c_mkd = din("c_mkd", [2, 128, 128])
    c_ma = din("c_ma", [2, 128, 128])
    c_indp = din("c_indp", [128, 8])
    c_inds = din("c_inds", [128, 16])
    c_msel = din("c_msel", [128, 16])
    c_mc0 = din("c_mc0", [128, 128])
    c_mns = din("c_mns", [128, 128])
    c_selT = din("c_selT", [128, 16, 64])

    y_seq = dout("y_seq", [SEQ, D])
    y_smp = dout("y_smp", [64, D])
    nk_p = dout("nk_p", [2, 128, 128])
    nv_p = dout("nv_p", [2, 128, 128])
    ns_p = dout("ns_p", [2, 4, 128, 128])
    nk_s = dout("nk_s", [2, NSMP, 128, 128])
    nv_s = dout("nv_s", [2, NSMP, 128, 128])
    ns_s = dout("ns_s", [2, NSMP, 4, 128, 128])

    with ExitStack() as es:
        def sb(name, shape, dt=F32):
            return es.enter_context(nc.sbuf_tensor(name, list(shape), dt))

        def ps(name, shape, dt=F32):
            return es.enter_context(nc.psum_tensor(name, list(shape), dt))

        NT = TPS + 1
        xres = sb("xres", [128, NT, D])
        hT_elems = 8 * NT * 128
        ring_slot = 8 * 512 + 4 * D
        uT_elems = 4 * GRP
        rg_b = max(8 * INW + 8 * D + ring_slot, hT_elems + ring_slot + 2 * uT_elems)
        rgn32 = sb("rgn", [128, rg_b // 2])
        rgnb = rgn32[:].bitcast(BF16)
        w_in_sb = rgnb[:, 0:8 * INW].rearrange("p (k n) -> p k n", k=8)
        w_o_sb = rgnb[:, 8 * INW:8 * INW + 8 * D].rearrange("p (k n) -> p k n", k=8)
        hT_all = rgnb[:, 0:hT_elems].rearrange("p (k t) -> p k t", k=8)
        ring = []
        for o0 in (8 * INW + 8 * D, hT_elems):
            wu = rgnb[:, o0:o0 + 4096].rearrange("p (k n) -> p k n", k=8)
            wd = rgnb[:, o0 + 4096:o0 + 8192].rearrange("p (k n) -> p k n", k=4)
            ring.append((wu, wd))
        o0 = hT_elems + ring_slot
        assert o0 + 2 * uT_elems <= 8 * INW + 8 * D
        uTs = []
        for s in range(2):
            uTs.append(rgnb[:, o0:o0 + uT_elems].rearrange("p (k n) -> p k n", k=4))
            o0 += uT_elems

        identf = sb("identf", [128, 128]); identb = sb("identb", [128, 128], BF16)
        cosp = sb("cosp", [128, TPS, 8]); sinp = sb("sinp", [128, TPS, 8])
        coss = sb("coss", [128, 8]); sins = sb("sins", [128, 8])
        negm = sb("negm", [128, 2, 512], BF16)
        mrel = sb("mrel", [128, 2, 128]); mkd = sb("mkd", [128, 2, 128]); ma = sb("ma", [128, 2, 128], BF16)
        indp = sb("indp", [128, 8]); inds = sb("inds", [128, 16]); msel = sb("msel", [128, 16])
        mc0 = sb("mc0", [128, 128], BF16); mns = sb("mns", [128, 128], BF16)
        selT = sb("selT", [128, 16, 64], BF16)
        gT = sb("gT", [128, 40])
        esink = sb("esink", [128, 2, 8])
        lb1 = sb("lb1", [128, 512])
        onb = sb("onb", [128, 2, 128])
        EPS_AP = sb("eps_ap", [128, 1])

        ss = sb("ss", [128, 1]); rstd = sb("rstd", [128, 1]); nrstd = sb("nrstd", [128, 1])
        rr2 = sb("rr2", [128, NT])
        dmy = sb("dmy", [128, 1])
        hT = sb("hT", [128, 8, 128], BF16)
        junk = hT[:].rearrange("p k t -> p (k t)")
        rl = junk[:, 0:512]
        ft = sb("ft", [128, 4096])
        fing = ft[:, 0:D]
        ysb = [ft[:, D:2 * D], ft[:, 2 * D:3 * D]]
        qa = ft[:, 0:512].rearrange("p (h d) -> p h d", h=8)
        ka = sb("ka", [128, 2, 64]); va = sb("va", [128, 128])
        qab = sb("qab", [128, 8, 64], BF16); kab = sb("kab", [128, 2, 64], BF16)
        qT = [sb("qT%d" % i, [64, 8, 128], BF16) for i in range(2)]
        kT = [[sb("kT%d_%d" % (l, i), [64, 2, 128], BF16) for i in range(2)] for l in range(2)]
        vaug = [[sb("vaug%d_%d" % (l, i), [128, 2, 65], BF16) for i in range(2)] for l in range(2)]
        pT = [sb("pT%d" % i, [128, 4, 128], BF16) for i in range(2)]
        den = sb("den", [128, 4]); rden = sb("rden", [128, 4])
        mix = sb("mix", [128, D], BF16)
        e1 = ft[:, 512:1024]
        qh = ft[:, 1024:1536]; fg = ft[:, 1536:2048]; gl = ft[:, 2048:2560]; kk = ft[:, 2560:3072]
        cl = ft[:, 3072:3584]; eq = ft[:, 3584:4096]; ek = e1
        rt = [cl[:, 0:64].rearrange("p (h d) -> p h d", h=8), cl[:, 64:128].rearrange("p (h d) -> p h d", h=8),
              eq[:, 0:64].rearrange("p (h d) -> p h d", h=8), eq[:, 64:128].rearrange("p (h d) -> p h d", h=8)]
        RTK = ["cl", "cl", "eq", "eq"]
        vh = [sb("vh%d" % i, [128, 512], BF16) for i in range(2)]
        gate = [sb("gate%d" % i, [128, 512], BF16) for i in range(2)]
        qe = [sb("qe%d" % i, [128, 4, 128], BF16) for i in range(2)]
        ke = [sb("ke%d" % i, [128, 4, 128], BF16) for i in range(2)]
        kd = [sb("kd%d" % i, [128, 512], BF16) for i in range(2)]
        bg = [sb("bg%d" % i, [128, 4, 16]) for i in range(2)]
        tq = sb("tq", [128, 8, 128], BF16)
        qeT = tq[:, 0:4, :]; keT = tq[:, 4:8, :]; mixT = tq[:]
        atm = sb("atm", [128, 4, 128], BF16)
        S = [sb("S%d" % l, [128, 4, 128]) for l in range(2)]
        Sp = [sb("Sp%d" % c, [128, 4, 128], BF16) for c in range(2)]
        sso = sb("sso", [128, 4]); rso = sb("rso", [128, 4])
        kcT2 = [Sp[1][0:64, 0:2, :], Sp[1][0:64, 2:4, :]]
        kcsA = ft[:, 0:1024].bitcast(BF16).rearrange("p (j d) -> p j d", j=NSMP)
        vcsA = ft[:, 1024:2064].bitcast(BF16).rearrange("p (j g d) -> p j g d", j=NSMP, g=2)
        sstA = [ft[:, 2064:2576].rearrange("p (h d) -> p h d", h=4), ft[:, 2576:3088].rearrange("p (h d) -> p h d", h=4)]
        stmp = ft[:, 3088:3600].rearrange("p (h d) -> p h d", h=4)
        sbf = ft[:, 3600:3856].bitcast(BF16).rearrange("p (h d) -> p h d", h=4)

        PT = ps("PT", [128, 512])
        PB = ps("PB", [128, 1024], BF16)
        PJ = [ps("PJ0", [128, 512]), ps("PJ1", [128, 512])]
        PS = [ps("PS0", [128, 512]), ps("PS1", [128, 512])]
        PV = ps("PV", [128, 512])
        PO = ps("PO", [128, 512])

        P = Prog(nc, es)
        cnt = {'pj': 0, 'ps': 0, 'y': 0, 'pt': 0, 'u': 0, 'pb': 0}

        def nxt(k, n=2):
            v = cnt[k] % n
            cnt[k] += 1
            return v


        WGRP = [(0, 512), (512, 768), (768, 1280), (1280, 1792), (1792, 2304), (2304, 2816)]

        def load_pass_l(l, p_):
            slot = p_ % 2
            wu, wd = ring[slot]
            for kc in range(8):
                P.op('pool', lambda e: e.dma_start(out=wu[:, kc, :], in_=w_up[l, kc * 128:(kc + 1) * 128, p_ * 512:(p_ + 1) * 512]), writes=['ringu%d_%d' % (slot, kc)], dma='lru%d' % slot)
            for fc in range(4):
                P.op('pool', lambda e: e.dma_start(out=wd[:, fc, :], in_=w_dn[l, p_ * 512 + fc * 128: p_ * 512 + (fc + 1) * 128, :]), writes=['ringd%d_%d' % (slot, fc)], dma='lrd%d' % slot)

        def load_phase_a(l):
            for gi, (c0, c1) in enumerate(WGRP):
                P.op('pool', lambda e: e.dma_start(out=w_in_sb[:, :, c0:c1], in_=w_in[l, :, c0:c1].rearrange("(k p) n -> p k n", p=128)), writes=['w_in_g%d' % gi], dma='lw_in%d' % gi)

        def load_phase_a2(l):
            for kc in range(8):
                P.op('pool', lambda e: e.dma_start(out=w_o_sb[:, kc, :], in_=w_o[l, kc * 128:(kc + 1) * 128, :]), writes=['w_o%d' % kc], dma='lw_o')
            load_pass_l(l, 0)

        P.op('sp', lambda e: e.dma_start(out=xres[:, 0, :], in_=x_seq[0:128, :]), writes=['x0'], dma='lx0')
        load_phase_a(0)

        cl_list = [(identf, c_identf), (coss, c_coss), (sins, c_sins),
                   (indp, c_indp), (inds, c_inds), (msel, c_msel)]
        for i, (t, d) in enumerate(cl_list):
            P.op('sp', lambda e: e.dma_start(out=t[:], in_=d), writes=[t.name], dma='c%d' % i)
        P.op('sp', lambda e: e.dma_start(out=mrel[:], in_=c_mrel.rearrange("a p n -> p a n")), writes=['mrel'], dma='c20')
        P.op('sp', lambda e: e.dma_start(out=mkd[:], in_=c_mkd.rearrange("a p n -> p a n")), writes=['mkd'], dma='c21')
        cb_list = [(identb, c_identb), (negm, c_negm), (mc0, c_mc0), (mns, c_mns), (selT, c_selT)]
        for i, (t, d) in enumerate(cb_list):
            P.op('pool', lambda e: e.dma_start(out=t[:], in_=d), writes=[t.name], dma='cb%d' % i)
        P.op('pool', lambda e: e.dma_start(out=ma[:], in_=c_ma.rearrange("a p n -> p a n")), writes=['ma'], dma='cb9')
        P.op('sp', lambda e: e.dma_start(out=esink[:].rearrange("p l h -> p (l h)"), in_=sinks.rearrange("l h -> (l h)").unsqueeze(0).partition_broadcast(128)), writes=['esink'], dma='c11')
        gstage = ft[0:40, 1024:1152]
        lbr = ft[:, 0:1024].rearrange("p (l c) -> p l c", l=2)
        P.op('sp', lambda e: e.dma_start(out=ft[:, 0:1024], in_=lbraw.rearrange("l c -> (l c)").unsqueeze(0).partition_broadcast(128)), writes=['lbr'], dma='c12')
        P.op('sp', lambda e: e.dma_start(out=onb[:].rearrange("p l h -> p (l h)"), in_=onorm.rearrange("l c -> (l c)").unsqueeze(0).partition_broadcast(128)), writes=['onb'], dma='c13')
        P.op('sp', lambda e: e.dma_start(out=gstage, in_=gains), writes=['gstage'], dma='c19')
        P.op('act', lambda e: e.activation(out=esink[:], in_=esink[:], func=AF.Exp), reads=['esink'], writes=['esink'])
        P.op('dve', lambda e: e.tensor_tensor(out=lb1[:], in0=lbr[:, 0, :], in1=lbr[:, 1, :], op=ALU.subtract), reads=['lbr'], writes=['lb1'])
        P.op('act', lambda e: e.activation(out=lb1[:], in_=lb1[:], func=AF.Exp), reads=['lb1'], writes=['lb1'])
        P.op('dve', lambda e: e.tensor_scalar_add(out=lb1[:], in0=lb1[:], scalar1=1.0), reads=['lb1'], writes=['lb1'])
        P.op('dve', lambda e: e.reciprocal(out=lb1[:], in_=lb1[:]), reads=['lb1'], writes=['lb1'])
        P.op('pe', lambda e: e.transpose(PT[:, 0:40], gstage, identf[0:40, 0:40]), reads=['gstage', 'identf'], writes=['PT'])
        P.op('dve', lambda e: e.tensor_copy(out=gT[:], in_=PT[:, 0:40]), reads=['PT'], writes=['gT'])
        P.op('pool', lambda e: e.memset(dmy[:], 0.0), writes=['dmy'])
        P.op('pool', lambda e: e.memset(EPS_AP[:], EPS), writes=['eps_ap'])
        for l in range(2):
            for i in range(2):
                P.op('pool', lambda e: e.memset(vaug[l][i][:], 1.0), writes=['vaug%d_%d' % (l, i)])
        P.op('pool', lambda e: e.memset(xres[:, TPS, :], 0.0), writes=['x%d' % TPS])
        for l in range(2):
            P.op('pool', lambda e: e.memset(S[l][:], 0.0), writes=['S%d' % l])
        P.barrier(skip=('lw_in', 'lw_o', 'lru', 'lrd', 'lx'))

        def rms_stats(xt_ap, xkey, out_r, out_key, also_neg=None, sq=None):
            def f(e):
                e.activation(out=junk, in_=xt_ap, func=AF.Square, accum_out=ss[:])
                return e.activation(out=dmy[:], in_=dmy[:], func=AF.Copy)
            P.op('act', f, reads=[xkey], writes=['hTa', 'hTb', 'ss', 'dmy'])
            P.op('act', lambda e: e.activation(out=ss[:], in_=ss[:], func=AF.Ln, scale=1.0 / D, bias=EPS_AP[:]), reads=['ss', 'eps_ap'], writes=['ss'])
            P.op('act', lambda e: e.activation(out=out_r, in_=ss[:], func=AF.Exp, scale=-0.5), reads=['ss'], writes=[out_key])
            if also_neg is not None:
                P.op('dve', lambda e: e.tensor_scalar_mul(out=also_neg, in0=out_r, scalar1=-1.0), reads=[out_key], writes=['nrstd'])
            if sq is not None:
                P.op('act', lambda e: e.activation(out=sq, in_=ss[:], func=AF.Exp, scale=-1.0), reads=['ss'], writes=['rr2'])

        def make_hT(ti, gcol, dst, dst_key, pj_only=False):
            for half in range(2):
                if half == 0 and not pj_only:
                    bank, bkey = PT, 'PT'
                else:
                    pj_ = nxt('pj')
                    bank, bkey = PJ[pj_], 'PJ%d' % pj_
                def f(e):
                    ins = None
                    for j in range(4):
                        kc = half * 4 + j
                        ins = e.transpose(bank[:, j * 128:(j + 1) * 128], xres[:, ti, kc * 128:(kc + 1) * 128], identf[:])
                    return ins
                P.op('pe', f, reads=['x%d' % ti, 'identf'], writes=[bkey])
                P.op('dve', lambda e: e.tensor_tensor(
                    out=dst[:, half * 4:half * 4 + 4, :], in0=bank[:].rearrange("p (k t) -> p k t", k=4),
                    in1=bc(gT[:, gcol * 8 + half * 4:gcol * 8 + half * 4 + 4], 2, [128, 4, 128]), op=ALU.mult),
                    reads=[bkey, 'gT'], writes=[(dst_key + 'ab'[half]) if dst_key == 'hT' else dst_key])
                yield

        WO_KEYS = ['w_o%d' % k_ for k_ in range(8)]

        def proj_group(c0, c1):
            pj = nxt('pj')
            for hf in range(2):
                def f(e):
                    ins = None
                    for kc in range(hf * 4, hf * 4 + 4):
                        ins = e.matmul(PJ[pj][:, 0:c1 - c0], lhsT=hT[:, kc, :], rhs=w_in_sb[:, kc, c0:c1], start=(kc == 0), stop=(kc == 7))
                    return ins
                P.op('pe', f, reads=['hT' + 'ab'[hf], 'w_in_g%d' % WGRP.index((c0, c1))], writes=['PJ%d' % pj])
            return pj

        def silu_from_psum(pj, dst, dst_key):
            k = 'PJ%d' % pj
            P.op('act', lambda e: e.activation(out=dst, in_=PJ[pj][:], func=AF.Copy, scale=rstd[:]), reads=[k, 'rstd'], writes=[dst_key])
            P.op('act', lambda e: e.activation(out=e1[:], in_=dst, func=AF.Exp, scale=-1.0), reads=[dst_key], writes=['e1'])
            P.op('act', lambda e: e.activation(out=e1[:], in_=e1[:], func=AF.Ln, bias=1.0), reads=['e1'], writes=['e1'])
            P.op('act', lambda e: e.activation(out=e1[:], in_=e1[:], func=AF.Exp, scale=-1.0), reads=['e1'], writes=['e1'])
            P.op('pool', lambda e: e.tensor_tensor(out=dst, in0=dst, in1=e1[:], op=ALU.mult), reads=[dst_key, 'e1'], writes=[dst_key])

        def rotary(src, nh, cos_ap, sin_ap, dstb, skey, dkey):
            x1 = src[:, :, 0:8]; x2 = src[:, :, 8:16]
            cb = bc(cos_ap, 1, [128, nh, 8]); sn = bc(sin_ap, 1, [128, nh, 8])
            r0, r1, r2, r3 = [t[:, 0:nh, :] for t in rt]
            ck_ = ['cosp', 'sinp']
            P.op('pool', lambda e: e.tensor_tensor(out=r0, in0=x1, in1=cb, op=ALU.mult), reads=[skey] + ck_, writes=[RTK[0]])
            P.op('pool', lambda e: e.tensor_tensor(out=r1, in0=x2, in1=sn, op=ALU.mult), reads=[skey] + ck_, writes=[RTK[1]])
            P.op('pool', lambda e: e.tensor_tensor(out=r2, in0=x2, in1=cb, op=ALU.mult), reads=[skey] + ck_, writes=[RTK[2]])
            P.op('pool', lambda e: e.tensor_tensor(out=r3, in0=x1, in1=sn, op=ALU.mult), reads=[skey] + ck_, writes=[RTK[3]])
            P.op('pool', lambda e: e.tensor_tensor(out=x1, in0=r0, in1=r1, op=ALU.subtract), reads=['cl', skey], writes=[skey])
            P.op('pool', lambda e: e.tensor_tensor(out=x2, in0=r2, in1=r3, op=ALU.add), reads=['eq', skey], writes=[skey])
            P.op('pool', lambda e: e.tensor_copy(out=dstb, in_=src), reads=[skey], writes=[dkey])

        def transposes_bf(srcs, src_keys, dst, dst_key, rows, evac='act'):
            n = len(srcs)
            dkeys = dst_key if isinstance(dst_key, list) else [dst_key]
            def f(e):
                ins = None
                for i, s_ in enumerate(srcs):
                    ins = e.transpose(PB[0:rows, i * 128:i * 128 + 128], s_, identb[:])
                return ins
            P.op('pe', f, reads=list(src_keys) + ['identb'], writes=['PB'])
            src_v = PB[0:rows, 0:n * 128].rearrange("p (k t) -> p k t", k=n)
            if evac == 'act':
                P.op('act', lambda e: e.activation(out=dst, in_=src_v, func=AF.Copy), reads=['PB'], writes=dkeys)
            else:
                P.op('dve', lambda e: e.tensor_copy(out=dst, in_=src_v), reads=['PB'], writes=dkeys)

        def F_tile(l, ti, seg, is_smp, par):
            xk = 'x%d' % ti
            rms_stats(xres[:, ti, :], xk, rstd[:], 'rstd', also_neg=nrstd[:])
            for _ in make_hT(ti, l, hT, 'hT'):
                yield
            kTc, vac = kT[l][par], vaug[l][par]
            kTck, vack = 'kT%d_%d' % (l, par), 'vaug%d_%d' % (l, par)
            if is_smp:
                cos_ap, sin_ap = coss[:], sins[:]
            else:
                cos_ap, sin_ap = cosp[:, ti, :], sinp[:, ti, :]
            pj = proj_group(0, 512)
            P.op('act', lambda e: e.activation(out=qa[:].rearrange("p h d -> p (h d)"), in_=PJ[pj][:], func=AF.Copy, scale=rstd[:]),
                 reads=['PJ%d' % pj, 'rstd'], writes=['qa'])
            yield
            rotary(qa[:], 8, cos_ap, sin_ap, qab[:], 'qa', 'qab')
            pj = proj_group(512, 768)
            P.op('act', lambda e: e.activation(out=ka[:].rearrange("p h d -> p (h d)"), in_=PJ[pj][:, 0:128], func=AF.Copy, scale=rstd[:]),
                 reads=['PJ%d' % pj, 'rstd'], writes=['ka'])
            P.op('act', lambda e: e.activation(out=va[:], in_=PJ[pj][:, 128:256], func=AF.Copy, scale=rstd[:]),
                 reads=['PJ%d' % pj, 'rstd'], writes=['va'])
            yield
            rotary(ka[:], 2, cos_ap, sin_ap, kab[:], 'ka', 'kab')
            P.op('pool', lambda e: e.tensor_copy(out=vac[:, :, 0:64], in_=va[:].rearrange("p (g d) -> p g d", g=2)), reads=['va'], writes=[vack])
            if is_smp:
                for t in range(4):
                    P.op('sp', lambda e: e.dma_start(out=nk_s[l, :, 124 + t, :], in_=ka[t * 16:(t + 1) * 16].rearrange("p h d -> p (h d)")), reads=['ka'], dma='o_nk%d' % t)
                    P.op('sp', lambda e: e.dma_start(out=nv_s[l, :, 124 + t, :], in_=va[t * 16:(t + 1) * 16, :]), reads=['va'], dma='o_nv%d' % t)
            elif seg == NSEG - 1 and ti == TPS - 1:
                P.op('sp', lambda e: e.dma_start(out=nk_p[l], in_=ka[:].rearrange("p h d -> p (h d)")), reads=['ka'], dma='o_nkp')
                P.op('sp', lambda e: e.dma_start(out=nv_p[l], in_=va[:]), reads=['va'], dma='o_nvp')
            pj = proj_group(768, 1280)
            silu_from_psum(pj, qh[:], 'qh')
            yield
            pj = proj_group(1280, 1792)
            k = 'PJ%d' % pj
            P.op('act', lambda e: e.activation(out=e1[:], in_=PJ[pj][:], func=AF.Exp, scale=nrstd[:]), reads=[k, 'nrstd'], writes=['e1'])
            P.op('act', lambda e: e.activation(out=e1[:], in_=e1[:], func=AF.Ln, bias=1.0), reads=['e1'], writes=['e1'])
            P.op('act', lambda e: e.activation(out=fg[:], in_=e1[:], func=AF.Exp, scale=-1.0), reads=['e1'], writes=['fg'])
            yield
            transposes_bf([qab[:, h, :] for h in range(8)], ['qab'], qT[par][:], 'qT%d' % par, 64)
            yield
            P.op('pool', lambda e: e.tensor_scalar(out=kk[:], in0=fg[:], scalar1=-1.0, scalar2=1.0, op0=ALU.mult, op1=ALU.add), reads=['fg'], writes=['kk'])
            if l == 1:
                P.op('dve', lambda e: e.tensor_tensor(out=fg[:], in0=kk[:], in1=lb1[:], op=ALU.mult), reads=['kk', 'lb1'], writes=['fg'])
                P.op('pool', lambda e: e.tensor_tensor(out=kk[:], in0=kk[:], in1=fg[:], op=ALU.subtract), reads=['kk', 'fg'], writes=['kk'])
                P.op('pool', lambda e: e.tensor_scalar(out=fg[:], in0=kk[:], scalar1=-1.0, scalar2=1.0, op0=ALU.mult, op1=ALU.add), reads=['kk'], writes=['fg'])
            P.op('act', lambda e: e.activation(out=gl[:], in_=fg[:], func=AF.Ln), reads=['fg'], writes=['gl'])
            pj = proj_group(1792, 2304)
            P.op('act', lambda e: e.activation(out=vh[par][:], in_=PJ[pj][:], func=AF.Copy, scale=rstd[:]), reads=['PJ%d' % pj, 'rstd'], writes=['vh%d' % par])
            yield
            pj = proj_group(2304, 2816)
            silu_from_psum(pj, fg[:], 'fg')
            P.op('pool', lambda e: e.tensor_tensor(out=gate[par][:].rearrange("p (h d) -> p h d", h=4), in0=fg[:].rearrange("p (h d) -> p h d", h=4),
                                                   in1=bc(onb[:, l, :], 1, [128, 4, 128]), op=ALU.mult), reads=['fg', 'onb'], writes=['gate%d' % par])
            yield
            mi = 1 if is_smp else 0
            p1 = nxt('pj'); p2 = nxt('pj')
            P.op('pe', lambda e: e.matmul(PJ[p1][:], lhsT=mrel[:, mi, :], rhs=gl[:], start=True, stop=True), reads=['mrel', 'gl'], writes=['PJ%d' % p1])
            P.op('pe', lambda e: e.matmul(PJ[p2][:], lhsT=mkd[:, mi, :], rhs=gl[:], start=True, stop=True), reads=['mkd', 'gl'], writes=['PJ%d' % p2])
            P.op('dve', lambda e: e.tensor_scalar(out=cl[:], in0=PJ[p1][:], scalar1=-40.0, scalar2=40.0, op0=ALU.max, op1=ALU.min), reads=['PJ%d' % p1], writes=['cl'])
            P.op('act', lambda e: e.activation(out=eq[:], in_=cl[:], func=AF.Exp), reads=['cl'], writes=['eq'])
            P.op('act', lambda e: e.activation(out=ek[:], in_=cl[:], func=AF.Exp, scale=-1.0), reads=['cl'], writes=['e1'])
            P.op('act', lambda e: e.activation(out=cl[:], in_=PJ[p2][:], func=AF.Exp), reads=['PJ%d' % p2, 'cl'], writes=['cl'])
            P.op('dve', lambda e: e.tensor_tensor(out=qe[par][:].rearrange("p h d -> p (h d)"), in0=qh[:], in1=eq[:], op=ALU.mult), reads=['qh', 'eq'], writes=['qe%d' % par])
            P.op('dve', lambda e: e.tensor_tensor(out=ke[par][:].rearrange("p h d -> p (h d)"), in0=kk[:], in1=ek[:], op=ALU.mult), reads=['kk', 'e1'], writes=['ke%d' % par])
            P.op('dve', lambda e: e.tensor_tensor(out=kd[par][:], in0=kk[:], in1=cl[:], op=ALU.mult), reads=['kk', 'cl'], writes=['kd%d' % par])
            yield
            nind = 16 if is_smp else 8
            ind_ap = inds[:] if is_smp else indp[:]
            p3 = nxt('pj')
            def fbg(e):
                ins = None
                for h in range(4):
                    ins = e.matmul(PJ[p3][:, h * 16:h * 16 + nind], lhsT=gl[:, h * 128:(h + 1) * 128], rhs=ind_ap, start=True, stop=True)
                return ins
            P.op('pe', fbg, reads=['gl', 'inds', 'indp'], writes=['PJ%d' % p3])
            P.op('act', lambda e: e.activation(out=bg[par][:, :, 0:nind], in_=PJ[p3][:, 0:64].rearrange("p (h c) -> p h c", h=4)[:, :, 0:nind], func=AF.Exp),
                 reads=['PJ%d' % p3], writes=['bg%d' % par])
            yield

            transposes_bf([kab[:, g, :] for g in range(2)], ['kab'], kTc[:], kTck, 64)
            yield

        def B_tile(l, ti, seg, is_smp, par):
            xk = 'x%d' % ti
            prv = 1 - par
            kTc, vac = kT[l][par], vaug[l][par]
            kTck, vack = 'kT%d_%d' % (l, par), 'vaug%d_%d' % (l, par)
            qTc, qTk = qT[par], 'qT%d' % par
            mi = 1 if is_smp else 0
            if is_smp:
                blocks = [('c', j) for j in range(NSMP)] + [('n', 0)]
            else:
                blocks = ([('p', 0)] if not (seg == 0 and ti == 0) else []) + [('u', 0)]
            nb = len(blocks)
            pv_bank = {0: (PV, 'PV'), 1: ((PT, 'PT') if is_smp else (PV, 'PV'))}

            def finish_g(g):
                bank, bkey = pv_bank[g]
                pv4 = bank[:].rearrange("p (h c) -> p h c", h=4)
                P.op('dve', lambda e: e.tensor_tensor(out=den[:], in0=pv4[:, :, 64], in1=esink[:, l, 4 * g:4 * g + 4], op=ALU.add), reads=[bkey, 'esink'], writes=['den'])
                P.op('dve', lambda e: e.reciprocal(out=rden[:], in_=den[:]), reads=['den'], writes=['rden'])
                P.op('dve', lambda e: e.tensor_tensor(out=mix[:, g * 256:(g + 1) * 256].rearrange("p (h d) -> p h d", h=4), in0=pv4[:, :, 0:64],
                                                      in1=bc(rden[:], 2, [128, 4, 64]), op=ALU.mult), reads=[bkey, 'rden'], writes=['mixa'])

            def block_g(g, bi, bk, j, mask_ap, mask_key, lhs, lk, rv, rvk, neg=None):
                bank, bkey = pv_bank[g]
                psi = nxt('ps')
                pk = 'PS%d' % psi
                def fsc(e):
                    ins = e.matmul(PS[psi][:], lhsT=lhs, rhs=qTc[:, 4 * g:4 * g + 4, :], start=True, stop=(neg is None))
                    if neg is not None:
                        ins = e.matmul(PS[psi][:], lhsT=identb[:], rhs=neg, start=False, stop=True)
                    return ins
                P.op('pe', fsc, reads=[lk, qTk, 'identb', 'negm'], writes=[pk])
                pi = nxt('pt')
                P.op('act', lambda e: e.activation(out=pT[pi][:].rearrange("p h t -> p (h t)"), in_=PS[psi][:], func=AF.Exp, scale=0.125),
                     reads=[pk], writes=['pT%d' % pi])
                if neg is None:
                    P.op('pool', lambda e: e.tensor_tensor(out=pT[pi][:], in0=pT[pi][:], in1=bc(mask_ap, 1, [128, 4, 128]), op=ALU.mult),
                         reads=['pT%d' % pi, mask_key], writes=['pT%d' % pi])
                yield 1
                def fpv(e):
                    ins = None
                    for hh in range(4):
                        ins = e.matmul(bank[:, hh * 128:hh * 128 + 65], lhsT=pT[pi][:, hh, :], rhs=rv,
                                       start=(bi == 0 and hh == 0), stop=(bi == nb - 1), skip_group_check=True)
                    return ins
                P.op('pe', fpv, reads=['pT%d' % pi, rvk], writes=[bkey])
                yield 0

            if not is_smp:
                for g in range(2):
                    for bi, (bk, j) in enumerate(blocks):
                        if bk == 'p':
                            yield from block_g(g, bi, bk, j, None, None, kT[l][prv][:, g, :], 'kT%d_%d' % (l, prv), vaug[l][prv][:, g, :], 'vaug%d_%d' % (l, prv), neg=negm[:, 1, :])
                        else:
                            yield from block_g(g, bi, bk, j, None, None, kTc[:, g, :], kTck, vac[:, g, :], vack, neg=negm[:, 0, :])
                    finish_g(g)
            else:
                P.barrier()
                P.op('pool', lambda e: e.memset(ft[:, 1024:2064], 0.0), writes=['vcsA'])
                P.op('dve', lambda e: e.memset(vcsA[:, :, :, 64], 1.0), reads=['vcsA'], writes=['vcsA'])
                for i_ in range(2):
                    P.op('dve', lambda e: e.memset(pT[i_][:], 0.0), writes=['pT%d' % i_])
                P.op('pool', lambda e: e.dma_start(out=kcsA, in_=ck[l].rearrange("j k d -> k j d")), writes=['kcsA'], dma='l_kc')
                for g in range(2):
                    P.op('pool', lambda e: e.dma_start(out=vcsA[:, :, g, 0:64], in_=cv[l, :, :, g * 64:(g + 1) * 64].rearrange("j k d -> k j d")), reads=['vcsA'], writes=['vcsA'], dma='l_vc%d' % g)

            def att_seq(bi, j):
                kcT = kcT2[j % 2]
                kck = 'kcT%d' % (j % 2)
                transposes_bf([kcsA[:, j, g * 64:(g + 1) * 64] for g in range(2)], ['kcsA'], kcT, kck, 64)
                for g in range(2):
                    bank, bkey = pv_bank[g]
                    psi = nxt('ps')
                    pk = 'PS%d' % psi
                    P.op('pe', lambda e: e.matmul(PS[psi][:, 0:16], lhsT=kcT[:, g, :], rhs=qTc[:, 4 * g:4 * g + 4, j:64:16], start=True, stop=True),
                         reads=[kck, qTk], writes=[pk])
                    pi = nxt('pt')
                    pslice = pT[pi][:, :, j:64:16]
                    P.op('act', lambda e: e.activation(out=pslice, in_=PS[psi][:, 0:16].rearrange("p (h t) -> p h t", h=4), func=AF.Exp, scale=0.125),
                         reads=[pk], writes=['pT%d' % pi])
                    P.op('dve', lambda e: e.tensor_tensor(out=pslice, in0=pslice, in1=bc(mc0[:, j:64:16], 1, [128, 4, 4]), op=ALU.mult),
                         reads=['pT%d' % pi, 'mc0'], writes=['pT%d' % pi])
                    def fpv(e):
                        ins = None
                        for hh in range(4):
                            ins = e.matmul(bank[:, hh * 128:hh * 128 + 65], lhsT=pT[pi][:, hh, :], rhs=vcsA[:, j, g, :],
                                           start=(bi == 0 and hh == 0), stop=False, skip_group_check=True)
                        return ins
                    P.op('pe', fpv, reads=['pT%d' % pi, 'vcsA'], writes=[bkey])
                    P.op('dve', lambda e: e.memset(pslice, 0.0), reads=['pT%d' % pi], writes=['pT%d' % pi])

            def att_epi():
                bi, (bk, j) = nb - 1, blocks[-1]
                for g in range(2):
                    for _ in block_g(g, bi, bk, j, mns[:], 'mns', kTc[:, g, :], kTck, vac[:, g, :], vack):
                        pass
                finish_g(0)
                finish_g(1)

            qec, kec, kdc, bgc, vhc = qe[par], ke[par], kd[par], bg[par], vh[par]
            qek, kek, kdk, bgk, vhk = 'qe%d' % par, 'ke%d' % par, 'kd%d' % par, 'bg%d' % par, 'vh%d' % par
            transposes_bf([qec[:, h, :] for h in range(4)], [qek], qeT[:], 'qeT', 128)
            yield 1
            transposes_bf([kec[:, h, :] for h in range(4)], [kek], keT[:], 'keT', 128, evac='dve')
            yield 1
            p4 = nxt('ps')
            def fa(e):
                ins = None
                for h in range(4):
                    ins = e.matmul(PS[p4][:, h * 128:(h + 1) * 128], lhsT=keT[:, h, :], rhs=qeT[:, h, :], start=True, stop=True)
                return ins
            P.op('pe', fa, reads=['keT', 'qeT'], writes=['PS%d' % p4])
            P.op('dve', lambda e: e.tensor_tensor(out=atm[:], in0=PS[p4][:].rearrange("p (h t) -> p h t", h=4), in1=bc(ma[:, mi, :], 1, [128, 4, 128]), op=ALU.mult),
                 reads=['PS%d' % p4, 'ma'], writes=['atm'])
            yield 0
            def fo(e):
                ins = None
                for h in range(4):
                    ins = e.matmul(PO[:, h * 128:(h + 1) * 128], lhsT=atm[:, h, :], rhs=vhc[:, h * 128:(h + 1) * 128],
                                   start=(h == 0), stop=False, skip_group_check=True)
                return ins
            P.op('pe', fo, reads=['atm', vhk], writes=['PO'])
            yield 0
            Sl = S[l]
            Sk = 'S%d' % l
            if not is_smp:
                qeTs = keT
                P.op('dve', lambda e: e.tensor_tensor(out=qeTs.rearrange("p h (c t) -> p h c t", c=4), in0=qeT.rearrange("p h (c t) -> p h c t", c=4),
                                                      in1=bc(bgc[:, :, 4:8], 3, [128, 4, 4, 32]), op=ALU.mult),
                     reads=['qeT', 'keT', bgk], writes=['keT'])
                P.op('dve', lambda e: e.tensor_copy(out=Sp[0][:], in_=Sl[:]), reads=[Sk], writes=['Sp0'])
                for c in range(4):
                    pu = nxt('ps')
                    def fu(e):
                        ins = None
                        for h in range(4):
                            ins = e.matmul(PS[pu][:, h * 128:(h + 1) * 128], lhsT=kdc[32 * c:32 * c + 32, h * 128:(h + 1) * 128],
                                           rhs=vhc[32 * c:32 * c + 32, h * 128:(h + 1) * 128], start=True, stop=True, tile_position=(32 * c, 0))
                        return ins
                    P.op('pe', fu, reads=[kdk, vhk], writes=['PS%d' % pu])
                    def fupd(e):
                        ins = None
                        for h in range(4):
                            ins = e.scalar_tensor_tensor(out=Sl[:, h, :], in0=Sl[:, h, :], scalar=bgc[:, h, c:c + 1], in1=PS[pu][:, h * 128:(h + 1) * 128],
                                                         op0=ALU.mult, op1=ALU.add)
                        return ins
                    if c < 3:
                        def fupb(e):
                            ins = None
                            for h in range(4):
                                ins = e.scalar_tensor_tensor(out=Sp[(c + 1) % 2][:, h, :], in0=Sl[:, h, :], scalar=bgc[:, h, c:c + 1], in1=PS[pu][:, h * 128:(h + 1) * 128],
                                                             op0=ALU.mult, op1=ALU.add)
                            return ins
                        P.op('dve', fupb, reads=[Sk, bgk, 'PS%d' % pu], writes=['Sp%d' % ((c + 1) % 2)])
                    P.op('dve', fupd, reads=[Sk, bgk, 'PS%d' % pu], writes=[Sk])
                    if c == 3:
                        yield 3
                    def foi(e):
                        ins = None
                        for h in range(4):
                            ins = e.matmul(PO[32 * c:32 * c + 32, h * 128:(h + 1) * 128], lhsT=qeTs[:, h, 32 * c:32 * c + 32], rhs=Sp[c % 2][:, h, :],
                                           start=False, stop=(c == 3 and h == 3), skip_group_check=True, tile_position=(0, 32 * c))
                        return ins
                    P.op('pe', foi, reads=['keT', 'Sp%d' % (c % 2)], writes=['PO'])
                    if c < 3:
                        yield 1
                if seg == NSEG - 1 and ti == TPS - 1:
                    P.op('sp', lambda e: e.dma_start(out=ns_p[l].rearrange("h k v -> k h v"), in_=Sl[:]), reads=[Sk], dma='o_nsp')
            else:
                kdm = atm[:].rearrange("p h d -> p (h d)")
                qeTm = Sp[0][:, :, 0:64]
                sbf2 = [sbf, mix[:, 512:1024].rearrange("p (h d) -> p h d", h=4)]
                sbk = ['sbfA', 'mixh']
                def ld_state(jj):
                    s_ = jj % 2
                    P.op('sp', lambda e: e.dma_start(out=sstA[s_], in_=st_in[l, jj].rearrange("h k v -> k h v")), writes=['sstA%d' % s_], dma='l_st%d' % s_)
                    P.op('pool', lambda e: e.dma_start(out=sbf2[s_], in_=st_in[l, jj].rearrange("h k v -> k h v")), writes=[sbk[s_]], dma='l_sb%d' % s_)
                ld_state(0)
                for j in range(NSMP):
                    si = j % 2
                    if j + 1 < NSMP:
                        ld_state(j + 1)
                    att_seq(j, j)
                    P.op('dve', lambda e: e.tensor_tensor(out=qeTm, in0=qeT[:, :, 0:64], in1=bc(selT[:, j, :], 1, [128, 4, 64]), op=ALU.mult),
                         reads=['qeT', 'selT'], writes=['Sp0'])
                    def foi(e):
                        ins = None
                        for h in range(4):
                            ins = e.matmul(PO[0:64, h * 128:(h + 1) * 128], lhsT=qeTm[:, h, :], rhs=sbf2[si][:, h, :],
                                           start=False, stop=(j == NSMP - 1), skip_group_check=True)
                        return ins
                    P.op('pe', foi, reads=['Sp0', sbk[si]], writes=['PO'])
                    P.op('act', lambda e: e.activation(out=kdm, in_=kdc[:], func=AF.Copy, scale=msel[:, j:j + 1]), reads=[kdk, 'msel'], writes=['atm'])
                    pu = nxt('ps')
                    def fu(e):
                        ins = None
                        for h in range(4):
                            ins = e.matmul(PS[pu][:, h * 128:(h + 1) * 128], lhsT=kdm[:, h * 128:(h + 1) * 128], rhs=vhc[:, h * 128:(h + 1) * 128], start=True, stop=True)
                        return ins
                    P.op('pe', fu, reads=['atm', vhk], writes=['PS%d' % pu])
                    P.op('dve', lambda e: e.tensor_tensor(out=stmp, in0=sstA[si], in1=bc(bgc[:, :, j], 2, [128, 4, 128]), op=ALU.mult),
                         reads=['sstA%d' % si, bgk], writes=['stmpA'])
                    P.op('dve', lambda e: e.tensor_tensor(out=stmp.rearrange("p h d -> p (h d)"), in0=stmp.rearrange("p h d -> p (h d)"), in1=PS[pu][:], op=ALU.add),
                         reads=['stmpA', 'PS%d' % pu], writes=['stmpA'])
                    P.op('sp', lambda e: e.dma_start(out=ns_s[l, j].rearrange("h k v -> k h v"), in_=stmp), reads=['stmpA'], dma='o_st')
                    yield 1
                att_epi()
            def fsq(e):
                for h in range(4):
                    e.activation(out=atm[:, h, :], in_=PO[:, h * 128:(h + 1) * 128], func=AF.Square, accum_out=sso[:, h:h + 1])
                return e.activation(out=dmy[:], in_=dmy[:], func=AF.Copy)
            P.op('act', fsq, reads=['PO'], writes=['atm', 'sso', 'dmy'])
            P.op('act', lambda e: e.activation(out=sso[:], in_=sso[:], func=AF.Ln, scale=1.0 / 128.0, bias=EPS_AP[:]), reads=['sso', 'eps_ap'], writes=['sso'])
            P.op('act', lambda e: e.activation(out=rso[:], in_=sso[:], func=AF.Exp, scale=-0.5), reads=['sso'], writes=['rso'])
            def fmx(e):
                ins = None
                for h in range(4):
                    ins = e.scalar_tensor_tensor(out=mix[:, 512 + h * 128:512 + (h + 1) * 128], in0=PO[:, h * 128:(h + 1) * 128], scalar=rso[:, h:h + 1],
                                                 in1=gate[par][:, h * 128:(h + 1) * 128], op0=ALU.mult, op1=ALU.mult)
                return ins
            P.op('dve', fmx, reads=['PO', 'rso', 'gate%d' % par], writes=['mixh'])
            transposes_bf([mix[:, i * 128:(i + 1) * 128] for i in range(4)], ['mixa'], mixT[:, 0:4, :], 'qeT', 128)
            wob = [(PV, 'PV'), (PT, 'PT')]
            for half in range(2):
                wbank, wkey = wob[half]
                def fw0(e):
                    ins = None
                    for kc in range(4):
                        ins = e.matmul(wbank[:], lhsT=mixT[:, kc, :], rhs=w_o_sb[:, kc, half * 512:(half + 1) * 512], start=(kc == 0), stop=False)
                    return ins
                P.op('pe', fw0, reads=['qeT'] + WO_KEYS, writes=[wkey])
            yield 99
            transposes_bf([mix[:, i * 128:(i + 1) * 128] for i in range(4, 8)], ['mixh'], mixT[:, 4:8, :], 'keT', 128, evac='dve')
            for half in range(2):
                wbank, wkey = wob[half]
                def fw1(e):
                    ins = None
                    for kc in range(4, 8):
                        ins = e.matmul(wbank[:], lhsT=mixT[:, kc, :], rhs=w_o_sb[:, kc, half * 512:(half + 1) * 512], start=False, stop=(kc == 7))
                    return ins
                P.op('pe', fw1, reads=['keT'] + WO_KEYS, writes=[wkey])
                P.op('dve', lambda e: e.tensor_tensor(out=xres[:, ti, half * 512:(half + 1) * 512], in0=xres[:, ti, half * 512:(half + 1) * 512], in1=wbank[:], op=ALU.add),
                     reads=[xk, wkey], writes=[xk])
                yield 0

        def drain(g):
            for _ in g:
                pass

        def interleave(gb, gf):
            done_f = False
            for nf in gb:
                for _ in range(nf or 0):
                    if done_f:
                        break
                    try:
                        next(gf)
                    except StopIteration:
                        done_f = True
            if not done_f:
                drain(gf)

        for seg in range(NSEG):
            tiles = list(range(TPS)) + ([TPS] if seg == 0 else [])
            for ti in range(TPS):
                if ti == 0 or seg > 0:
                    continue
                P.op('sp', lambda e: e.dma_start(out=xres[:, ti, :], in_=x_seq[(seg * TPS + ti) * 128:(seg * TPS + ti + 1) * 128, :]),
                     writes=['x%d' % ti], dma='lx%d' % ti)
            P.op('sp', lambda e: e.dma_start(out=cosp[:], in_=c_cosp[:, seg * TPS:(seg + 1) * TPS, :]), writes=['cosp'], dma='l_cos')
            P.op('sp', lambda e: e.dma_start(out=sinp[:], in_=c_sinp[:, seg * TPS:(seg + 1) * TPS, :]), writes=['sinp'], dma='l_sin')
            if seg == 0:
                P.op('sp', lambda e: e.dma_start(out=xres[0:64, TPS, :], in_=x_smp), writes=['x%d' % TPS], dma='lx%d' % TPS)
            for l in range(2):
                if not (seg == 0 and l == 0):
                    P.barrier()
                def load_pass(p_):
                    load_pass_l(l, p_)
                if l != 0:
                    load_phase_a(l)
                if seg == 0:
                    P.op('sp', lambda e: e.dma_start(out=nk_s[l, :, 0:124, :], in_=ck[l, :, 4:128, :]), dma='o_ckc')
                    P.op('sp', lambda e: e.dma_start(out=nv_s[l, :, 0:124, :], in_=cv[l, :, 4:128, :]), dma='o_cvc')
                n = len(tiles)
                prepped = set()
                drain(F_tile(l, tiles[0], seg, tiles[0] == TPS, 0))
                load_phase_a2(l)
                for tix, ti in enumerate(tiles):
                    gb = B_tile(l, ti, seg, ti == TPS, tix % 2)
                    if tix + 1 < n:
                        gf = F_tile(l, tiles[tix + 1], seg, tiles[tix + 1] == TPS, (tix + 1) % 2)
                        interleave(gb, gf)
                    elif ti == TPS:
                        def prep_gen():
                            for t_ in range(TPS):
                                rms_stats(xres[:, t_, :], 'x%d' % t_, rstd[:], 'rstd', sq=rr2[:, t_:t_ + 1])
                                for _ in make_hT(t_, 2 + l, hT_all[:, :, t_ * 128:(t_ + 1) * 128], 'hTall', pj_only=True):
                                    pass
                                prepped.add(t_)
                                yield
                        interleave(gb, prep_gen())
                    else:
                        drain(gb)
                P.barrier()
                ntok = n * 128
                for ti in tiles:
                    if ti in prepped:
                        continue
                    rms_stats(xres[:, ti, :], 'x%d' % ti, rstd[:], 'rstd', sq=rr2[:, ti:ti + 1])
                    drain(make_hT(ti, 2 + l, hT_all[:, :, ti * 128:(ti + 1) * 128], 'hTall'))
                groups = [(g0, min(GRP, ntok - g0)) for g0 in range(0, ntok, GRP)]

                def up(p_, g0, gn, ui):
                    slot = p_ % 2
                    wu, wd = ring[slot]
                    uT = uTs[ui]
                    for fc in range(4):
                        psi = nxt('ps')
                        def fup(e):
                            ins = None
                            for kc in range(8):
                                ins = e.matmul(PS[psi][:, 0:gn], lhsT=wu[:, kc, fc * 128:(fc + 1) * 128], rhs=hT_all[:, kc, g0:g0 + gn], start=(kc == 0), stop=(kc == 7))
                            return ins
                        P.op('pe', fup, reads=['ringu%d_%d' % (slot, k_) for k_ in range(8)] + ['hTall'], writes=['PS%d' % psi])
                        P.op('act', lambda e: e.activation(out=rl[:, 0:gn], in_=PS[psi][:, 0:gn], func=AF.Relu), reads=['PS%d' % psi], writes=['hTa'])
                        P.op('dve', lambda e: e.tensor_tensor(out=uT[:, fc, 0:gn], in0=rl[:, 0:gn], in1=rl[:, 0:gn], op=ALU.mult), reads=['hTa'], writes=['uT%d' % ui])

                def down(p_, g0, gn, ui):
                    slot = p_ % 2
                    wu, wd = ring[slot]
                    uT = uTs[ui]
                    for tt in range(gn // 128):
                        ti = (g0 // 128) + tt
                        for half in range(2):
                            pj = nxt('pj')
                            def fdn(e):
                                ins = None
                                for fc in range(4):
                                    ins = e.matmul(PJ[pj][:], lhsT=uT[:, fc, tt * 128:(tt + 1) * 128], rhs=wd[:, fc, half * 512:(half + 1) * 512], start=(fc == 0), stop=(fc == 3))
                                return ins
                            P.op('pe', fdn, reads=['uT%d' % ui] + ['ringd%d_%d' % (slot, k_) for k_ in range(4)], writes=['PJ%d' % pj])
                            P.op('dve', lambda e: e.scalar_tensor_tensor(
                                out=xres[:, ti, half * 512:(half + 1) * 512], in0=PJ[pj][:], scalar=rr2[:, ti:ti + 1],
                                in1=xres[:, ti, half * 512:(half + 1) * 512], op0=ALU.mult, op1=ALU.add),
                                reads=['PJ%d' % pj, 'rr2', 'x%d' % ti], writes=['x%d' % ti])

                def final_tile(ti):
                    rms_stats(xres[:, ti, :], 'x%d' % ti, rstd[:], 'rstd')
                    yi = nxt('y')
                    P.op('dve', lambda e: e.scalar_tensor_tensor(out=ysb[yi], in0=xres[:, ti, :], scalar=rstd[:], in1=fing, op0=ALU.mult, op1=ALU.mult),
                         reads=['x%d' % ti, 'rstd', 'fing'], writes=['ysb%d' % yi])
                    if ti == TPS:
                        P.op('sp', lambda e: e.dma_start(out=y_smp, in_=ysb[yi][0:64, :]), reads=['ysb%d' % yi], dma='o_y%d' % yi)
                    else:
                        P.op('sp', lambda e: e.dma_start(out=y_seq[(seg * TPS + ti) * 128:(seg * TPS + ti + 1) * 128, :], in_=ysb[yi]),
                             reads=['ysb%d' % yi], dma='o_y%d' % yi)
                        if seg + 1 < NSEG:
                            P.op('sp', lambda e: e.dma_start(out=xres[:, ti, :], in_=x_seq[((seg + 1) * TPS + ti) * 128:((seg + 1) * TPS + ti + 1) * 128, :]),
                                 writes=['x%d' % ti], dma='lx%d' % ti)

                def after_down(pv):
                    if l == 1 and pv[0] == NPASS - 1:
                        for tt in range(pv[2] // 128):
                            final_tile(pv[1] // 128 + tt)

                if l == 1:
                    P.op('sp', lambda e: e.dma_start(out=fing, in_=fin_g.partition_broadcast(128)), writes=['fing'], dma='c10')
                work = [(p_, g0, gn) for p_ in range(NPASS) for (g0, gn) in groups]
                prev = None
                for wi, (p_, g0, gn) in enumerate(work):
                    ui = nxt('u')
                    up(p_, g0, gn, ui)
                    if prev is not None:
                        down(*prev)
                        after_down(prev)
                    if g0 == 0 and p_ + 1 < NPASS:
                        load_pass(p_ + 1)
                    prev = (p_, g0, gn, ui)
                down(*prev)
                after_down(prev)
            P.barrier()
            if seg + 1 < NSEG:
                load_phase_a(0)
            P.barrier()
        P.emit()
    return nc


def _consts():
    c = {}
    c['c_identf'] = np.eye(128, dtype=np.float32)
    c['c_identb'] = np.eye(128, dtype=np.float32)
    half = 8
    inv_freq = np.power(np.float32(500000.0), -np.arange(half, dtype=np.float32) * np.float32(2.0 / 16)).astype(np.float32)
    pos = np.arange(SEQ, dtype=np.float32)
    ang = (pos[:, None] * inv_freq[None, :]).astype(np.float32)
    c['c_cosp'] = np.cos(ang).astype(np.float32).reshape(32, 128, 8).transpose(1, 0, 2).copy()
    c['c_sinp'] = np.sin(ang).astype(np.float32).reshape(32, 128, 8).transpose(1, 0, 2).copy()
    p = np.arange(128)
    tt = p // 16
    jj = p % 16
    valid = p < 64
    pos_s = (PAST + tt).astype(np.float32)
    ang_s = (pos_s[:, None] * inv_freq[None, :]).astype(np.float32)
    c['c_coss'] = np.where(valid[:, None], np.cos(ang_s), 1.0).astype(np.float32)
    c['c_sins'] = np.where(valid[:, None], np.sin(ang_s), 0.0).astype(np.float32)
    s = p[:, None]; t = p[None, :]
    ncur = np.where(s <= t, 0.0, -30000.0).astype(np.float32)
    nprev = np.where(s >= t, 0.0, -30000.0).astype(np.float32)
    c['c_negm'] = np.stack([np.tile(ncur, (1, 4)), np.tile(nprev, (1, 4))], axis=1).astype(np.float32)
    ch = p // 32
    same = ch[:, None] == ch[None, :]
    ref = ch * 32 + 15
    mrel_p = same * ((s <= t).astype(np.float32) - (s <= ref[None, :]).astype(np.float32))
    mkd_p = same * (s > t)
    ma_p = same * (s <= t)
    same_s = (jj[:, None] == jj[None, :]) & valid[:, None] & valid[None, :]
    mrel_s = same_s * (tt[:, None] <= tt[None, :])
    mkd_s = same_s * (tt[:, None] > tt[None, :])
    ma_s = same_s * (tt[:, None] <= tt[None, :])
    c['c_mrel'] = np.stack([mrel_p, mrel_s]).astype(np.float32)
    c['c_mkd'] = np.stack([mkd_p, mkd_s]).astype(np.float32)
    c['c_ma'] = np.stack([ma_p, ma_s]).astype(np.float32)
    indp = np.zeros((128, 8), np.float32)
    for cc in range(4):
        indp[cc * 32:(cc + 1) * 32, cc] = 1.0
        indp[cc * 32:cc * 32 + 16, 4 + cc] = 1.0
    c['c_indp'] = indp
    inds = np.zeros((128, 16), np.float32)
    inds[p[valid], jj[valid]] = 1.0
    c['c_inds'] = inds
    c['c_msel'] = inds.copy()
    c['c_mc0'] = ((p[:, None] >= tt[None, :]) & valid[None, :]).astype(np.float32)
    mns = (same_s & (tt[:, None] <= tt[None, :])).astype(np.float32)
    c['c_mns'] = mns
    selT = np.zeros((128, 16, 64), np.float32)
    for j in range(16):
        selT[:, j, :] = ((jj == j) & valid)[None, 0:64]
    c['c_selT'] = selT
    return c


_CACHE = {}
PCORES = [0, 1, 4, 5]


def kernel(x_prompt, x_sample, cache_k, cache_v, state_hgrn, attn_norm, w_in, att_sinks,
           hgrn_lower_bounds, hgrn_out_norm, w_o, mlp_norm, w_up, w_down, final_norm):
    f = lambda a: np.ascontiguousarray(np.asarray(a, dtype=np.float32))
    x_prompt, x_sample, cache_k, cache_v, state_hgrn = map(f, (x_prompt, x_sample, cache_k, cache_v, state_hgrn))
    if 'nc' not in _CACHE:
        _CACHE['nc'] = build_program()
        _CACHE['consts'] = _consts()
    nc = _CACHE['nc']
    consts = _CACHE['consts']
    gains = np.concatenate([f(attn_norm).reshape(2, 8, 128), f(mlp_norm).reshape(2, 8, 128), f(final_norm).reshape(1, 8, 128)], axis=0).reshape(40, 128)
    shared = dict(gains=np.ascontiguousarray(gains), fin_g=f(final_norm).reshape(1, D), w_in=f(w_in), sinks=f(att_sinks),
                  lbraw=f(hgrn_lower_bounds), onorm=f(hgrn_out_norm), w_o=f(w_o), w_up=f(w_up), w_dn=f(w_down))
    shared.update(consts)
    zero_seq = np.zeros((SEQ, D), np.float32)
    in_maps = []
    for c in range(8):
        sl = slice(c * NSMP, (c + 1) * NSMP)
        m = dict(shared)
        m['x_seq'] = x_prompt[PCORES.index(c)] if c in PCORES else zero_seq
        m['x_smp'] = np.ascontiguousarray(x_sample[sl].transpose(1, 0, 2).reshape(64, D))
        m['ck'] = np.ascontiguousarray(cache_k[:, sl].reshape(2, NSMP, 128, 128))
        m['cv'] = np.ascontiguousarray(cache_v[:, sl].reshape(2, NSMP, 128, 128))
        m['st_in'] = np.ascontiguousarray(state_hgrn[:, sl])
        in_maps.append(m)
    res = run_bass_kernel_spmd(nc, in_maps, core_ids=list(range(8)))
    R = res.results
    y_prompt = np.stack([R[b]['y_seq'] for b in PCORES]).astype(np.float32)
    y_sample = np.concatenate([R[c]['y_smp'].reshape(4, NSMP, D).transpose(1, 0, 2) for c in range(8)], axis=0).astype(np.float32)
    nk_p = np.stack([R[b]['nk_p'] for b in PCORES], axis=1).reshape(2, 4, 128, 2, 64)
    nv_p = np.stack([R[b]['nv_p'] for b in PCORES], axis=1).reshape(2, 4, 128, 2, 64)
    ns_p = np.stack([R[b]['ns_p'] for b in PCORES], axis=1)
    nk_s = np.concatenate([R[c]['nk_s'] for c in range(8)], axis=1).reshape(2, 128, 128, 2, 64)
    nv_s = np.concatenate([R[c]['nv_s'] for c in range(8)], axis=1).reshape(2, 128, 128, 2, 64)
    ns_s = np.concatenate([R[c]['ns_s'] for c in range(8)], axis=1)
    return (y_prompt, y_sample, np.ascontiguousarray(nk_p), np.ascontiguousarray(nv_p), np.ascontiguousarray(ns_p),
            np.ascontiguousarray(nk_s), np.ascontiguousarray(nv_s), np.ascontiguousarray(ns_s))
```

```python
import numpy as np
from contextlib import ExitStack
import concourse.bass as bass
import concourse.mybir as mybir
from concourse.bass_utils import run_bass_kernel_spmd

F32 = mybir.dt.float32
BF16 = mybir.dt.bfloat16
AF = mybir.ActivationFunctionType
ALU = mybir.AluOpType
AX = mybir.AxisListType

D = 1024
SEQ = 4096
NSEG = 2
TPS = 16
NSMP = 16
PAST = 16384
EPS = 1e-6
INW = 2816
DFF = 4096
NPASS = 8
GRP = 256


class Prog:
    ENGS = ('pe', 'act', 'dve', 'pool', 'sp')

    def __init__(self, nc, es):
        self.nc = nc
        self.es = es
        self.eng = {'pe': nc.tensor, 'act': nc.scalar, 'dve': nc.vector, 'pool': nc.gpsimd, 'sp': nc.sync}
        self.cnt = {e: 0 for e in self.ENGS}
        self.semh = {k: es.enter_context(nc.semaphore('s_' + k)) for k in self.ENGS}
        self.dma_cnt = {}
        self.last_write = {}
        self.reads_since = {}
        self.seen = {e: {} for e in self.ENGS}
        self.know = {}
        self.nins = 0
        self.nwait = 0

    def _wait(self, e, tok):
        key, val = tok
        if self.seen[e].get(key, 0) >= val:
            return
        se = self.seen[e]
        se[key] = val
        self.eng[e].wait_ge(self.semh[key], val)
        self.nwait += 1
        kn = self.know.get(tok)
        if kn:
            for k2, v2 in kn.items():
                if se.get(k2, 0) < v2:
                    se[k2] = v2

    def op(self, e, fn, reads=(), writes=(), dma=None):
        deps = []
        for r in reads:
            if r in self.last_write:
                deps.append(self.last_write[r])
        for w in writes:
            rs = self.reads_since.get(w, ())
            if rs:
                deps.extend(rs)
            elif w in self.last_write:
                deps.append(self.last_write[w])
        mx = {}
        for key, val in deps:
            if key == 'pe' and e == 'pe' and dma is None:
                continue
            if val > mx.get(key, 0):
                mx[key] = val
        for key, val in mx.items():
            self._wait(e, (key, val))
        self.nins += 1
        if dma is None:
            self.cnt[e] += 1
            tok = (e, self.cnt[e])
            fn(self.eng[e]).then_inc(self.semh[e], 1)
        else:
            k = 'dma:' + dma
            if k not in self.semh:
                self.semh[k] = self.es.enter_context(self.nc.semaphore('d%d' % len(self.semh)))
            self.dma_cnt[k] = self.dma_cnt.get(k, 0) + 16
            tok = (k, self.dma_cnt[k])
            fn(self.eng[e]).then_inc(self.semh[k], 16)
        self.know[tok] = dict(self.seen[e])
        for r in reads:
            self.reads_since.setdefault(r, []).append(tok)
        for w in writes:
            self.last_write[w] = tok
            self.reads_since[w] = []
        return tok

    def barrier(self, skip=()):
        toks = [(k, v) for k, v in self.dma_cnt.items() if not any(k.startswith('dma:' + s) for s in skip)] + \
               [(k, self.cnt[k]) for k in self.ENGS if self.cnt[k]]
        for e in self.ENGS:
            for t in toks:
                self._wait(e, t)

    def emit(self):
        self.barrier()


def bc(ap, axis, shape):
    return ap.unsqueeze(axis).broadcast_to(shape)


def build_program():
    nc = bass.Bass("TRN2", target_bir_lowering=False)

    def din(name, shape, dt=F32):
        return nc.dram_tensor(name, list(shape), dt, kind="ExternalInput").ap()

    def dout(name, shape):
        return nc.dram_tensor(name, list(shape), F32, kind="ExternalOutput").ap()

    x_seq = din("x_seq", [SEQ, D])
    x_smp = din("x_smp", [64, D])
    ck = din("ck", [2, NSMP, 128, 128])
    cv = din("cv", [2, NSMP, 128, 128])
    st_in = din("st_in", [2, NSMP, 4, 128, 128])
    gains = din("gains", [40, 128])
    fin_g = din("fin_g", [1, D])
    w_in = din("w_in", [2, D, INW])
    sinks = din("sinks", [2, 8])
    lbraw = din("lbraw", [2, 512])
    onorm = din("onorm", [2, 128])
    w_o = din("w_o", [2, D, D])
    w_up = din("w_up", [2, D, DFF])
    w_dn = din("w_dn", [2, DFF, D])
    c_identf = din("c_identf", [128, 128])
    c_identb = din("c_identb", [128, 128])
    c_cosp = din("c_cosp", [128, 32, 8])
    c_sinp = din("c_sinp", [128, 32, 8])
    c_coss = din("c_coss", [128, 8])
    c_sins = din("c_sins", [128, 8])
    c_negm = din("c_negm", [128, 2, 512])
    c_mrel = din("c_mrel", [2, 128, 128])
    c_mkd = din("c_mkd", [2, 128, 128])
    c_ma = din("c_ma", [2, 128, 128])
    c_indp = din("c_indp", [128, 8])
    c_inds = din("c_inds", [128, 16])
    c_msel = din("c_msel", [128, 16])
    c_mc0 = din("c_mc0", [128, 128])
    c_mns = din("c_mns", [128, 128])
    c_selT = din("c_selT", [128, 16, 64])

    y_seq = dout("y_seq", [SEQ, D])
    y_smp = dout("y_smp", [64, D])
    nk_p = dout("nk_p", [2, 128, 128])
    nv_p = dout("nv_p", [2, 128, 128])
    ns_p = dout("ns_p", [2, 4, 128, 128])
    nk_s = dout("nk_s", [2, NSMP, 128, 128])
    nv_s = dout("nv_s", [2, NSMP, 128, 128])
    ns_s = dout("ns_s", [2, NSMP, 4, 128, 128])

    with ExitStack() as es:
        def sb(name, shape, dt=F32):
            return es.enter_context(nc.sbuf_tensor(name, list(shape), dt))

        def ps(name, shape, dt=F32):
            return es.enter_context(nc.psum_tensor(name, list(shape), dt))

        NT = TPS + 1
        xres = sb("xres", [128, NT, D])
        hT_elems = 8 * NT * 128
        ring_slot = 8 * 512 + 4 * D
        uT_elems = 4 * GRP
        rg_b = max(8 * INW + 8 * D + ring_slot, hT_elems + ring_slot + 2 * uT_elems)
        rgn32 = sb("rgn", [128, rg_b // 2])
        rgnb = rgn32[:].bitcast(BF16)
        w_in_sb = rgnb[:, 0:8 * INW].rearrange("p (k n) -> p k n", k=8)
        w_o_sb = rgnb[:, 8 * INW:8 * INW + 8 * D].rearrange("p (k n) -> p k n", k=8)
        hT_all = rgnb[:, 0:hT_elems].rearrange("p (k t) -> p k t", k=8)
        ring = []
        for o0 in (8 * INW + 8 * D, hT_elems):
            wu = rgnb[:, o0:o0 + 4096].rearrange("p (k n) -> p k n", k=8)
            wd = rgnb[:, o0 + 4096:o0 + 8192].rearrange("p (k n) -> p k n", k=4)
            ring.append((wu, wd))
        o0 = hT_elems + ring_slot
        assert o0 + 2 * uT_elems <= 8 * INW + 8 * D
        uTs = []
        for s in range(2):
            uTs.append(rgnb[:, o0:o0 + uT_elems].rearrange("p (k n) -> p k n", k=4))
            o0 += uT_elems

        identf = sb("identf", [128, 128]); identb = sb("identb", [128, 128], BF16)
        cosp = sb("cosp", [128, TPS, 8]); sinp = sb("sinp", [128, TPS, 8])
        coss = sb("coss", [128, 8]); sins = sb("sins", [128, 8])
        negm = sb("negm", [128, 2, 512], BF16)
        mrel = sb("mrel", [128, 2, 128]); mkd = sb("mkd", [128, 2, 128]); ma = sb("ma", [128, 2, 128], BF16)
        indp = sb("indp", [128, 8]); inds = sb("inds", [128, 16]); msel = sb("msel", [128, 16])
        mc0 = sb("mc0", [128, 128], BF16); mns = sb("mns", [128, 128], BF16)
        selT = sb("selT", [128, 16, 64], BF16)
        gT = sb("gT", [128, 40])
        esink = sb("esink", [128, 2, 8])
        lb1 = sb("lb1", [128, 512])
        onb = sb("onb", [128, 2, 128])
        EPS_AP = sb("eps_ap", [128, 1])

        ss = sb("ss", [128, 1]); rstd = sb("rstd", [128, 1]); nrstd = sb("nrstd", [128, 1])
        rr2 = sb("rr2", [128, NT])
        dmy = sb("dmy", [128, 1])
        hT = sb("hT", [128, 8, 128], BF16)
        junk = hT[:].rearrange("p k t -> p (k t)")
        rl = junk[:, 0:512]
        ft = sb("ft", [128, 4096])
        fing = ft[:, 0:D]
        ysb = [ft[:, D:2 * D], ft[:, 2 * D:3 * D]]
        qa = ft[:, 0:512].rearrange("p (h d) -> p h d", h=8)
        ka = sb("ka", [128, 2, 64]); va = sb("va", [128, 128])
        qab = sb("qab", [128, 8, 64], BF16); kab = sb("kab", [128, 2, 64], BF16)
        qT = [sb("qT%d" % i, [64, 8, 128], BF16) for i in range(2)]
        kT = [[sb("kT%d_%d" % (l, i), [64, 2, 128], BF16) for i in range(2)] for l in range(2)]
        vaug = [[sb("vaug%d_%d" % (l, i), [128, 2, 65], BF16) for i in range(2)] for l in range(2)]
        pT = [sb("pT%d" % i, [128, 4, 128], BF16) for i in range(2)]
        den = sb("den", [128, 4]); rden = sb("rden", [128, 4])
        mix = sb("mix", [128, D], BF16)
        e1 = ft[:, 512:1024]
        qh = ft[:, 1024:1536]; fg = ft[:, 1536:2048]; gl = ft[:, 2048:2560]; kk = ft[:, 2560:3072]
        cl = ft[:, 3072:3584]; eq = ft[:, 3584:4096]; ek = e1
        rt = [cl[:, 0:64].rearrange("p (h d) -> p h d", h=8), cl[:, 64:128].rearrange("p (h d) -> p h d", h=8),
              eq[:, 0:64].rearrange("p (h d) -> p h d", h=8), eq[:, 64:128].rearrange("p (h d) -> p h d", h=8)]
        RTK = ["cl", "cl", "eq", "eq"]
        vh = [sb("vh%d" % i, [128, 512], BF16) for i in range(2)]
        gate = [sb("gate%d" % i, [128, 512], BF16) for i in range(2)]
        qe = [sb("qe%d" % i, [128, 4, 128], BF16) for i in range(2)]
        ke = [sb("ke%d" % i, [128, 4, 128], BF16) for i in range(2)]
        kd = [sb("kd%d" % i, [128, 512], BF16) for i in range(2)]
        bg = [sb("bg%d" % i, [128, 4, 16]) for i in range(2)]
        tq = sb("tq", [128, 8, 128], BF16)
        qeT = tq[:, 0:4, :]; keT = tq[:, 4:8, :]; mixT = tq[:]
        atm = sb("atm", [128, 4, 128], BF16)
        S = [sb("S%d" % l, [128, 4, 128]) for l in range(2)]
        Sp = [sb("Sp%d" % c, [128, 4, 128], BF16) for c in range(2)]
        sso = sb("sso", [128, 4]); rso = sb("rso", [128, 4])
        kcT2 = [Sp[1][0:64, 0:2, :], Sp[1][0:64, 2:4, :]]
        kcsA = ft[:, 0:1024].bitcast(BF16).rearrange("p (j d) -> p j d", j=NSMP)
        vcsA = ft[:, 1024:2064].bitcast(BF16).rearrange("p (j g d) -> p j g d", j=NSMP, g=2)
        sstA = [ft[:, 2064:2576].rearrange("p (h d) -> p h d", h=4), ft[:, 2576:3088].rearrange("p (h d) -> p h d", h=4)]
        stmp = ft[:, 3088:3600].rearrange("p (h d) -> p h d", h=4)
        sbf = ft[:, 3600:3856].bitcast(BF16).rearrange("p (h d) -> p h d", h=4)

        PT = ps("PT", [128, 512])
        PB = ps("PB", [128, 1024], BF16)
        PJ = [ps("PJ0", [128, 512]), ps("PJ1", [128, 512])]
        PS = [ps("PS0", [128, 512]), ps("PS1", [128, 512])]
        PV = ps("PV", [128, 512])
        PO = ps("PO", [128, 512])

        P = Prog(nc, es)
        cnt = {'pj': 0, 'ps': 0, 'y': 0, 'pt': 0, 'u': 0, 'pb': 0}

        def nxt(k, n=2):
            v = cnt[k] % n
            cnt[k] += 1
            return v


        WGRP = [(0, 512), (512, 768), (768, 1280), (1280, 1792), (1792, 2304), (2304, 2816)]

        def load_pass_l(l, p_):
            slot = p_ % 2
            wu, wd = ring[slot]
            for kc in range(8):
                P.op('pool', lambda e: e.dma_start(out=wu[:, kc, :], in_=w_up[l, kc * 128:(kc + 1) * 128, p_ * 512:(p_ + 1) * 512]), writes=['ringu%d_%d' % (slot, kc)], dma='lru%d' % slot)
            for fc in range(4):
                P.op('pool', lambda e: e.dma_start(out=wd[:, fc, :], in_=w_dn[l, p_ * 512 + fc * 128: p_ * 512 + (fc + 1) * 128, :]), writes=['ringd%d_%d' % (slot, fc)], dma='lrd%d' % slot)

        def load_phase_a(l):
            for gi, (c0, c1) in enumerate(WGRP):
                P.op('pool', lambda e: e.dma_start(out=w_in_sb[:, :, c0:c1], in_=w_in[l, :, c0:c1].rearrange("(k p) n -> p k n", p=128)), writes=['w_in_g%d' % gi], dma='lw_in%d' % gi)

        def load_phase_a2(l):
            for kc in range(8):
                P.op('pool', lambda e: e.dma_start(out=w_o_sb[:, kc, :], in_=w_o[l, kc * 128:(kc + 1) * 128, :]), writes=['w_o%d' % kc], dma='lw_o')
            load_pass_l(l, 0)

        P.op('sp', lambda e: e.dma_start(out=xres[:, 0, :], in_=x_seq[0:128, :]), writes=['x0'], dma='lx0')
        load_phase_a(0)

        cl_list = [(identf, c_identf), (coss, c_coss), (sins, c_sins),
                   (indp, c_indp), (inds, c_inds), (msel, c_msel)]
        for i, (t, d) in enumerate(cl_list):
            P.op('sp', lambda e: e.dma_start(out=t[:], in_=d), writes=[t.name], dma='c%d' % i)
        P.op('sp', lambda e: e.dma_start(out=mrel[:], in_=c_mrel.rearrange("a p n -> p a n")), writes=['mrel'], dma='c20')
        P.op('sp', lambda e: e.dma_start(out=mkd[:], in_=c_mkd.rearrange("a p n -> p a n")), writes=['mkd'], dma='c21')
        cb_list = [(identb, c_identb), (negm, c_negm), (mc0, c_mc0), (mns, c_mns), (selT, c_selT)]
        for i, (t, d) in enumerate(cb_list):
            P.op('pool', lambda e: e.dma_start(out=t[:], in_=d), writes=[t.name], dma='cb%d' % i)
        P.op('pool', lambda e: e.dma_start(out=ma[:], in_=c_ma.rearrange("a p n -> p a n")), writes=['ma'], dma='cb9')
        P.op('sp', lambda e: e.dma_start(out=esink[:].rearrange("p l h -> p (l h)"), in_=sinks.rearrange("l h -> (l h)").unsqueeze(0).partition_broadcast(128)), writes=['esink'], dma='c11')
        gstage = ft[0:40, 1024:1152]
        lbr = ft[:, 0:1024].rearrange("p (l c) -> p l c", l=2)
        P.op('sp', lambda e: e.dma_start(out=ft[:, 0:1024], in_=lbraw.rearrange("l c -> (l c)").unsqueeze(0).partition_broadcast(128)), writes=['lbr'], dma='c12')
        P.op('sp', lambda e: e.dma_start(out=onb[:].rearrange("p l h -> p (l h)"), in_=onorm.rearrange("l c -> (l c)").unsqueeze(0).partition_broadcast(128)), writes=['onb'], dma='c13')
        P.op('sp', lambda e: e.dma_start(out=gstage, in_=gains), writes=['gstage'], dma='c19')
        P.op('act', lambda e: e.activation(out=esink[:], in_=esink[:], func=AF.Exp), reads=['esink'], writes=['esink'])
        P.op('dve', lambda e: e.tensor_tensor(out=lb1[:], in0=lbr[:, 0, :], in1=lbr[:, 1, :], op=ALU.subtract), reads=['lbr'], writes=['lb1'])
        P.op('act', lambda e: e.activation(out=lb1[:], in_=lb1[:], func=AF.Exp), reads=['lb1'], writes=['lb1'])
        P.op('dve', lambda e: e.tensor_scalar_add(out=lb1[:], in0=lb1[:], scalar1=1.0), reads=['lb1'], writes=['lb1'])
        P.op('dve', lambda e: e.reciprocal(out=lb1[:], in_=lb1[:]), reads=['lb1'], writes=['lb1'])
        P.op('pe', lambda e: e.transpose(PT[:, 0:40], gstage, identf[0:40, 0:40]), reads=['gstage', 'identf'], writes=['PT'])
        P.op('dve', lambda e: e.tensor_copy(out=gT[:], in_=PT[:, 0:40]), reads=['PT'], writes=['gT'])
        P.op('pool', lambda e: e.memset(dmy[:], 0.0), writes=['dmy'])
        P.op('pool', lambda e: e.memset(EPS_AP[:], EPS), writes=['eps_ap'])
        for l in range(2):
            for i in range(2):
                P.op('pool', lambda e: e.memset(vaug[l][i][:], 1.0), writes=['vaug%d_%d' % (l, i)])
        P.op('pool', lambda e: e.memset(xres[:, TPS, :], 0.0), writes=['x%d' % TPS])
        for l in range(2):
            P.op('pool', lambda e: e.memset(S[l][:], 0.0), writes=['S%d' % l])
        P.barrier(skip=('lw_in', 'lw_o', 'lru', 'lrd', 'lx'))

        def rms_stats(xt_ap, xkey, out_r, out_key, also_neg=None, sq=None):
            def f(e):
                e.activation(out=junk, in_=xt_ap, func=AF.Square, accum_out=ss[:])
                return e.activation(out=dmy[:], in_=dmy[:], func=AF.Copy)
            P.op('act', f, reads=[xkey], writes=['hTa', 'hTb', 'ss', 'dmy'])
            P.op('act', lambda e: e.activation(out=ss[:], in_=ss[:], func=AF.Ln, scale=1.0 / D, bias=EPS_AP[:]), reads=['ss', 'eps_ap'], writes=['ss'])
            P.op('act', lambda e: e.activation(out=out_r, in_=ss[:], func=AF.Exp, scale=-0.5), reads=['ss'], writes=[out_key])
            if also_neg is not None:
                P.op('dve', lambda e: e.tensor_scalar_mul(out=also_neg, in0=out_r, scalar1=-1.0), reads=[out_key], writes=['nrstd'])
            if sq is not None:
                P.op('act', lambda e: e.activation(out=sq, in_=ss[:], func=AF.Exp, scale=-1.0), reads=['ss'], writes=['rr2'])

        def make_hT(ti, gcol, dst, dst_key, pj_only=False):
            for half in range(2):
                if half == 0 and not pj_only:
                    bank, bkey = PT, 'PT'
                else:
                    pj_ = nxt('pj')
                    bank, bkey = PJ[pj_], 'PJ%d' % pj_
                def f(e):
                    ins = None
                    for j in range(4):
                        kc = half * 4 + j
                        ins = e.transpose(bank[:, j * 128:(j + 1) * 128], xres[:, ti, kc * 128:(kc + 1) * 128], identf[:])
                    return ins
                P.op('pe', f, reads=['x%d' % ti, 'identf'], writes=[bkey])
                P.op('dve', lambda e: e.tensor_tensor(
                    out=dst[:, half * 4:half * 4 + 4, :], in0=bank[:].rearrange("p (k t) -> p k t", k=4),
                    in1=bc(gT[:, gcol * 8 + half * 4:gcol * 8 + half * 4 + 4], 2, [128, 4, 128]), op=ALU.mult),
                    reads=[bkey, 'gT'], writes=[(dst_key + 'ab'[half]) if dst_key == 'hT' else dst_key])
                yield

        WO_KEYS = ['w_o%d' % k_ for k_ in range(8)]

        def proj_group(c0, c1):
            pj = nxt('pj')
            for hf in range(2):
                def f(e):
                    ins = None
                    for kc in range(hf * 4, hf * 4 + 4):
                        ins = e.matmul(PJ[pj][:, 0:c1 - c0], lhsT=hT[:, kc, :], rhs=w_in_sb[:, kc, c0:c1], start=(kc == 0), stop=(kc == 7))
                    return ins
                P.op('pe', f, reads=['hT' + 'ab'[hf], 'w_in_g%d' % WGRP.index((c0, c1))], writes=['PJ%d' % pj])
            return pj

        def silu_from_psum(pj, dst, dst_key):
            k = 'PJ%d' % pj
            P.op('act', lambda e: e.activation(out=dst, in_=PJ[pj][:], func=AF.Copy, scale=rstd[:]), reads=[k, 'rstd'], writes=[dst_key])
            P.op('act', lambda e: e.activation(out=e1[:], in_=dst, func=AF.Exp, scale=-1.0), reads=[dst_key], writes=['e1'])
            P.op('act', lambda e: e.activation(out=e1[:], in_=e1[:], func=AF.Ln, bias=1.0), reads=['e1'], writes=['e1'])
            P.op('act', lambda e: e.activation(out=e1[:], in_=e1[:], func=AF.Exp, scale=-1.0), reads=['e1'], writes=['e1'])
            P.op('pool', lambda e: e.tensor_tensor(out=dst, in0=dst, in1=e1[:], op=ALU.mult), reads=[dst_key, 'e1'], writes=[dst_key])

        def rotary(src, nh, cos_ap, sin_ap, dstb, skey, dkey):
            x1 = src[:, :, 0:8]; x2 = src[:, :, 8:16]
            cb = bc(cos_ap, 1, [128, nh, 8]); sn = bc(sin_ap, 1, [128, nh, 8])
            r0, r1, r2, r3 = [t[:, 0:nh, :] for t in rt]
            ck_ = ['cosp', 'sinp']
            P.op('pool', lambda e: e.tensor_tensor(out=r0, in0=x1, in1=cb, op=ALU.mult), reads=[skey] + ck_, writes=[RTK[0]])
            P.op('pool', lambda e: e.tensor_tensor(out=r1, in0=x2, in1=sn, op=ALU.mult), reads=[skey] + ck_, writes=[RTK[1]])
            P.op('pool', lambda e: e.tensor_tensor(out=r2, in0=x2, in1=cb, op=ALU.mult), reads=[skey] + ck_, writes=[RTK[2]])
            P.op('pool', lambda e: e.tensor_tensor(out=r3, in0=x1, in1=sn, op=ALU.mult), reads=[skey] + ck_, writes=[RTK[3]])
            P.op('pool', lambda e: e.tensor_tensor(out=x1, in0=r0, in1=r1, op=ALU.subtract), reads=['cl', skey], writes=[skey])
            P.op('pool', lambda e: e.tensor_tensor(out=x2, in0=r2, in1=r3, op=ALU.add), reads=['eq', skey], writes=[skey])
            P.op('pool', lambda e: e.tensor_copy(out=dstb, in_=src), reads=[skey], writes=[dkey])

        def transposes_bf(srcs, src_keys, dst, dst_key, rows, evac='act'):
            n = len(srcs)
            dkeys = dst_key if isinstance(dst_key, list) else [dst_key]
            def f(e):
                ins = None
                for i, s_ in enumerate(srcs):
                    ins = e.transpose(PB[0:rows, i * 128:i * 128 + 128], s_, identb[:])
                return ins
            P.op('pe', f, reads=list(src_keys) + ['identb'], writes=['PB'])
            src_v = PB[0:rows, 0:n * 128].rearrange("p (k t) -> p k t", k=n)
            if evac == 'act':
                P.op('act', lambda e: e.activation(out=dst, in_=src_v, func=AF.Copy), reads=['PB'], writes=dkeys)
            else:
                P.op('dve', lambda e: e.tensor_copy(out=dst, in_=src_v), reads=['PB'], writes=dkeys)

        def F_tile(l, ti, seg, is_smp, par):
            xk = 'x%d' % ti
            rms_stats(xres[:, ti, :], xk, rstd[:], 'rstd', also_neg=nrstd[:])
            for _ in make_hT(ti, l, hT, 'hT'):
                yield
            kTc, vac = kT[l][par], vaug[l][par]
            kTck, vack = 'kT%d_%d' % (l, par), 'vaug%d_%d' % (l, par)
            if is_smp:
                cos_ap, sin_ap = coss[:], sins[:]
            else:
                cos_ap, sin_ap = cosp[:, ti, :], sinp[:, ti, :]
            pj = proj_group(0, 512)
            P.op('act', lambda e: e.activation(out=qa[:].rearrange("p h d -> p (h d)"), in_=PJ[pj][:], func=AF.Copy, scale=rstd[:]),
                 reads=['PJ%d' % pj, 'rstd'], writes=['qa'])
            yield
            rotary(qa[:], 8, cos_ap, sin_ap, qab[:], 'qa', 'qab')
            pj = proj_group(512, 768)
            P.op('act', lambda e: e.activation(out=ka[:].rearrange("p h d -> p (h d)"), in_=PJ[pj][:, 0:128], func=AF.Copy, scale=rstd[:]),
                 reads=['PJ%d' % pj, 'rstd'], writes=['ka'])
            P.op('act', lambda e: e.activation(out=va[:], in_=PJ[pj][:, 128:256], func=AF.Copy, scale=rstd[:]),
                 reads=['PJ%d' % pj, 'rstd'], writes=['va'])
            yield
            rotary(ka[:], 2, cos_ap, sin_ap, kab[:], 'ka', 'kab')
            P.op('pool', lambda e: e.tensor_copy(out=vac[:, :, 0:64], in_=va[:].rearrange("p (g d) -> p g d", g=2)), reads=['va'], writes=[vack])
            if is_smp:
                for t in range(4):
                    P.op('sp', lambda e: e.dma_start(out=nk_s[l, :, 124 + t, :], in_=ka[t * 16:(t + 1) * 16].rearrange("p h d -> p (h d)")), reads=['ka'], dma='o_nk%d' % t)
                    P.op('sp', lambda e: e.dma_start(out=nv_s[l, :, 124 + t, :], in_=va[t * 16:(t + 1) * 16, :]), reads=['va'], dma='o_nv%d' % t)
            elif seg == NSEG - 1 and ti == TPS - 1:
                P.op('sp', lambda e: e.dma_start(out=nk_p[l], in_=ka[:].rearrange("p h d -> p (h d)")), reads=['ka'], dma='o_nkp')
                P.op('sp', lambda e: e.dma_start(out=nv_p[l], in_=va[:]), reads=['va'], dma='o_nvp')
            pj = proj_group(768, 1280)
            silu_from_psum(pj, qh[:], 'qh')
            yield
            pj = proj_group(1280, 1792)
            k = 'PJ%d' % pj
            P.op('act', lambda e: e.activation(out=e1[:], in_=PJ[pj][:], func=AF.Exp, scale=nrstd[:]), reads=[k, 'nrstd'], writes=['e1'])
            P.op('act', lambda e: e.activation(out=e1[:], in_=e1[:], func=AF.Ln, bias=1.0), reads=['e1'], writes=['e1'])
            P.op('act', lambda e: e.activation(out=fg[:], in_=e1[:], func=AF.Exp, scale=-1.0), reads=['e1'], writes=['fg'])
            yield
            transposes_bf([qab[:, h, :] for h in range(8)], ['qab'], qT[par][:], 'qT%d' % par, 64)
            yield
            P.op('pool', lambda e: e.tensor_scalar(out=kk[:], in0=fg[:], scalar1=-1.0, scalar2=1.0, op0=ALU.mult, op1=ALU.add), reads=['fg'], writes=['kk'])
            if l == 1:
                P.op('dve', lambda e: e.tensor_tensor(out=fg[:], in0=kk[:], in1=lb1[:], op=ALU.mult), reads=['kk', 'lb1'], writes=['fg'])
                P.op('pool', lambda e: e.tensor_tensor(out=kk[:], in0=kk[:], in1=fg[:], op=ALU.subtract), reads=['kk', 'fg'], writes=['kk'])
                P.op('pool', lambda e: e.tensor_scalar(out=fg[:], in0=kk[:], scalar1=-1.0, scalar2=1.0, op0=ALU.mult, op1=ALU.add), reads=['kk'], writes=['fg'])
            P.op('act', lambda e: e.activation(out=gl[:], in_=fg[:], func=AF.Ln), reads=['fg'], writes=['gl'])
            pj = proj_group(1792, 2304)
            P.op('act', lambda e: e.activation(out=vh[par][:], in_=PJ[pj][:], func=AF.Copy, scale=rstd[:]), reads=['PJ%d' % pj, 'rstd'], writes=['vh%d' % par])
            yield
            pj = proj_group(2304, 2816)
            silu_from_psum(pj, fg[:], 'fg')
            P.op('pool', lambda e: e.tensor_tensor(out=gate[par][:].rearrange("p (h d) -> p h d", h=4), in0=fg[:].rearrange("p (h d) -> p h d", h=4),
                                                   in1=bc(onb[:, l, :], 1, [128, 4, 128]), op=ALU.mult), reads=['fg', 'onb'], writes=['gate%d' % par])
            yield
            mi = 1 if is_smp else 0
            p1 = nxt('pj'); p2 = nxt('pj')
            P.op('pe', lambda e: e.matmul(PJ[p1][:], lhsT=mrel[:, mi, :], rhs=gl[:], start=True, stop=True), reads=['mrel', 'gl'], writes=['PJ%d' % p1])
            P.op('pe', lambda e: e.matmul(PJ[p2][:], lhsT=mkd[:, mi, :], rhs=gl[:], start=True, stop=True), reads=['mkd', 'gl'], writes=['PJ%d' % p2])
            P.op('dve', lambda e: e.tensor_scalar(out=cl[:], in0=PJ[p1][:], scalar1=-40.0, scalar2=40.0, op0=ALU.max, op1=ALU.min), reads=['PJ%d' % p1], writes=['cl'])
            P.op('act', lambda e: e.activation(out=eq[:], in_=cl[:], func=AF.Exp), reads=['cl'], writes=['eq'])
            P.op('act', lambda e: e.activation(out=ek[:], in_=cl[:], func=AF.Exp, scale=-1.0), reads=['cl'], writes=['e1'])
            P.op('act', lambda e: e.activation(out=cl[:], in_=PJ[p2][:], func=AF.Exp), reads=['PJ%d' % p2, 'cl'], writes=['cl'])
            P.op('dve', lambda e: e.tensor_tensor(out=qe[par][:].rearrange("p h d -> p (h d)"), in0=qh[:], in1=eq[:], op=ALU.mult), reads=['qh', 'eq'], writes=['qe%d' % par])
            P.op('dve', lambda e: e.tensor_tensor(out=ke[par][:].rearrange("p h d -> p (h d)"), in0=kk[:], in1=ek[:], op=ALU.mult), reads=['kk', 'e1'], writes=['ke%d' % par])
            P.op('dve', lambda e: e.tensor_tensor(out=kd[par][:], in0=kk[:], in1=cl[:], op=ALU.mult), reads=['kk', 'cl'], writes=['kd%d' % par])
            yield
            nind = 16 if is_smp else 8
            ind_ap = inds[:] if is_smp else indp[:]
            p3 = nxt('pj')
            def fbg(e):
                ins = None
                for h in range(4):
                    ins = e.matmul(PJ[p3][:, h * 16:h * 16 + nind], lhsT=gl[:, h * 128:(h + 1) * 128], rhs=ind_ap, start=True, stop=True)
                return ins
            P.op('pe', fbg, reads=['gl', 'inds', 'indp'], writes=['PJ%d' % p3])
            P.op('act', lambda e: e.activation(out=bg[par][:, :, 0:nind], in_=PJ[p3][:, 0:64].rearrange("p (h c) -> p h c", h=4)[:, :, 0:nind], func=AF.Exp),
                 reads=['PJ%d' % p3], writes=['bg%d' % par])
            yield

            transposes_bf([kab[:, g, :] for g in range(2)], ['kab'], kTc[:], kTck, 64)
            yield

        def B_tile(l, ti, seg, is_smp, par):
            xk = 'x%d' % ti
            prv = 1 - par
            kTc, vac = kT[l][par], vaug[l][par]
            kTck, vack = 'kT%d_%d' % (l, par), 'vaug%d_%d' % (l, par)
            qTc, qTk = qT[par], 'qT%d' % par
            mi = 1 if is_smp else 0
            if is_smp:
                blocks = [('c', j) for j in range(NSMP)] + [('n', 0)]
            else:
                blocks = ([('p', 0)] if not (seg == 0 and ti == 0) else []) + [('u', 0)]
            nb = len(blocks)
            pv_bank = {0: (PV, 'PV'), 1: ((PT, 'PT') if is_smp else (PV, 'PV'))}

            def finish_g(g):
                bank, bkey = pv_bank[g]
                pv4 = bank[:].rearrange("p (h c) -> p h c", h=4)
                P.op('dve', lambda e: e.tensor_tensor(out=den[:], in0=pv4[:, :, 64], in1=esink[:, l, 4 * g:4 * g + 4], op=ALU.add), reads=[bkey, 'esink'], writes=['den'])
                P.op('dve', lambda e: e.reciprocal(out=rden[:], in_=den[:]), reads=['den'], writes=['rden'])
                P.op('dve', lambda e: e.tensor_tensor(out=mix[:, g * 256:(g + 1) * 256].rearrange("p (h d) -> p h d", h=4), in0=pv4[:, :, 0:64],
                                                      in1=bc(rden[:], 2, [128, 4, 64]), op=ALU.mult), reads=[bkey, 'rden'], writes=['mixa'])

            def block_g(g, bi, bk, j, mask_ap, mask_key, lhs, lk, rv, rvk, neg=None):
                bank, bkey = pv_bank[g]
                psi = nxt('ps')
                pk = 'PS%d' % psi
                def fsc(e):
                    ins = e.matmul(PS[psi][:], lhsT=lhs, rhs=qTc[:, 4 * g:4 * g + 4, :], start=True, stop=(neg is None))
                    if neg is not None:
                        ins = e.matmul(PS[psi][:], lhsT=identb[:], rhs=neg, start=False, stop=True)
                    return ins
                P.op('pe', fsc, reads=[lk, qTk, 'identb', 'negm'], writes=[pk])
                pi = nxt('pt')
                P.op('act', lambda e: e.activation(out=pT[pi][:].rearrange("p h t -> p (h t)"), in_=PS[psi][:], func=AF.Exp, scale=0.125),
                     reads=[pk], writes=['pT%d' % pi])
                if neg is None:
                    P.op('pool', lambda e: e.tensor_tensor(out=pT[pi][:], in0=pT[pi][:], in1=bc(mask_ap, 1, [128, 4, 128]), op=ALU.mult),
                         reads=['pT%d' % pi, mask_key], writes=['pT%d' % pi])
                yield 1
                def fpv(e):
                    ins = None
                    for hh in range(4):
                        ins = e.matmul(bank[:, hh * 128:hh * 128 + 65], lhsT=pT[pi][:, hh, :], rhs=rv,
                                       start=(bi == 0 and hh == 0), stop=(bi == nb - 1), skip_group_check=True)
                    return ins
                P.op('pe', fpv, reads=['pT%d' % pi, rvk], writes=[bkey])
                yield 0

            if not is_smp:
                for g in range(2):
                    for bi, (bk, j) in enumerate(blocks):
                        if bk == 'p':
                            yield from block_g(g, bi, bk, j, None, None, kT[l][prv][:, g, :], 'kT%d_%d' % (l, prv), vaug[l][prv][:, g, :], 'vaug%d_%d' % (l, prv), neg=negm[:, 1, :])
                        else:
                            yield from block_g(g, bi, bk, j, None, None, kTc[:, g, :], kTck, vac[:, g, :], vack, neg=negm[:, 0, :])
                    finish_g(g)
            else:
                P.barrier()
                P.op('pool', lambda e: e.memset(ft[:, 1024:2064], 0.0), writes=['vcsA'])
                P.op('dve', lambda e: e.memset(vcsA[:, :, :, 64], 1.0), reads=['vcsA'], writes=['vcsA'])
                for i_ in range(2):
                    P.op('dve', lambda e: e.memset(pT[i_][:], 0.0), writes=['pT%d' % i_])
                P.op('pool', lambda e: e.dma_start(out=kcsA, in_=ck[l].rearrange("j k d -> k j d")), writes=['kcsA'], dma='l_kc')
                for g in range(2):
                    P.op('pool', lambda e: e.dma_start(out=vcsA[:, :, g, 0:64], in_=cv[l, :, :, g * 64:(g + 1) * 64].rearrange("j k d -> k j d")), reads=['vcsA'], writes=['vcsA'], dma='l_vc%d' % g)

            def att_seq(bi, j):
                kcT = kcT2[j % 2]
                kck = 'kcT%d' % (j % 2)
                transposes_bf([kcsA[:, j, g * 64:(g + 1) * 64] for g in range(2)], ['kcsA'], kcT, kck, 64)
                for g in range(2):
                    bank, bkey = pv_bank[g]
                    psi = nxt('ps')
                    pk = 'PS%d' % psi
                    P.op('pe', lambda e: e.matmul(PS[psi][:, 0:16], lhsT=kcT[:, g, :], rhs=qTc[:, 4 * g:4 * g + 4, j:64:16], start=True, stop=True),
                         reads=[kck, qTk], writes=[pk])
                    pi = nxt('pt')
                    pslice = pT[pi][:, :, j:64:16]
                    P.op('act', lambda e: e.activation(out=pslice, in_=PS[psi][:, 0:16].rearrange("p (h t) -> p h t", h=4), func=AF.Exp, scale=0.125),
                         reads=[pk], writes=['pT%d' % pi])
                    P.op('dve', lambda e: e.tensor_tensor(out=pslice, in0=pslice, in1=bc(mc0[:, j:64:16], 1, [128, 4, 4]), op=ALU.mult),
                         reads=['pT%d' % pi, 'mc0'], writes=['pT%d' % pi])
                    def fpv(e):
                        ins = None
                        for hh in range(4):
                            ins = e.matmul(bank[:, hh * 128:hh * 128 + 65], lhsT=pT[pi][:, hh, :], rhs=vcsA[:, j, g, :],
                                           start=(bi == 0 and hh == 0), stop=False, skip_group_check=True)
                        return ins
                    P.op('pe', fpv, reads=['pT%d' % pi, 'vcsA'], writes=[bkey])
                    P.op('dve', lambda e: e.memset(pslice, 0.0), reads=['pT%d' % pi], writes=['pT%d' % pi])

            def att_epi():
                bi, (bk, j) = nb - 1, blocks[-1]
                for g in range(2):
                    for _ in block_g(g, bi, bk, j, mns[:], 'mns', kTc[:, g, :], kTck, vac[:, g, :], vack):
                        pass
                finish_g(0)
                finish_g(1)

            qec, kec, kdc, bgc, vhc = qe[par], ke[par], kd[par], bg[par], vh[par]
            qek, kek, kdk, bgk, vhk = 'qe%d' % par, 'ke%d' % par, 'kd%d' % par, 'bg%d' % par, 'vh%d' % par
            transposes_bf([qec[:, h, :] for h in range(4)], [qek], qeT[:], 'qeT', 128)
            yield 1
            transposes_bf([kec[:, h, :] for h in range(4)], [kek], keT[:], 'keT', 128, evac='dve')
            yield 1
            p4 = nxt('ps')
            def fa(e):
                ins = None
                for h in range(4):
                    ins = e.matmul(PS[p4][:, h * 128:(h + 1) * 128], lhsT=keT[:, h, :], rhs=qeT[:, h, :], start=True, stop=True)
                return ins
            P.op('pe', fa, reads=['keT', 'qeT'], writes=['PS%d' % p4])
            P.op('dve', lambda e: e.tensor_tensor(out=atm[:], in0=PS[p4][:].rearrange("p (h t) -> p h t", h=4), in1=bc(ma[:, mi, :], 1, [128, 4, 128]), op=ALU.mult),
                 reads=['PS%d' % p4, 'ma'], writes=['atm'])
            yield 0
            def fo(e):
                ins = None
                for h in range(4):
                    ins = e.matmul(PO[:, h * 128:(h + 1) * 128], lhsT=atm[:, h, :], rhs=vhc[:, h * 128:(h + 1) * 128],
                                   start=(h == 0), stop=False, skip_group_check=True)
                return ins
            P.op('pe', fo, reads=['atm', vhk], writes=['PO'])
            yield 0
            Sl = S[l]
            Sk = 'S%d' % l
            if not is_smp:
                qeTs = keT
                P.op('dve', lambda e: e.tensor_tensor(out=qeTs.rearrange("p h (c t) -> p h c t", c=4), in0=qeT.rearrange("p h (c t) -> p h c t", c=4),
                                                      in1=bc(bgc[:, :, 4:8], 3, [128, 4, 4, 32]), op=ALU.mult),
                     reads=['qeT', 'keT', bgk], writes=['keT'])
                P.op('dve', lambda e: e.tensor_copy(out=Sp[0][:], in_=Sl[:]), reads=[Sk], writes=['Sp0'])
                for c in range(4):
                    pu = nxt('ps')
                    def fu(e):
                        ins = None
                        for h in range(4):
                            ins = e.matmul(PS[pu][:, h * 128:(h + 1) * 128], lhsT=kdc[32 * c:32 * c + 32, h * 128:(h + 1) * 128],
                                           rhs=vhc[32 * c:32 * c + 32, h * 128:(h + 1) * 128], start=True, stop=True, tile_position=(32 * c, 0))
                        return ins
                    P.op('pe', fu, reads=[kdk, vhk], writes=['PS%d' % pu])
                    def fupd(e):
                        ins = None
                        for h in range(4):
                            ins = e.scalar_tensor_tensor(out=Sl[:, h, :], in0=Sl[:, h, :], scalar=bgc[:, h, c:c + 1], in1=PS[pu][:, h * 128:(h + 1) * 128],
                                                         op0=ALU.mult, op1=ALU.add)
                        return ins
                    if c < 3:
                        def fupb(e):
                            ins = None
                            for h in range(4):
                                ins = e.scalar_tensor_tensor(out=Sp[(c + 1) % 2][:, h, :], in0=Sl[:, h, :], scalar=bgc[:, h, c:c + 1], in1=PS[pu][:, h * 128:(h + 1) * 128],
                                                             op0=ALU.mult, op1=ALU.add)
                            return ins
                        P.op('dve', fupb, reads=[Sk, bgk, 'PS%d' % pu], writes=['Sp%d' % ((c + 1) % 2)])
                    P.op('dve', fupd, reads=[Sk, bgk, 'PS%d' % pu], writes=[Sk])
                    if c == 3:
                        yield 3
                    def foi(e):
                        ins = None
                        for h in range(4):
                            ins = e.matmul(PO[32 * c:32 * c + 32, h * 128:(h + 1) * 128], lhsT=qeTs[:, h, 32 * c:32 * c + 32], rhs=Sp[c % 2][:, h, :],
                                           start=False, stop=(c == 3 and h == 3), skip_group_check=True, tile_position=(0, 32 * c))
                        return ins
                    P.op('pe', foi, reads=['keT', 'Sp%d' % (c % 2)], writes=['PO'])
                    if c < 3:
                        yield 1
                if seg == NSEG - 1 and ti == TPS - 1:
                    P.op('sp', lambda e: e.dma_start(out=ns_p[l].rearrange("h k v -> k h v"), in_=Sl[:]), reads=[Sk], dma='o_nsp')
            else:
                kdm = atm[:].rearrange("p h d -> p (h d)")
                qeTm = Sp[0][:, :, 0:64]
                sbf2 = [sbf, mix[:, 512:1024].rearrange("p (h d) -> p h d", h=4)]
                sbk = ['sbfA', 'mixh']
                def ld_state(jj):
                    s_ = jj % 2
                    P.op('sp', lambda e: e.dma_start(out=sstA[s_], in_=st_in[l, jj].rearrange("h k v -> k h v")), writes=['sstA%d' % s_], dma='l_st%d' % s_)
                    P.op('pool', lambda e: e.dma_start(out=sbf2[s_], in_=st_in[l, jj].rearrange("h k v -> k h v")), writes=[sbk[s_]], dma='l_sb%d' % s_)
                ld_state(0)
                for j in range(NSMP):
                    si = j % 2
                    if j + 1 < NSMP:
                        ld_state(j + 1)
                    att_seq(j, j)
                    P.op('dve', lambda e: e.tensor_tensor(out=qeTm, in0=qeT[:, :, 0:64], in1=bc(selT[:, j, :], 1, [128, 4, 64]), op=ALU.mult),
                         reads=['qeT', 'selT'], writes=['Sp0'])
                    def foi(e):
                        ins = None
                        for h in range(4):
                            ins = e.matmul(PO[0:64, h * 128:(h + 1) * 128], lhsT=qeTm[:, h, :], rhs=sbf2[si][:, h, :],
                                           start=False, stop=(j == NSMP - 1), skip_group_check=True)
                        return ins
                    P.op('pe', foi, reads=['Sp0', sbk[si]], writes=['PO'])
                    P.op('act', lambda e: e.activation(out=kdm, in_=kdc[:], func=AF.Copy, scale=msel[:, j:j + 1]), reads=[kdk, 'msel'], writes=['atm'])
                    pu = nxt('ps')
                    def fu(e):
                        ins = None
                        for h in range(4):
                            ins = e.matmul(PS[pu][:, h * 128:(h + 1) * 128], lhsT=kdm[:, h * 128:(h + 1) * 128], rhs=vhc[:, h * 128:(h + 1) * 128], start=True, stop=True)
                        return ins
                    P.op('pe', fu, reads=['atm', vhk], writes=['PS%d' % pu])
                    P.op('dve', lambda e: e.tensor_tensor(out=stmp, in0=sstA[si], in1=bc(bgc[:, :, j], 2, [128, 4, 128]), op=ALU.mult),
                         reads=['sstA%d' % si, bgk], writes=['stmpA'])
                    P.op('dve', lambda e: e.tensor_tensor(out=stmp.rearrange("p h d -> p (h d)"), in0=stmp.rearrange("p h d -> p (h d)"), in1=PS[pu][:], op=ALU.add),
                         reads=['stmpA', 'PS%d' % pu], writes=['stmpA'])
                    P.op('sp', lambda e: e.dma_start(out=ns_s[l, j].rearrange("h k v -> k h v"), in_=stmp), reads=['stmpA'], dma='o_st')
                    yield 1
                att_epi()
            def fsq(e):
                for h in range(4):
                    e.activation(out=atm[:, h, :], in_=PO[:, h * 128:(h + 1) * 128], func=AF.Square, accum_out=sso[:, h:h + 1])
                return e.activation(out=dmy[:], in_=dmy[:], func=AF.Copy)
            P.op('act', fsq, reads=['PO'], writes=['atm', 'sso', 'dmy'])
            P.op('act', lambda e: e.activation(out=sso[:], in_=sso[:], func=AF.Ln, scale=1.0 / 128.0, bias=EPS_AP[:]), reads=['sso', 'eps_ap'], writes=['sso'])
            P.op('act', lambda e: e.activation(out=rso[:], in_=sso[:], func=AF.Exp, scale=-0.5), reads=['sso'], writes=['rso'])
            def fmx(e):
                ins = None
                for h in range(4):
                    ins = e.scalar_tensor_tensor(out=mix[:, 512 + h * 128:512 + (h + 1) * 128], in0=PO[:, h * 128:(h + 1) * 128], scalar=rso[:, h:h + 1],
                                                 in1=gate[par][:, h * 128:(h + 1) * 128], op0=ALU.mult, op1=ALU.mult)
                return ins
            P.op('dve', fmx, reads=['PO', 'rso', 'gate%d' % par], writes=['mixh'])
            transposes_bf([mix[:, i * 128:(i + 1) * 128] for i in range(4)], ['mixa'], mixT[:, 0:4, :], 'qeT', 128)
            wob = [(PV, 'PV'), (PT, 'PT')]
            for half in range(2):
                wbank, wkey = wob[half]
                def fw0(e):
                    ins = None
                    for kc in range(4):
                        ins = e.matmul(wbank[:], lhsT=mixT[:, kc, :], rhs=w_o_sb[:, kc, half * 512:(half + 1) * 512], start=(kc == 0), stop=False)
                    return ins
                P.op('pe', fw0, reads=['qeT'] + WO_KEYS, writes=[wkey])
            yield 99
            transposes_bf([mix[:, i * 128:(i + 1) * 128] for i in range(4, 8)], ['mixh'], mixT[:, 4:8, :], 'keT', 128, evac='dve')
            for half in range(2):
                wbank, wkey = wob[half]
                def fw1(e):
                    ins = None
                    for kc in range(4, 8):
                        ins = e.matmul(wbank[:], lhsT=mixT[:, kc, :], rhs=w_o_sb[:, kc, half * 512:(half + 1) * 512], start=False, stop=(kc == 7))
                    return ins
                P.op('pe', fw1, reads=['keT'] + WO_KEYS, writes=[wkey])
                P.op('dve', lambda e: e.tensor_tensor(out=xres[:, ti, half * 512:(half + 1) * 512], in0=xres[:, ti, half * 512:(half + 1) * 512], in1=wbank[:], op=ALU.add),
                     reads=[xk, wkey], writes=[xk])
                yield 0

        def drain(g):
            for _ in g:
                pass

        def interleave(gb, gf):
            done_f = False
            for nf in gb:
                for _ in range(nf or 0):
                    if done_f:
                        break
                    try:
                        next(gf)
                    except StopIteration:
                        done_f = True
            if not done_f:
                drain(gf)

        for seg in range(NSEG):
            tiles = list(range(TPS)) + ([TPS] if seg == 0 else [])
            for ti in range(TPS):
                if ti == 0 or seg > 0:
                    continue
                P.op('sp', lambda e: e.dma_start(out=xres[:, ti, :], in_=x_seq[(seg * TPS + ti) * 128:(seg * TPS + ti + 1) * 128, :]),
                     writes=['x%d' % ti], dma='lx%d' % ti)
            P.op('sp', lambda e: e.dma_start(out=cosp[:], in_=c_cosp[:, seg * TPS:(seg + 1) * TPS, :]), writes=['cosp'], dma='l_cos')
            P.op('sp', lambda e: e.dma_start(out=sinp[:], in_=c_sinp[:, seg * TPS:(seg + 1) * TPS, :]), writes=['sinp'], dma='l_sin')
            if seg == 0:
                P.op('sp', lambda e: e.dma_start(out=xres[0:64, TPS, :], in_=x_smp), writes=['x%d' % TPS], dma='lx%d' % TPS)
            for l in range(2):
                if not (seg == 0 and l == 0):
                    P.barrier()
                def load_pass(p_):
                    load_pass_l(l, p_)
                if l != 0:
                    load_phase_a(l)
                if seg == 0:
                    P.op('sp', lambda e: e.dma_start(out=nk_s[l, :, 0:124, :], in_=ck[l, :, 4:128, :]), dma='o_ckc')
                    P.op('sp', lambda e: e.dma_start(out=nv_s[l, :, 0:124, :], in_=cv[l, :, 4:128, :]), dma='o_cvc')
                n = len(tiles)
                prepped = set()
                drain(F_tile(l, tiles[0], seg, tiles[0] == TPS, 0))
                load_phase_a2(l)
                for tix, ti in enumerate(tiles):
                    gb = B_tile(l, ti, seg, ti == TPS, tix % 2)
                    if tix + 1 < n:
                        gf = F_tile(l, tiles[tix + 1], seg, tiles[tix + 1] == TPS, (tix + 1) % 2)
                        interleave(gb, gf)
                    elif ti == TPS:
                        def prep_gen():
                            for t_ in range(TPS):
                                rms_stats(xres[:, t_, :], 'x%d' % t_, rstd[:], 'rstd', sq=rr2[:, t_:t_ + 1])
                                for _ in make_hT(t_, 2 + l, hT_all[:, :, t_ * 128:(t_ + 1) * 128], 'hTall', pj_only=True):
                                    pass
                                prepped.add(t_)
                                yield
                        interleave(gb, prep_gen())
                    else:
                        drain(gb)
                P.barrier()
                ntok = n * 128
                for ti in tiles:
                    if ti in prepped:
                        continue
                    rms_stats(xres[:, ti, :], 'x%d' % ti, rstd[:], 'rstd', sq=rr2[:, ti:ti + 1])
                    drain(make_hT(ti, 2 + l, hT_all[:, :, ti * 128:(ti + 1) * 128], 'hTall'))
                groups = [(g0, min(GRP, ntok - g0)) for g0 in range(0, ntok, GRP)]

                def up(p_, g0, gn, ui):
                    slot = p_ % 2
                    wu, wd = ring[slot]
                    uT = uTs[ui]
                    for fc in range(4):
                        psi = nxt('ps')
                        def fup(e):
                            ins = None
                            for kc in range(8):
                                ins = e.matmul(PS[psi][:, 0:gn], lhsT=wu[:, kc, fc * 128:(fc + 1) * 128], rhs=hT_all[:, kc, g0:g0 + gn], start=(kc == 0), stop=(kc == 7))
                            return ins
                        P.op('pe', fup, reads=['ringu%d_%d' % (slot, k_) for k_ in range(8)] + ['hTall'], writes=['PS%d' % psi])
                        P.op('act', lambda e: e.activation(out=rl[:, 0:gn], in_=PS[psi][:, 0:gn], func=AF.Relu), reads=['PS%d' % psi], writes=['hTa'])
                        P.op('dve', lambda e: e.tensor_tensor(out=uT[:, fc, 0:gn], in0=rl[:, 0:gn], in1=rl[:, 0:gn], op=ALU.mult), reads=['hTa'], writes=['uT%d' % ui])

                def down(p_, g0, gn, ui):
                    slot = p_ % 2
                    wu, wd = ring[slot]
                    uT = uTs[ui]
                    for tt in range(gn // 128):
                        ti = (g0 // 128) + tt
                        for half in range(2):
                            pj = nxt('pj')
                            def fdn(e):
                                ins = None
                                for fc in range(4):
                                    ins = e.matmul(PJ[pj][:], lhsT=uT[:, fc, tt * 128:(tt + 1) * 128], rhs=wd[:, fc, half * 512:(half + 1) * 512], start=(fc == 0), stop=(fc == 3))
                                return ins
                            P.op('pe', fdn, reads=['uT%d' % ui] + ['ringd%d_%d' % (slot, k_) for k_ in range(4)], writes=['PJ%d' % pj])
                            P.op('dve', lambda e: e.scalar_tensor_tensor(
                                out=xres[:, ti, half * 512:(half + 1) * 512], in0=PJ[pj][:], scalar=rr2[:, ti:ti + 1],
                                in1=xres[:, ti, half * 512:(half + 1) * 512], op0=ALU.mult, op1=ALU.add),
                                reads=['PJ%d' % pj, 'rr2', 'x%d' % ti], writes=['x%d' % ti])

                def final_tile(ti):
                    rms_stats(xres[:, ti, :], 'x%d' % ti, rstd[:], 'rstd')
                    yi = nxt('y')
                    P.op('dve', lambda e: e.scalar_tensor_tensor(out=ysb[yi], in0=xres[:, ti, :], scalar=rstd[:], in1=fing, op0=ALU.mult, op1=ALU.mult),
                         reads=['x%d' % ti, 'rstd', 'fing'], writes=['ysb%d' % yi])
                    if ti == TPS:
                        P.op('sp', lambda e: e.dma_start(out=y_smp, in_=ysb[yi][0:64, :]), reads=['ysb%d' % yi], dma='o_y%d' % yi)
                    else:
                        P.op('sp', lambda e: e.dma_start(out=y_seq[(seg * TPS + ti) * 128:(seg * TPS + ti + 1) * 128, :], in_=ysb[yi]),
                             reads=['ysb%d' % yi], dma='o_y%d' % yi)
                        if seg + 1 < NSEG:
                            P.op('sp', lambda e: e.dma_start(out=xres[:, ti, :], in_=x_seq[((seg + 1) * TPS + ti) * 128:((seg + 1) * TPS + ti + 1) * 128, :]),
                                 writes=['x%d' % ti], dma='lx%d' % ti)

                def after_down(pv):
                    if l == 1 and pv[0] == NPASS - 1:
                        for tt in range(pv[2] // 128):
                            final_tile(pv[1] // 128 + tt)

                if l == 1:
                    P.op('sp', lambda e: e.dma_start(out=fing, in_=fin_g.partition_broadcast(128)), writes=['fing'], dma='c10')
                work = [(p_, g0, gn) for p_ in range(NPASS) for (g0, gn) in groups]
                prev = None
                for wi, (p_, g0, gn) in enumerate(work):
                    ui = nxt('u')
                    up(p_, g0, gn, ui)
                    if prev is not None:
                        down(*prev)
                        after_down(prev)
                    if g0 == 0 and p_ + 1 < NPASS:
                        load_pass(p_ + 1)
                    prev = (p_, g0, gn, ui)
                down(*prev)
                after_down(prev)
            P.barrier()
            if seg + 1 < NSEG:
                load_phase_a(0)
            P.barrier()
        P.emit()
    return nc


def _consts():
    c = {}
    c['c_identf'] = np.eye(128, dtype=np.float32)
    c['c_identb'] = np.eye(128, dtype=np.float32)
    half = 8
    inv_freq = np.power(np.float32(500000.0), -np.arange(half, dtype=np.float32) * np.float32(2.0 / 16)).astype(np.float32)
    pos = np.arange(SEQ, dtype=np.float32)
    ang = (pos[:, None] * inv_freq[None, :]).astype(np.float32)
    c['c_cosp'] = np.cos(ang).astype(np.float32).reshape(32, 128, 8).transpose(1, 0, 2).copy()
    c['c_sinp'] = np.sin(ang).astype(np.float32).reshape(32, 128, 8).transpose(1, 0, 2).copy()
    p = np.arange(128)
    tt = p // 16
    jj = p % 16
    valid = p < 64
    pos_s = (PAST + tt).astype(np.float32)
    ang_s = (pos_s[:, None] * inv_freq[None, :]).astype(np.float32)
    c['c_coss'] = np.where(valid[:, None], np.cos(ang_s), 1.0).astype(np.float32)
    c['c_sins'] = np.where(valid[:, None], np.sin(ang_s), 0.0).astype(np.float32)
    s = p[:, None]; t = p[None, :]
    ncur = np.where(s <= t, 0.0, -30000.0).astype(np.float32)
    nprev = np.where(s >= t, 0.0, -30000.0).astype(np.float32)
    c['c_negm'] = np.stack([np.tile(ncur, (1, 4)), np.tile(nprev, (1, 4))], axis=1).astype(np.float32)
    ch = p // 32
    same = ch[:, None] == ch[None, :]
    ref = ch * 32 + 15
    mrel_p = same * ((s <= t).astype(np.float32) - (s <= ref[None, :]).astype(np.float32))
    mkd_p = same * (s > t)
    ma_p = same * (s <= t)
    same_s = (jj[:, None] == jj[None, :]) & valid[:, None] & valid[None, :]
    mrel_s = same_s * (tt[:, None] <= tt[None, :])
    mkd_s = same_s * (tt[:, None] > tt[None, :])
    ma_s = same_s * (tt[:, None] <= tt[None, :])
    c['c_mrel'] = np.stack([mrel_p, mrel_s]).astype(np.float32)
    c['c_mkd'] = np.stack([mkd_p, mkd_s]).astype(np.float32)
    c['c_ma'] = np.stack([ma_p, ma_s]).astype(np.float32)
    indp = np.zeros((128, 8), np.float32)
    for cc in range(4):
        indp[cc * 32:(cc + 1) * 32, cc] = 1.0
        indp[cc * 32:cc * 32 + 16, 4 + cc] = 1.0
    c['c_indp'] = indp
    inds = np.zeros((128, 16), np.float32)
    inds[p[valid], jj[valid]] = 1.0
    c['c_inds'] = inds
    c['c_msel'] = inds.copy()
    c['c_mc0'] = ((p[:, None] >= tt[None, :]) & valid[None, :]).astype(np.float32)
    mns = (same_s & (tt[:, None] <= tt[None, :])).astype(np.float32)
    c['c_mns'] = mns
    selT = np.zeros((128, 16, 64), np.float32)
    for j in range(16):
        selT[:, j, :] = ((jj == j) & valid)[None, 0:64]
    c['c_selT'] = selT
    return c


_CACHE = {}
PCORES = [0, 1, 4, 5]


def kernel(x_prompt, x_sample, cache_k, cache_v, state_hgrn, attn_norm, w_in, att_sinks,
           hgrn_lower_bounds, hgrn_out_norm, w_o, mlp_norm, w_up, w_down, final_norm):
    f = lambda a: np.ascontiguousarray(np.asarray(a, dtype=np.float32))
    x_prompt, x_sample, cache_k, cache_v, state_hgrn = map(f, (x_prompt, x_sample, cache_k, cache_v, state_hgrn))
    if 'nc' not in _CACHE:
        _CACHE['nc'] = build_program()
        _CACHE['consts'] = _consts()
    nc = _CACHE['nc']
    consts = _CACHE['consts']
    gains = np.concatenate([f(attn_norm).reshape(2, 8, 128), f(mlp_norm).reshape(2, 8, 128), f(final_norm).reshape(1, 8, 128)], axis=0).reshape(40, 128)
    shared = dict(gains=np.ascontiguousarray(gains), fin_g=f(final_norm).reshape(1, D), w_in=f(w_in), sinks=f(att_sinks),
                  lbraw=f(hgrn_lower_bounds), onorm=f(hgrn_out_norm), w_o=f(w_o), w_up=f(w_up), w_dn=f(w_down))
    shared.update(consts)
    zero_seq = np.zeros((SEQ, D), np.float32)
    in_maps = []
    for c in range(8):
        sl = slice(c * NSMP, (c + 1) * NSMP)
        m = dict(shared)
        m['x_seq'] = x_prompt[PCORES.index(c)] if c in PCORES else zero_seq
        m['x_smp'] = np.ascontiguousarray(x_sample[sl].transpose(1, 0, 2).reshape(64, D))
        m['ck'] = np.ascontiguousarray(cache_k[:, sl].reshape(2, NSMP, 128, 128))
        m['cv'] = np.ascontiguousarray(cache_v[:, sl].reshape(2, NSMP, 128, 128))
        m['st_in'] = np.ascontiguousarray(state_hgrn[:, sl])
        in_maps.append(m)
    res = run_bass_kernel_spmd(nc, in_maps, core_ids=list(range(8)))
    R = res.results
    y_prompt = np.stack([R[b]['y_seq'] for b in PCORES]).astype(np.float32)
    y_sample = np.concatenate([R[c]['y_smp'].reshape(4, NSMP, D).transpose(1, 0, 2) for c in range(8)], axis=0).astype(np.float32)
    nk_p = np.stack([R[b]['nk_p'] for b in PCORES], axis=1).reshape(2, 4, 128, 2, 64)
    nv_p = np.stack([R[b]['nv_p'] for b in PCORES], axis=1).reshape(2, 4, 128, 2, 64)
    ns_p = np.stack([R[b]['ns_p'] for b in PCORES], axis=1)
    nk_s = np.concatenate([R[c]['nk_s'] for c in range(8)], axis=1).reshape(2, 128, 128, 2, 64)
    nv_s = np.concatenate([R[c]['nv_s'] for c in range(8)], axis=1).reshape(2, 128, 128, 2, 64)
    ns_s = np.concatenate([R[c]['ns_s'] for c in range(8)], axis=1)
    return (y_prompt, y_sample, np.ascontiguousarray(nk_p), np.ascontiguousarray(nv_p), np.ascontiguousarray(ns_p),
            np.ascontiguousarray(nk_s), np.ascontiguousarray(nv_s), np.ascontiguousarray(ns_s))
```

```python
import numpy as np
from contextlib import ExitStack
import concourse.bass as bass
import concourse.mybir as mybir
from concourse.bass_utils import run_bass_kernel_spmd

F32 = mybir.dt.float32
BF16 = mybir.dt.bfloat16
AF = mybir.ActivationFunctionType
ALU = mybir.AluOpType
AX = mybir.AxisListType

D = 1024
SEQ = 4096
NSEG = 2
TPS = 16
NSMP = 16
PAST = 16384
EPS = 1e-6
INW = 2816
DFF = 4096
NPASS = 8
GRP = 256


class Prog:
    ENGS = ('pe', 'act', 'dve', 'pool', 'sp')

    def __init__(self, nc, es):
        self.nc = nc
        self.es = es
        self.eng = {'pe': nc.tensor, 'act': nc.scalar, 'dve': nc.vector, 'pool': nc.gpsimd, 'sp': nc.sync}
        self.cnt = {e: 0 for e in self.ENGS}
        self.semh = {k: es.enter_context(nc.semaphore('s_' + k)) for k in self.ENGS}
        self.dma_cnt = {}
        self.last_write = {}
        self.reads_since = {}
        self.seen = {e: {} for e in self.ENGS}
        self.know = {}
        self.nins = 0
        self.nwait = 0

    def _wait(self, e, tok):
        key, val = tok
        if self.seen[e].get(key, 0) >= val:
            return
        se = self.seen[e]
        se[key] = val
        self.eng[e].wait_ge(self.semh[key], val)
        self.nwait += 1
        kn = self.know.get(tok)
        if kn:
            for k2, v2 in kn.items():
                if se.get(k2, 0) < v2:
                    se[k2] = v2

    def op(self, e, fn, reads=(), writes=(), dma=None):
        deps = []
        for r in reads:
            if r in self.last_write:
                deps.append(self.last_write[r])
        for w in writes:
            rs = self.reads_since.get(w, ())
            if rs:
                deps.extend(rs)
            elif w in self.last_write:
                deps.append(self.last_write[w])
        mx = {}
        for key, val in deps:
            if key == 'pe' and e == 'pe' and dma is None:
                continue
            if val > mx.get(key, 0):
                mx[key] = val
        for key, val in mx.items():
            self._wait(e, (key, val))
        self.nins += 1
        if dma is None:
            self.cnt[e] += 1
            tok = (e, self.cnt[e])
            fn(self.eng[e]).then_inc(self.semh[e], 1)
        else:
            k = 'dma:' + dma
            if k not in self.semh:
                self.semh[k] = self.es.enter_context(self.nc.semaphore('d%d' % len(self.semh)))
            self.dma_cnt[k] = self.dma_cnt.get(k, 0) + 16
            tok = (k, self.dma_cnt[k])
            fn(self.eng[e]).then_inc(self.semh[k], 16)
        self.know[tok] = dict(self.seen[e])
        for r in reads:
            self.reads_since.setdefault(r, []).append(tok)
        for w in writes:
            self.last_write[w] = tok
            self.reads_since[w] = []
        return tok

    def barrier(self, skip=()):
        toks = [(k, v) for k, v in self.dma_cnt.items() if not any(k.startswith('dma:' + s) for s in skip)] + \
               [(k, self.cnt[k]) for k in self.ENGS if self.cnt[k]]
        for e in self.ENGS:
            for t in toks:
                self._wait(e, t)

    def emit(self):
        self.barrier()


def bc(ap, axis, shape):
    return ap.unsqueeze(axis).broadcast_to(shape)


def build_program():
    nc = bass.Bass("TRN2", target_bir_lowering=False)

    def din(name, shape, dt=F32):
        return nc.dram_tensor(name, list(shape), dt, kind="ExternalInput").ap()

    def dout(name, shape):
        return nc.dram_tensor(name, list(shape), F32, kind="ExternalOutput").ap()

    x_seq = din("x_seq", [SEQ, D])
    x_smp = din("x_smp", [64, D])
    ck = din("ck", [2, NSMP, 128, 128])
    cv = din("cv", [2, NSMP, 128, 128])
    st_in = din("st_in", [2, NSMP, 4, 128, 128])
    gains = din("gains", [40, 128])
    fin_g = din("fin_g", [1, D])
    w_in = din("w_in", [2, D, INW])
    sinks = din("sinks", [2, 8])
    lbraw = din("lbraw", [2, 512])
    onorm = din("onorm", [2, 128])
    w_o = din("w_o", [2, D, D])
    w_up = din("w_up", [2, D, DFF])
    w_dn = din("w_dn", [2, DFF, D])
    c_identf = din("c_identf", [128, 128])
    c_identb = din("c_identb", [128, 128])
    c_cosp = din("c_cosp", [128, 32, 8])
    c_sinp = din("c_sinp", [128, 32, 8])
    c_coss = din("c_coss", [128, 8])
    c_sins = din("c_sins", [128, 8])
    c_negm = din("c_negm", [128, 2, 512])
    c_mrel = din("c_mrel", [2, 128, 128])
    c_mkd = din("c_mkd", [2, 128, 128])
    c_ma = din("c_ma", [2, 128, 128])
    c_indp = din("c_indp", [128, 8])
    c_inds = din("c_inds", [128, 16])
    c_msel = din("c_msel", [128, 16])
    c_mc0 = din("c_mc0", [128, 128])
    c_mns = din("c_mns", [128, 128])
    c_selT = din("c_selT", [128, 16, 64])

    y_seq = dout("y_seq", [SEQ, D])
    y_smp = dout("y_smp", [64, D])
    nk_p = dout("nk_p", [2, 128, 128])
    nv_p = dout("nv_p", [2, 128, 128])
    ns_p = dout("ns_p", [2, 4, 128, 128])
    nk_s = dout("nk_s", [2, NSMP, 128, 128])
    nv_s = dout("nv_s", [2, NSMP, 128, 128])
    ns_s = dout("ns_s", [2, NSMP, 4, 128, 128])

    with ExitStack() as es:
        def sb(name, shape, dt=F32):
            return es.enter_context(nc.sbuf_tensor(name, list(shape), dt))

        def ps(name, shape, dt=F32):
            return es.enter_context(nc.psum_tensor(name, list(shape), dt))

        NT = TPS + 1
        xres = sb("xres", [128, NT, D])
        hT_elems = 8 * NT * 128
        ring_slot = 8 * 512 + 4 * D
        uT_elems = 4 * GRP
        rg_b = max(8 * INW + 8 * D + ring_slot, hT_elems + ring_slot + 2 * uT_elems)
        rgn32 = sb("rgn", [128, rg_b // 2])
        rgnb = rgn32[:].bitcast(BF16)
        w_in_sb = rgnb[:, 0:8 * INW].rearrange("p (k n) -> p k n", k=8)
        w_o_sb = rgnb[:, 8 * INW:8 * INW + 8 * D].rearrange("p (k n) -> p k n", k=8)
        hT_all = rgnb[:, 0:hT_elems].rearrange("p (k t) -> p k t", k=8)
        ring = []
        for o0 in (8 * INW + 8 * D, hT_elems):
            wu = rgnb[:, o0:o0 + 4096].rearrange("p (k n) -> p k n", k=8)
            wd = rgnb[:, o0 + 4096:o0 + 8192].rearrange("p (k n) -> p k n", k=4)
            ring.append((wu, wd))
        o0 = hT_elems + ring_slot
        assert o0 + 2 * uT_elems <= 8 * INW + 8 * D
        uTs = []
        for s in range(2):
            uTs.append(rgnb[:, o0:o0 + uT_elems].rearrange("p (k n) -> p k n", k=4))
            o0 += uT_elems

        identf = sb("identf", [128, 128]); identb = sb("identb", [128, 128], BF16)
        cosp = sb("cosp", [128, TPS, 8]); sinp = sb("sinp", [128, TPS, 8])
        coss = sb("coss", [128, 8]); sins = sb("sins", [128, 8])
        negm = sb("negm", [128, 2, 512], BF16)
        mrel = sb("mrel", [128, 2, 128]); mkd = sb("mkd", [128, 2, 128]); ma = sb("ma", [128, 2, 128], BF16)
        indp = sb("indp", [128, 8]); inds = sb("inds", [128, 16]); msel = sb("msel", [128, 16])
        mc0 = sb("mc0", [128, 128], BF16); mns = sb("mns", [128, 128], BF16)
        selT = sb("selT", [128, 16, 64], BF16)
        gT = sb("gT", [128, 40])
        esink = sb("esink", [128, 2, 8])
        lb1 = sb("lb1", [128, 512])
        onb = sb("onb", [128, 2, 128])
        EPS_AP = sb("eps_ap", [128, 1])

        ss = sb("ss", [128, 1]); rstd = sb("rstd", [128, 1]); nrstd = sb("nrstd", [128, 1])
        rr2 = sb("rr2", [128, NT])
        dmy = sb("dmy", [128, 1])
        hT = sb("hT", [128, 8, 128], BF16)
        junk = hT[:].rearrange("p k t -> p (k t)")
        rl = junk[:, 0:512]
        ft = sb("ft", [128, 4096])
        fing = ft[:, 0:D]
        ysb = [ft[:, D:2 * D], ft[:, 2 * D:3 * D]]
        qa = ft[:, 0:512].rearrange("p (h d) -> p h d", h=8)
        ka = sb("ka", [128, 2, 64]); va = sb("va", [128, 128])
        qab = sb("qab", [128, 8, 64], BF16); kab = sb("kab", [128, 2, 64], BF16)
        qT = [sb("qT%d" % i, [64, 8, 128], BF16) for i in range(2)]
        kT = [[sb("kT%d_%d" % (l, i), [64, 2, 128], BF16) for i in range(2)] for l in range(2)]
        vaug = [[sb("vaug%d_%d" % (l, i), [128, 2, 65], BF16) for i in range(2)] for l in range(2)]
        pT = [sb("pT%d" % i, [128, 4, 128], BF16) for i in range(2)]
        den = sb("den", [128, 4]); rden = sb("rden", [128, 4])
        mix = sb("mix", [128, D], BF16)
        e1 = ft[:, 512:1024]
        qh = ft[:, 1024:1536]; fg = ft[:, 1536:2048]; gl = ft[:, 2048:2560]; kk = ft[:, 2560:3072]
        cl = ft[:, 3072:3584]; eq = ft[:, 3584:4096]; ek = e1
        rt = [cl[:, 0:64].rearrange("p (h d) -> p h d", h=8), cl[:, 64:128].rearrange("p (h d) -> p h d", h=8),
              eq[:, 0:64].rearrange("p (h d) -> p h d", h=8), eq[:, 64:128].rearrange("p (h d) -> p h d", h=8)]
        RTK = ["cl", "cl", "eq", "eq"]
        vh = [sb("vh%d" % i, [128, 512], BF16) for i in range(2)]
        gate = [sb("gate%d" % i, [128, 512], BF16) for i in range(2)]
        qe = [sb("qe%d" % i, [128, 4, 128], BF16) for i in range(2)]
        ke = [sb("ke%d" % i, [128, 4, 128], BF16) for i in range(2)]
        kd = [sb("kd%d" % i, [128, 512], BF16) for i in range(2)]
        bg = [sb("bg%d" % i, [128, 4, 16]) for i in range(2)]
        tq = sb("tq", [128, 8, 128], BF16)
        qeT = tq[:, 0:4, :]; keT = tq[:, 4:8, :]; mixT = tq[:]
        atm = sb("atm", [128, 4, 128], BF16)
        S = [sb("S%d" % l, [128, 4, 128]) for l in range(2)]
        Sp = [sb("Sp%d" % c, [128, 4, 128], BF16) for c in range(2)]
        sso = sb("sso", [128, 4]); rso = sb("rso", [128, 4])
        kcT2 = [Sp[1][0:64, 0:2, :], Sp[1][0:64, 2:4, :]]
        kcsA = ft[:, 0:1024].bitcast(BF16).rearrange("p (j d) -> p j d", j=NSMP)
        vcsA = ft[:, 1024:2064].bitcast(BF16).rearrange("p (j g d) -> p j g d", j=NSMP, g=2)
        sstA = [ft[:, 2064:2576].rearrange("p (h d) -> p h d", h=4), ft[:, 2576:3088].rearrange("p (h d) -> p h d", h=4)]
        stmp = ft[:, 3088:3600].rearrange("p (h d) -> p h d", h=4)
        sbf = ft[:, 3600:3856].bitcast(BF16).rearrange("p (h d) -> p h d", h=4)

        PT = ps("PT", [128, 512])
        PB = ps("PB", [128, 1024], BF16)
        PJ = [ps("PJ0", [128, 512]), ps("PJ1", [128, 512])]
        PS = [ps("PS0", [128, 512]), ps("PS1", [128, 512])]
        PV = ps("PV", [128, 512])
        PO = ps("PO", [128, 512])

        P = Prog(nc, es)
        cnt = {'pj': 0, 'ps': 0, 'y': 0, 'pt': 0, 'u': 0, 'pb': 0}

        def nxt(k, n=2):
            v = cnt[k] % n
            cnt[k] += 1
            return v


        WGRP = [(0, 512), (512, 768), (768, 1280), (1280, 1792), (1792, 2304), (2304, 2816)]

        def load_pass_l(l, p_):
            slot = p_ % 2
            wu, wd = ring[slot]
            for kc in range(8):
                P.op('pool', lambda e: e.dma_start(out=wu[:, kc, :], in_=w_up[l, kc * 128:(kc + 1) * 128, p_ * 512:(p_ + 1) * 512]), writes=['ringu%d_%d' % (slot, kc)], dma='lru%d' % slot)
            for fc in range(4):
                P.op('pool', lambda e: e.dma_start(out=wd[:, fc, :], in_=w_dn[l, p_ * 512 + fc * 128: p_ * 512 + (fc + 1) * 128, :]), writes=['ringd%d_%d' % (slot, fc)], dma='lrd%d' % slot)

        def load_phase_a(l):
            for gi, (c0, c1) in enumerate(WGRP):
                P.op('pool', lambda e: e.dma_start(out=w_in_sb[:, :, c0:c1], in_=w_in[l, :, c0:c1].rearrange("(k p) n -> p k n", p=128)), writes=['w_in_g%d' % gi], dma='lw_in%d' % gi)

        def load_phase_a2(l):
            for kc in range(8):
                P.op('pool', lambda e: e.dma_start(out=w_o_sb[:, kc, :], in_=w_o[l, kc * 128:(kc + 1) * 128, :]), writes=['w_o%d' % kc], dma='lw_o')
            load_pass_l(l, 0)

        P.op('sp', lambda e: e.dma_start(out=xres[:, 0, :], in_=x_seq[0:128, :]), writes=['x0'], dma='lx0')
        load_phase_a(0)

        cl_list = [(identf, c_identf), (coss, c_coss), (sins, c_sins),
                   (indp, c_indp), (inds, c_inds), (msel, c_msel)]
        for i, (t, d) in enumerate(cl_list):
            P.op('sp', lambda e: e.dma_start(out=t[:], in_=d), writes=[t.name], dma='c%d' % i)
        P.op('sp', lambda e: e.dma_start(out=mrel[:], in_=c_mrel.rearrange("a p n -> p a n")), writes=['mrel'], dma='c20')
        P.op('sp', lambda e: e.dma_start(out=mkd[:], in_=c_mkd.rearrange("a p n -> p a n")), writes=['mkd'], dma='c21')
        cb_list = [(identb, c_identb), (negm, c_negm), (mc0, c_mc0), (mns, c_mns), (selT, c_selT)]
        for i, (t, d) in enumerate(cb_list):
            P.op('pool', lambda e: e.dma_start(out=t[:], in_=d), writes=[t.name], dma='cb%d' % i)
        P.op('pool', lambda e: e.dma_start(out=ma[:], in_=c_ma.rearrange("a p n -> p a n")), writes=['ma'], dma='cb9')
        P.op('sp', lambda e: e.dma_start(out=esink[:].rearrange("p l h -> p (l h)"), in_=sinks.rearrange("l h -> (l h)").unsqueeze(0).partition_broadcast(128)), writes=['esink'], dma='c11')
        gstage = ft[0:40, 1024:1152]
        lbr = ft[:, 0:1024].rearrange("p (l c) -> p l c", l=2)
        P.op('sp', lambda e: e.dma_start(out=ft[:, 0:1024], in_=lbraw.rearrange("l c -> (l c)").unsqueeze(0).partition_broadcast(128)), writes=['lbr'], dma='c12')
        P.op('sp', lambda e: e.dma_start(out=onb[:].rearrange("p l h -> p (l h)"), in_=onorm.rearrange("l c -> (l c)").unsqueeze(0).partition_broadcast(128)), writes=['onb'], dma='c13')
        P.op('sp', lambda e: e.dma_start(out=gstage, in_=gains), writes=['gstage'], dma='c19')
        P.op('act', lambda e: e.activation(out=esink[:], in_=esink[:], func=AF.Exp), reads=['esink'], writes=['esink'])
        P.op('dve', lambda e: e.tensor_tensor(out=lb1[:], in0=lbr[:, 0, :], in1=lbr[:, 1, :], op=ALU.subtract), reads=['lbr'], writes=['lb1'])
        P.op('act', lambda e: e.activation(out=lb1[:], in_=lb1[:], func=AF.Exp), reads=['lb1'], writes=['lb1'])
        P.op('dve', lambda e: e.tensor_scalar_add(out=lb1[:], in0=lb1[:], scalar1=1.0), reads=['lb1'], writes=['lb1'])
        P.op('dve', lambda e: e.reciprocal(out=lb1[:], in_=lb1[:]), reads=['lb1'], writes=['lb1'])
        P.op('pe', lambda e: e.transpose(PT[:, 0:40], gstage, identf[0:40, 0:40]), reads=['gstage', 'identf'], writes=['PT'])
        P.op('dve', lambda e: e.tensor_copy(out=gT[:], in_=PT[:, 0:40]), reads=['PT'], writes=['gT'])
        P.op('pool', lambda e: e.memset(dmy[:], 0.0), writes=['dmy'])
        P.op('pool', lambda e: e.memset(EPS_AP[:], EPS), writes=['eps_ap'])
        for l in range(2):
            for i in range(2):
                P.op('pool', lambda e: e.memset(vaug[l][i][:], 1.0), writes=['vaug%d_%d' % (l, i)])
        P.op('pool', lambda e: e.memset(xres[:, TPS, :], 0.0), writes=['x%d' % TPS])
        for l in range(2):
            P.op('pool', lambda e: e.memset(S[l][:], 0.0), writes=['S%d' % l])
        P.barrier(skip=('lw_in', 'lw_o', 'lru', 'lrd', 'lx'))

        def rms_stats(xt_ap, xkey, out_r, out_key, also_neg=None, sq=None):
            def f(e):
                e.activation(out=junk, in_=xt_ap, func=AF.Square, accum_out=ss[:])
                return e.activation(out=dmy[:], in_=dmy[:], func=AF.Copy)
            P.op('act', f, reads=[xkey], writes=['hTa', 'hTb', 'ss', 'dmy'])
            P.op('act', lambda e: e.activation(out=ss[:], in_=ss[:], func=AF.Ln, scale=1.0 / D, bias=EPS_AP[:]), reads=['ss', 'eps_ap'], writes=['ss'])
            P.op('act', lambda e: e.activation(out=out_r, in_=ss[:], func=AF.Exp, scale=-0.5), reads=['ss'], writes=[out_key])
            if also_neg is not None:
                P.op('dve', lambda e: e.tensor_scalar_mul(out=also_neg, in0=out_r, scalar1=-1.0), reads=[out_key], writes=['nrstd'])
            if sq is not None:
                P.op('act', lambda e: e.activation(out=sq, in_=ss[:], func=AF.Exp, scale=-1.0), reads=['ss'], writes=['rr2'])

        def make_hT(ti, gcol, dst, dst_key, pj_only=False):
            for half in range(2):
                if half == 0 and not pj_only:
                    bank, bkey = PT, 'PT'
                else:
                    pj_ = nxt('pj')
                    bank, bkey = PJ[pj_], 'PJ%d' % pj_
                def f(e):
                    ins = None
                    for j in range(4):
                        kc = half * 4 + j
                        ins = e.transpose(bank[:, j * 128:(j + 1) * 128], xres[:, ti, kc * 128:(kc + 1) * 128], identf[:])
                    return ins
                P.op('pe', f, reads=['x%d' % ti, 'identf'], writes=[bkey])
                P.op('dve', lambda e: e.tensor_tensor(
                    out=dst[:, half * 4:half * 4 + 4, :], in0=bank[:].rearrange("p (k t) -> p k t", k=4),
                    in1=bc(gT[:, gcol * 8 + half * 4:gcol * 8 + half * 4 + 4], 2, [128, 4, 128]), op=ALU.mult),
                    reads=[bkey, 'gT'], writes=[(dst_key + 'ab'[half]) if dst_key == 'hT' else dst_key])
                yield

        WO_KEYS = ['w_o%d' % k_ for k_ in range(8)]

        def proj_group(c0, c1):
            pj = nxt('pj')
            for hf in range(2):
                def f(e):
                    ins = None
                    for kc in range(hf * 4, hf * 4 + 4):
                        ins = e.matmul(PJ[pj][:, 0:c1 - c0], lhsT=hT[:, kc, :], rhs=w_in_sb[:, kc, c0:c1], start=(kc == 0), stop=(kc == 7))
                    return ins
                P.op('pe', f, reads=['hT' + 'ab'[hf], 'w_in_g%d' % WGRP.index((c0, c1))], writes=['PJ%d' % pj])
            return pj

        def silu_from_psum(pj, dst, dst_key):
            k = 'PJ%d' % pj
            P.op('act', lambda e: e.activation(out=dst, in_=PJ[pj][:], func=AF.Copy, scale=rstd[:]), reads=[k, 'rstd'], writes=[dst_key])
            P.op('act', lambda e: e.activation(out=e1[:], in_=dst, func=AF.Exp, scale=-1.0), reads=[dst_key], writes=['e1'])
            P.op('act', lambda e: e.activation(out=e1[:], in_=e1[:], func=AF.Ln, bias=1.0), reads=['e1'], writes=['e1'])
            P.op('act', lambda e: e.activation(out=e1[:], in_=e1[:], func=AF.Exp, scale=-1.0), reads=['e1'], writes=['e1'])
            P.op('pool', lambda e: e.tensor_tensor(out=dst, in0=dst, in1=e1[:], op=ALU.mult), reads=[dst_key, 'e1'], writes=[dst_key])

        def rotary(src, nh, cos_ap, sin_ap, dstb, skey, dkey):
            x1 = src[:, :, 0:8]; x2 = src[:, :, 8:16]
            cb = bc(cos_ap, 1, [128, nh, 8]); sn = bc(sin_ap, 1, [128, nh, 8])
            r0, r1, r2, r3 = [t[:, 0:nh, :] for t in rt]
            ck_ = ['cosp', 'sinp']
            P.op('pool', lambda e: e.tensor_tensor(out=r0, in0=x1, in1=cb, op=ALU.mult), reads=[skey] + ck_, writes=[RTK[0]])
            P.op('pool', lambda e: e.tensor_tensor(out=r1, in0=x2, in1=sn, op=ALU.mult), reads=[skey] + ck_, writes=[RTK[1]])
            P.op('pool', lambda e: e.tensor_tensor(out=r2, in0=x2, in1=cb, op=ALU.mult), reads=[skey] + ck_, writes=[RTK[2]])
            P.op('pool', lambda e: e.tensor_tensor(out=r3, in0=x1, in1=sn, op=ALU.mult), reads=[skey] + ck_, writes=[RTK[3]])
            P.op('pool', lambda e: e.tensor_tensor(out=x1, in0=r0, in1=r1, op=ALU.subtract), reads=['cl', skey], writes=[skey])
            P.op('pool', lambda e: e.tensor_tensor(out=x2, in0=r2, in1=r3, op=ALU.add), reads=['eq', skey], writes=[skey])
            P.op('pool', lambda e: e.tensor_copy(out=dstb, in_=src), reads=[skey], writes=[dkey])

        def transposes_bf(srcs, src_keys, dst, dst_key, rows, evac='act'):
            n = len(srcs)
            dkeys = dst_key if isinstance(dst_key, list) else [dst_key]
            def f(e):
                ins = None
                for i, s_ in enumerate(srcs):
                    ins = e.transpose(PB[0:rows, i * 128:i * 128 + 128], s_, identb[:])
                return ins
            P.op('pe', f, reads=list(src_keys) + ['identb'], writes=['PB'])
            src_v = PB[0:rows, 0:n * 128].rearrange("p (k t) -> p k t", k=n)
            if evac == 'act':
                P.op('act', lambda e: e.activation(out=dst, in_=src_v, func=AF.Copy), reads=['PB'], writes=dkeys)
            else:
                P.op('dve', lambda e: e.tensor_copy(out=dst, in_=src_v), reads=['PB'], writes=dkeys)

        def F_tile(l, ti, seg, is_smp, par):
            xk = 'x%d' % ti
            rms_stats(xres[:, ti, :], xk, rstd[:], 'rstd', also_neg=nrstd[:])
            for _ in make_hT(ti, l, hT, 'hT'):
                yield
            kTc, vac = kT[l][par], vaug[l][par]
            kTck, vack = 'kT%d_%d' % (l, par), 'vaug%d_%d' % (l, par)
            if is_smp:
                cos_ap, sin_ap = coss[:], sins[:]
            else:
                cos_ap, sin_ap = cosp[:, ti, :], sinp[:, ti, :]
            pj = proj_group(0, 512)
            P.op('act', lambda e: e.activation(out=qa[:].rearrange("p h d -> p (h d)"), in_=PJ[pj][:], func=AF.Copy, scale=rstd[:]),
                 reads=['PJ%d' % pj, 'rstd'], writes=['qa'])
            yield
            rotary(qa[:], 8, cos_ap, sin_ap, qab[:], 'qa', 'qab')
            pj = proj_group(512, 768)
            P.op('act', lambda e: e.activation(out=ka[:].rearrange("p h d -> p (h d)"), in_=PJ[pj][:, 0:128], func=AF.Copy, scale=rstd[:]),
                 reads=['PJ%d' % pj, 'rstd'], writes=['ka'])
            P.op('act', lambda e: e.activation(out=va[:], in_=PJ[pj][:, 128:256], func=AF.Copy, scale=rstd[:]),
                 reads=['PJ%d' % pj, 'rstd'], writes=['va'])
            yield
            rotary(ka[:], 2, cos_ap, sin_ap, kab[:], 'ka', 'kab')
            P.op('pool', lambda e: e.tensor_copy(out=vac[:, :, 0:64], in_=va[:].rearrange("p (g d) -> p g d", g=2)), reads=['va'], writes=[vack])
            if is_smp:
                for t in range(4):
                    P.op('sp', lambda e: e.dma_start(out=nk_s[l, :, 124 + t, :], in_=ka[t * 16:(t + 1) * 16].rearrange("p h d -> p (h d)")), reads=['ka'], dma='o_nk%d' % t)
                    P.op('sp', lambda e: e.dma_start(out=nv_s[l, :, 124 + t, :], in_=va[t * 16:(t + 1) * 16, :]), reads=['va'], dma='o_nv%d' % t)
            elif seg == NSEG - 1 and ti == TPS - 1:
                P.op('sp', lambda e: e.dma_start(out=nk_p[l], in_=ka[:].rearrange("p h d -> p (h d)")), reads=['ka'], dma='o_nkp')
                P.op('sp', lambda e: e.dma_start(out=nv_p[l], in_=va[:]), reads=['va'], dma='o_nvp')
            pj = proj_group(768, 1280)
            silu_from_psum(pj, qh[:], 'qh')
            yield
            pj = proj_group(1280, 1792)
            k = 'PJ%d' % pj
            P.op('act', lambda e: e.activation(out=e1[:], in_=PJ[pj][:], func=AF.Exp, scale=nrstd[:]), reads=[k, 'nrstd'], writes=['e1'])
            P.op('act', lambda e: e.activation(out=e1[:], in_=e1[:], func=AF.Ln, bias=1.0), reads=['e1'], writes=['e1'])
            P.op('act', lambda e: e.activation(out=fg[:], in_=e1[:], func=AF.Exp, scale=-1.0), reads=['e1'], writes=['fg'])
            yield
            transposes_bf([qab[:, h, :] for h in range(8)], ['qab'], qT[par][:], 'qT%d' % par, 64)
            yield
            P.op('pool', lambda e: e.tensor_scalar(out=kk[:], in0=fg[:], scalar1=-1.0, scalar2=1.0, op0=ALU.mult, op1=ALU.add), reads=['fg'], writes=['kk'])
            if l == 1:
                P.op('dve', lambda e: e.tensor_tensor(out=fg[:], in0=kk[:], in1=lb1[:], op=ALU.mult), reads=['kk', 'lb1'], writes=['fg'])
                P.op('pool', lambda e: e.tensor_tensor(out=kk[:], in0=kk[:], in1=fg[:], op=ALU.subtract), reads=['kk', 'fg'], writes=['kk'])
                P.op('pool', lambda e: e.tensor_scalar(out=fg[:], in0=kk[:], scalar1=-1.0, scalar2=1.0, op0=ALU.mult, op1=ALU.add), reads=['kk'], writes=['fg'])
            P.op('act', lambda e: e.activation(out=gl[:], in_=fg[:], func=AF.Ln), reads=['fg'], writes=['gl'])
            pj = proj_group(1792, 2304)
            P.op('act', lambda e: e.activation(out=vh[par][:], in_=PJ[pj][:], func=AF.Copy, scale=rstd[:]), reads=['PJ%d' % pj, 'rstd'], writes=['vh%d' % par])
            yield
            pj = proj_group(2304, 2816)
            silu_from_psum(pj, fg[:], 'fg')
            P.op('pool', lambda e: e.tensor_tensor(out=gate[par][:].rearrange("p (h d) -> p h d", h=4), in0=fg[:].rearrange("p (h d) -> p h d", h=4),
                                                   in1=bc(onb[:, l, :], 1, [128, 4, 128]), op=ALU.mult), reads=['fg', 'onb'], writes=['gate%d' % par])
            yield
            mi = 1 if is_smp else 0
            p1 = nxt('pj'); p2 = nxt('pj')
            P.op('pe', lambda e: e.matmul(PJ[p1][:], lhsT=mrel[:, mi, :], rhs=gl[:], start=True, stop=True), reads=['mrel', 'gl'], writes=['PJ%d' % p1])
            P.op('pe', lambda e: e.matmul(PJ[p2][:], lhsT=mkd[:, mi, :], rhs=gl[:], start=True, stop=True), reads=['mkd', 'gl'], writes=['PJ%d' % p2])
            P.op('dve', lambda e: e.tensor_scalar(out=cl[:], in0=PJ[p1][:], scalar1=-40.0, scalar2=40.0, op0=ALU.max, op1=ALU.min), reads=['PJ%d' % p1], writes=['cl'])
            P.op('act', lambda e: e.activation(out=eq[:], in_=cl[:], func=AF.Exp), reads=['cl'], writes=['eq'])
            P.op('act', lambda e: e.activation(out=ek[:], in_=cl[:], func=AF.Exp, scale=-1.0), reads=['cl'], writes=['e1'])
            P.op('act', lambda e: e.activation(out=cl[:], in_=PJ[p2][:], func=AF.Exp), reads=['PJ%d' % p2, 'cl'], writes=['cl'])
            P.op('dve', lambda e: e.tensor_tensor(out=qe[par][:].rearrange("p h d -> p (h d)"), in0=qh[:], in1=eq[:], op=ALU.mult), reads=['qh', 'eq'], writes=['qe%d' % par])
            P.op('dve', lambda e: e.tensor_tensor(out=ke[par][:].rearrange("p h d -> p (h d)"), in0=kk[:], in1=ek[:], op=ALU.mult), reads=['kk', 'e1'], writes=['ke%d' % par])
            P.op('dve', lambda e: e.tensor_tensor(out=kd[par][:], in0=kk[:], in1=cl[:], op=ALU.mult), reads=['kk', 'cl'], writes=['kd%d' % par])
            yield
            nind = 16 if is_smp else 8
            ind_ap = inds[:] if is_smp else indp[:]
            p3 = nxt('pj')
            def fbg(e):
                ins = None
                for h in range(4):
                    ins = e.matmul(PJ[p3][:, h * 16:h * 16 + nind], lhsT=gl[:, h * 128:(h + 1) * 128], rhs=ind_ap, start=True, stop=True)
                return ins
            P.op('pe', fbg, reads=['gl', 'inds', 'indp'], writes=['PJ%d' % p3])
            P.op('act', lambda e: e.activation(out=bg[par][:, :, 0:nind], in_=PJ[p3][:, 0:64].rearrange("p (h c) -> p h c", h=4)[:, :, 0:nind], func=AF.Exp),
                 reads=['PJ%d' % p3], writes=['bg%d' % par])
            yield

            transposes_bf([kab[:, g, :] for g in range(2)], ['kab'], kTc[:], kTck, 64)
            yield

        def B_tile(l, ti, seg, is_smp, par, pending):
            xk = 'x%d' % ti
            prv = 1 - par
            kTc, vac = kT[l][par], vaug[l][par]
            kTck, vack = 'kT%d_%d' % (l, par), 'vaug%d_%d' % (l, par)
            qTc, qTk = qT[par], 'qT%d' % par
            mi = 1 if is_smp else 0
            if is_smp:
                blocks = [('c', j) for j in range(NSMP)] + [('n', 0)]
            else:
                blocks = ([('p', 0)] if not (seg == 0 and ti == 0) else []) + [('u', 0)]
            nb = len(blocks)
            pv_bank = {0: (PV, 'PV'), 1: ((PT, 'PT') if is_smp else (PV, 'PV'))}

            def finish_g(g):
                bank, bkey = pv_bank[g]
                pv4 = bank[:].rearrange("p (h c) -> p h c", h=4)
                P.op('dve', lambda e: e.tensor_tensor(out=den[:], in0=pv4[:, :, 64], in1=esink[:, l, 4 * g:4 * g + 4], op=ALU.add), reads=[bkey, 'esink'], writes=['den'])
                P.op('dve', lambda e: e.reciprocal(out=rden[:], in_=den[:]), reads=['den'], writes=['rden'])
                P.op('dve', lambda e: e.tensor_tensor(out=mix[:, g * 256:(g + 1) * 256].rearrange("p (h d) -> p h d", h=4), in0=pv4[:, :, 0:64],
                                                      in1=bc(rden[:], 2, [128, 4, 64]), op=ALU.mult), reads=[bkey, 'rden'], writes=['mixa'])

            def block_g(g, bi, bk, j, mask_ap, mask_key, lhs, lk, rv, rvk, neg=None):
                bank, bkey = pv_bank[g]
                psi = nxt('ps')
                pk = 'PS%d' % psi
                def fsc(e):
                    ins = e.matmul(PS[psi][:], lhsT=lhs, rhs=qTc[:, 4 * g:4 * g + 4, :], start=True, stop=(neg is None))
                    if neg is not None:
                        ins = e.matmul(PS[psi][:], lhsT=identb[:], rhs=neg, start=False, stop=True)
                    return ins
                P.op('pe', fsc, reads=[lk, qTk, 'identb', 'negm'], writes=[pk])
                pi = nxt('pt')
                P.op('act', lambda e: e.activation(out=pT[pi][:].rearrange("p h t -> p (h t)"), in_=PS[psi][:], func=AF.Exp, scale=0.125),
                     reads=[pk], writes=['pT%d' % pi])
                if neg is None:
                    P.op('pool', lambda e: e.tensor_tensor(out=pT[pi][:], in0=pT[pi][:], in1=bc(mask_ap, 1, [128, 4, 128]), op=ALU.mult),
                         reads=['pT%d' % pi, mask_key], writes=['pT%d' % pi])
                yield 1
                def fpv(e):
                    ins = None
                    for hh in range(4):
                        ins = e.matmul(bank[:, hh * 128:hh * 128 + 65], lhsT=pT[pi][:, hh, :], rhs=rv,
                                       start=(bi == 0 and hh == 0), stop=(bi == nb - 1), skip_group_check=True)
                    return ins
                P.op('pe', fpv, reads=['pT%d' % pi, rvk], writes=[bkey])
                yield 0

            if not is_smp:
                for g in range(2):
                    for bi, (bk, j) in enumerate(blocks):
                        if bk == 'p':
                            yield from block_g(g, bi, bk, j, None, None, kT[l][prv][:, g, :], 'kT%d_%d' % (l, prv), vaug[l][prv][:, g, :], 'vaug%d_%d' % (l, prv), neg=negm[:, 1, :])
                        else:
                            yield from block_g(g, bi, bk, j, None, None, kTc[:, g, :], kTck, vac[:, g, :], vack, neg=negm[:, 0, :])
                    finish_g(g)
                    if g == 0:
                        flush(pending)
            else:
                flush(pending)
                P.barrier()
                P.op('pool', lambda e: e.memset(ft[:, 1024:2064], 0.0), writes=['vcsA'])
                P.op('dve', lambda e: e.memset(vcsA[:, :, :, 64], 1.0), reads=['vcsA'], writes=['vcsA'])
                for i_ in range(2):
                    P.op('dve', lambda e: e.memset(pT[i_][:], 0.0), writes=['pT%d' % i_])
                P.op('pool', lambda e: e.dma_start(out=kcsA, in_=ck[l].rearrange("j k d -> k j d")), writes=['kcsA'], dma='l_kc')
                for g in range(2):
                    P.op('pool', lambda e: e.dma_start(out=vcsA[:, :, g, 0:64], in_=cv[l, :, :, g * 64:(g + 1) * 64].rearrange("j k d -> k j d")), reads=['vcsA'], writes=['vcsA'], dma='l_vc%d' % g)

            def att_seq(bi, j):
                kcT = kcT2[j % 2]
                kck = 'kcT%d' % (j % 2)
                transposes_bf([kcsA[:, j, g * 64:(g + 1) * 64] for g in range(2)], ['kcsA'], kcT, kck, 64)
                for g in range(2):
                    bank, bkey = pv_bank[g]
                    psi = nxt('ps')
                    pk = 'PS%d' % psi
                    P.op('pe', lambda e: e.matmul(PS[psi][:, 0:16], lhsT=kcT[:, g, :], rhs=qTc[:, 4 * g:4 * g + 4, j:64:16], start=True, stop=True),
                         reads=[kck, qTk], writes=[pk])
                    pi = nxt('pt')
                    pslice = pT[pi][:, :, j:64:16]
                    P.op('act', lambda e: e.activation(out=pslice, in_=PS[psi][:, 0:16].rearrange("p (h t) -> p h t", h=4), func=AF.Exp, scale=0.125),
                         reads=[pk], writes=['pT%d' % pi])
                    P.op('dve', lambda e: e.tensor_tensor(out=pslice, in0=pslice, in1=bc(mc0[:, j:64:16], 1, [128, 4, 4]), op=ALU.mult),
                         reads=['pT%d' % pi, 'mc0'], writes=['pT%d' % pi])
                    def fpv(e):
                        ins = None
                        for hh in range(4):
                            ins = e.matmul(bank[:, hh * 128:hh * 128 + 65], lhsT=pT[pi][:, hh, :], rhs=vcsA[:, j, g, :],
                                           start=(bi == 0 and hh == 0), stop=False, skip_group_check=True)
                        return ins
                    P.op('pe', fpv, reads=['pT%d' % pi, 'vcsA'], writes=[bkey])
                    P.op('dve', lambda e: e.memset(pslice, 0.0), reads=['pT%d' % pi], writes=['pT%d' % pi])

            def att_epi():
                bi, (bk, j) = nb - 1, blocks[-1]
                for g in range(2):
                    for _ in block_g(g, bi, bk, j, mns[:], 'mns', kTc[:, g, :], kTck, vac[:, g, :], vack):
                        pass
                finish_g(0)
                finish_g(1)

            qec, kec, kdc, bgc, vhc = qe[par], ke[par], kd[par], bg[par], vh[par]
            qek, kek, kdk, bgk, vhk = 'qe%d' % par, 'ke%d' % par, 'kd%d' % par, 'bg%d' % par, 'vh%d' % par
            transposes_bf([qec[:, h, :] for h in range(4)], [qek], qeT[:], 'qeT', 128)
            yield 1
            transposes_bf([kec[:, h, :] for h in range(4)], [kek], keT[:], 'keT', 128, evac='dve')
            yield 1
            p4 = nxt('ps')
            def fa(e):
                ins = None
                for h in range(4):
                    ins = e.matmul(PS[p4][:, h * 128:(h + 1) * 128], lhsT=keT[:, h, :], rhs=qeT[:, h, :], start=True, stop=True)
                return ins
            P.op('pe', fa, reads=['keT', 'qeT'], writes=['PS%d' % p4])
            P.op('dve', lambda e: e.tensor_tensor(out=atm[:], in0=PS[p4][:].rearrange("p (h t) -> p h t", h=4), in1=bc(ma[:, mi, :], 1, [128, 4, 128]), op=ALU.mult),
                 reads=['PS%d' % p4, 'ma'], writes=['atm'])
            yield 0
            def fo(e):
                ins = None
                for h in range(4):
                    ins = e.matmul(PO[:, h * 128:(h + 1) * 128], lhsT=atm[:, h, :], rhs=vhc[:, h * 128:(h + 1) * 128],
                                   start=(h == 0), stop=False, skip_group_check=True)
                return ins
            P.op('pe', fo, reads=['atm', vhk], writes=['PO'])
            yield 0
            Sl = S[l]
            Sk = 'S%d' % l
            if not is_smp:
                qeTs = keT
                P.op('dve', lambda e: e.tensor_tensor(out=qeTs.rearrange("p h (c t) -> p h c t", c=4), in0=qeT.rearrange("p h (c t) -> p h c t", c=4),
                                                      in1=bc(bgc[:, :, 4:8], 3, [128, 4, 4, 32]), op=ALU.mult),
                     reads=['qeT', 'keT', bgk], writes=['keT'])
                P.op('dve', lambda e: e.tensor_copy(out=Sp[0][:], in_=Sl[:]), reads=[Sk], writes=['Sp0'])
                for c in range(4):
                    pu = nxt('ps')
                    def fu(e):
                        ins = None
                        for h in range(4):
                            ins = e.matmul(PS[pu][:, h * 128:(h + 1) * 128], lhsT=kdc[32 * c:32 * c + 32, h * 128:(h + 1) * 128],
                                           rhs=vhc[32 * c:32 * c + 32, h * 128:(h + 1) * 128], start=True, stop=True, tile_position=(32 * c, 0))
                        return ins
                    P.op('pe', fu, reads=[kdk, vhk], writes=['PS%d' % pu])
                    def fupd(e):
                        ins = None
                        for h in range(4):
                            ins = e.scalar_tensor_tensor(out=Sl[:, h, :], in0=Sl[:, h, :], scalar=bgc[:, h, c:c + 1], in1=PS[pu][:, h * 128:(h + 1) * 128],
                                                         op0=ALU.mult, op1=ALU.add)
                        return ins
                    if c < 3:
                        def fupb(e):
                            ins = None
                            for h in range(4):
                                ins = e.scalar_tensor_tensor(out=Sp[(c + 1) % 2][:, h, :], in0=Sl[:, h, :], scalar=bgc[:, h, c:c + 1], in1=PS[pu][:, h * 128:(h + 1) * 128],
                                                             op0=ALU.mult, op1=ALU.add)
                            return ins
                        P.op('dve', fupb, reads=[Sk, bgk, 'PS%d' % pu], writes=['Sp%d' % ((c + 1) % 2)])
                    P.op('dve', fupd, reads=[Sk, bgk, 'PS%d' % pu], writes=[Sk])
                    if c == 3:
                        yield 3
                    def foi(e):
                        ins = None
                        for h in range(4):
                            ins = e.matmul(PO[32 * c:32 * c + 32, h * 128:(h + 1) * 128], lhsT=qeTs[:, h, 32 * c:32 * c + 32], rhs=Sp[c % 2][:, h, :],
                                           start=False, stop=(c == 3 and h == 3), skip_group_check=True, tile_position=(0, 32 * c))
                        return ins
                    P.op('pe', foi, reads=['keT', 'Sp%d' % (c % 2)], writes=['PO'])
                    if c < 3:
                        yield 1
                if seg == NSEG - 1 and ti == TPS - 1:
                    P.op('sp', lambda e: e.dma_start(out=ns_p[l].rearrange("h k v -> k h v"), in_=Sl[:]), reads=[Sk], dma='o_nsp')
            else:
                kdm = atm[:].rearrange("p h d -> p (h d)")
                qeTm = Sp[0][:, :, 0:64]
                sbf2 = [sbf, mix[:, 512:1024].rearrange("p (h d) -> p h d", h=4)]
                sbk = ['sbfA', 'mixh']
                def ld_state(jj):
                    s_ = jj % 2
                    P.op('sp', lambda e: e.dma_start(out=sstA[s_], in_=st_in[l, jj].rearrange("h k v -> k h v")), writes=['sstA%d' % s_], dma='l_st%d' % s_)
                    P.op('pool', lambda e: e.dma_start(out=sbf2[s_], in_=st_in[l, jj].rearrange("h k v -> k h v")), writes=[sbk[s_]], dma='l_sb%d' % s_)
                ld_state(0)
                for j in range(NSMP):
                    si = j % 2
                    if j + 1 < NSMP:
                        ld_state(j + 1)
                    att_seq(j, j)
                    P.op('dve', lambda e: e.tensor_tensor(out=qeTm, in0=qeT[:, :, 0:64], in1=bc(selT[:, j, :], 1, [128, 4, 64]), op=ALU.mult),
                         reads=['qeT', 'selT'], writes=['Sp0'])
                    def foi(e):
                        ins = None
                        for h in range(4):
                            ins = e.matmul(PO[0:64, h * 128:(h + 1) * 128], lhsT=qeTm[:, h, :], rhs=sbf2[si][:, h, :],
                                           start=False, stop=(j == NSMP - 1), skip_group_check=True)
                        return ins
                    P.op('pe', foi, reads=['Sp0', sbk[si]], writes=['PO'])
                    P.op('act', lambda e: e.activation(out=kdm, in_=kdc[:], func=AF.Copy, scale=msel[:, j:j + 1]), reads=[kdk, 'msel'], writes=['atm'])
                    pu = nxt('ps')
                    def fu(e):
                        ins = None
                        for h in range(4):
                            ins = e.matmul(PS[pu][:, h * 128:(h + 1) * 128], lhsT=kdm[:, h * 128:(h + 1) * 128], rhs=vhc[:, h * 128:(h + 1) * 128], start=True, stop=True)
                        return ins
                    P.op('pe', fu, reads=['atm', vhk], writes=['PS%d' % pu])
                    P.op('dve', lambda e: e.tensor_tensor(out=stmp, in0=sstA[si], in1=bc(bgc[:, :, j], 2, [128, 4, 128]), op=ALU.mult),
                         reads=['sstA%d' % si, bgk], writes=['stmpA'])
                    P.op('dve', lambda e: e.tensor_tensor(out=stmp.rearrange("p h d -> p (h d)"), in0=stmp.rearrange("p h d -> p (h d)"), in1=PS[pu][:], op=ALU.add),
                         reads=['stmpA', 'PS%d' % pu], writes=['stmpA'])
                    P.op('sp', lambda e: e.dma_start(out=ns_s[l, j].rearrange("h k v -> k h v"), in_=stmp), reads=['stmpA'], dma='o_st')
                    yield 1
                att_epi()
            def fsq(e):
                for h in range(4):
                    e.activation(out=atm[:, h, :], in_=PO[:, h * 128:(h + 1) * 128], func=AF.Square, accum_out=sso[:, h:h + 1])
                return e.activation(out=dmy[:], in_=dmy[:], func=AF.Copy)
            P.op('act', fsq, reads=['PO'], writes=['atm', 'sso', 'dmy'])
            P.op('act', lambda e: e.activation(out=sso[:], in_=sso[:], func=AF.Ln, scale=1.0 / 128.0, bias=EPS_AP[:]), reads=['sso', 'eps_ap'], writes=['sso'])
            P.op('act', lambda e: e.activation(out=rso[:], in_=sso[:], func=AF.Exp, scale=-0.5), reads=['sso'], writes=['rso'])
            def fmx(e):
                ins = None
                for h in range(4):
                    ins = e.scalar_tensor_tensor(out=mix[:, 512 + h * 128:512 + (h + 1) * 128], in0=PO[:, h * 128:(h + 1) * 128], scalar=rso[:, h:h + 1],
                                                 in1=gate[par][:, h * 128:(h + 1) * 128], op0=ALU.mult, op1=ALU.mult)
                return ins
            P.op('dve', fmx, reads=['PO', 'rso', 'gate%d' % par], writes=['mixh'])
            transposes_bf([mix[:, i * 128:(i + 1) * 128] for i in range(4)], ['mixa'], mixT[:, 0:4, :], 'qeT', 128)
            yield 99

            def tail():
                transposes_bf([mix[:, i * 128:(i + 1) * 128] for i in range(4, 8)], ['mixh'], mixT[:, 4:8, :], 'keT', 128, evac='dve')
                for half in range(2):
                    pj = nxt('pj')
                    def fw(e):
                        ins = None
                        for kc in range(8):
                            ins = e.matmul(PJ[pj][:], lhsT=mixT[:, kc, :], rhs=w_o_sb[:, kc, half * 512:(half + 1) * 512], start=(kc == 0), stop=(kc == 7))
                        return ins
                    P.op('pe', fw, reads=['qeT', 'keT'] + WO_KEYS, writes=['PJ%d' % pj])
                    P.op('dve', lambda e: e.tensor_tensor(out=xres[:, ti, half * 512:(half + 1) * 512], in0=xres[:, ti, half * 512:(half + 1) * 512], in1=PJ[pj][:], op=ALU.add),
                         reads=[xk, 'PJ%d' % pj], writes=[xk])
            pending.append(tail)

        def flush(pending):
            while pending:
                pending.pop(0)()

        def drain(g):
            for _ in g:
                pass

        def interleave(gb, gf):
            done_f = False
            for nf in gb:
                for _ in range(nf or 0):
                    if done_f:
                        break
                    try:
                        next(gf)
                    except StopIteration:
                        done_f = True
            if not done_f:
                drain(gf)

        for seg in range(NSEG):
            tiles = list(range(TPS)) + ([TPS] if seg == 0 else [])
            for ti in range(TPS):
                if ti == 0 or seg > 0:
                    continue
                P.op('sp', lambda e: e.dma_start(out=xres[:, ti, :], in_=x_seq[(seg * TPS + ti) * 128:(seg * TPS + ti + 1) * 128, :]),
                     writes=['x%d' % ti], dma='lx%d' % ti)
            P.op('sp', lambda e: e.dma_start(out=cosp[:], in_=c_cosp[:, seg * TPS:(seg + 1) * TPS, :]), writes=['cosp'], dma='l_cos')
            P.op('sp', lambda e: e.dma_start(out=sinp[:], in_=c_sinp[:, seg * TPS:(seg + 1) * TPS, :]), writes=['sinp'], dma='l_sin')
            if seg == 0:
                P.op('sp', lambda e: e.dma_start(out=xres[0:64, TPS, :], in_=x_smp), writes=['x%d' % TPS], dma='lx%d' % TPS)
            for l in range(2):
                if not (seg == 0 and l == 0):
                    P.barrier()
                def load_pass(p_):
                    load_pass_l(l, p_)
                if l != 0:
                    load_phase_a(l)
                if seg == 0:
                    P.op('sp', lambda e: e.dma_start(out=nk_s[l, :, 0:124, :], in_=ck[l, :, 4:128, :]), dma='o_ckc')
                    P.op('sp', lambda e: e.dma_start(out=nv_s[l, :, 0:124, :], in_=cv[l, :, 4:128, :]), dma='o_cvc')
                n = len(tiles)
                prepped = set()
                drain(F_tile(l, tiles[0], seg, tiles[0] == TPS, 0))
                load_phase_a2(l)
                pending = []
                for tix, ti in enumerate(tiles):
                    gb = B_tile(l, ti, seg, ti == TPS, tix % 2, pending)
                    if tix + 1 < n:
                        gf = F_tile(l, tiles[tix + 1], seg, tiles[tix + 1] == TPS, (tix + 1) % 2)
                        interleave(gb, gf)
                    elif ti == TPS:
                        def prep_gen():
                            for t_ in range(TPS):
                                rms_stats(xres[:, t_, :], 'x%d' % t_, rstd[:], 'rstd', sq=rr2[:, t_:t_ + 1])
                                for _ in make_hT(t_, 2 + l, hT_all[:, :, t_ * 128:(t_ + 1) * 128], 'hTall', pj_only=True):
                                    pass
                                prepped.add(t_)
                                yield
                        interleave(gb, prep_gen())
                    else:
                        drain(gb)
                flush(pending)
                P.barrier()
                ntok = n * 128
                for ti in tiles:
                    if ti in prepped:
                        continue
                    rms_stats(xres[:, ti, :], 'x%d' % ti, rstd[:], 'rstd', sq=rr2[:, ti:ti + 1])
                    drain(make_hT(ti, 2 + l, hT_all[:, :, ti * 128:(ti + 1) * 128], 'hTall'))
                groups = [(g0, min(GRP, ntok - g0)) for g0 in range(0, ntok, GRP)]

                def up(p_, g0, gn, ui):
                    slot = p_ % 2
                    wu, wd = ring[slot]
                    uT = uTs[ui]
                    for fc in range(4):
                        psi = nxt('ps')
                        def fup(e):
                            ins = None
                            for kc in range(8):
                                ins = e.matmul(PS[psi][:, 0:gn], lhsT=wu[:, kc, fc * 128:(fc + 1) * 128], rhs=hT_all[:, kc, g0:g0 + gn], start=(kc == 0), stop=(kc == 7))
                            return ins
                        P.op('pe', fup, reads=['ringu%d_%d' % (slot, k_) for k_ in range(8)] + ['hTall'], writes=['PS%d' % psi])
                        P.op('act', lambda e: e.activation(out=rl[:, 0:gn], in_=PS[psi][:, 0:gn], func=AF.Relu), reads=['PS%d' % psi], writes=['hTa'])
                        P.op('dve', lambda e: e.tensor_tensor(out=uT[:, fc, 0:gn], in0=rl[:, 0:gn], in1=rl[:, 0:gn], op=ALU.mult), reads=['hTa'], writes=['uT%d' % ui])

                def down(p_, g0, gn, ui):
                    slot = p_ % 2
                    wu, wd = ring[slot]
                    uT = uTs[ui]
                    for tt in range(gn // 128):
                        ti = (g0 // 128) + tt
                        for half in range(2):
                            pj = nxt('pj')
                            def fdn(e):
                                ins = None
                                for fc in range(4):
                                    ins = e.matmul(PJ[pj][:], lhsT=uT[:, fc, tt * 128:(tt + 1) * 128], rhs=wd[:, fc, half * 512:(half + 1) * 512], start=(fc == 0), stop=(fc == 3))
                                return ins
                            P.op('pe', fdn, reads=['uT%d' % ui] + ['ringd%d_%d' % (slot, k_) for k_ in range(4)], writes=['PJ%d' % pj])
                            P.op('dve', lambda e: e.scalar_tensor_tensor(
                                out=xres[:, ti, half * 512:(half + 1) * 512], in0=PJ[pj][:], scalar=rr2[:, ti:ti + 1],
                                in1=xres[:, ti, half * 512:(half + 1) * 512], op0=ALU.mult, op1=ALU.add),
                                reads=['PJ%d' % pj, 'rr2', 'x%d' % ti], writes=['x%d' % ti])

                def final_tile(ti):
                    rms_stats(xres[:, ti, :], 'x%d' % ti, rstd[:], 'rstd')
                    yi = nxt('y')
                    P.op('dve', lambda e: e.scalar_tensor_tensor(out=ysb[yi], in0=xres[:, ti, :], scalar=rstd[:], in1=fing, op0=ALU.mult, op1=ALU.mult),
                         reads=['x%d' % ti, 'rstd', 'fing'], writes=['ysb%d' % yi])
                    if ti == TPS:
                        P.op('sp', lambda e: e.dma_start(out=y_smp, in_=ysb[yi][0:64, :]), reads=['ysb%d' % yi], dma='o_y%d' % yi)
                    else:
                        P.op('sp', lambda e: e.dma_start(out=y_seq[(seg * TPS + ti) * 128:(seg * TPS + ti + 1) * 128, :], in_=ysb[yi]),
                             reads=['ysb%d' % yi], dma='o_y%d' % yi)
                        if seg + 1 < NSEG:
                            P.op('sp', lambda e: e.dma_start(out=xres[:, ti, :], in_=x_seq[((seg + 1) * TPS + ti) * 128:((seg + 1) * TPS + ti + 1) * 128, :]),
                                 writes=['x%d' % ti], dma='lx%d' % ti)

                def after_down(pv):
                    if l == 1 and pv[0] == NPASS - 1:
                        for tt in range(pv[2] // 128):
                            final_tile(pv[1] // 128 + tt)

                if l == 1:
                    P.op('sp', lambda e: e.dma_start(out=fing, in_=fin_g.partition_broadcast(128)), writes=['fing'], dma='c10')
                work = [(p_, g0, gn) for p_ in range(NPASS) for (g0, gn) in groups]
                prev = None
                for wi, (p_, g0, gn) in enumerate(work):
                    ui = nxt('u')
                    up(p_, g0, gn, ui)
                    if prev is not None:
                        down(*prev)
                        after_down(prev)
                    if g0 == 0 and p_ + 1 < NPASS:
                        load_pass(p_ + 1)
                    prev = (p_, g0, gn, ui)
                down(*prev)
                after_down(prev)
            P.barrier()
            if seg + 1 < NSEG:
                load_phase_a(0)
            P.barrier()
        P.emit()
    return nc


def _consts():
    c = {}
    c['c_identf'] = np.eye(128, dtype=np.float32)
    c['c_identb'] = np.eye(128, dtype=np.float32)
    half = 8
    inv_freq = np.power(np.float32(500000.0), -np.arange(half, dtype=np.float32) * np.float32(2.0 / 16)).astype(np.float32)
    pos = np.arange(SEQ, dtype=np.float32)
    ang = (pos[:, None] * inv_freq[None, :]).astype(np.float32)
    c['c_cosp'] = np.cos(ang).astype(np.float32).reshape(32, 128, 8).transpose(1, 0, 2).copy()
    c['c_sinp'] = np.sin(ang).astype(np.float32).reshape(32, 128, 8).transpose(1, 0, 2).copy()
    p = np.arange(128)
    tt = p // 16
    jj = p % 16
    valid = p < 64
    pos_s = (PAST + tt).astype(np.float32)
    ang_s = (pos_s[:, None] * inv_freq[None, :]).astype(np.float32)
    c['c_coss'] = np.where(valid[:, None], np.cos(ang_s), 1.0).astype(np.float32)
    c['c_sins'] = np.where(valid[:, None], np.sin(ang_s), 0.0).astype(np.float32)
    s = p[:, None]; t = p[None, :]
    ncur = np.where(s <= t, 0.0, -30000.0).astype(np.float32)
    nprev = np.where(s >= t, 0.0, -30000.0).astype(np.float32)
    c['c_negm'] = np.stack([np.tile(ncur, (1, 4)), np.tile(nprev, (1, 4))], axis=1).astype(np.float32)
    ch = p // 32
    same = ch[:, None] == ch[None, :]
    ref = ch * 32 + 15
    mrel_p = same * ((s <= t).astype(np.float32) - (s <= ref[None, :]).astype(np.float32))
    mkd_p = same * (s > t)
    ma_p = same * (s <= t)
    same_s = (jj[:, None] == jj[None, :]) & valid[:, None] & valid[None, :]
    mrel_s = same_s * (tt[:, None] <= tt[None, :])
    mkd_s = same_s * (tt[:, None] > tt[None, :])
    ma_s = same_s * (tt[:, None] <= tt[None, :])
    c['c_mrel'] = np.stack([mrel_p, mrel_s]).astype(np.float32)
    c['c_mkd'] = np.stack([mkd_p, mkd_s]).astype(np.float32)
    c['c_ma'] = np.stack([ma_p, ma_s]).astype(np.float32)
    indp = np.zeros((128, 8), np.float32)
    for cc in range(4):
        indp[cc * 32:(cc + 1) * 32, cc] = 1.0
        indp[cc * 32:cc * 32 + 16, 4 + cc] = 1.0
    c['c_indp'] = indp
    inds = np.zeros((128, 16), np.float32)
    inds[p[valid], jj[valid]] = 1.0
    c['c_inds'] = inds
    c['c_msel'] = inds.copy()
    c['c_mc0'] = ((p[:, None] >= tt[None, :]) & valid[None, :]).astype(np.float32)
    mns = (same_s & (tt[:, None] <= tt[None, :])).astype(np.float32)
    c['c_mns'] = mns
    selT = np.zeros((128, 16, 64), np.float32)
    for j in range(16):
        selT[:, j, :] = ((jj == j) & valid)[None, 0:64]
    c['c_selT'] = selT
    return c


_CACHE = {}
PCORES = [0, 1, 4, 5]


def kernel(x_prompt, x_sample, cache_k, cache_v, state_hgrn, attn_norm, w_in, att_sinks,
           hgrn_lower_bounds, hgrn_out_norm, w_o, mlp_norm, w_up, w_down, final_norm):
    f = lambda a: np.ascontiguousarray(np.asarray(a, dtype=np.float32))
    x_prompt, x_sample, cache_k, cache_v, state_hgrn = map(f, (x_prompt, x_sample, cache_k, cache_v, state_hgrn))
    if 'nc' not in _CACHE:
        _CACHE['nc'] = build_program()
        _CACHE['consts'] = _consts()
    nc = _CACHE['nc']
    consts = _CACHE['consts']
    gains = np.concatenate([f(attn_norm).reshape(2, 8, 128), f(mlp_norm).reshape(2, 8, 128), f(final_norm).reshape(1, 8, 128)], axis=0).reshape(40, 128)
    shared = dict(gains=np.ascontiguousarray(gains), fin_g=f(final_norm).reshape(1, D), w_in=f(w_in), sinks=f(att_sinks),
                  lbraw=f(hgrn_lower_bounds), onorm=f(hgrn_out_norm), w_o=f(w_o), w_up=f(w_up), w_dn=f(w_down))
    shared.update(consts)
    zero_seq = np.zeros((SEQ, D), np.float32)
    in_maps = []
    for c in range(8):
        sl = slice(c * NSMP, (c + 1) * NSMP)
        m = dict(shared)
        m['x_seq'] = x_prompt[PCORES.index(c)] if c in PCORES else zero_seq
        m['x_smp'] = np.ascontiguousarray(x_sample[sl].transpose(1, 0, 2).reshape(64, D))
        m['ck'] = np.ascontiguousarray(cache_k[:, sl].reshape(2, NSMP, 128, 128))
        m['cv'] = np.ascontiguousarray(cache_v[:, sl].reshape(2, NSMP, 128, 128))
        m['st_in'] = np.ascontiguousarray(state_hgrn[:, sl])
        in_maps.append(m)
    res = run_bass_kernel_spmd(nc, in_maps, core_ids=list(range(8)))
    R = res.results
    y_prompt = np.stack([R[b]['y_seq'] for b in PCORES]).astype(np.float32)
    y_sample = np.concatenate([R[c]['y_smp'].reshape(4, NSMP, D).transpose(1, 0, 2) for c in range(8)], axis=0).astype(np.float32)
    nk_p = np.stack([R[b]['nk_p'] for b in PCORES], axis=1).reshape(2, 4, 128, 2, 64)
    nv_p = np.stack([R[b]['nv_p'] for b in PCORES], axis=1).reshape(2, 4, 128, 2, 64)
    ns_p = np.stack([R[b]['ns_p'] for b in PCORES], axis=1)
    nk_s = np.concatenate([R[c]['nk_s'] for c in range(8)], axis=1).reshape(2, 128, 128, 2, 64)
    nv_s = np.concatenate([R[c]['nv_s'] for c in range(8)], axis=1).reshape(2, 128, 128, 2, 64)
    ns_s = np.concatenate([R[c]['ns_s'] for c in range(8)], axis=1)
    return (y_prompt, y_sample, np.ascontiguousarray(nk_p), np.ascontiguousarray(nv_p), np.ascontiguousarray(ns_p),
            np.ascontiguousarray(nk_s), np.ascontiguousarray(nv_s), np.ascontiguousarray(ns_s))
```

```python
import numpy as np
from contextlib import ExitStack
import concourse.bass as bass
import concourse.mybir as mybir
from concourse.bass_utils import run_bass_kernel_spmd

F32 = mybir.dt.float32
BF16 = mybir.dt.bfloat16
AF = mybir.ActivationFunctionType
ALU = mybir.AluOpType
AX = mybir.AxisListType

D = 1024
SEQ = 4096
NSEG = 2
TPS = 16
NSMP = 16
PAST = 16384
EPS = 1e-6
INW = 2816
DFF = 4096
NPASS = 8
GRP = 256


class Prog:
    ENGS = ('pe', 'act', 'dve', 'pool', 'sp')

    def __init__(self, nc, es):
        self.nc = nc
        self.es = es
        self.eng = {'pe': nc.tensor, 'act': nc.scalar, 'dve': nc.vector, 'pool': nc.gpsimd, 'sp': nc.sync}
        self.cnt = {e: 0 for e in self.ENGS}
        self.semh = {k: es.enter_context(nc.semaphore('s_' + k)) for k in self.ENGS}
        self.dma_cnt = {}
        self.last_write = {}
        self.reads_since = {}
        self.seen = {e: {} for e in self.ENGS}
        self.know = {}
        self.nins = 0
        self.nwait = 0

    def _wait(self, e, tok):
        key, val = tok
        if self.seen[e].get(key, 0) >= val:
            return
        se = self.seen[e]
        se[key] = val
        self.eng[e].wait_ge(self.semh[key], val)
        self.nwait += 1
        kn = self.know.get(tok)
        if kn:
            for k2, v2 in kn.items():
                if se.get(k2, 0) < v2:
                    se[k2] = v2

    def op(self, e, fn, reads=(), writes=(), dma=None):
        deps = []
        for r in reads:
            if r in self.last_write:
                deps.append(self.last_write[r])
        for w in writes:
            rs = self.reads_since.get(w, ())
            if rs:
                deps.extend(rs)
            elif w in self.last_write:
                deps.append(self.last_write[w])
        mx = {}
        for key, val in deps:
            if key == 'pe' and e == 'pe' and dma is None:
                continue
            if val > mx.get(key, 0):
                mx[key] = val
        for key, val in mx.items():
            self._wait(e, (key, val))
        self.nins += 1
        if dma is None:
            self.cnt[e] += 1
            tok = (e, self.cnt[e])
            fn(self.eng[e]).then_inc(self.semh[e], 1)
        else:
            k = 'dma:' + dma
            if k not in self.semh:
                self.semh[k] = self.es.enter_context(self.nc.semaphore('d%d' % len(self.semh)))
            self.dma_cnt[k] = self.dma_cnt.get(k, 0) + 16
            tok = (k, self.dma_cnt[k])
            fn(self.eng[e]).then_inc(self.semh[k], 16)
        self.know[tok] = dict(self.seen[e])
        for r in reads:
            self.reads_since.setdefault(r, []).append(tok)
        for w in writes:
            self.last_write[w] = tok
            self.reads_since[w] = []
        return tok

    def barrier(self, skip=()):
        toks = [(k, v) for k, v in self.dma_cnt.items() if not any(k.startswith('dma:' + s) for s in skip)] + \
               [(k, self.cnt[k]) for k in self.ENGS if self.cnt[k]]
        for e in self.ENGS:
            for t in toks:
                self._wait(e, t)

    def emit(self):
        self.barrier()


def bc(ap, axis, shape):
    return ap.unsqueeze(axis).broadcast_to(shape)


def build_program():
    nc = bass.Bass("TRN2", target_bir_lowering=False)

    def din(name, shape, dt=F32):
        return nc.dram_tensor(name, list(shape), dt, kind="ExternalInput").ap()

    def dout(name, shape):
        return nc.dram_tensor(name, list(shape), F32, kind="ExternalOutput").ap()

    x_seq = din("x_seq", [SEQ, D])
    x_smp = din("x_smp", [64, D])
    ck = din("ck", [2, NSMP, 128, 128])
    cv = din("cv", [2, NSMP, 128, 128])
    st_in = din("st_in", [2, NSMP, 4, 128, 128])
    gains = din("gains", [40, 128])
    fin_g = din("fin_g", [1, D])
    w_in = din("w_in", [2, D, INW])
    sinks = din("sinks", [2, 8])
    lbraw = din("lbraw", [2, 512])
    onorm = din("onorm", [2, 128])
    w_o = din("w_o", [2, D, D])
    w_up = din("w_up", [2, D, DFF])
    w_dn = din("w_dn", [2, DFF, D])
    c_identf = din("c_identf", [128, 128])
    c_identb = din("c_identb", [128, 128])
    c_cosp = din("c_cosp", [128, 32, 8])
    c_sinp = din("c_sinp", [128, 32, 8])
    c_coss = din("c_coss", [128, 8])
    c_sins = din("c_sins", [128, 8])
    c_negm = din("c_negm", [128, 2, 512])
    c_mrel = din("c_mrel", [2, 128, 128])
    c_mkd = din("c_mkd", [2, 128, 128])
    c_ma = din("c_ma", [2, 128, 128])
    c_indp = din("c_indp", [128, 8])
    c_inds = din("c_inds", [128, 16])
    c_msel = din("c_msel", [128, 16])
    c_mc0 = din("c_mc0", [128, 128])
    c_mns = din("c_mns", [128, 128])
    c_selT = din("c_selT", [128, 16, 64])

    y_seq = dout("y_seq", [SEQ, D])
    y_smp = dout("y_smp", [64, D])
    nk_p = dout("nk_p", [2, 128, 128])
    nv_p = dout("nv_p", [2, 128, 128])
    ns_p = dout("ns_p", [2, 4, 128, 128])
    nk_s = dout("nk_s", [2, NSMP, 128, 128])
    nv_s = dout("nv_s", [2, NSMP, 128, 128])
    ns_s = dout("ns_s", [2, NSMP, 4, 128, 128])

    with ExitStack() as es:
        def sb(name, shape, dt=F32):
            return es.enter_context(nc.sbuf_tensor(name, list(shape), dt))

        def ps(name, shape, dt=F32):
            return es.enter_context(nc.psum_tensor(name, list(shape), dt))

        NT = TPS + 1
        xres = sb("xres", [128, NT, D])
        hT_elems = 8 * NT * 128
        ring_slot = 8 * 512 + 4 * D
        uT_elems = 4 * GRP
        rg_b = max(8 * INW + 8 * D + ring_slot, hT_elems + ring_slot + 2 * uT_elems)
        rgn32 = sb("rgn", [128, rg_b // 2])
        rgnb = rgn32[:].bitcast(BF16)
        w_in_sb = rgnb[:, 0:8 * INW].rearrange("p (k n) -> p k n", k=8)
        w_o_sb = rgnb[:, 8 * INW:8 * INW + 8 * D].rearrange("p (k n) -> p k n", k=8)
        hT_all = rgnb[:, 0:hT_elems].rearrange("p (k t) -> p k t", k=8)
        ring = []
        for o0 in (8 * INW + 8 * D, hT_elems):
            wu = rgnb[:, o0:o0 + 4096].rearrange("p (k n) -> p k n", k=8)
            wd = rgnb[:, o0 + 4096:o0 + 8192].rearrange("p (k n) -> p k n", k=4)
            ring.append((wu, wd))
        o0 = hT_elems + ring_slot
        assert o0 + 2 * uT_elems <= 8 * INW + 8 * D
        uTs = []
        for s in range(2):
            uTs.append(rgnb[:, o0:o0 + uT_elems].rearrange("p (k n) -> p k n", k=4))
            o0 += uT_elems

        identf = sb("identf", [128, 128]); identb = sb("identb", [128, 128], BF16)
        cosp = sb("cosp", [128, TPS, 8]); sinp = sb("sinp", [128, TPS, 8])
        coss = sb("coss", [128, 8]); sins = sb("sins", [128, 8])
        negm = sb("negm", [128, 2, 512], BF16)
        mrel = sb("mrel", [128, 2, 128]); mkd = sb("mkd", [128, 2, 128]); ma = sb("ma", [128, 2, 128], BF16)
        indp = sb("indp", [128, 8]); inds = sb("inds", [128, 16]); msel = sb("msel", [128, 16])
        mc0 = sb("mc0", [128, 128], BF16); mns = sb("mns", [128, 128], BF16)
        selT = sb("selT", [128, 16, 64], BF16)
        gT = sb("gT", [128, 40])
        esink = sb("esink", [128, 2, 8])
        lb1 = sb("lb1", [128, 512])
        onb = sb("onb", [128, 2, 128])
        EPS_AP = sb("eps_ap", [128, 1])

        ss = sb("ss", [128, 1]); rstd = sb("rstd", [128, 1]); nrstd = sb("nrstd", [128, 1])
        rr2 = sb("rr2", [128, NT])
        dmy = sb("dmy", [128, 1])
        hT = sb("hT", [128, 8, 128], BF16)
        junk = hT[:].rearrange("p k t -> p (k t)")
        rl = junk[:, 0:512]
        ft = sb("ft", [128, 4096])
        fing = ft[:, 0:D]
        ysb = [ft[:, D:2 * D], ft[:, 2 * D:3 * D]]
        qa = ft[:, 0:512].rearrange("p (h d) -> p h d", h=8)
        ka = sb("ka", [128, 2, 64]); va = sb("va", [128, 128])
        qab = sb("qab", [128, 8, 64], BF16); kab = sb("kab", [128, 2, 64], BF16)
        qT = [sb("qT%d" % i, [64, 8, 128], BF16) for i in range(2)]
        kT = [[sb("kT%d_%d" % (l, i), [64, 2, 128], BF16) for i in range(2)] for l in range(2)]
        vaug = [[sb("vaug%d_%d" % (l, i), [128, 2, 65], BF16) for i in range(2)] for l in range(2)]
        pT = [sb("pT%d" % i, [128, 4, 128], BF16) for i in range(2)]
        den = sb("den", [128, 4]); rden = sb("rden", [128, 4])
        mix = sb("mix", [128, D], BF16)
        e1 = ft[:, 512:1024]
        qh = ft[:, 1024:1536]; fg = ft[:, 1536:2048]; gl = ft[:, 2048:2560]; kk = ft[:, 2560:3072]
        cl = ft[:, 3072:3584]; eq = ft[:, 3584:4096]; ek = e1
        rt = [cl[:, 0:64].rearrange("p (h d) -> p h d", h=8), cl[:, 64:128].rearrange("p (h d) -> p h d", h=8),
              eq[:, 0:64].rearrange("p (h d) -> p h d", h=8), eq[:, 64:128].rearrange("p (h d) -> p h d", h=8)]
        RTK = ["cl", "cl", "eq", "eq"]
        vh = [sb("vh%d" % i, [128, 512], BF16) for i in range(2)]
        gate = [sb("gate%d" % i, [128, 512], BF16) for i in range(2)]
        qe = [sb("qe%d" % i, [128, 4, 128], BF16) for i in range(2)]
        ke = [sb("ke%d" % i, [128, 4, 128], BF16) for i in range(2)]
        kd = [sb("kd%d" % i, [128, 512], BF16) for i in range(2)]
        bg = [sb("bg%d" % i, [128, 4, 16]) for i in range(2)]
        tq = sb("tq", [128, 8, 128], BF16)
        qeT = tq[:, 0:4, :]; keT = tq[:, 4:8, :]; mixT = tq[:]
        atm = sb("atm", [128, 4, 128], BF16)
        S = [sb("S%d" % l, [128, 4, 128]) for l in range(2)]
        Sp = [sb("Sp%d" % c, [128, 4, 128], BF16) for c in range(2)]
        sso = sb("sso", [128, 4]); rso = sb("rso", [128, 4])
        kcT2 = [Sp[1][0:64, 0:2, :], Sp[1][0:64, 2:4, :]]
        kcsA = ft[:, 0:1024].bitcast(BF16).rearrange("p (j d) -> p j d", j=NSMP)
        vcsA = ft[:, 1024:2064].bitcast(BF16).rearrange("p (j g d) -> p j g d", j=NSMP, g=2)
        sstA = [ft[:, 2064:2576].rearrange("p (h d) -> p h d", h=4), ft[:, 2576:3088].rearrange("p (h d) -> p h d", h=4)]
        stmp = ft[:, 3088:3600].rearrange("p (h d) -> p h d", h=4)
        sbf = ft[:, 3600:3856].bitcast(BF16).rearrange("p (h d) -> p h d", h=4)

        PT = ps("PT", [128, 512])
        PB = ps("PB", [128, 1024], BF16)
        PJ = [ps("PJ0", [128, 512]), ps("PJ1", [128, 512])]
        PS = [ps("PS0", [128, 512]), ps("PS1", [128, 512])]
        PV = ps("PV", [128, 512])
        PO = ps("PO", [128, 512])

        P = Prog(nc, es)
        cnt = {'pj': 0, 'ps': 0, 'y': 0, 'pt': 0, 'u': 0, 'pb': 0}

        def nxt(k, n=2):
            v = cnt[k] % n
            cnt[k] += 1
            return v


        WGRP = [(0, 512), (512, 768), (768, 1280), (1280, 1792), (1792, 2304), (2304, 2816)]

        def load_pass_l(l, p_):
            slot = p_ % 2
            wu, wd = ring[slot]
            for kc in range(8):
                P.op('pool', lambda e: e.dma_start(out=wu[:, kc, :], in_=w_up[l, kc * 128:(kc + 1) * 128, p_ * 512:(p_ + 1) * 512]), writes=['ringu%d_%d' % (slot, kc)], dma='lru%d' % slot)
            for fc in range(4):
                P.op('pool', lambda e: e.dma_start(out=wd[:, fc, :], in_=w_dn[l, p_ * 512 + fc * 128: p_ * 512 + (fc + 1) * 128, :]), writes=['ringd%d_%d' % (slot, fc)], dma='lrd%d' % slot)

        def load_phase_a(l):
            for gi, (c0, c1) in enumerate(WGRP):
                P.op('pool', lambda e: e.dma_start(out=w_in_sb[:, :, c0:c1], in_=w_in[l, :, c0:c1].rearrange("(k p) n -> p k n", p=128)), writes=['w_in_g%d' % gi], dma='lw_in%d' % gi)

        def load_phase_a2(l):
            for kc in range(8):
                P.op('pool', lambda e: e.dma_start(out=w_o_sb[:, kc, :], in_=w_o[l, kc * 128:(kc + 1) * 128, :]), writes=['w_o%d' % kc], dma='lw_o')
            load_pass_l(l, 0)

        cl_list = [(identf, c_identf), (coss, c_coss), (sins, c_sins),
                   (indp, c_indp), (inds, c_inds), (msel, c_msel)]
        for i, (t, d) in enumerate(cl_list):
            P.op('sp', lambda e: e.dma_start(out=t[:], in_=d), writes=[t.name], dma='c%d' % i)
        P.op('sp', lambda e: e.dma_start(out=mrel[:], in_=c_mrel.rearrange("a p n -> p a n")), writes=['mrel'], dma='c20')
        P.op('sp', lambda e: e.dma_start(out=mkd[:], in_=c_mkd.rearrange("a p n -> p a n")), writes=['mkd'], dma='c21')
        cb_list = [(identb, c_identb), (negm, c_negm), (mc0, c_mc0), (mns, c_mns), (selT, c_selT)]
        for i, (t, d) in enumerate(cb_list):
            P.op('pool', lambda e: e.dma_start(out=t[:], in_=d), writes=[t.name], dma='cb%d' % i)
        P.op('pool', lambda e: e.dma_start(out=ma[:], in_=c_ma.rearrange("a p n -> p a n")), writes=['ma'], dma='cb9')
        P.op('sp', lambda e: e.dma_start(out=xres[:, 0, :], in_=x_seq[0:128, :]), writes=['x0'], dma='lx0')
        load_phase_a(0)
        P.op('sp', lambda e: e.dma_start(out=esink[:].rearrange("p l h -> p (l h)"), in_=sinks.rearrange("l h -> (l h)").unsqueeze(0).partition_broadcast(128)), writes=['esink'], dma='c11')
        gstage = ft[0:40, 1024:1152]
        lbr = ft[:, 0:1024].rearrange("p (l c) -> p l c", l=2)
        P.op('sp', lambda e: e.dma_start(out=ft[:, 0:1024], in_=lbraw.rearrange("l c -> (l c)").unsqueeze(0).partition_broadcast(128)), writes=['lbr'], dma='c12')
        P.op('sp', lambda e: e.dma_start(out=onb[:].rearrange("p l h -> p (l h)"), in_=onorm.rearrange("l c -> (l c)").unsqueeze(0).partition_broadcast(128)), writes=['onb'], dma='c13')
        P.op('sp', lambda e: e.dma_start(out=gstage, in_=gains), writes=['gstage'], dma='c19')
        P.op('act', lambda e: e.activation(out=esink[:], in_=esink[:], func=AF.Exp), reads=['esink'], writes=['esink'])
        P.op('dve', lambda e: e.tensor_tensor(out=lb1[:], in0=lbr[:, 0, :], in1=lbr[:, 1, :], op=ALU.subtract), reads=['lbr'], writes=['lb1'])
        P.op('act', lambda e: e.activation(out=lb1[:], in_=lb1[:], func=AF.Exp), reads=['lb1'], writes=['lb1'])
        P.op('dve', lambda e: e.tensor_scalar_add(out=lb1[:], in0=lb1[:], scalar1=1.0), reads=['lb1'], writes=['lb1'])
        P.op('dve', lambda e: e.reciprocal(out=lb1[:], in_=lb1[:]), reads=['lb1'], writes=['lb1'])
        P.op('pe', lambda e: e.transpose(PT[:, 0:40], gstage, identf[0:40, 0:40]), reads=['gstage', 'identf'], writes=['PT'])
        P.op('dve', lambda e: e.tensor_copy(out=gT[:], in_=PT[:, 0:40]), reads=['PT'], writes=['gT'])
        P.op('pool', lambda e: e.memset(dmy[:], 0.0), writes=['dmy'])
        P.op('pool', lambda e: e.memset(EPS_AP[:], EPS), writes=['eps_ap'])
        for l in range(2):
            for i in range(2):
                P.op('pool', lambda e: e.memset(vaug[l][i][:], 1.0), writes=['vaug%d_%d' % (l, i)])
        P.op('pool', lambda e: e.memset(xres[:, TPS, :], 0.0), writes=['x%d' % TPS])
        for l in range(2):
            P.op('pool', lambda e: e.memset(S[l][:], 0.0), writes=['S%d' % l])
        P.barrier(skip=('lw_in', 'lw_o', 'lru', 'lrd', 'lx'))

        def rms_stats(xt_ap, xkey, out_r, out_key, also_neg=None, sq=None):
            def f(e):
                e.activation(out=junk, in_=xt_ap, func=AF.Square, accum_out=ss[:])
                return e.activation(out=dmy[:], in_=dmy[:], func=AF.Copy)
            P.op('act', f, reads=[xkey], writes=['hTa', 'hTb', 'ss', 'dmy'])
            P.op('act', lambda e: e.activation(out=ss[:], in_=ss[:], func=AF.Ln, scale=1.0 / D, bias=EPS_AP[:]), reads=['ss', 'eps_ap'], writes=['ss'])
            P.op('act', lambda e: e.activation(out=out_r, in_=ss[:], func=AF.Exp, scale=-0.5), reads=['ss'], writes=[out_key])
            if also_neg is not None:
                P.op('dve', lambda e: e.tensor_scalar_mul(out=also_neg, in0=out_r, scalar1=-1.0), reads=[out_key], writes=['nrstd'])
            if sq is not None:
                P.op('act', lambda e: e.activation(out=sq, in_=ss[:], func=AF.Exp, scale=-1.0), reads=['ss'], writes=['rr2'])

        def make_hT(ti, gcol, dst, dst_key, pj_only=False):
            for half in range(2):
                if half == 0 and not pj_only:
                    bank, bkey = PT, 'PT'
                else:
                    pj_ = nxt('pj')
                    bank, bkey = PJ[pj_], 'PJ%d' % pj_
                def f(e):
                    ins = None
                    for j in range(4):
                        kc = half * 4 + j
                        ins = e.transpose(bank[:, j * 128:(j + 1) * 128], xres[:, ti, kc * 128:(kc + 1) * 128], identf[:])
                    return ins
                P.op('pe', f, reads=['x%d' % ti, 'identf'], writes=[bkey])
                P.op('dve', lambda e: e.tensor_tensor(
                    out=dst[:, half * 4:half * 4 + 4, :], in0=bank[:].rearrange("p (k t) -> p k t", k=4),
                    in1=bc(gT[:, gcol * 8 + half * 4:gcol * 8 + half * 4 + 4], 2, [128, 4, 128]), op=ALU.mult),
                    reads=[bkey, 'gT'], writes=[(dst_key + 'ab'[half]) if dst_key == 'hT' else dst_key])
                yield

        WO_KEYS = ['w_o%d' % k_ for k_ in range(8)]

        def proj_group(c0, c1):
            pj = nxt('pj')
            for hf in range(2):
                def f(e):
                    ins = None
                    for kc in range(hf * 4, hf * 4 + 4):
                        ins = e.matmul(PJ[pj][:, 0:c1 - c0], lhsT=hT[:, kc, :], rhs=w_in_sb[:, kc, c0:c1], start=(kc == 0), stop=(kc == 7))
                    return ins
                P.op('pe', f, reads=['hT' + 'ab'[hf], 'w_in_g%d' % WGRP.index((c0, c1))], writes=['PJ%d' % pj])
            return pj

        def silu_from_psum(pj, dst, dst_key):
            k = 'PJ%d' % pj
            P.op('act', lambda e: e.activation(out=dst, in_=PJ[pj][:], func=AF.Copy, scale=rstd[:]), reads=[k, 'rstd'], writes=[dst_key])
            P.op('act', lambda e: e.activation(out=e1[:], in_=dst, func=AF.Exp, scale=-1.0), reads=[dst_key], writes=['e1'])
            P.op('act', lambda e: e.activation(out=e1[:], in_=e1[:], func=AF.Ln, bias=1.0), reads=['e1'], writes=['e1'])
            P.op('act', lambda e: e.activation(out=e1[:], in_=e1[:], func=AF.Exp, scale=-1.0), reads=['e1'], writes=['e1'])
            P.op('pool', lambda e: e.tensor_tensor(out=dst, in0=dst, in1=e1[:], op=ALU.mult), reads=[dst_key, 'e1'], writes=[dst_key])

        def rotary(src, nh, cos_ap, sin_ap, dstb, skey, dkey):
            x1 = src[:, :, 0:8]; x2 = src[:, :, 8:16]
            cb = bc(cos_ap, 1, [128, nh, 8]); sn = bc(sin_ap, 1, [128, nh, 8])
            r0, r1, r2, r3 = [t[:, 0:nh, :] for t in rt]
            ck_ = ['cosp', 'sinp']
            P.op('pool', lambda e: e.tensor_tensor(out=r0, in0=x1, in1=cb, op=ALU.mult), reads=[skey] + ck_, writes=[RTK[0]])
            P.op('pool', lambda e: e.tensor_tensor(out=r1, in0=x2, in1=sn, op=ALU.mult), reads=[skey] + ck_, writes=[RTK[1]])
            P.op('pool', lambda e: e.tensor_tensor(out=r2, in0=x2, in1=cb, op=ALU.mult), reads=[skey] + ck_, writes=[RTK[2]])
            P.op('pool', lambda e: e.tensor_tensor(out=r3, in0=x1, in1=sn, op=ALU.mult), reads=[skey] + ck_, writes=[RTK[3]])
            P.op('pool', lambda e: e.tensor_tensor(out=x1, in0=r0, in1=r1, op=ALU.subtract), reads=['cl', skey], writes=[skey])
            P.op('pool', lambda e: e.tensor_tensor(out=x2, in0=r2, in1=r3, op=ALU.add), reads=['eq', skey], writes=[skey])
            P.op('pool', lambda e: e.tensor_copy(out=dstb, in_=src), reads=[skey], writes=[dkey])

        def transposes_bf(srcs, src_keys, dst, dst_key, rows, evac='act'):
            n = len(srcs)
            dkeys = dst_key if isinstance(dst_key, list) else [dst_key]
            def f(e):
                ins = None
                for i, s_ in enumerate(srcs):
                    ins = e.transpose(PB[0:rows, i * 128:i * 128 + 128], s_, identb[:])
                return ins
            P.op('pe', f, reads=list(src_keys) + ['identb'], writes=['PB'])
            src_v = PB[0:rows, 0:n * 128].rearrange("p (k t) -> p k t", k=n)
            if evac == 'act':
                P.op('act', lambda e: e.activation(out=dst, in_=src_v, func=AF.Copy), reads=['PB'], writes=dkeys)
            else:
                P.op('dve', lambda e: e.tensor_copy(out=dst, in_=src_v), reads=['PB'], writes=dkeys)

        def F_tile(l, ti, seg, is_smp, par):
            xk = 'x%d' % ti
            rms_stats(xres[:, ti, :], xk, rstd[:], 'rstd', also_neg=nrstd[:])
            for _ in make_hT(ti, l, hT, 'hT'):
                yield
            kTc, vac = kT[l][par], vaug[l][par]
            kTck, vack = 'kT%d_%d' % (l, par), 'vaug%d_%d' % (l, par)
            if is_smp:
                cos_ap, sin_ap = coss[:], sins[:]
            else:
                cos_ap, sin_ap = cosp[:, ti, :], sinp[:, ti, :]
            pj = proj_group(0, 512)
            P.op('act', lambda e: e.activation(out=qa[:].rearrange("p h d -> p (h d)"), in_=PJ[pj][:], func=AF.Copy, scale=rstd[:]),
                 reads=['PJ%d' % pj, 'rstd'], writes=['qa'])
            yield
            rotary(qa[:], 8, cos_ap, sin_ap, qab[:], 'qa', 'qab')
            pj = proj_group(512, 768)
            P.op('act', lambda e: e.activation(out=ka[:].rearrange("p h d -> p (h d)"), in_=PJ[pj][:, 0:128], func=AF.Copy, scale=rstd[:]),
                 reads=['PJ%d' % pj, 'rstd'], writes=['ka'])
            P.op('act', lambda e: e.activation(out=va[:], in_=PJ[pj][:, 128:256], func=AF.Copy, scale=rstd[:]),
                 reads=['PJ%d' % pj, 'rstd'], writes=['va'])
            yield
            rotary(ka[:], 2, cos_ap, sin_ap, kab[:], 'ka', 'kab')
            P.op('pool', lambda e: e.tensor_copy(out=vac[:, :, 0:64], in_=va[:].rearrange("p (g d) -> p g d", g=2)), reads=['va'], writes=[vack])
            if is_smp:
                for t in range(4):
                    P.op('sp', lambda e: e.dma_start(out=nk_s[l, :, 124 + t, :], in_=ka[t * 16:(t + 1) * 16].rearrange("p h d -> p (h d)")), reads=['ka'], dma='o_nk%d' % t)
                    P.op('sp', lambda e: e.dma_start(out=nv_s[l, :, 124 + t, :], in_=va[t * 16:(t + 1) * 16, :]), reads=['va'], dma='o_nv%d' % t)
            elif seg == NSEG - 1 and ti == TPS - 1:
                P.op('sp', lambda e: e.dma_start(out=nk_p[l], in_=ka[:].rearrange("p h d -> p (h d)")), reads=['ka'], dma='o_nkp')
                P.op('sp', lambda e: e.dma_start(out=nv_p[l], in_=va[:]), reads=['va'], dma='o_nvp')
            pj = proj_group(768, 1280)
            silu_from_psum(pj, qh[:], 'qh')
            yield
            pj = proj_group(1280, 1792)
            k = 'PJ%d' % pj
            P.op('act', lambda e: e.activation(out=e1[:], in_=PJ[pj][:], func=AF.Exp, scale=nrstd[:]), reads=[k, 'nrstd'], writes=['e1'])
            P.op('act', lambda e: e.activation(out=e1[:], in_=e1[:], func=AF.Ln, bias=1.0), reads=['e1'], writes=['e1'])
            P.op('act', lambda e: e.activation(out=fg[:], in_=e1[:], func=AF.Exp, scale=-1.0), reads=['e1'], writes=['fg'])
            yield
            transposes_bf([qab[:, h, :] for h in range(8)], ['qab'], qT[par][:], 'qT%d' % par, 64)
            yield
            P.op('pool', lambda e: e.tensor_scalar(out=kk[:], in0=fg[:], scalar1=-1.0, scalar2=1.0, op0=ALU.mult, op1=ALU.add), reads=['fg'], writes=['kk'])
            if l == 1:
                P.op('dve', lambda e: e.tensor_tensor(out=fg[:], in0=kk[:], in1=lb1[:], op=ALU.mult), reads=['kk', 'lb1'], writes=['fg'])
                P.op('pool', lambda e: e.tensor_tensor(out=kk[:], in0=kk[:], in1=fg[:], op=ALU.subtract), reads=['kk', 'fg'], writes=['kk'])
                P.op('pool', lambda e: e.tensor_scalar(out=fg[:], in0=kk[:], scalar1=-1.0, scalar2=1.0, op0=ALU.mult, op1=ALU.add), reads=['kk'], writes=['fg'])
            P.op('act', lambda e: e.activation(out=gl[:], in_=fg[:], func=AF.Ln), reads=['fg'], writes=['gl'])
            pj = proj_group(1792, 2304)
            P.op('act', lambda e: e.activation(out=vh[par][:], in_=PJ[pj][:], func=AF.Copy, scale=rstd[:]), reads=['PJ%d' % pj, 'rstd'], writes=['vh%d' % par])
            yield
            pj = proj_group(2304, 2816)
            silu_from_psum(pj, fg[:], 'fg')
            P.op('pool', lambda e: e.tensor_tensor(out=gate[par][:].rearrange("p (h d) -> p h d", h=4), in0=fg[:].rearrange("p (h d) -> p h d", h=4),
                                                   in1=bc(onb[:, l, :], 1, [128, 4, 128]), op=ALU.mult), reads=['fg', 'onb'], writes=['gate%d' % par])
            yield
            mi = 1 if is_smp else 0
            p1 = nxt('pj'); p2 = nxt('pj')
            P.op('pe', lambda e: e.matmul(PJ[p1][:], lhsT=mrel[:, mi, :], rhs=gl[:], start=True, stop=True), reads=['mrel', 'gl'], writes=['PJ%d' % p1])
            P.op('pe', lambda e: e.matmul(PJ[p2][:], lhsT=mkd[:, mi, :], rhs=gl[:], start=True, stop=True), reads=['mkd', 'gl'], writes=['PJ%d' % p2])
            P.op('dve', lambda e: e.tensor_scalar(out=cl[:], in0=PJ[p1][:], scalar1=-40.0, scalar2=40.0, op0=ALU.max, op1=ALU.min), reads=['PJ%d' % p1], writes=['cl'])
            P.op('act', lambda e: e.activation(out=eq[:], in_=cl[:], func=AF.Exp), reads=['cl'], writes=['eq'])
            P.op('act', lambda e: e.activation(out=ek[:], in_=cl[:], func=AF.Exp, scale=-1.0), reads=['cl'], writes=['e1'])
            P.op('act', lambda e: e.activation(out=cl[:], in_=PJ[p2][:], func=AF.Exp), reads=['PJ%d' % p2, 'cl'], writes=['cl'])
            P.op('dve', lambda e: e.tensor_tensor(out=qe[par][:].rearrange("p h d -> p (h d)"), in0=qh[:], in1=eq[:], op=ALU.mult), reads=['qh', 'eq'], writes=['qe%d' % par])
            P.op('dve', lambda e: e.tensor_tensor(out=ke[par][:].rearrange("p h d -> p (h d)"), in0=kk[:], in1=ek[:], op=ALU.mult), reads=['kk', 'e1'], writes=['ke%d' % par])
            P.op('dve', lambda e: e.tensor_tensor(out=kd[par][:], in0=kk[:], in1=cl[:], op=ALU.mult), reads=['kk', 'cl'], writes=['kd%d' % par])
            yield
            nind = 16 if is_smp else 8
            ind_ap = inds[:] if is_smp else indp[:]
            p3 = nxt('pj')
            def fbg(e):
                ins = None
                for h in range(4):
                    ins = e.matmul(PJ[p3][:, h * 16:h * 16 + nind], lhsT=gl[:, h * 128:(h + 1) * 128], rhs=ind_ap, start=True, stop=True)
                return ins
            P.op('pe', fbg, reads=['gl', 'inds', 'indp'], writes=['PJ%d' % p3])
            P.op('act', lambda e: e.activation(out=bg[par][:, :, 0:nind], in_=PJ[p3][:, 0:64].rearrange("p (h c) -> p h c", h=4)[:, :, 0:nind], func=AF.Exp),
                 reads=['PJ%d' % p3], writes=['bg%d' % par])
            yield

            transposes_bf([kab[:, g, :] for g in range(2)], ['kab'], kTc[:], kTck, 64)
            yield

        def B_tile(l, ti, seg, is_smp, par):
            xk = 'x%d' % ti
            prv = 1 - par
            kTc, vac = kT[l][par], vaug[l][par]
            kTck, vack = 'kT%d_%d' % (l, par), 'vaug%d_%d' % (l, par)
            qTc, qTk = qT[par], 'qT%d' % par
            mi = 1 if is_smp else 0
            if is_smp:
                blocks = [('c', j) for j in range(NSMP)] + [('n', 0)]
            else:
                blocks = ([('p', 0)] if not (seg == 0 and ti == 0) else []) + [('u', 0)]
            nb = len(blocks)
            pv_bank = {0: (PV, 'PV'), 1: ((PT, 'PT') if is_smp else (PV, 'PV'))}

            def finish_g(g):
                bank, bkey = pv_bank[g]
                pv4 = bank[:].rearrange("p (h c) -> p h c", h=4)
                P.op('dve', lambda e: e.tensor_tensor(out=den[:], in0=pv4[:, :, 64], in1=esink[:, l, 4 * g:4 * g + 4], op=ALU.add), reads=[bkey, 'esink'], writes=['den'])
                P.op('dve', lambda e: e.reciprocal(out=rden[:], in_=den[:]), reads=['den'], writes=['rden'])
                P.op('dve', lambda e: e.tensor_tensor(out=mix[:, g * 256:(g + 1) * 256].rearrange("p (h d) -> p h d", h=4), in0=pv4[:, :, 0:64],
                                                      in1=bc(rden[:], 2, [128, 4, 64]), op=ALU.mult), reads=[bkey, 'rden'], writes=['mixa'])

            def block_g(g, bi, bk, j, mask_ap, mask_key, lhs, lk, rv, rvk, neg=None):
                bank, bkey = pv_bank[g]
                psi = nxt('ps')
                pk = 'PS%d' % psi
                def fsc(e):
                    ins = e.matmul(PS[psi][:], lhsT=lhs, rhs=qTc[:, 4 * g:4 * g + 4, :], start=True, stop=(neg is None))
                    if neg is not None:
                        ins = e.matmul(PS[psi][:], lhsT=identb[:], rhs=neg, start=False, stop=True)
                    return ins
                P.op('pe', fsc, reads=[lk, qTk, 'identb', 'negm'], writes=[pk])
                pi = nxt('pt')
                P.op('act', lambda e: e.activation(out=pT[pi][:].rearrange("p h t -> p (h t)"), in_=PS[psi][:], func=AF.Exp, scale=0.125),
                     reads=[pk], writes=['pT%d' % pi])
                if neg is None:
                    P.op('pool', lambda e: e.tensor_tensor(out=pT[pi][:], in0=pT[pi][:], in1=bc(mask_ap, 1, [128, 4, 128]), op=ALU.mult),
                         reads=['pT%d' % pi, mask_key], writes=['pT%d' % pi])
                yield 1
                def fpv(e):
                    ins = None
                    for hh in range(4):
                        ins = e.matmul(bank[:, hh * 128:hh * 128 + 65], lhsT=pT[pi][:, hh, :], rhs=rv,
                                       start=(bi == 0 and hh == 0), stop=(bi == nb - 1), skip_group_check=True)
                    return ins
                P.op('pe', fpv, reads=['pT%d' % pi, rvk], writes=[bkey])
                yield 0

            if not is_smp:
                for g in range(2):
                    for bi, (bk, j) in enumerate(blocks):
                        if bk == 'p':
                            yield from block_g(g, bi, bk, j, None, None, kT[l][prv][:, g, :], 'kT%d_%d' % (l, prv), vaug[l][prv][:, g, :], 'vaug%d_%d' % (l, prv), neg=negm[:, 1, :])
                        else:
                            yield from block_g(g, bi, bk, j, None, None, kTc[:, g, :], kTck, vac[:, g, :], vack, neg=negm[:, 0, :])
                    finish_g(g)
            else:
                P.barrier()
                P.op('pool', lambda e: e.memset(ft[:, 1024:2064], 0.0), writes=['vcsA'])
                P.op('dve', lambda e: e.memset(vcsA[:, :, :, 64], 1.0), reads=['vcsA'], writes=['vcsA'])
                for i_ in range(2):
                    P.op('dve', lambda e: e.memset(pT[i_][:], 0.0), writes=['pT%d' % i_])
                P.op('pool', lambda e: e.dma_start(out=kcsA, in_=ck[l].rearrange("j k d -> k j d")), writes=['kcsA'], dma='l_kc')
                for g in range(2):
                    P.op('pool', lambda e: e.dma_start(out=vcsA[:, :, g, 0:64], in_=cv[l, :, :, g * 64:(g + 1) * 64].rearrange("j k d -> k j d")), reads=['vcsA'], writes=['vcsA'], dma='l_vc%d' % g)

            def att_seq(bi, j):
                kcT = kcT2[j % 2]
                kck = 'kcT%d' % (j % 2)
                transposes_bf([kcsA[:, j, g * 64:(g + 1) * 64] for g in range(2)], ['kcsA'], kcT, kck, 64)
                for g in range(2):
                    bank, bkey = pv_bank[g]
                    psi = nxt('ps')
                    pk = 'PS%d' % psi
                    P.op('pe', lambda e: e.matmul(PS[psi][:, 0:16], lhsT=kcT[:, g, :], rhs=qTc[:, 4 * g:4 * g + 4, j:64:16], start=True, stop=True),
                         reads=[kck, qTk], writes=[pk])
                    pi = nxt('pt')
                    pslice = pT[pi][:, :, j:64:16]
                    P.op('act', lambda e: e.activation(out=pslice, in_=PS[psi][:, 0:16].rearrange("p (h t) -> p h t", h=4), func=AF.Exp, scale=0.125),
                         reads=[pk], writes=['pT%d' % pi])
                    P.op('dve', lambda e: e.tensor_tensor(out=pslice, in0=pslice, in1=bc(mc0[:, j:64:16], 1, [128, 4, 4]), op=ALU.mult),
                         reads=['pT%d' % pi, 'mc0'], writes=['pT%d' % pi])
                    def fpv(e):
                        ins = None
                        for hh in range(4):
                            ins = e.matmul(bank[:, hh * 128:hh * 128 + 65], lhsT=pT[pi][:, hh, :], rhs=vcsA[:, j, g, :],
                                           start=(bi == 0 and hh == 0), stop=False, skip_group_check=True)
                        return ins
                    P.op('pe', fpv, reads=['pT%d' % pi, 'vcsA'], writes=[bkey])
                    P.op('dve', lambda e: e.memset(pslice, 0.0), reads=['pT%d' % pi], writes=['pT%d' % pi])

            def att_epi():
                bi, (bk, j) = nb - 1, blocks[-1]
                for g in range(2):
                    for _ in block_g(g, bi, bk, j, mns[:], 'mns', kTc[:, g, :], kTck, vac[:, g, :], vack):
                        pass
                finish_g(0)
                finish_g(1)

            qec, kec, kdc, bgc, vhc = qe[par], ke[par], kd[par], bg[par], vh[par]
            qek, kek, kdk, bgk, vhk = 'qe%d' % par, 'ke%d' % par, 'kd%d' % par, 'bg%d' % par, 'vh%d' % par
            transposes_bf([qec[:, h, :] for h in range(4)], [qek], qeT[:], 'qeT', 128)
            yield 1
            transposes_bf([kec[:, h, :] for h in range(4)], [kek], keT[:], 'keT', 128, evac='dve')
            yield 1
            p4 = nxt('ps')
            def fa(e):
                ins = None
                for h in range(4):
                    ins = e.matmul(PS[p4][:, h * 128:(h + 1) * 128], lhsT=keT[:, h, :], rhs=qeT[:, h, :], start=True, stop=True)
                return ins
            P.op('pe', fa, reads=['keT', 'qeT'], writes=['PS%d' % p4])
            P.op('dve', lambda e: e.tensor_tensor(out=atm[:], in0=PS[p4][:].rearrange("p (h t) -> p h t", h=4), in1=bc(ma[:, mi, :], 1, [128, 4, 128]), op=ALU.mult),
                 reads=['PS%d' % p4, 'ma'], writes=['atm'])
            yield 0
            def fo(e):
                ins = None
                for h in range(4):
                    ins = e.matmul(PO[:, h * 128:(h + 1) * 128], lhsT=atm[:, h, :], rhs=vhc[:, h * 128:(h + 1) * 128],
                                   start=(h == 0), stop=False, skip_group_check=True)
                return ins
            P.op('pe', fo, reads=['atm', vhk], writes=['PO'])
            yield 0
            Sl = S[l]
            Sk = 'S%d' % l
            if not is_smp:
                qeTs = keT
                P.op('dve', lambda e: e.tensor_tensor(out=qeTs.rearrange("p h (c t) -> p h c t", c=4), in0=qeT.rearrange("p h (c t) -> p h c t", c=4),
                                                      in1=bc(bgc[:, :, 4:8], 3, [128, 4, 4, 32]), op=ALU.mult),
                     reads=['qeT', 'keT', bgk], writes=['keT'])
                P.op('dve', lambda e: e.tensor_copy(out=Sp[0][:], in_=Sl[:]), reads=[Sk], writes=['Sp0'])
                for c in range(4):
                    pu = nxt('ps')
                    def fu(e):
                        ins = None
                        for h in range(4):
                            ins = e.matmul(PS[pu][:, h * 128:(h + 1) * 128], lhsT=kdc[32 * c:32 * c + 32, h * 128:(h + 1) * 128],
                                           rhs=vhc[32 * c:32 * c + 32, h * 128:(h + 1) * 128], start=True, stop=True, tile_position=(32 * c, 0))
                        return ins
                    P.op('pe', fu, reads=[kdk, vhk], writes=['PS%d' % pu])
                    def fupd(e):
                        ins = None
                        for h in range(4):
                            ins = e.scalar_tensor_tensor(out=Sl[:, h, :], in0=Sl[:, h, :], scalar=bgc[:, h, c:c + 1], in1=PS[pu][:, h * 128:(h + 1) * 128],
                                                         op0=ALU.mult, op1=ALU.add)
                        return ins
                    if c < 3:
                        def fupb(e):
                            ins = None
                            for h in range(4):
                                ins = e.scalar_tensor_tensor(out=Sp[(c + 1) % 2][:, h, :], in0=Sl[:, h, :], scalar=bgc[:, h, c:c + 1], in1=PS[pu][:, h * 128:(h + 1) * 128],
                                                             op0=ALU.mult, op1=ALU.add)
                            return ins
                        P.op('dve', fupb, reads=[Sk, bgk, 'PS%d' % pu], writes=['Sp%d' % ((c + 1) % 2)])
                    P.op('dve', fupd, reads=[Sk, bgk, 'PS%d' % pu], writes=[Sk])
                    if c == 3:
                        yield 3
                    def foi(e):
                        ins = None
                        for h in range(4):
                            ins = e.matmul(PO[32 * c:32 * c + 32, h * 128:(h + 1) * 128], lhsT=qeTs[:, h, 32 * c:32 * c + 32], rhs=Sp[c % 2][:, h, :],
                                           start=False, stop=(c == 3 and h == 3), skip_group_check=True, tile_position=(0, 32 * c))
                        return ins
                    P.op('pe', foi, reads=['keT', 'Sp%d' % (c % 2)], writes=['PO'])
                    if c < 3:
                        yield 1
                if seg == NSEG - 1 and ti == TPS - 1:
                    P.op('sp', lambda e: e.dma_start(out=ns_p[l].rearrange("h k v -> k h v"), in_=Sl[:]), reads=[Sk], dma='o_nsp')
            else:
                kdm = atm[:].rearrange("p h d -> p (h d)")
                qeTm = Sp[0][:, :, 0:64]
                sbf2 = [sbf, mix[:, 512:1024].rearrange("p (h d) -> p h d", h=4)]
                sbk = ['sbfA', 'mixh']
                def ld_state(jj):
                    s_ = jj % 2
                    P.op('sp', lambda e: e.dma_start(out=sstA[s_], in_=st_in[l, jj].rearrange("h k v -> k h v")), writes=['sstA%d' % s_], dma='l_st%d' % s_)
                    P.op('pool', lambda e: e.dma_start(out=sbf2[s_], in_=st_in[l, jj].rearrange("h k v -> k h v")), writes=[sbk[s_]], dma='l_sb%d' % s_)
                ld_state(0)
                for j in range(NSMP):
                    si = j % 2
                    if j + 1 < NSMP:
                        ld_state(j + 1)
                    att_seq(j, j)
                    P.op('dve', lambda e: e.tensor_tensor(out=qeTm, in0=qeT[:, :, 0:64], in1=bc(selT[:, j, :], 1, [128, 4, 64]), op=ALU.mult),
                         reads=['qeT', 'selT'], writes=['Sp0'])
                    def foi(e):
                        ins = None
                        for h in range(4):
                            ins = e.matmul(PO[0:64, h * 128:(h + 1) * 128], lhsT=qeTm[:, h, :], rhs=sbf2[si][:, h, :],
                                           start=False, stop=(j == NSMP - 1), skip_group_check=True)
                        return ins
                    P.op('pe', foi, reads=['Sp0', sbk[si]], writes=['PO'])
                    P.op('act', lambda e: e.activation(out=kdm, in_=kdc[:], func=AF.Copy, scale=msel[:, j:j + 1]), reads=[kdk, 'msel'], writes=['atm'])
                    pu = nxt('ps')
                    def fu(e):
                        ins = None
                        for h in range(4):
                            ins = e.matmul(PS[pu][:, h * 128:(h + 1) * 128], lhsT=kdm[:, h * 128:(h + 1) * 128], rhs=vhc[:, h * 128:(h + 1) * 128], start=True, stop=True)
                        return ins
                    P.op('pe', fu, reads=['atm', vhk], writes=['PS%d' % pu])
                    P.op('dve', lambda e: e.tensor_tensor(out=stmp, in0=sstA[si], in1=bc(bgc[:, :, j], 2, [128, 4, 128]), op=ALU.mult),
                         reads=['sstA%d' % si, bgk], writes=['stmpA'])
                    P.op('dve', lambda e: e.tensor_tensor(out=stmp.rearrange("p h d -> p (h d)"), in0=stmp.rearrange("p h d -> p (h d)"), in1=PS[pu][:], op=ALU.add),
                         reads=['stmpA', 'PS%d' % pu], writes=['stmpA'])
                    P.op('sp', lambda e: e.dma_start(out=ns_s[l, j].rearrange("h k v -> k h v"), in_=stmp), reads=['stmpA'], dma='o_st')
                    yield 1
                att_epi()
            def fsq(e):
                for h in range(4):
                    e.activation(out=atm[:, h, :], in_=PO[:, h * 128:(h + 1) * 128], func=AF.Square, accum_out=sso[:, h:h + 1])
                return e.activation(out=dmy[:], in_=dmy[:], func=AF.Copy)
            P.op('act', fsq, reads=['PO'], writes=['atm', 'sso', 'dmy'])
            P.op('act', lambda e: e.activation(out=sso[:], in_=sso[:], func=AF.Ln, scale=1.0 / 128.0, bias=EPS_AP[:]), reads=['sso', 'eps_ap'], writes=['sso'])
            P.op('act', lambda e: e.activation(out=rso[:], in_=sso[:], func=AF.Exp, scale=-0.5), reads=['sso'], writes=['rso'])
            def fmx(e):
                ins = None
                for h in range(4):
                    ins = e.scalar_tensor_tensor(out=mix[:, 512 + h * 128:512 + (h + 1) * 128], in0=PO[:, h * 128:(h + 1) * 128], scalar=rso[:, h:h + 1],
                                                 in1=gate[par][:, h * 128:(h + 1) * 128], op0=ALU.mult, op1=ALU.mult)
                return ins
            P.op('dve', fmx, reads=['PO', 'rso', 'gate%d' % par], writes=['mixh'])
            transposes_bf([mix[:, i * 128:(i + 1) * 128] for i in range(4)], ['mixa'], mixT[:, 0:4, :], 'qeT', 128)
            wob = [(PV, 'PV'), (PT, 'PT')]
            for half in range(2):
                wbank, wkey = wob[half]
                def fw0(e):
                    ins = None
                    for kc in range(4):
                        ins = e.matmul(wbank[:], lhsT=mixT[:, kc, :], rhs=w_o_sb[:, kc, half * 512:(half + 1) * 512], start=(kc == 0), stop=False)
                    return ins
                P.op('pe', fw0, reads=['qeT'] + WO_KEYS, writes=[wkey])
            yield 99
            transposes_bf([mix[:, i * 128:(i + 1) * 128] for i in range(4, 8)], ['mixh'], mixT[:, 4:8, :], 'keT', 128, evac='dve')
            for half in range(2):
                wbank, wkey = wob[half]
                def fw1(e):
                    ins = None
                    for kc in range(4, 8):
                        ins = e.matmul(wbank[:], lhsT=mixT[:, kc, :], rhs=w_o_sb[:, kc, half * 512:(half + 1) * 512], start=False, stop=(kc == 7))
                    return ins
                P.op('pe', fw1, reads=['keT'] + WO_KEYS, writes=[wkey])
                P.op('dve', lambda e: e.tensor_tensor(out=xres[:, ti, half * 512:(half + 1) * 512], in0=xres[:, ti, half * 512:(half + 1) * 512], in1=wbank[:], op=ALU.add),
                     reads=[xk, wkey], writes=[xk])
                yield 0

        def drain(g):
            for _ in g:
                pass

        def interleave(gb, gf):
            done_f = False
            for nf in gb:
                for _ in range(nf or 0):
                    if done_f:
                        break
                    try:
                        next(gf)
                    except StopIteration:
                        done_f = True
            if not done_f:
                drain(gf)

        for seg in range(NSEG):
            tiles = list(range(TPS)) + ([TPS] if seg == 0 else [])
            for ti in range(TPS):
                if ti == 0 or seg > 0:
                    continue
                P.op('sp', lambda e: e.dma_start(out=xres[:, ti, :], in_=x_seq[(seg * TPS + ti) * 128:(seg * TPS + ti + 1) * 128, :]),
                     writes=['x%d' % ti], dma='lx%d' % ti)
            P.op('sp', lambda e: e.dma_start(out=cosp[:], in_=c_cosp[:, seg * TPS:(seg + 1) * TPS, :]), writes=['cosp'], dma='l_cos')
            P.op('sp', lambda e: e.dma_start(out=sinp[:], in_=c_sinp[:, seg * TPS:(seg + 1) * TPS, :]), writes=['sinp'], dma='l_sin')
            if seg == 0:
                P.op('sp', lambda e: e.dma_start(out=xres[0:64, TPS, :], in_=x_smp), writes=['x%d' % TPS], dma='lx%d' % TPS)
            for l in range(2):
                if not (seg == 0 and l == 0):
                    P.barrier()
                def load_pass(p_):
                    load_pass_l(l, p_)
                if l != 0:
                    load_phase_a(l)
                if seg == 0:
                    P.op('sp', lambda e: e.dma_start(out=nk_s[l, :, 0:124, :], in_=ck[l, :, 4:128, :]), dma='o_ckc')
                    P.op('sp', lambda e: e.dma_start(out=nv_s[l, :, 0:124, :], in_=cv[l, :, 4:128, :]), dma='o_cvc')
                n = len(tiles)
                prepped = set()
                drain(F_tile(l, tiles[0], seg, tiles[0] == TPS, 0))
                load_phase_a2(l)
                for tix, ti in enumerate(tiles):
                    gb = B_tile(l, ti, seg, ti == TPS, tix % 2)
                    if tix + 1 < n:
                        gf = F_tile(l, tiles[tix + 1], seg, tiles[tix + 1] == TPS, (tix + 1) % 2)
                        interleave(gb, gf)
                    elif ti == TPS:
                        def prep_gen():
                            for t_ in range(TPS):
                                rms_stats(xres[:, t_, :], 'x%d' % t_, rstd[:], 'rstd', sq=rr2[:, t_:t_ + 1])
                                for _ in make_hT(t_, 2 + l, hT_all[:, :, t_ * 128:(t_ + 1) * 128], 'hTall', pj_only=True):
                                    pass
                                prepped.add(t_)
                                yield
                        interleave(gb, prep_gen())
                    else:
                        drain(gb)
                P.barrier()
                ntok = n * 128
                for ti in tiles:
                    if ti in prepped:
                        continue
                    rms_stats(xres[:, ti, :], 'x%d' % ti, rstd[:], 'rstd', sq=rr2[:, ti:ti + 1])
                    drain(make_hT(ti, 2 + l, hT_all[:, :, ti * 128:(ti + 1) * 128], 'hTall'))
                groups = [(g0, min(GRP, ntok - g0)) for g0 in range(0, ntok, GRP)]

                def up(p_, g0, gn, ui):
                    slot = p_ % 2
                    wu, wd = ring[slot]
                    uT = uTs[ui]
                    for fc in range(4):
                        psi = nxt('ps')
                        def fup(e):
                            ins = None
                            for kc in range(8):
                                ins = e.matmul(PS[psi][:, 0:gn], lhsT=wu[:, kc, fc * 128:(fc + 1) * 128], rhs=hT_all[:, kc, g0:g0 + gn], start=(kc == 0), stop=(kc == 7))
                            return ins
                        P.op('pe', fup, reads=['ringu%d_%d' % (slot, k_) for k_ in range(8)] + ['hTall'], writes=['PS%d' % psi])
                        P.op('act', lambda e: e.activation(out=rl[:, 0:gn], in_=PS[psi][:, 0:gn], func=AF.Relu), reads=['PS%d' % psi], writes=['hTa'])
                        P.op('dve', lambda e: e.tensor_tensor(out=uT[:, fc, 0:gn], in0=rl[:, 0:gn], in1=rl[:, 0:gn], op=ALU.mult), reads=['hTa'], writes=['uT%d' % ui])

                def down(p_, g0, gn, ui):
                    slot = p_ % 2
                    wu, wd = ring[slot]
                    uT = uTs[ui]
                    for tt in range(gn // 128):
                        ti = (g0 // 128) + tt
                        for half in range(2):
                            pj = nxt('pj')
                            def fdn(e):
                                ins = None
                                for fc in range(4):
                                    ins = e.matmul(PJ[pj][:], lhsT=uT[:, fc, tt * 128:(tt + 1) * 128], rhs=wd[:, fc, half * 512:(half + 1) * 512], start=(fc == 0), stop=(fc == 3))
                                return ins
                            P.op('pe', fdn, reads=['uT%d' % ui] + ['ringd%d_%d' % (slot, k_) for k_ in range(4)], writes=['PJ%d' % pj])
                            P.op('dve', lambda e: e.scalar_tensor_tensor(
                                out=xres[:, ti, half * 512:(half + 1) * 512], in0=PJ[pj][:], scalar=rr2[:, ti:ti + 1],
                                in1=xres[:, ti, half * 512:(half + 1) * 512], op0=ALU.mult, op1=ALU.add),
                                reads=['PJ%d' % pj, 'rr2', 'x%d' % ti], writes=['x%d' % ti])

                def final_tile(ti):
                    rms_stats(xres[:, ti, :], 'x%d' % ti, rstd[:], 'rstd')
                    yi = nxt('y')
                    P.op('dve', lambda e: e.scalar_tensor_tensor(out=ysb[yi], in0=xres[:, ti, :], scalar=rstd[:], in1=fing, op0=ALU.mult, op1=ALU.mult),
                         reads=['x%d' % ti, 'rstd', 'fing'], writes=['ysb%d' % yi])
                    if ti == TPS:
                        P.op('sp', lambda e: e.dma_start(out=y_smp, in_=ysb[yi][0:64, :]), reads=['ysb%d' % yi], dma='o_y%d' % yi)
                    else:
                        P.op('sp', lambda e: e.dma_start(out=y_seq[(seg * TPS + ti) * 128:(seg * TPS + ti + 1) * 128, :], in_=ysb[yi]),
                             reads=['ysb%d' % yi], dma='o_y%d' % yi)
                        if seg + 1 < NSEG:
                            P.op('sp', lambda e: e.dma_start(out=xres[:, ti, :], in_=x_seq[((seg + 1) * TPS + ti) * 128:((seg + 1) * TPS + ti + 1) * 128, :]),
                                 writes=['x%d' % ti], dma='lx%d' % ti)

                def after_down(pv):
                    if l == 1 and pv[0] == NPASS - 1:
                        for tt in range(pv[2] // 128):
                            final_tile(pv[1] // 128 + tt)

                if l == 1:
                    P.op('sp', lambda e: e.dma_start(out=fing, in_=fin_g.partition_broadcast(128)), writes=['fing'], dma='c10')
                work = [(p_, g0, gn) for p_ in range(NPASS) for (g0, gn) in groups]
                prev = None
                for wi, (p_, g0, gn) in enumerate(work):
                    ui = nxt('u')
                    up(p_, g0, gn, ui)
                    if prev is not None:
                        down(*prev)
                        after_down(prev)
                    if g0 == 0 and p_ + 1 < NPASS:
                        load_pass(p_ + 1)
                    prev = (p_, g0, gn, ui)
                down(*prev)
                after_down(prev)
            P.barrier()
            if seg + 1 < NSEG:
                load_phase_a(0)
            P.barrier()
        P.emit()
    return nc


def _consts():
    c = {}
    c['c_identf'] = np.eye(128, dtype=np.float32)
    c['c_identb'] = np.eye(128, dtype=np.float32)
    half = 8
    inv_freq = np.power(np.float32(500000.0), -np.arange(half, dtype=np.float32) * np.float32(2.0 / 16)).astype(np.float32)
    pos = np.arange(SEQ, dtype=np.float32)
    ang = (pos[:, None] * inv_freq[None, :]).astype(np.float32)
    c['c_cosp'] = np.cos(ang).astype(np.float32).reshape(32, 128, 8).transpose(1, 0, 2).copy()
    c['c_sinp'] = np.sin(ang).astype(np.float32).reshape(32, 128, 8).transpose(1, 0, 2).copy()
    p = np.arange(128)
    tt = p // 16
    jj = p % 16
    valid = p < 64
    pos_s = (PAST + tt).astype(np.float32)
    ang_s = (pos_s[:, None] * inv_freq[None, :]).astype(np.float32)
    c['c_coss'] = np.where(valid[:, None], np.cos(ang_s), 1.0).astype(np.float32)
    c['c_sins'] = np.where(valid[:, None], np.sin(ang_s), 0.0).astype(np.float32)
    s = p[:, None]; t = p[None, :]
    ncur = np.where(s <= t, 0.0, -30000.0).astype(np.float32)
    nprev = np.where(s >= t, 0.0, -30000.0).astype(np.float32)
    c['c_negm'] = np.stack([np.tile(ncur, (1, 4)), np.tile(nprev, (1, 4))], axis=1).astype(np.float32)
    ch = p // 32
    same = ch[:, None] == ch[None, :]
    ref = ch * 32 + 15
    mrel_p = same * ((s <= t).astype(np.float32) - (s <= ref[None, :]).astype(np.float32))
    mkd_p = same * (s > t)
    ma_p = same * (s <= t)
    same_s = (jj[:, None] == jj[None, :]) & valid[:, None] & valid[None, :]
    mrel_s = same_s * (tt[:, None] <= tt[None, :])
    mkd_s = same_s * (tt[:, None] > tt[None, :])
    ma_s = same_s * (tt[:, None] <= tt[None, :])
    c['c_mrel'] = np.stack([mrel_p, mrel_s]).astype(np.float32)
    c['c_mkd'] = np.stack([mkd_p, mkd_s]).astype(np.float32)
    c['c_ma'] = np.stack([ma_p, ma_s]).astype(np.float32)
    indp = np.zeros((128, 8), np.float32)
    for cc in range(4):
        indp[cc * 32:(cc + 1) * 32, cc] = 1.0
        indp[cc * 32:cc * 32 + 16, 4 + cc] = 1.0
    c['c_indp'] = indp
    inds = np.zeros((128, 16), np.float32)
    inds[p[valid], jj[valid]] = 1.0
    c['c_inds'] = inds
    c['c_msel'] = inds.copy()
    c['c_mc0'] = ((p[:, None] >= tt[None, :]) & valid[None, :]).astype(np.float32)
    mns = (same_s & (tt[:, None] <= tt[None, :])).astype(np.float32)
    c['c_mns'] = mns
    selT = np.zeros((128, 16, 64), np.float32)
    for j in range(16):
        selT[:, j, :] = ((jj == j) & valid)[None, 0:64]
    c['c_selT'] = selT
    return c


_CACHE = {}
PCORES = [0, 1, 4, 5]


def kernel(x_prompt, x_sample, cache_k, cache_v, state_hgrn, attn_norm, w_in, att_sinks,
           hgrn_lower_bounds, hgrn_out_norm, w_o, mlp_norm, w_up, w_down, final_norm):
    f = lambda a: np.ascontiguousarray(np.asarray(a, dtype=np.float32))
    x_prompt, x_sample, cache_k, cache_v, state_hgrn = map(f, (x_prompt, x_sample, cache_k, cache_v, state_hgrn))
    if 'nc' not in _CACHE:
        _CACHE['nc'] = build_program()
        _CACHE['consts'] = _consts()
    nc = _CACHE['nc']
    consts = _CACHE['consts']
    gains = np.concatenate([f(attn_norm).reshape(2, 8, 128), f(mlp_norm).reshape(2, 8, 128), f(final_norm).reshape(1, 8, 128)], axis=0).reshape(40, 128)
    shared = dict(gains=np.ascontiguousarray(gains), fin_g=f(final_norm).reshape(1, D), w_in=f(w_in), sinks=f(att_sinks),
                  lbraw=f(hgrn_lower_bounds), onorm=f(hgrn_out_norm), w_o=f(w_o), w_up=f(w_up), w_dn=f(w_down))
    shared.update(consts)
    zero_seq = np.zeros((SEQ, D), np.float32)
    in_maps = []
    for c in range(8):
        sl = slice(c * NSMP, (c + 1) * NSMP)
        m = dict(shared)
        m['x_seq'] = x_prompt[PCORES.index(c)] if c in PCORES else zero_seq
        m['x_smp'] = np.ascontiguousarray(x_sample[sl].transpose(1, 0, 2).reshape(64, D))
        m['ck'] = np.ascontiguousarray(cache_k[:, sl].reshape(2, NSMP, 128, 128))
        m['cv'] = np.ascontiguousarray(cache_v[:, sl].reshape(2, NSMP, 128, 128))
        m['st_in'] = np.ascontiguousarray(state_hgrn[:, sl])
        in_maps.append(m)
    res = run_bass_kernel_spmd(nc, in_maps, core_ids=list(range(8)))
    R = res.results
    y_prompt = np.stack([R[b]['y_seq'] for b in PCORES]).astype(np.float32)
    y_sample = np.concatenate([R[c]['y_smp'].reshape(4, NSMP, D).transpose(1, 0, 2) for c in range(8)], axis=0).astype(np.float32)
    nk_p = np.stack([R[b]['nk_p'] for b in PCORES], axis=1).reshape(2, 4, 128, 2, 64)
    nv_p = np.stack([R[b]['nv_p'] for b in PCORES], axis=1).reshape(2, 4, 128, 2, 64)
    ns_p = np.stack([R[b]['ns_p'] for b in PCORES], axis=1)
    nk_s = np.concatenate([R[c]['nk_s'] for c in range(8)], axis=1).reshape(2, 128, 128, 2, 64)
    nv_s = np.concatenate([R[c]['nv_s'] for c in range(8)], axis=1).reshape(2, 128, 128, 2, 64)
    ns_s = np.concatenate([R[c]['ns_s'] for c in range(8)], axis=1)
    return (y_prompt, y_sample, np.ascontiguousarray(nk_p), np.ascontiguousarray(nv_p), np.ascontiguousarray(ns_p),
            np.ascontiguousarray(nk_s), np.ascontiguousarray(nv_s), np.ascontiguousarray(ns_s))
```

```python
import numpy as np
from contextlib import ExitStack
import concourse.bass as bass
import concourse.mybir as mybir
from concourse.bass_utils import run_bass_kernel_spmd

F32 = mybir.dt.float32
BF16 = mybir.dt.bfloat16
AF = mybir.ActivationFunctionType
ALU = mybir.AluOpType
AX = mybir.AxisListType

D = 1024
SEQ = 4096
NSEG = 2
TPS = 16
NSMP = 16
PAST = 16384
EPS = 1e-6
INW = 2816
DFF = 4096
NPASS = 8
GRP = 256


class Prog:
    ENGS = ('pe', 'act', 'dve', 'pool', 'sp')

    def __init__(self, nc, es):
        self.nc = nc
        self.es = es
        self.eng = {'pe': nc.tensor, 'act': nc.scalar, 'dve': nc.vector, 'pool': nc.gpsimd, 'sp': nc.sync}
        self.cnt = {e: 0 for e in self.ENGS}
        self.semh = {k: es.enter_context(nc.semaphore('s_' + k)) for k in self.ENGS}
        self.dma_cnt = {}
        self.last_write = {}
        self.reads_since = {}
        self.seen = {e: {} for e in self.ENGS}
        self.know = {}
        self.nins = 0
        self.nwait = 0

    def _wait(self, e, tok):
        key, val = tok
        if self.seen[e].get(key, 0) >= val:
            return
        se = self.seen[e]
        se[key] = val
        self.eng[e].wait_ge(self.semh[key], val)
        self.nwait += 1
        kn = self.know.get(tok)
        if kn:
            for k2, v2 in kn.items():
                if se.get(k2, 0) < v2:
                    se[k2] = v2

    def op(self, e, fn, reads=(), writes=(), dma=None):
        deps = []
        for r in reads:
            if r in self.last_write:
                deps.append(self.last_write[r])
        for w in writes:
            rs = self.reads_since.get(w, ())
            if rs:
                deps.extend(rs)
            elif w in self.last_write:
                deps.append(self.last_write[w])
        mx = {}
        for key, val in deps:
            if key == 'pe' and e == 'pe' and dma is None:
                continue
            if val > mx.get(key, 0):
                mx[key] = val
        for key, val in mx.items():
            self._wait(e, (key, val))
        self.nins += 1
        if dma is None:
            self.cnt[e] += 1
            tok = (e, self.cnt[e])
            fn(self.eng[e]).then_inc(self.semh[e], 1)
        else:
            k = 'dma:' + dma
            if k not in self.semh:
                self.semh[k] = self.es.enter_context(self.nc.semaphore('d%d' % len(self.semh)))
            self.dma_cnt[k] = self.dma_cnt.get(k, 0) + 16
            tok = (k, self.dma_cnt[k])
            fn(self.eng[e]).then_inc(self.semh[k], 16)
        self.know[tok] = dict(self.seen[e])
        for r in reads:
            self.reads_since.setdefault(r, []).append(tok)
        for w in writes:
            self.last_write[w] = tok
            self.reads_since[w] = []
        return tok

    def barrier(self, skip=()):
        toks = [(k, v) for k, v in self.dma_cnt.items() if not any(k.startswith('dma:' + s) for s in skip)] + \
               [(k, self.cnt[k]) for k in self.ENGS if self.cnt[k]]
        for e in self.ENGS:
            for t in toks:
                self._wait(e, t)

    def emit(self):
        self.barrier()


def bc(ap, axis, shape):
    return ap.unsqueeze(axis).broadcast_to(shape)


def build_program():
    nc = bass.Bass("TRN2", target_bir_lowering=False)

    def din(name, shape, dt=F32):
        return nc.dram_tensor(name, list(shape), dt, kind="ExternalInput").ap()

    def dout(name, shape):
        return nc.dram_tensor(name, list(shape), F32, kind="ExternalOutput").ap()

    x_seq = din("x_seq", [SEQ, D])
    x_smp = din("x_smp", [64, D])
    ck = din("ck", [2, NSMP, 128, 128])
    cv = din("cv", [2, NSMP, 128, 128])
    st_in = din("st_in", [2, NSMP, 4, 128, 128])
    gains = din("gains", [40, 128])
    fin_g = din("fin_g", [1, D])
    w_in = din("w_in", [2, D, INW])
    sinks = din("sinks", [2, 8])
    lbraw = din("lbraw", [2, 512])
    onorm = din("onorm", [2, 128])
    w_o = din("w_o", [2, D, D])
    w_up = din("w_up", [2, D, DFF])
    w_dn = din("w_dn", [2, DFF, D])
    c_identf = din("c_identf", [128, 128])
    c_identb = din("c_identb", [128, 128])
    c_cosp = din("c_cosp", [128, 32, 8])
    c_sinp = din("c_sinp", [128, 32, 8])
    c_coss = din("c_coss", [128, 8])
    c_sins = din("c_sins", [128, 8])
    c_negm = din("c_negm", [128, 2, 512])
    c_mrel = din("c_mrel", [2, 128, 128])
    c_mkd = din("c_mkd", [2, 128, 128])
    c_ma = din("c_ma", [2, 128, 128])
    c_indp = din("c_indp", [128, 8])
    c_inds = din("c_inds", [128, 16])
    c_msel = din("c_msel", [128, 16])
    c_mc0 = din("c_mc0", [128, 128])
    c_mns = din("c_mns", [128, 128])
    c_selT = din("c_selT", [128, 16, 64])

    y_seq = dout("y_seq", [SEQ, D])
    y_smp = dout("y_smp", [64, D])
    nk_p = dout("nk_p", [2, 128, 128])
    nv_p = dout("nv_p", [2, 128, 128])
    ns_p = dout("ns_p", [2, 4, 128, 128])
    nk_s = dout("nk_s", [2, NSMP, 128, 128])
    nv_s = dout("nv_s", [2, NSMP, 128, 128])
    ns_s = dout("ns_s", [2, NSMP, 4, 128, 128])

    with ExitStack() as es:
        def sb(name, shape, dt=F32):
            return es.enter_context(nc.sbuf_tensor(name, list(shape), dt))

        def ps(name, shape, dt=F32):
            return es.enter_context(nc.psum_tensor(name, list(shape), dt))

        NT = TPS + 1
        xres = sb("xres", [128, NT, D])
        hT_elems = 8 * NT * 128
        ring_slot = 8 * 512 + 4 * D
        uT_elems = 4 * GRP
        rg_b = max(8 * INW + 8 * D + ring_slot, hT_elems + ring_slot + 2 * uT_elems)
        rgn32 = sb("rgn", [128, rg_b // 2])
        rgnb = rgn32[:].bitcast(BF16)
        w_in_sb = rgnb[:, 0:8 * INW].rearrange("p (k n) -> p k n", k=8)
        w_o_sb = rgnb[:, 8 * INW:8 * INW + 8 * D].rearrange("p (k n) -> p k n", k=8)
        hT_all = rgnb[:, 0:hT_elems].rearrange("p (k t) -> p k t", k=8)
        ring = []
        for o0 in (8 * INW + 8 * D, hT_elems):
            wu = rgnb[:, o0:o0 + 4096].rearrange("p (k n) -> p k n", k=8)
            wd = rgnb[:, o0 + 4096:o0 + 8192].rearrange("p (k n) -> p k n", k=4)
            ring.append((wu, wd))
        o0 = hT_elems + ring_slot
        assert o0 + 2 * uT_elems <= 8 * INW + 8 * D
        uTs = []
        for s in range(2):
            uTs.append(rgnb[:, o0:o0 + uT_elems].rearrange("p (k n) -> p k n", k=4))
            o0 += uT_elems

        identf = sb("identf", [128, 128]); identb = sb("identb", [128, 128], BF16)
        cosp = sb("cosp", [128, TPS, 8]); sinp = sb("sinp", [128, TPS, 8])
        coss = sb("coss", [128, 8]); sins = sb("sins", [128, 8])
        negm = sb("negm", [128, 2, 512], BF16)
        mrel = sb("mrel", [128, 2, 128]); mkd = sb("mkd", [128, 2, 128]); ma = sb("ma", [128, 2, 128], BF16)
        indp = sb("indp", [128, 8]); inds = sb("inds", [128, 16]); msel = sb("msel", [128, 16])
        mc0 = sb("mc0", [128, 128], BF16); mns = sb("mns", [128, 128], BF16)
        selT = sb("selT", [128, 16, 64], BF16)
        gT = sb("gT", [128, 40])
        esink = sb("esink", [128, 2, 8])
        lb1 = sb("lb1", [128, 512])
        onb = sb("onb", [128, 2, 128])
        EPS_AP = sb("eps_ap", [128, 1])

        ss = sb("ss", [128, 1]); rstd = sb("rstd", [128, 1]); nrstd = sb("nrstd", [128, 1])
        rr2 = sb("rr2", [128, NT])
        dmy = sb("dmy", [128, 1])
        hT = sb("hT", [128, 8, 128], BF16)
        junk = hT[:].rearrange("p k t -> p (k t)")
        rl = junk[:, 0:512]
        ft = sb("ft", [128, 4096])
        fing = ft[:, 0:D]
        ysb = [ft[:, D:2 * D], ft[:, 2 * D:3 * D]]
        qa = ft[:, 0:512].rearrange("p (h d) -> p h d", h=8)
        ka = sb("ka", [128, 2, 64]); va = sb("va", [128, 128])
        qab = sb("qab", [128, 8, 64], BF16); kab = sb("kab", [128, 2, 64], BF16)
        qT = [sb("qT%d" % i, [64, 8, 128], BF16) for i in range(2)]
        kT = [[sb("kT%d_%d" % (l, i), [64, 2, 128], BF16) for i in range(2)] for l in range(2)]
        vaug = [[sb("vaug%d_%d" % (l, i), [128, 2, 65], BF16) for i in range(2)] for l in range(2)]
        pT = [sb("pT%d" % i, [128, 4, 128], BF16) for i in range(2)]
        den = sb("den", [128, 4]); rden = sb("rden", [128, 4])
        mix = sb("mix", [128, D], BF16)
        e1 = ft[:, 512:1024]
        qh = ft[:, 1024:1536]; fg = ft[:, 1536:2048]; gl = ft[:, 2048:2560]; kk = ft[:, 2560:3072]
        cl = ft[:, 3072:3584]; eq = ft[:, 3584:4096]; ek = e1
        rt = [cl[:, 0:64].rearrange("p (h d) -> p h d", h=8), cl[:, 64:128].rearrange("p (h d) -> p h d", h=8),
              eq[:, 0:64].rearrange("p (h d) -> p h d", h=8), eq[:, 64:128].rearrange("p (h d) -> p h d", h=8)]
        RTK = ["cl", "cl", "eq", "eq"]
        vh = [sb("vh%d" % i, [128, 512], BF16) for i in range(2)]
        gate = [sb("gate%d" % i, [128, 512], BF16) for i in range(2)]
        qe = [sb("qe%d" % i, [128, 4, 128], BF16) for i in range(2)]
        ke = [sb("ke%d" % i, [128, 4, 128], BF16) for i in range(2)]
        kd = [sb("kd%d" % i, [128, 512], BF16) for i in range(2)]
        bg = [sb("bg%d" % i, [128, 4, 16]) for i in range(2)]
        tq = sb("tq", [128, 8, 128], BF16)
        qeT = tq[:, 0:4, :]; keT = tq[:, 4:8, :]; mixT = tq[:]
        atm = sb("atm", [128, 4, 128], BF16)
        S = [sb("S%d" % l, [128, 4, 128]) for l in range(2)]
        Sp = [sb("Sp%d" % c, [128, 4, 128], BF16) for c in range(2)]
        sso = sb("sso", [128, 4]); rso = sb("rso", [128, 4])
        kcT2 = [Sp[1][0:64, 0:2, :], Sp[1][0:64, 2:4, :]]
        kcsA = ft[:, 0:1024].bitcast(BF16).rearrange("p (j d) -> p j d", j=NSMP)
        vcsA = ft[:, 1024:2064].bitcast(BF16).rearrange("p (j g d) -> p j g d", j=NSMP, g=2)
        sstA = [ft[:, 2064:2576].rearrange("p (h d) -> p h d", h=4), ft[:, 2576:3088].rearrange("p (h d) -> p h d", h=4)]
        stmp = ft[:, 3088:3600].rearrange("p (h d) -> p h d", h=4)
        sbf = ft[:, 3600:3856].bitcast(BF16).rearrange("p (h d) -> p h d", h=4)

        PT = ps("PT", [128, 512])
        PB = ps("PB", [128, 1024], BF16)
        PJ = [ps("PJ0", [128, 512]), ps("PJ1", [128, 512])]
        PS = [ps("PS0", [128, 512]), ps("PS1", [128, 512])]
        PV = ps("PV", [128, 512])
        PO = ps("PO", [128, 512])

        P = Prog(nc, es)
        cnt = {'pj': 0, 'ps': 0, 'y': 0, 'pt': 0, 'u': 0, 'pb': 0}

        def nxt(k, n=2):
            v = cnt[k] % n
            cnt[k] += 1
            return v


        WGRP = [(0, 512), (512, 768), (768, 1280), (1280, 1792), (1792, 2304), (2304, 2816)]

        def load_pass_l(l, p_):
            slot = p_ % 2
            wu, wd = ring[slot]
            for kc in range(8):
                P.op('pool', lambda e: e.dma_start(out=wu[:, kc, :], in_=w_up[l, kc * 128:(kc + 1) * 128, p_ * 512:(p_ + 1) * 512]), writes=['ringu%d_%d' % (slot, kc)], dma='lru%d' % slot)
            for fc in range(4):
                P.op('pool', lambda e: e.dma_start(out=wd[:, fc, :], in_=w_dn[l, p_ * 512 + fc * 128: p_ * 512 + (fc + 1) * 128, :]), writes=['ringd%d_%d' % (slot, fc)], dma='lrd%d' % slot)

        def load_phase_a(l, only=None):
            for gi, (c0, c1) in enumerate(WGRP):
                if only is not None and gi not in only:
                    continue
                P.op('pool', lambda e: e.dma_start(out=w_in_sb[:, :, c0:c1], in_=w_in[l, :, c0:c1].rearrange("(k p) n -> p k n", p=128)), writes=['w_in_g%d' % gi], dma='lw_in%d' % gi)

        def load_phase_a2(l):
            for kc in range(8):
                P.op('pool', lambda e: e.dma_start(out=w_o_sb[:, kc, :], in_=w_o[l, kc * 128:(kc + 1) * 128, :]), writes=['w_o%d' % kc], dma='lw_o')
            load_pass_l(l, 0)

        P.op('sp', lambda e: e.dma_start(out=xres[:, 0, :], in_=x_seq[0:128, :]), writes=['x0'], dma='lx0')
        load_phase_a(0, only=(0,))

        cl_list = [(identf, c_identf), (coss, c_coss), (sins, c_sins),
                   (indp, c_indp), (inds, c_inds), (msel, c_msel)]
        for i, (t, d) in enumerate(cl_list):
            P.op('sp', lambda e: e.dma_start(out=t[:], in_=d), writes=[t.name], dma='c%d' % i)
        P.op('sp', lambda e: e.dma_start(out=mrel[:], in_=c_mrel.rearrange("a p n -> p a n")), writes=['mrel'], dma='c20')
        P.op('sp', lambda e: e.dma_start(out=mkd[:], in_=c_mkd.rearrange("a p n -> p a n")), writes=['mkd'], dma='c21')
        cb_list = [(identb, c_identb), (negm, c_negm), (mc0, c_mc0), (mns, c_mns), (selT, c_selT)]
        for i, (t, d) in enumerate(cb_list):
            P.op('pool', lambda e: e.dma_start(out=t[:], in_=d), writes=[t.name], dma='cb%d' % i)
        P.op('pool', lambda e: e.dma_start(out=ma[:], in_=c_ma.rearrange("a p n -> p a n")), writes=['ma'], dma='cb9')
        load_phase_a(0, only=(1, 2, 3, 4, 5))
        P.op('sp', lambda e: e.dma_start(out=esink[:].rearrange("p l h -> p (l h)"), in_=sinks.rearrange("l h -> (l h)").unsqueeze(0).partition_broadcast(128)), writes=['esink'], dma='c11')
        gstage = ft[0:40, 1024:1152]
        lbr = ft[:, 0:1024].rearrange("p (l c) -> p l c", l=2)
        P.op('sp', lambda e: e.dma_start(out=ft[:, 0:1024], in_=lbraw.rearrange("l c -> (l c)").unsqueeze(0).partition_broadcast(128)), writes=['lbr'], dma='c12')
        P.op('sp', lambda e: e.dma_start(out=onb[:].rearrange("p l h -> p (l h)"), in_=onorm.rearrange("l c -> (l c)").unsqueeze(0).partition_broadcast(128)), writes=['onb'], dma='c13')
        P.op('sp', lambda e: e.dma_start(out=gstage, in_=gains), writes=['gstage'], dma='c19')
        P.op('act', lambda e: e.activation(out=esink[:], in_=esink[:], func=AF.Exp), reads=['esink'], writes=['esink'])
        P.op('dve', lambda e: e.tensor_tensor(out=lb1[:], in0=lbr[:, 0, :], in1=lbr[:, 1, :], op=ALU.subtract), reads=['lbr'], writes=['lb1'])
        P.op('act', lambda e: e.activation(out=lb1[:], in_=lb1[:], func=AF.Exp), reads=['lb1'], writes=['lb1'])
        P.op('dve', lambda e: e.tensor_scalar_add(out=lb1[:], in0=lb1[:], scalar1=1.0), reads=['lb1'], writes=['lb1'])
        P.op('dve', lambda e: e.reciprocal(out=lb1[:], in_=lb1[:]), reads=['lb1'], writes=['lb1'])
        P.op('pe', lambda e: e.transpose(PT[:, 0:40], gstage, identf[0:40, 0:40]), reads=['gstage', 'identf'], writes=['PT'])
        P.op('dve', lambda e: e.tensor_copy(out=gT[:], in_=PT[:, 0:40]), reads=['PT'], writes=['gT'])
        P.op('pool', lambda e: e.memset(dmy[:], 0.0), writes=['dmy'])
        P.op('pool', lambda e: e.memset(EPS_AP[:], EPS), writes=['eps_ap'])
        for l in range(2):
            for i in range(2):
                P.op('pool', lambda e: e.memset(vaug[l][i][:], 1.0), writes=['vaug%d_%d' % (l, i)])
        P.op('pool', lambda e: e.memset(xres[:, TPS, :], 0.0), writes=['x%d' % TPS])
        for l in range(2):
            P.op('pool', lambda e: e.memset(S[l][:], 0.0), writes=['S%d' % l])
        P.barrier(skip=('lw_in', 'lw_o', 'lru', 'lrd', 'lx'))

        def rms_stats(xt_ap, xkey, out_r, out_key, also_neg=None, sq=None):
            def f(e):
                e.activation(out=junk, in_=xt_ap, func=AF.Square, accum_out=ss[:])
                return e.activation(out=dmy[:], in_=dmy[:], func=AF.Copy)
            P.op('act', f, reads=[xkey], writes=['hTa', 'hTb', 'ss', 'dmy'])
            P.op('act', lambda e: e.activation(out=ss[:], in_=ss[:], func=AF.Ln, scale=1.0 / D, bias=EPS_AP[:]), reads=['ss', 'eps_ap'], writes=['ss'])
            P.op('act', lambda e: e.activation(out=out_r, in_=ss[:], func=AF.Exp, scale=-0.5), reads=['ss'], writes=[out_key])
            if also_neg is not None:
                P.op('dve', lambda e: e.tensor_scalar_mul(out=also_neg, in0=out_r, scalar1=-1.0), reads=[out_key], writes=['nrstd'])
            if sq is not None:
                P.op('act', lambda e: e.activation(out=sq, in_=ss[:], func=AF.Exp, scale=-1.0), reads=['ss'], writes=['rr2'])

        def make_hT(ti, gcol, dst, dst_key, pj_only=False):
            for half in range(2):
                if half == 0 and not pj_only:
                    bank, bkey = PT, 'PT'
                else:
                    pj_ = nxt('pj')
                    bank, bkey = PJ[pj_], 'PJ%d' % pj_
                def f(e):
                    ins = None
                    for j in range(4):
                        kc = half * 4 + j
                        ins = e.transpose(bank[:, j * 128:(j + 1) * 128], xres[:, ti, kc * 128:(kc + 1) * 128], identf[:])
                    return ins
                P.op('pe', f, reads=['x%d' % ti, 'identf'], writes=[bkey])
                P.op('dve', lambda e: e.tensor_tensor(
                    out=dst[:, half * 4:half * 4 + 4, :], in0=bank[:].rearrange("p (k t) -> p k t", k=4),
                    in1=bc(gT[:, gcol * 8 + half * 4:gcol * 8 + half * 4 + 4], 2, [128, 4, 128]), op=ALU.mult),
                    reads=[bkey, 'gT'], writes=[(dst_key + 'ab'[half]) if dst_key == 'hT' else dst_key])
                yield

        WO_KEYS = ['w_o%d' % k_ for k_ in range(8)]

        def proj_group(c0, c1):
            pj = nxt('pj')
            for hf in range(2):
                def f(e):
                    ins = None
                    for kc in range(hf * 4, hf * 4 + 4):
                        ins = e.matmul(PJ[pj][:, 0:c1 - c0], lhsT=hT[:, kc, :], rhs=w_in_sb[:, kc, c0:c1], start=(kc == 0), stop=(kc == 7))
                    return ins
                P.op('pe', f, reads=['hT' + 'ab'[hf], 'w_in_g%d' % WGRP.index((c0, c1))], writes=['PJ%d' % pj])
            return pj

        def silu_from_psum(pj, dst, dst_key):
            k = 'PJ%d' % pj
            P.op('act', lambda e: e.activation(out=dst, in_=PJ[pj][:], func=AF.Copy, scale=rstd[:]), reads=[k, 'rstd'], writes=[dst_key])
            P.op('act', lambda e: e.activation(out=e1[:], in_=dst, func=AF.Exp, scale=-1.0), reads=[dst_key], writes=['e1'])
            P.op('act', lambda e: e.activation(out=e1[:], in_=e1[:], func=AF.Ln, bias=1.0), reads=['e1'], writes=['e1'])
            P.op('act', lambda e: e.activation(out=e1[:], in_=e1[:], func=AF.Exp, scale=-1.0), reads=['e1'], writes=['e1'])
            P.op('pool', lambda e: e.tensor_tensor(out=dst, in0=dst, in1=e1[:], op=ALU.mult), reads=[dst_key, 'e1'], writes=[dst_key])

        def rotary(src, nh, cos_ap, sin_ap, dstb, skey, dkey):
            x1 = src[:, :, 0:8]; x2 = src[:, :, 8:16]
            cb = bc(cos_ap, 1, [128, nh, 8]); sn = bc(sin_ap, 1, [128, nh, 8])
            r0, r1, r2, r3 = [t[:, 0:nh, :] for t in rt]
            ck_ = ['cosp', 'sinp']
            P.op('pool', lambda e: e.tensor_tensor(out=r0, in0=x1, in1=cb, op=ALU.mult), reads=[skey] + ck_, writes=[RTK[0]])
            P.op('pool', lambda e: e.tensor_tensor(out=r1, in0=x2, in1=sn, op=ALU.mult), reads=[skey] + ck_, writes=[RTK[1]])
            P.op('pool', lambda e: e.tensor_tensor(out=r2, in0=x2, in1=cb, op=ALU.mult), reads=[skey] + ck_, writes=[RTK[2]])
            P.op('pool', lambda e: e.tensor_tensor(out=r3, in0=x1, in1=sn, op=ALU.mult), reads=[skey] + ck_, writes=[RTK[3]])
            P.op('pool', lambda e: e.tensor_tensor(out=x1, in0=r0, in1=r1, op=ALU.subtract), reads=['cl', skey], writes=[skey])
            P.op('pool', lambda e: e.tensor_tensor(out=x2, in0=r2, in1=r3, op=ALU.add), reads=['eq', skey], writes=[skey])
            P.op('pool', lambda e: e.tensor_copy(out=dstb, in_=src), reads=[skey], writes=[dkey])

        def transposes_bf(srcs, src_keys, dst, dst_key, rows, evac='act'):
            n = len(srcs)
            dkeys = dst_key if isinstance(dst_key, list) else [dst_key]
            def f(e):
                ins = None
                for i, s_ in enumerate(srcs):
                    ins = e.transpose(PB[0:rows, i * 128:i * 128 + 128], s_, identb[:])
                return ins
            P.op('pe', f, reads=list(src_keys) + ['identb'], writes=['PB'])
            src_v = PB[0:rows, 0:n * 128].rearrange("p (k t) -> p k t", k=n)
            if evac == 'act':
                P.op('act', lambda e: e.activation(out=dst, in_=src_v, func=AF.Copy), reads=['PB'], writes=dkeys)
            else:
                P.op('dve', lambda e: e.tensor_copy(out=dst, in_=src_v), reads=['PB'], writes=dkeys)

        def F_tile(l, ti, seg, is_smp, par):
            xk = 'x%d' % ti
            rms_stats(xres[:, ti, :], xk, rstd[:], 'rstd', also_neg=nrstd[:])
            for _ in make_hT(ti, l, hT, 'hT'):
                yield
            kTc, vac = kT[l][par], vaug[l][par]
            kTck, vack = 'kT%d_%d' % (l, par), 'vaug%d_%d' % (l, par)
            if is_smp:
                cos_ap, sin_ap = coss[:], sins[:]
            else:
                cos_ap, sin_ap = cosp[:, ti, :], sinp[:, ti, :]
            pj = proj_group(0, 512)
            P.op('act', lambda e: e.activation(out=qa[:].rearrange("p h d -> p (h d)"), in_=PJ[pj][:], func=AF.Copy, scale=rstd[:]),
                 reads=['PJ%d' % pj, 'rstd'], writes=['qa'])
            yield
            rotary(qa[:], 8, cos_ap, sin_ap, qab[:], 'qa', 'qab')
            pj = proj_group(512, 768)
            P.op('act', lambda e: e.activation(out=ka[:].rearrange("p h d -> p (h d)"), in_=PJ[pj][:, 0:128], func=AF.Copy, scale=rstd[:]),
                 reads=['PJ%d' % pj, 'rstd'], writes=['ka'])
            P.op('act', lambda e: e.activation(out=va[:], in_=PJ[pj][:, 128:256], func=AF.Copy, scale=rstd[:]),
                 reads=['PJ%d' % pj, 'rstd'], writes=['va'])
            yield
            rotary(ka[:], 2, cos_ap, sin_ap, kab[:], 'ka', 'kab')
            P.op('pool', lambda e: e.tensor_copy(out=vac[:, :, 0:64], in_=va[:].rearrange("p (g d) -> p g d", g=2)), reads=['va'], writes=[vack])
            if is_smp:
                for t in range(4):
                    P.op('sp', lambda e: e.dma_start(out=nk_s[l, :, 124 + t, :], in_=ka[t * 16:(t + 1) * 16].rearrange("p h d -> p (h d)")), reads=['ka'], dma='o_nk%d' % t)
                    P.op('sp', lambda e: e.dma_start(out=nv_s[l, :, 124 + t, :], in_=va[t * 16:(t + 1) * 16, :]), reads=['va'], dma='o_nv%d' % t)
            elif seg == NSEG - 1 and ti == TPS - 1:
                P.op('sp', lambda e: e.dma_start(out=nk_p[l], in_=ka[:].rearrange("p h d -> p (h d)")), reads=['ka'], dma='o_nkp')
                P.op('sp', lambda e: e.dma_start(out=nv_p[l], in_=va[:]), reads=['va'], dma='o_nvp')
            pj = proj_group(768, 1280)
            silu_from_psum(pj, qh[:], 'qh')
            yield
            pj = proj_group(1280, 1792)
            k = 'PJ%d' % pj
            P.op('act', lambda e: e.activation(out=e1[:], in_=PJ[pj][:], func=AF.Exp, scale=nrstd[:]), reads=[k, 'nrstd'], writes=['e1'])
            P.op('act', lambda e: e.activation(out=e1[:], in_=e1[:], func=AF.Ln, bias=1.0), reads=['e1'], writes=['e1'])
            P.op('act', lambda e: e.activation(out=fg[:], in_=e1[:], func=AF.Exp, scale=-1.0), reads=['e1'], writes=['fg'])
            yield
            transposes_bf([qab[:, h, :] for h in range(8)], ['qab'], qT[par][:], 'qT%d' % par, 64)
            yield
            P.op('pool', lambda e: e.tensor_scalar(out=kk[:], in0=fg[:], scalar1=-1.0, scalar2=1.0, op0=ALU.mult, op1=ALU.add), reads=['fg'], writes=['kk'])
            if l == 1:
                P.op('dve', lambda e: e.tensor_tensor(out=fg[:], in0=kk[:], in1=lb1[:], op=ALU.mult), reads=['kk', 'lb1'], writes=['fg'])
                P.op('pool', lambda e: e.tensor_tensor(out=kk[:], in0=kk[:], in1=fg[:], op=ALU.subtract), reads=['kk', 'fg'], writes=['kk'])
                P.op('pool', lambda e: e.tensor_scalar(out=fg[:], in0=kk[:], scalar1=-1.0, scalar2=1.0, op0=ALU.mult, op1=ALU.add), reads=['kk'], writes=['fg'])
            P.op('act', lambda e: e.activation(out=gl[:], in_=fg[:], func=AF.Ln), reads=['fg'], writes=['gl'])
            pj = proj_group(1792, 2304)
            P.op('act', lambda e: e.activation(out=vh[par][:], in_=PJ[pj][:], func=AF.Copy, scale=rstd[:]), reads=['PJ%d' % pj, 'rstd'], writes=['vh%d' % par])
            yield
            pj = proj_group(2304, 2816)
            silu_from_psum(pj, fg[:], 'fg')
            P.op('pool', lambda e: e.tensor_tensor(out=gate[par][:].rearrange("p (h d) -> p h d", h=4), in0=fg[:].rearrange("p (h d) -> p h d", h=4),
                                                   in1=bc(onb[:, l, :], 1, [128, 4, 128]), op=ALU.mult), reads=['fg', 'onb'], writes=['gate%d' % par])
            yield
            mi = 1 if is_smp else 0
            p1 = nxt('pj'); p2 = nxt('pj')
            P.op('pe', lambda e: e.matmul(PJ[p1][:], lhsT=mrel[:, mi, :], rhs=gl[:], start=True, stop=True), reads=['mrel', 'gl'], writes=['PJ%d' % p1])
            P.op('pe', lambda e: e.matmul(PJ[p2][:], lhsT=mkd[:, mi, :], rhs=gl[:], start=True, stop=True), reads=['mkd', 'gl'], writes=['PJ%d' % p2])
            P.op('dve', lambda e: e.tensor_scalar(out=cl[:], in0=PJ[p1][:], scalar1=-40.0, scalar2=40.0, op0=ALU.max, op1=ALU.min), reads=['PJ%d' % p1], writes=['cl'])
            P.op('act', lambda e: e.activation(out=eq[:], in_=cl[:], func=AF.Exp), reads=['cl'], writes=['eq'])
            P.op('act', lambda e: e.activation(out=ek[:], in_=cl[:], func=AF.Exp, scale=-1.0), reads=['cl'], writes=['e1'])
            P.op('act', lambda e: e.activation(out=cl[:], in_=PJ[p2][:], func=AF.Exp), reads=['PJ%d' % p2, 'cl'], writes=['cl'])
            P.op('dve', lambda e: e.tensor_tensor(out=qe[par][:].rearrange("p h d -> p (h d)"), in0=qh[:], in1=eq[:], op=ALU.mult), reads=['qh', 'eq'], writes=['qe%d' % par])
            P.op('dve', lambda e: e.tensor_tensor(out=ke[par][:].rearrange("p h d -> p (h d)"), in0=kk[:], in1=ek[:], op=ALU.mult), reads=['kk', 'e1'], writes=['ke%d' % par])
            P.op('dve', lambda e: e.tensor_tensor(out=kd[par][:], in0=kk[:], in1=cl[:], op=ALU.mult), reads=['kk', 'cl'], writes=['kd%d' % par])
            yield
            nind = 16 if is_smp else 8
            ind_ap = inds[:] if is_smp else indp[:]
            p3 = nxt('pj')
            def fbg(e):
                ins = None
                for h in range(4):
                    ins = e.matmul(PJ[p3][:, h * 16:h * 16 + nind], lhsT=gl[:, h * 128:(h + 1) * 128], rhs=ind_ap, start=True, stop=True)
                return ins
            P.op('pe', fbg, reads=['gl', 'inds', 'indp'], writes=['PJ%d' % p3])
            P.op('act', lambda e: e.activation(out=bg[par][:, :, 0:nind], in_=PJ[p3][:, 0:64].rearrange("p (h c) -> p h c", h=4)[:, :, 0:nind], func=AF.Exp),
                 reads=['PJ%d' % p3], writes=['bg%d' % par])
            yield

            transposes_bf([kab[:, g, :] for g in range(2)], ['kab'], kTc[:], kTck, 64)
            yield

        def B_tile(l, ti, seg, is_smp, par):
            xk = 'x%d' % ti
            prv = 1 - par
            kTc, vac = kT[l][par], vaug[l][par]
            kTck, vack = 'kT%d_%d' % (l, par), 'vaug%d_%d' % (l, par)
            qTc, qTk = qT[par], 'qT%d' % par
            mi = 1 if is_smp else 0
            if is_smp:
                blocks = [('c', j) for j in range(NSMP)] + [('n', 0)]
            else:
                blocks = ([('p', 0)] if not (seg == 0 and ti == 0) else []) + [('u', 0)]
            nb = len(blocks)
            pv_bank = {0: (PV, 'PV'), 1: ((PT, 'PT') if is_smp else (PV, 'PV'))}

            def finish_g(g):
                bank, bkey = pv_bank[g]
                pv4 = bank[:].rearrange("p (h c) -> p h c", h=4)
                P.op('dve', lambda e: e.tensor_tensor(out=den[:], in0=pv4[:, :, 64], in1=esink[:, l, 4 * g:4 * g + 4], op=ALU.add), reads=[bkey, 'esink'], writes=['den'])
                P.op('dve', lambda e: e.reciprocal(out=rden[:], in_=den[:]), reads=['den'], writes=['rden'])
                P.op('dve', lambda e: e.tensor_tensor(out=mix[:, g * 256:(g + 1) * 256].rearrange("p (h d) -> p h d", h=4), in0=pv4[:, :, 0:64],
                                                      in1=bc(rden[:], 2, [128, 4, 64]), op=ALU.mult), reads=[bkey, 'rden'], writes=['mixa'])

            def block_g(g, bi, bk, j, mask_ap, mask_key, lhs, lk, rv, rvk, neg=None):
                bank, bkey = pv_bank[g]
                psi = nxt('ps')
                pk = 'PS%d' % psi
                def fsc(e):
                    ins = e.matmul(PS[psi][:], lhsT=lhs, rhs=qTc[:, 4 * g:4 * g + 4, :], start=True, stop=(neg is None))
                    if neg is not None:
                        ins = e.matmul(PS[psi][:], lhsT=identb[:], rhs=neg, start=False, stop=True)
                    return ins
                P.op('pe', fsc, reads=[lk, qTk, 'identb', 'negm'], writes=[pk])
                pi = nxt('pt')
                P.op('act', lambda e: e.activation(out=pT[pi][:].rearrange("p h t -> p (h t)"), in_=PS[psi][:], func=AF.Exp, scale=0.125),
                     reads=[pk], writes=['pT%d' % pi])
                if neg is None:
                    P.op('pool', lambda e: e.tensor_tensor(out=pT[pi][:], in0=pT[pi][:], in1=bc(mask_ap, 1, [128, 4, 128]), op=ALU.mult),
                         reads=['pT%d' % pi, mask_key], writes=['pT%d' % pi])
                yield 1
                def fpv(e):
                    ins = None
                    for hh in range(4):
                        ins = e.matmul(bank[:, hh * 128:hh * 128 + 65], lhsT=pT[pi][:, hh, :], rhs=rv,
                                       start=(bi == 0 and hh == 0), stop=(bi == nb - 1), skip_group_check=True)
                    return ins
                P.op('pe', fpv, reads=['pT%d' % pi, rvk], writes=[bkey])
                yield 0

            if not is_smp:
                for g in range(2):
                    for bi, (bk, j) in enumerate(blocks):
                        if bk == 'p':
                            yield from block_g(g, bi, bk, j, None, None, kT[l][prv][:, g, :], 'kT%d_%d' % (l, prv), vaug[l][prv][:, g, :], 'vaug%d_%d' % (l, prv), neg=negm[:, 1, :])
                        else:
                            yield from block_g(g, bi, bk, j, None, None, kTc[:, g, :], kTck, vac[:, g, :], vack, neg=negm[:, 0, :])
                    finish_g(g)
            else:
                P.barrier()
                P.op('pool', lambda e: e.memset(ft[:, 1024:2064], 0.0), writes=['vcsA'])
                P.op('dve', lambda e: e.memset(vcsA[:, :, :, 64], 1.0), reads=['vcsA'], writes=['vcsA'])
                for i_ in range(2):
                    P.op('dve', lambda e: e.memset(pT[i_][:], 0.0), writes=['pT%d' % i_])
                P.op('pool', lambda e: e.dma_start(out=kcsA, in_=ck[l].rearrange("j k d -> k j d")), writes=['kcsA'], dma='l_kc')
                for g in range(2):
                    P.op('pool', lambda e: e.dma_start(out=vcsA[:, :, g, 0:64], in_=cv[l, :, :, g * 64:(g + 1) * 64].rearrange("j k d -> k j d")), reads=['vcsA'], writes=['vcsA'], dma='l_vc%d' % g)

            def att_seq(bi, j):
                kcT = kcT2[j % 2]
                kck = 'kcT%d' % (j % 2)
                transposes_bf([kcsA[:, j, g * 64:(g + 1) * 64] for g in range(2)], ['kcsA'], kcT, kck, 64)
                for g in range(2):
                    bank, bkey = pv_bank[g]
                    psi = nxt('ps')
                    pk = 'PS%d' % psi
                    P.op('pe', lambda e: e.matmul(PS[psi][:, 0:16], lhsT=kcT[:, g, :], rhs=qTc[:, 4 * g:4 * g + 4, j:64:16], start=True, stop=True),
                         reads=[kck, qTk], writes=[pk])
                    pi = nxt('pt')
                    pslice = pT[pi][:, :, j:64:16]
                    P.op('act', lambda e: e.activation(out=pslice, in_=PS[psi][:, 0:16].rearrange("p (h t) -> p h t", h=4), func=AF.Exp, scale=0.125),
                         reads=[pk], writes=['pT%d' % pi])
                    P.op('dve', lambda e: e.tensor_tensor(out=pslice, in0=pslice, in1=bc(mc0[:, j:64:16], 1, [128, 4, 4]), op=ALU.mult),
                         reads=['pT%d' % pi, 'mc0'], writes=['pT%d' % pi])
                    def fpv(e):
                        ins = None
                        for hh in range(4):
                            ins = e.matmul(bank[:, hh * 128:hh * 128 + 65], lhsT=pT[pi][:, hh, :], rhs=vcsA[:, j, g, :],
                                           start=(bi == 0 and hh == 0), stop=False, skip_group_check=True)
                        return ins
                    P.op('pe', fpv, reads=['pT%d' % pi, 'vcsA'], writes=[bkey])
                    P.op('dve', lambda e: e.memset(pslice, 0.0), reads=['pT%d' % pi], writes=['pT%d' % pi])

            def att_epi():
                bi, (bk, j) = nb - 1, blocks[-1]
                for g in range(2):
                    for _ in block_g(g, bi, bk, j, mns[:], 'mns', kTc[:, g, :], kTck, vac[:, g, :], vack):
                        pass
                finish_g(0)
                finish_g(1)

            qec, kec, kdc, bgc, vhc = qe[par], ke[par], kd[par], bg[par], vh[par]
            qek, kek, kdk, bgk, vhk = 'qe%d' % par, 'ke%d' % par, 'kd%d' % par, 'bg%d' % par, 'vh%d' % par
            transposes_bf([qec[:, h, :] for h in range(4)], [qek], qeT[:], 'qeT', 128)
            yield 1
            transposes_bf([kec[:, h, :] for h in range(4)], [kek], keT[:], 'keT', 128, evac='dve')
            yield 1
            p4 = nxt('ps')
            def fa(e):
                ins = None
                for h in range(4):
                    ins = e.matmul(PS[p4][:, h * 128:(h + 1) * 128], lhsT=keT[:, h, :], rhs=qeT[:, h, :], start=True, stop=True)
                return ins
            P.op('pe', fa, reads=['keT', 'qeT'], writes=['PS%d' % p4])
            P.op('dve', lambda e: e.tensor_tensor(out=atm[:], in0=PS[p4][:].rearrange("p (h t) -> p h t", h=4), in1=bc(ma[:, mi, :], 1, [128, 4, 128]), op=ALU.mult),
                 reads=['PS%d' % p4, 'ma'], writes=['atm'])
            yield 0
            def fo(e):
                ins = None
                for h in range(4):
                    ins = e.matmul(PO[:, h * 128:(h + 1) * 128], lhsT=atm[:, h, :], rhs=vhc[:, h * 128:(h + 1) * 128],
                                   start=(h == 0), stop=False, skip_group_check=True)
                return ins
            P.op('pe', fo, reads=['atm', vhk], writes=['PO'])
            yield 0
            Sl = S[l]
            Sk = 'S%d' % l
            if not is_smp:
                qeTs = keT
                P.op('dve', lambda e: e.tensor_tensor(out=qeTs.rearrange("p h (c t) -> p h c t", c=4), in0=qeT.rearrange("p h (c t) -> p h c t", c=4),
                                                      in1=bc(bgc[:, :, 4:8], 3, [128, 4, 4, 32]), op=ALU.mult),
                     reads=['qeT', 'keT', bgk], writes=['keT'])
                P.op('dve', lambda e: e.tensor_copy(out=Sp[0][:], in_=Sl[:]), reads=[Sk], writes=['Sp0'])
                for c in range(4):
                    pu = nxt('ps')
                    def fu(e):
                        ins = None
                        for h in range(4):
                            ins = e.matmul(PS[pu][:, h * 128:(h + 1) * 128], lhsT=kdc[32 * c:32 * c + 32, h * 128:(h + 1) * 128],
                                           rhs=vhc[32 * c:32 * c + 32, h * 128:(h + 1) * 128], start=True, stop=True, tile_position=(32 * c, 0))
                        return ins
                    P.op('pe', fu, reads=[kdk, vhk], writes=['PS%d' % pu])
                    def fupd(e):
                        ins = None
                        for h in range(4):
                            ins = e.scalar_tensor_tensor(out=Sl[:, h, :], in0=Sl[:, h, :], scalar=bgc[:, h, c:c + 1], in1=PS[pu][:, h * 128:(h + 1) * 128],
                                                         op0=ALU.mult, op1=ALU.add)
                        return ins
                    if c < 3:
                        def fupb(e):
                            ins = None
                            for h in range(4):
                                ins = e.scalar_tensor_tensor(out=Sp[(c + 1) % 2][:, h, :], in0=Sl[:, h, :], scalar=bgc[:, h, c:c + 1], in1=PS[pu][:, h * 128:(h + 1) * 128],
                                                             op0=ALU.mult, op1=ALU.add)
                            return ins
                        P.op('dve', fupb, reads=[Sk, bgk, 'PS%d' % pu], writes=['Sp%d' % ((c + 1) % 2)])
                    P.op('dve', fupd, reads=[Sk, bgk, 'PS%d' % pu], writes=[Sk])
                    if c == 3:
                        yield 3
                    def foi(e):
                        ins = None
                        for h in range(4):
                            ins = e.matmul(PO[32 * c:32 * c + 32, h * 128:(h + 1) * 128], lhsT=qeTs[:, h, 32 * c:32 * c + 32], rhs=Sp[c % 2][:, h, :],
                                           start=False, stop=(c == 3 and h == 3), skip_group_check=True, tile_position=(0, 32 * c))
                        return ins
                    P.op('pe', foi, reads=['keT', 'Sp%d' % (c % 2)], writes=['PO'])
                    if c < 3:
                        yield 1
                if seg == NSEG - 1 and ti == TPS - 1:
                    P.op('sp', lambda e: e.dma_start(out=ns_p[l].rearrange("h k v -> k h v"), in_=Sl[:]), reads=[Sk], dma='o_nsp')
            else:
                kdm = atm[:].rearrange("p h d -> p (h d)")
                qeTm = Sp[0][:, :, 0:64]
                sbf2 = [sbf, mix[:, 512:1024].rearrange("p (h d) -> p h d", h=4)]
                sbk = ['sbfA', 'mixh']
                def ld_state(jj):
                    s_ = jj % 2
                    P.op('sp', lambda e: e.dma_start(out=sstA[s_], in_=st_in[l, jj].rearrange("h k v -> k h v")), writes=['sstA%d' % s_], dma='l_st%d' % s_)
                    P.op('pool', lambda e: e.dma_start(out=sbf2[s_], in_=st_in[l, jj].rearrange("h k v -> k h v")), writes=[sbk[s_]], dma='l_sb%d' % s_)
                ld_state(0)
                for j in range(NSMP):
                    si = j % 2
                    if j + 1 < NSMP:
                        ld_state(j + 1)
                    att_seq(j, j)
                    P.op('dve', lambda e: e.tensor_tensor(out=qeTm, in0=qeT[:, :, 0:64], in1=bc(selT[:, j, :], 1, [128, 4, 64]), op=ALU.mult),
                         reads=['qeT', 'selT'], writes=['Sp0'])
                    def foi(e):
                        ins = None
                        for h in range(4):
                            ins = e.matmul(PO[0:64, h * 128:(h + 1) * 128], lhsT=qeTm[:, h, :], rhs=sbf2[si][:, h, :],
                                           start=False, stop=(j == NSMP - 1), skip_group_check=True)
                        return ins
                    P.op('pe', foi, reads=['Sp0', sbk[si]], writes=['PO'])
                    P.op('act', lambda e: e.activation(out=kdm, in_=kdc[:], func=AF.Copy, scale=msel[:, j:j + 1]), reads=[kdk, 'msel'], writes=['atm'])
                    pu = nxt('ps')
                    def fu(e):
                        ins = None
                        for h in range(4):
                            ins = e.matmul(PS[pu][:, h * 128:(h + 1) * 128], lhsT=kdm[:, h * 128:(h + 1) * 128], rhs=vhc[:, h * 128:(h + 1) * 128], start=True, stop=True)
                        return ins
                    P.op('pe', fu, reads=['atm', vhk], writes=['PS%d' % pu])
                    P.op('dve', lambda e: e.tensor_tensor(out=stmp, in0=sstA[si], in1=bc(bgc[:, :, j], 2, [128, 4, 128]), op=ALU.mult),
                         reads=['sstA%d' % si, bgk], writes=['stmpA'])
                    P.op('dve', lambda e: e.tensor_tensor(out=stmp.rearrange("p h d -> p (h d)"), in0=stmp.rearrange("p h d -> p (h d)"), in1=PS[pu][:], op=ALU.add),
                         reads=['stmpA', 'PS%d' % pu], writes=['stmpA'])
                    P.op('sp', lambda e: e.dma_start(out=ns_s[l, j].rearrange("h k v -> k h v"), in_=stmp), reads=['stmpA'], dma='o_st')
                    yield 1
                att_epi()
            def fsq(e):
                for h in range(4):
                    e.activation(out=atm[:, h, :], in_=PO[:, h * 128:(h + 1) * 128], func=AF.Square, accum_out=sso[:, h:h + 1])
                return e.activation(out=dmy[:], in_=dmy[:], func=AF.Copy)
            P.op('act', fsq, reads=['PO'], writes=['atm', 'sso', 'dmy'])
            P.op('act', lambda e: e.activation(out=sso[:], in_=sso[:], func=AF.Ln, scale=1.0 / 128.0, bias=EPS_AP[:]), reads=['sso', 'eps_ap'], writes=['sso'])
            P.op('act', lambda e: e.activation(out=rso[:], in_=sso[:], func=AF.Exp, scale=-0.5), reads=['sso'], writes=['rso'])
            def fmx(e):
                ins = None
                for h in range(4):
                    ins = e.scalar_tensor_tensor(out=mix[:, 512 + h * 128:512 + (h + 1) * 128], in0=PO[:, h * 128:(h + 1) * 128], scalar=rso[:, h:h + 1],
                                                 in1=gate[par][:, h * 128:(h + 1) * 128], op0=ALU.mult, op1=ALU.mult)
                return ins
            P.op('dve', fmx, reads=['PO', 'rso', 'gate%d' % par], writes=['mixh'])
            transposes_bf([mix[:, i * 128:(i + 1) * 128] for i in range(4)], ['mixa'], mixT[:, 0:4, :], 'qeT', 128)
            wob = [(PV, 'PV'), (PT, 'PT')]
            for half in range(2):
                wbank, wkey = wob[half]
                def fw0(e):
                    ins = None
                    for kc in range(4):
                        ins = e.matmul(wbank[:], lhsT=mixT[:, kc, :], rhs=w_o_sb[:, kc, half * 512:(half + 1) * 512], start=(kc == 0), stop=False)
                    return ins
                P.op('pe', fw0, reads=['qeT'] + WO_KEYS, writes=[wkey])
            yield 99
            transposes_bf([mix[:, i * 128:(i + 1) * 128] for i in range(4, 8)], ['mixh'], mixT[:, 4:8, :], 'keT', 128, evac='dve')
            for half in range(2):
                wbank, wkey = wob[half]
                def fw1(e):
                    ins = None
                    for kc in range(4, 8):
                        ins = e.matmul(wbank[:], lhsT=mixT[:, kc, :], rhs=w_o_sb[:, kc, half * 512:(half + 1) * 512], start=False, stop=(kc == 7))
                    return ins
                P.op('pe', fw1, reads=['keT'] + WO_KEYS, writes=[wkey])
                P.op('dve', lambda e: e.tensor_tensor(out=xres[:, ti, half * 512:(half + 1) * 512], in0=xres[:, ti, half * 512:(half + 1) * 512], in1=wbank[:], op=ALU.add),
                     reads=[xk, wkey], writes=[xk])
                yield 0

        def drain(g):
            for _ in g:
                pass

        def interleave(gb, gf):
            done_f = False
            for nf in gb:
                for _ in range(nf or 0):
                    if done_f:
                        break
                    try:
                        next(gf)
                    except StopIteration:
                        done_f = True
            if not done_f:
                drain(gf)

        for seg in range(NSEG):
            tiles = list(range(TPS)) + ([TPS] if seg == 0 else [])
            for ti in range(TPS):
                if ti == 0 or seg > 0:
                    continue
                P.op('sp', lambda e: e.dma_start(out=xres[:, ti, :], in_=x_seq[(seg * TPS + ti) * 128:(seg * TPS + ti + 1) * 128, :]),
                     writes=['x%d' % ti], dma='lx%d' % ti)
            P.op('sp', lambda e: e.dma_start(out=cosp[:], in_=c_cosp[:, seg * TPS:(seg + 1) * TPS, :]), writes=['cosp'], dma='l_cos')
            P.op('sp', lambda e: e.dma_start(out=sinp[:], in_=c_sinp[:, seg * TPS:(seg + 1) * TPS, :]), writes=['sinp'], dma='l_sin')
            if seg == 0:
                P.op('sp', lambda e: e.dma_start(out=xres[0:64, TPS, :], in_=x_smp), writes=['x%d' % TPS], dma='lx%d' % TPS)
            for l in range(2):
                if not (seg == 0 and l == 0):
                    P.barrier()
                def load_pass(p_):
                    load_pass_l(l, p_)
                if l != 0:
                    load_phase_a(l)
                if seg == 0:
                    P.op('sp', lambda e: e.dma_start(out=nk_s[l, :, 0:124, :], in_=ck[l, :, 4:128, :]), dma='o_ckc')
                    P.op('sp', lambda e: e.dma_start(out=nv_s[l, :, 0:124, :], in_=cv[l, :, 4:128, :]), dma='o_cvc')
                n = len(tiles)
                prepped = set()
                drain(F_tile(l, tiles[0], seg, tiles[0] == TPS, 0))
                load_phase_a2(l)
                for tix, ti in enumerate(tiles):
                    gb = B_tile(l, ti, seg, ti == TPS, tix % 2)
                    if tix + 1 < n:
                        gf = F_tile(l, tiles[tix + 1], seg, tiles[tix + 1] == TPS, (tix + 1) % 2)
                        interleave(gb, gf)
                    elif ti == TPS:
                        def prep_gen():
                            for t_ in range(TPS):
                                rms_stats(xres[:, t_, :], 'x%d' % t_, rstd[:], 'rstd', sq=rr2[:, t_:t_ + 1])
                                for _ in make_hT(t_, 2 + l, hT_all[:, :, t_ * 128:(t_ + 1) * 128], 'hTall', pj_only=True):
                                    pass
                                prepped.add(t_)
                                yield
                        interleave(gb, prep_gen())
                    else:
                        drain(gb)
                P.barrier()
                ntok = n * 128
                for ti in tiles:
                    if ti in prepped:
                        continue
                    rms_stats(xres[:, ti, :], 'x%d' % ti, rstd[:], 'rstd', sq=rr2[:, ti:ti + 1])
                    drain(make_hT(ti, 2 + l, hT_all[:, :, ti * 128:(ti + 1) * 128], 'hTall'))
                groups = [(g0, min(GRP, ntok - g0)) for g0 in range(0, ntok, GRP)]

                def up(p_, g0, gn, ui):
                    slot = p_ % 2
                    wu, wd = ring[slot]
                    uT = uTs[ui]
                    for fc in range(4):
                        psi = nxt('ps')
                        def fup(e):
                            ins = None
                            for kc in range(8):
                                ins = e.matmul(PS[psi][:, 0:gn], lhsT=wu[:, kc, fc * 128:(fc + 1) * 128], rhs=hT_all[:, kc, g0:g0 + gn], start=(kc == 0), stop=(kc == 7))
                            return ins
                        P.op('pe', fup, reads=['ringu%d_%d' % (slot, k_) for k_ in range(8)] + ['hTall'], writes=['PS%d' % psi])
                        P.op('act', lambda e: e.activation(out=rl[:, 0:gn], in_=PS[psi][:, 0:gn], func=AF.Relu), reads=['PS%d' % psi], writes=['hTa'])
                        P.op('dve', lambda e: e.tensor_tensor(out=uT[:, fc, 0:gn], in0=rl[:, 0:gn], in1=rl[:, 0:gn], op=ALU.mult), reads=['hTa'], writes=['uT%d' % ui])

                def down(p_, g0, gn, ui):
                    slot = p_ % 2
                    wu, wd = ring[slot]
                    uT = uTs[ui]
                    for tt in range(gn // 128):
                        ti = (g0 // 128) + tt
                        for half in range(2):
                            pj = nxt('pj')
                            def fdn(e):
                                ins = None
                                for fc in range(4):
                                    ins = e.matmul(PJ[pj][:], lhsT=uT[:, fc, tt * 128:(tt + 1) * 128], rhs=wd[:, fc, half * 512:(half + 1) * 512], start=(fc == 0), stop=(fc == 3))
                                return ins
                            P.op('pe', fdn, reads=['uT%d' % ui] + ['ringd%d_%d' % (slot, k_) for k_ in range(4)], writes=['PJ%d' % pj])
                            P.op('dve', lambda e: e.scalar_tensor_tensor(
                                out=xres[:, ti, half * 512:(half + 1) * 512], in0=PJ[pj][:], scalar=rr2[:, ti:ti + 1],
                                in1=xres[:, ti, half * 512:(half + 1) * 512], op0=ALU.mult, op1=ALU.add),
                                reads=['PJ%d' % pj, 'rr2', 'x%d' % ti], writes=['x%d' % ti])

                def final_tile(ti):
                    rms_stats(xres[:, ti, :], 'x%d' % ti, rstd[:], 'rstd')
                    yi = nxt('y')
                    P.op('dve', lambda e: e.scalar_tensor_tensor(out=ysb[yi], in0=xres[:, ti, :], scalar=rstd[:], in1=fing, op0=ALU.mult, op1=ALU.mult),
                         reads=['x%d' % ti, 'rstd', 'fing'], writes=['ysb%d' % yi])
                    if ti == TPS:
                        P.op('sp', lambda e: e.dma_start(out=y_smp, in_=ysb[yi][0:64, :]), reads=['ysb%d' % yi], dma='o_y%d' % yi)
                    else:
                        P.op('sp', lambda e: e.dma_start(out=y_seq[(seg * TPS + ti) * 128:(seg * TPS + ti + 1) * 128, :], in_=ysb[yi]),
                             reads=['ysb%d' % yi], dma='o_y%d' % yi)
                        if seg + 1 < NSEG:
                            P.op('sp', lambda e: e.dma_start(out=xres[:, ti, :], in_=x_seq[((seg + 1) * TPS + ti) * 128:((seg + 1) * TPS + ti + 1) * 128, :]),
                                 writes=['x%d' % ti], dma='lx%d' % ti)

                def after_down(pv):
                    if l == 1 and pv[0] == NPASS - 1:
                        for tt in range(pv[2] // 128):
                            final_tile(pv[1] // 128 + tt)

                if l == 1:
                    P.op('sp', lambda e: e.dma_start(out=fing, in_=fin_g.partition_broadcast(128)), writes=['fing'], dma='c10')
                work = [(p_, g0, gn) for p_ in range(NPASS) for (g0, gn) in groups]
                prev = None
                for wi, (p_, g0, gn) in enumerate(work):
                    ui = nxt('u')
                    up(p_, g0, gn, ui)
                    if prev is not None:
                        down(*prev)
                        after_down(prev)
                    if g0 == 0 and p_ + 1 < NPASS:
                        load_pass(p_ + 1)
                    prev = (p_, g0, gn, ui)
                down(*prev)
                after_down(prev)
            P.barrier()
            if seg + 1 < NSEG:
                load_phase_a(0)
            P.barrier()
        P.emit()
    return nc


def _consts():
    c = {}
    c['c_identf'] = np.eye(128, dtype=np.float32)
    c['c_identb'] = np.eye(128, dtype=np.float32)
    half = 8
    inv_freq = np.power(np.float32(500000.0), -np.arange(half, dtype=np.float32) * np.float32(2.0 / 16)).astype(np.float32)
    pos = np.arange(SEQ, dtype=np.float32)
    ang = (pos[:, None] * inv_freq[None, :]).astype(np.float32)
    c['c_cosp'] = np.cos(ang).astype(np.float32).reshape(32, 128, 8).transpose(1, 0, 2).copy()
    c['c_sinp'] = np.sin(ang).astype(np.float32).reshape(32, 128, 8).transpose(1, 0, 2).copy()
    p = np.arange(128)
    tt = p // 16
    jj = p % 16
    valid = p < 64
    pos_s = (PAST + tt).astype(np.float32)
    ang_s = (pos_s[:, None] * inv_freq[None, :]).astype(np.float32)
    c['c_coss'] = np.where(valid[:, None], np.cos(ang_s), 1.0).astype(np.float32)
    c['c_sins'] = np.where(valid[:, None], np.sin(ang_s), 0.0).astype(np.float32)
    s = p[:, None]; t = p[None, :]
    ncur = np.where(s <= t, 0.0, -30000.0).astype(np.float32)
    nprev = np.where(s >= t, 0.0, -30000.0).astype(np.float32)
    c['c_negm'] = np.stack([np.tile(ncur, (1, 4)), np.tile(nprev, (1, 4))], axis=1).astype(np.float32)
    ch = p // 32
    same = ch[:, None] == ch[None, :]
    ref = ch * 32 + 15
    mrel_p = same * ((s <= t).astype(np.float32) - (s <= ref[None, :]).astype(np.float32))
    mkd_p = same * (s > t)
    ma_p = same * (s <= t)
    same_s = (jj[:, None] == jj[None, :]) & valid[:, None] & valid[None, :]
    mrel_s = same_s * (tt[:, None] <= tt[None, :])
    mkd_s = same_s * (tt[:, None] > tt[None, :])
    ma_s = same_s * (tt[:, None] <= tt[None, :])
    c['c_mrel'] = np.stack([mrel_p, mrel_s]).astype(np.float32)
    c['c_mkd'] = np.stack([mkd_p, mkd_s]).astype(np.float32)
    c['c_ma'] = np.stack([ma_p, ma_s]).astype(np.float32)
    indp = np.zeros((128, 8), np.float32)
    for cc in range(4):
        indp[cc * 32:(cc + 1) * 32, cc] = 1.0
        indp[cc * 32:cc * 32 + 16, 4 + cc] = 1.0
    c['c_indp'] = indp
    inds = np.zeros((128, 16), np.float32)
    inds[p[valid], jj[valid]] = 1.0
    c['c_inds'] = inds
    c['c_msel'] = inds.copy()
    c['c_mc0'] = ((p[:, None] >= tt[None, :]) & valid[None, :]).astype(np.float32)
    mns = (same_s & (tt[:, None] <= tt[None, :])).astype(np.float32)
    c['c_mns'] = mns
    selT = np.zeros((128, 16, 64), np.float32)
    for j in range(16):
        selT[:, j, :] = ((jj == j) & valid)[None, 0:64]
    c['c_selT'] = selT
    return c


_CACHE = {}
PCORES = [0, 1, 4, 5]


def kernel(x_prompt, x_sample, cache_k, cache_v, state_hgrn, attn_norm, w_in, att_sinks,
           hgrn_lower_bounds, hgrn_out_norm, w_o, mlp_norm, w_up, w_down, final_norm):
    f = lambda a: np.ascontiguousarray(np.asarray(a, dtype=np.float32))
    x_prompt, x_sample, cache_k, cache_v, state_hgrn = map(f, (x_prompt, x_sample, cache_k, cache_v, state_hgrn))
    if 'nc' not in _CACHE:
        _CACHE['nc'] = build_program()
        _CACHE['consts'] = _consts()
    nc = _CACHE['nc']
    consts = _CACHE['consts']
    gains = np.concatenate([f(attn_norm).reshape(2, 8, 128), f(mlp_norm).reshape(2, 8, 128), f(final_norm).reshape(1, 8, 128)], axis=0).reshape(40, 128)
    shared = dict(gains=np.ascontiguousarray(gains), fin_g=f(final_norm).reshape(1, D), w_in=f(w_in), sinks=f(att_sinks),
                  lbraw=f(hgrn_lower_bounds), onorm=f(hgrn_out_norm), w_o=f(w_o), w_up=f(w_up), w_dn=f(w_down))
    shared.update(consts)
    zero_seq = np.zeros((SEQ, D), np.float32)
    in_maps = []
    for c in range(8):
        sl = slice(c * NSMP, (c + 1) * NSMP)
        m = dict(shared)
        m['x_seq'] = x_prompt[PCORES.index(c)] if c in PCORES else zero_seq
        m['x_smp'] = np.ascontiguousarray(x_sample[sl].transpose(1, 0, 2).reshape(64, D))
        m['ck'] = np.ascontiguousarray(cache_k[:, sl].reshape(2, NSMP, 128, 128))
        m['cv'] = np.ascontiguousarray(cache_v[:, sl].reshape(2, NSMP, 128, 128))
        m['st_in'] = np.ascontiguousarray(state_hgrn[:, sl])
        in_maps.append(m)
    res = run_bass_kernel_spmd(nc, in_maps, core_ids=list(range(8)))
    R = res.results
    y_prompt = np.stack([R[b]['y_seq'] for b in PCORES]).astype(np.float32)
    y_sample = np.concatenate([R[c]['y_smp'].reshape(4, NSMP, D).transpose(1, 0, 2) for c in range(8)], axis=0).astype(np.float32)
    nk_p = np.stack([R[b]['nk_p'] for b in PCORES], axis=1).reshape(2, 4, 128, 2, 64)
    nv_p = np.stack([R[b]['nv_p'] for b in PCORES], axis=1).reshape(2, 4, 128, 2, 64)
    ns_p = np.stack([R[b]['ns_p'] for b in PCORES], axis=1)
    nk_s = np.concatenate([R[c]['nk_s'] for c in range(8)], axis=1).reshape(2, 128, 128, 2, 64)
    nv_s = np.concatenate([R[c]['nv_s'] for c in range(8)], axis=1).reshape(2, 128, 128, 2, 64)
    ns_s = np.concatenate([R[c]['ns_s'] for c in range(8)], axis=1)
    return (y_prompt, y_sample, np.ascontiguousarray(nk_p), np.ascontiguousarray(nv_p), np.ascontiguousarray(ns_p),
            np.ascontiguousarray(nk_s), np.ascontiguousarray(nv_s), np.ascontiguousarray(ns_s))
```

```python
import numpy as np
from contextlib import ExitStack
import concourse.bass as bass
import concourse.mybir as mybir
from concourse.bass_utils import run_bass_kernel_spmd

F32 = mybir.dt.float32
BF16 = mybir.dt.bfloat16
AF = mybir.ActivationFunctionType
ALU = mybir.AluOpType
AX = mybir.AxisListType

D = 1024
SEQ = 4096
NSEG = 2
TPS = 16
NSMP = 16
PAST = 16384
EPS = 1e-6
INW = 2816
DFF = 4096
NPASS = 8
GRP = 256


class Prog:
    ENGS = ('pe', 'act', 'dve', 'pool', 'sp')

    def __init__(self, nc, es):
        self.nc = nc
        self.es = es
        self.eng = {'pe': nc.tensor, 'act': nc.scalar, 'dve': nc.vector, 'pool': nc.gpsimd, 'sp': nc.sync}
        self.cnt = {e: 0 for e in self.ENGS}
        self.semh = {k: es.enter_context(nc.semaphore('s_' + k)) for k in self.ENGS}
        self.dma_cnt = {}
        self.last_write = {}
        self.reads_since = {}
        self.seen = {e: {} for e in self.ENGS}
        self.know = {}
        self.nins = 0
        self.nwait = 0

    def _wait(self, e, tok):
        key, val = tok
        if self.seen[e].get(key, 0) >= val:
            return
        se = self.seen[e]
        se[key] = val
        self.eng[e].wait_ge(self.semh[key], val)
        self.nwait += 1
        kn = self.know.get(tok)
        if kn:
            for k2, v2 in kn.items():
                if se.get(k2, 0) < v2:
                    se[k2] = v2

    def op(self, e, fn, reads=(), writes=(), dma=None):
        deps = []
        for r in reads:
            if r in self.last_write:
                deps.append(self.last_write[r])
        for w in writes:
            rs = self.reads_since.get(w, ())
            if rs:
                deps.extend(rs)
            elif w in self.last_write:
                deps.append(self.last_write[w])
        mx = {}
        for key, val in deps:
            if key == 'pe' and e == 'pe' and dma is None:
                continue
            if val > mx.get(key, 0):
                mx[key] = val
        for key, val in mx.items():
            self._wait(e, (key, val))
        self.nins += 1
        if dma is None:
            self.cnt[e] += 1
            tok = (e, self.cnt[e])
            fn(self.eng[e]).then_inc(self.semh[e], 1)
        else:
            k = 'dma:' + dma
            if k not in self.semh:
                self.semh[k] = self.es.enter_context(self.nc.semaphore('d%d' % len(self.semh)))
            self.dma_cnt[k] = self.dma_cnt.get(k, 0) + 16
            tok = (k, self.dma_cnt[k])
            fn(self.eng[e]).then_inc(self.semh[k], 16)
        self.know[tok] = dict(self.seen[e])
        for r in reads:
            self.reads_since.setdefault(r, []).append(tok)
        for w in writes:
            self.last_write[w] = tok
            self.reads_since[w] = []
        return tok

    def barrier(self, skip=()):
        toks = [(k, v) for k, v in self.dma_cnt.items() if not any(k.startswith('dma:' + s) for s in skip)] + \
               [(k, self.cnt[k]) for k in self.ENGS if self.cnt[k]]
        for e in self.ENGS:
            for t in toks:
                self._wait(e, t)

    def emit(self):
        self.barrier()


def bc(ap, axis, shape):
    return ap.unsqueeze(axis).broadcast_to(shape)


def build_program():
    nc = bass.Bass("TRN2", target_bir_lowering=False)

    def din(name, shape, dt=F32):
        return nc.dram_tensor(name, list(shape), dt, kind="ExternalInput").ap()

    def dout(name, shape):
        return nc.dram_tensor(name, list(shape), F32, kind="ExternalOutput").ap()

    x_seq = din("x_seq", [SEQ, D])
    x_smp = din("x_smp", [64, D])
    ck = din("ck", [2, NSMP, 128, 128])
    cv = din("cv", [2, NSMP, 128, 128])
    st_in = din("st_in", [2, NSMP, 4, 128, 128])
    gains = din("gains", [40, 128])
    fin_g = din("fin_g", [1, D])
    w_in = din("w_in", [2, D, INW])
    sinks = din("sinks", [2, 8])
    lbraw = din("lbraw", [2, 512])
    onorm = din("onorm", [2, 128])
    w_o = din("w_o", [2, D, D])
    w_up = din("w_up", [2, D, DFF])
    w_dn = din("w_dn", [2, DFF, D])
    c_identf = din("c_identf", [128, 128])
    c_identb = din("c_identb", [128, 128])
    c_cosp = din("c_cosp", [128, 32, 8])
    c_sinp = din("c_sinp", [128, 32, 8])
    c_coss = din("c_coss", [128, 8])
    c_sins = din("c_sins", [128, 8])
    c_negm = din("c_negm", [128, 2, 512])
    c_mrel = din("c_mrel", [2, 128, 128])
    c_mkd = din("c_mkd", [2, 128, 128])
    c_ma = din("c_ma", [2, 128, 128])
    c_indp = din("c_indp", [128, 8])
    c_inds = din("c_inds", [128, 16])
    c_msel = din("c_msel", [128, 16])
    c_mc0 = din("c_mc0", [128, 128])
    c_mns = din("c_mns", [128, 128])
    c_selT = din("c_selT", [128, 16, 64])

    y_seq = dout("y_seq", [SEQ, D])
    y_smp = dout("y_smp", [64, D])
    nk_p = dout("nk_p", [2, 128, 128])
    nv_p = dout("nv_p", [2, 128, 128])
    ns_p = dout("ns_p", [2, 4, 128, 128])
    nk_s = dout("nk_s", [2, NSMP, 128, 128])
    nv_s = dout("nv_s", [2, NSMP, 128, 128])
    ns_s = dout("ns_s", [2, NSMP, 4, 128, 128])

    with ExitStack() as es:
        def sb(name, shape, dt=F32):
            return es.enter_context(nc.sbuf_tensor(name, list(shape), dt))

        def ps(name, shape, dt=F32):
            return es.enter_context(nc.psum_tensor(name, list(shape), dt))

        NT = TPS + 1
        xres = sb("xres", [128, NT, D])
        hT_elems = 8 * NT * 128
        ring_slot = 8 * 512 + 4 * D
        uT_elems = 4 * GRP
        rg_b = max(8 * INW + 8 * D + ring_slot, hT_elems + ring_slot + 2 * uT_elems)
        rgn32 = sb("rgn", [128, rg_b // 2])
        rgnb = rgn32[:].bitcast(BF16)
        w_in_sb = rgnb[:, 0:8 * INW].rearrange("p (k n) -> p k n", k=8)
        w_o_sb = rgnb[:, 8 * INW:8 * INW + 8 * D].rearrange("p (k n) -> p k n", k=8)
        hT_all = rgnb[:, 0:hT_elems].rearrange("p (k t) -> p k t", k=8)
        ring = []
        for o0 in (8 * INW + 8 * D, hT_elems):
            wu = rgnb[:, o0:o0 + 4096].rearrange("p (k n) -> p k n", k=8)
            wd = rgnb[:, o0 + 4096:o0 + 8192].rearrange("p (k n) -> p k n", k=4)
            ring.append((wu, wd))
        o0 = hT_elems + ring_slot
        assert o0 + 2 * uT_elems <= 8 * INW + 8 * D
        uTs = []
        for s in range(2):
            uTs.append(rgnb[:, o0:o0 + uT_elems].rearrange("p (k n) -> p k n", k=4))
            o0 += uT_elems

        identf = sb("identf", [128, 128]); identb = sb("identb", [128, 128], BF16)
        cosp = sb("cosp", [128, TPS, 8]); sinp = sb("sinp", [128, TPS, 8])
        coss = sb("coss", [128, 8]); sins = sb("sins", [128, 8])
        negm = sb("negm", [128, 2, 512], BF16)
        mrel = sb("mrel", [128, 2, 128]); mkd = sb("mkd", [128, 2, 128]); ma = sb("ma", [128, 2, 128], BF16)
        indp = sb("indp", [128, 8]); inds = sb("inds", [128, 16]); msel = sb("msel", [128, 16])
        mc0 = sb("mc0", [128, 128], BF16); mns = sb("mns", [128, 128], BF16)
        selT = sb("selT", [128, 16, 64], BF16)
        gT = sb("gT", [128, 40])
        esink = sb("esink", [128, 2, 8])
        lb1 = sb("lb1", [128, 512])
        onb = sb("onb", [128, 2, 128])
        EPS_AP = sb("eps_ap", [128, 1])

        ss = sb("ss", [128, 1]); rstd = sb("rstd", [128, 1]); nrstd = sb("nrstd", [128, 1])
        rr2 = sb("rr2", [128, NT])
        dmy = sb("dmy", [128, 1])
        hT = sb("hT", [128, 8, 128], BF16)
        junk = hT[:].rearrange("p k t -> p (k t)")
        rl = junk[:, 0:512]
        ft = sb("ft", [128, 4096])
        fing = ft[:, 0:D]
        ysb = [ft[:, D:2 * D], ft[:, 2 * D:3 * D]]
        qa = ft[:, 0:512].rearrange("p (h d) -> p h d", h=8)
        ka = sb("ka", [128, 2, 64]); va = sb("va", [128, 128])
        qab = sb("qab", [128, 8, 64], BF16); kab = sb("kab", [128, 2, 64], BF16)
        qT = [sb("qT%d" % i, [64, 8, 128], BF16) for i in range(2)]
        kT = [[sb("kT%d_%d" % (l, i), [64, 2, 128], BF16) for i in range(2)] for l in range(2)]
        vaug = [[sb("vaug%d_%d" % (l, i), [128, 2, 65], BF16) for i in range(2)] for l in range(2)]
        pT = [sb("pT%d" % i, [128, 4, 128], BF16) for i in range(2)]
        den = sb("den", [128, 4]); rden = sb("rden", [128, 4])
        mix = sb("mix", [128, D], BF16)
        e1 = ft[:, 512:1024]
        qh = ft[:, 1024:1536]; fg = ft[:, 1536:2048]; gl = ft[:, 2048:2560]; kk = ft[:, 2560:3072]
        cl = ft[:, 3072:3584]; eq = ft[:, 3584:4096]; ek = e1
        rt = [cl[:, 0:64].rearrange("p (h d) -> p h d", h=8), cl[:, 64:128].rearrange("p (h d) -> p h d", h=8),
              eq[:, 0:64].rearrange("p (h d) -> p h d", h=8), eq[:, 64:128].rearrange("p (h d) -> p h d", h=8)]
        RTK = ["cl", "cl", "eq", "eq"]
        vh = [sb("vh%d" % i, [128, 512], BF16) for i in range(2)]
        gate = [sb("gate%d" % i, [128, 512], BF16) for i in range(2)]
        qe = [sb("qe%d" % i, [128, 4, 128], BF16) for i in range(2)]
        ke = [sb("ke%d" % i, [128, 4, 128], BF16) for i in range(2)]
        kd = [sb("kd%d" % i, [128, 512], BF16) for i in range(2)]
        bg = [sb("bg%d" % i, [128, 4, 16]) for i in range(2)]
        tq = sb("tq", [128, 8, 128], BF16)
        qeT = tq[:, 0:4, :]; keT = tq[:, 4:8, :]; mixT = tq[:]
        atm = sb("atm", [128, 4, 128], BF16)
        S = [sb("S%d" % l, [128, 4, 128]) for l in range(2)]
        Sp = [sb("Sp%d" % c, [128, 4, 128], BF16) for c in range(2)]
        sso = sb("sso", [128, 4]); rso = sb("rso", [128, 4])
        kcT2 = [Sp[1][0:64, 0:2, :], Sp[1][0:64, 2:4, :]]
        kcsA = ft[:, 0:1024].bitcast(BF16).rearrange("p (j d) -> p j d", j=NSMP)
        vcsA = ft[:, 1024:2064].bitcast(BF16).rearrange("p (j g d) -> p j g d", j=NSMP, g=2)
        sstA = [ft[:, 2064:2576].rearrange("p (h d) -> p h d", h=4), ft[:, 2576:3088].rearrange("p (h d) -> p h d", h=4)]
        stmp = ft[:, 3088:3600].rearrange("p (h d) -> p h d", h=4)
        sbf = ft[:, 3600:3856].bitcast(BF16).rearrange("p (h d) -> p h d", h=4)

        PT = ps("PT", [128, 512])
        PB = ps("PB", [128, 1024], BF16)
        PJ = [ps("PJ0", [128, 512]), ps("PJ1", [128, 512])]
        PS = [ps("PS0", [128, 512]), ps("PS1", [128, 512])]
        PV = ps("PV", [128, 512])
        PO = ps("PO", [128, 512])

        P = Prog(nc, es)
        cnt = {'pj': 0, 'ps': 0, 'y': 0, 'pt': 0, 'u': 0, 'pb': 0}

        def nxt(k, n=2):
            v = cnt[k] % n
            cnt[k] += 1
            return v


        WGRP = [(0, 512), (512, 768), (768, 1280), (1280, 1792), (1792, 2304), (2304, 2816)]

        def load_pass_l(l, p_):
            slot = p_ % 2
            wu, wd = ring[slot]
            for kc in range(8):
                P.op('pool', lambda e: e.dma_start(out=wu[:, kc, :], in_=w_up[l, kc * 128:(kc + 1) * 128, p_ * 512:(p_ + 1) * 512]), writes=['ringu%d_%d' % (slot, kc)], dma='lru%d' % slot)
            for fc in range(4):
                P.op('pool', lambda e: e.dma_start(out=wd[:, fc, :], in_=w_dn[l, p_ * 512 + fc * 128: p_ * 512 + (fc + 1) * 128, :]), writes=['ringd%d_%d' % (slot, fc)], dma='lrd%d' % slot)

        def load_phase_a(l):
            for gi, (c0, c1) in enumerate(WGRP):
                P.op('pool', lambda e: e.dma_start(out=w_in_sb[:, :, c0:c1], in_=w_in[l, :, c0:c1].rearrange("(k p) n -> p k n", p=128)), writes=['w_in_g%d' % gi], dma='lw_in%d' % gi)

        def load_phase_a2(l):
            for kc in range(8):
                P.op('pool', lambda e: e.dma_start(out=w_o_sb[:, kc, :], in_=w_o[l, kc * 128:(kc + 1) * 128, :]), writes=['w_o%d' % kc], dma='lw_o')
            load_pass_l(l, 0)

        P.op('sp', lambda e: e.dma_start(out=xres[:, 0, :], in_=x_seq[0:128, :]), writes=['x0'], dma='lx0')
        load_phase_a(0)

        cl_list = [(identf, c_identf), (coss, c_coss), (sins, c_sins),
                   (indp, c_indp), (inds, c_inds), (msel, c_msel)]
        for i, (t, d) in enumerate(cl_list):
            P.op('sp', lambda e: e.dma_start(out=t[:], in_=d), writes=[t.name], dma='c%d' % i)
        P.op('sp', lambda e: e.dma_start(out=mrel[:], in_=c_mrel.rearrange("a p n -> p a n")), writes=['mrel'], dma='c20')
        P.op('sp', lambda e: e.dma_start(out=mkd[:], in_=c_mkd.rearrange("a p n -> p a n")), writes=['mkd'], dma='c21')
        cb_list = [(identb, c_identb), (negm, c_negm), (mc0, c_mc0), (mns, c_mns), (selT, c_selT)]
        for i, (t, d) in enumerate(cb_list):
            P.op('pool', lambda e: e.dma_start(out=t[:], in_=d), writes=[t.name], dma='cb%d' % i)
        P.op('pool', lambda e: e.dma_start(out=ma[:], in_=c_ma.rearrange("a p n -> p a n")), writes=['ma'], dma='cb9')
        P.op('sp', lambda e: e.dma_start(out=esink[:].rearrange("p l h -> p (l h)"), in_=sinks.rearrange("l h -> (l h)").unsqueeze(0).partition_broadcast(128)), writes=['esink'], dma='c11')
        gstage = ft[0:40, 1024:1152]
        lbr = ft[:, 0:1024].rearrange("p (l c) -> p l c", l=2)
        P.op('sp', lambda e: e.dma_start(out=ft[:, 0:1024], in_=lbraw.rearrange("l c -> (l c)").unsqueeze(0).partition_broadcast(128)), writes=['lbr'], dma='c12')
        P.op('sp', lambda e: e.dma_start(out=onb[:].rearrange("p l h -> p (l h)"), in_=onorm.rearrange("l c -> (l c)").unsqueeze(0).partition_broadcast(128)), writes=['onb'], dma='c13')
        P.op('sp', lambda e: e.dma_start(out=gstage, in_=gains), writes=['gstage'], dma='c19')
        P.op('act', lambda e: e.activation(out=esink[:], in_=esink[:], func=AF.Exp), reads=['esink'], writes=['esink'])
        P.op('dve', lambda e: e.tensor_tensor(out=lb1[:], in0=lbr[:, 0, :], in1=lbr[:, 1, :], op=ALU.subtract), reads=['lbr'], writes=['lb1'])
        P.op('act', lambda e: e.activation(out=lb1[:], in_=lb1[:], func=AF.Exp), reads=['lb1'], writes=['lb1'])
        P.op('dve', lambda e: e.tensor_scalar_add(out=lb1[:], in0=lb1[:], scalar1=1.0), reads=['lb1'], writes=['lb1'])
        P.op('dve', lambda e: e.reciprocal(out=lb1[:], in_=lb1[:]), reads=['lb1'], writes=['lb1'])
        P.op('pe', lambda e: e.transpose(PT[:, 0:40], gstage, identf[0:40, 0:40]), reads=['gstage', 'identf'], writes=['PT'])
        P.op('dve', lambda e: e.tensor_copy(out=gT[:], in_=PT[:, 0:40]), reads=['PT'], writes=['gT'])
        P.op('pool', lambda e: e.memset(dmy[:], 0.0), writes=['dmy'])
        P.op('pool', lambda e: e.memset(EPS_AP[:], EPS), writes=['eps_ap'])
        for l in range(2):
            for i in range(2):
                P.op('pool', lambda e: e.memset(vaug[l][i][:], 1.0), writes=['vaug%d_%d' % (l, i)])
        P.op('pool', lambda e: e.memset(xres[:, TPS, :], 0.0), writes=['x%d' % TPS])
        for l in range(2):
            P.op('pool', lambda e: e.memset(S[l][:], 0.0), writes=['S%d' % l])
        P.barrier(skip=('lw_in', 'lw_o', 'lru', 'lrd', 'lx'))

        def rms_stats(xt_ap, xkey, out_r, out_key, also_neg=None, sq=None):
            def f(e):
                e.activation(out=junk, in_=xt_ap, func=AF.Square, accum_out=ss[:])
                return e.activation(out=dmy[:], in_=dmy[:], func=AF.Copy)
            P.op('act', f, reads=[xkey], writes=['hTa', 'hTb', 'ss', 'dmy'])
            P.op('act', lambda e: e.activation(out=ss[:], in_=ss[:], func=AF.Ln, scale=1.0 / D, bias=EPS_AP[:]), reads=['ss', 'eps_ap'], writes=['ss'])
            P.op('act', lambda e: e.activation(out=out_r, in_=ss[:], func=AF.Exp, scale=-0.5), reads=['ss'], writes=[out_key])
            if also_neg is not None:
                P.op('dve', lambda e: e.tensor_scalar_mul(out=also_neg, in0=out_r, scalar1=-1.0), reads=[out_key], writes=['nrstd'])
            if sq is not None:
                P.op('act', lambda e: e.activation(out=sq, in_=ss[:], func=AF.Exp, scale=-1.0), reads=['ss'], writes=['rr2'])

        def make_hT(ti, gcol, dst, dst_key, pj_only=False):
            for half in range(2):
                if half == 0 and not pj_only:
                    bank, bkey = PT, 'PT'
                else:
                    pj_ = nxt('pj')
                    bank, bkey = PJ[pj_], 'PJ%d' % pj_
                def f(e):
                    ins = None
                    for j in range(4):
                        kc = half * 4 + j
                        ins = e.transpose(bank[:, j * 128:(j + 1) * 128], xres[:, ti, kc * 128:(kc + 1) * 128], identf[:])
                    return ins
                P.op('pe', f, reads=['x%d' % ti, 'identf'], writes=[bkey])
                P.op('dve', lambda e: e.tensor_tensor(
                    out=dst[:, half * 4:half * 4 + 4, :], in0=bank[:].rearrange("p (k t) -> p k t", k=4),
                    in1=bc(gT[:, gcol * 8 + half * 4:gcol * 8 + half * 4 + 4], 2, [128, 4, 128]), op=ALU.mult),
                    reads=[bkey, 'gT'], writes=[(dst_key + 'ab'[half]) if dst_key == 'hT' else dst_key])
                yield

        WO_KEYS = ['w_o%d' % k_ for k_ in range(8)]

        def proj_group(c0, c1):
            pj = nxt('pj')
            for hf in range(2):
                def f(e):
                    ins = None
                    for kc in range(hf * 4, hf * 4 + 4):
                        ins = e.matmul(PJ[pj][:, 0:c1 - c0], lhsT=hT[:, kc, :], rhs=w_in_sb[:, kc, c0:c1], start=(kc == 0), stop=(kc == 7))
                    return ins
                P.op('pe', f, reads=['hT' + 'ab'[hf], 'w_in_g%d' % WGRP.index((c0, c1))], writes=['PJ%d' % pj])
            return pj

        def silu_from_psum(pj, dst, dst_key):
            k = 'PJ%d' % pj
            P.op('act', lambda e: e.activation(out=dst, in_=PJ[pj][:], func=AF.Copy, scale=rstd[:]), reads=[k, 'rstd'], writes=[dst_key])
            P.op('act', lambda e: e.activation(out=e1[:], in_=dst, func=AF.Exp, scale=-1.0), reads=[dst_key], writes=['e1'])
            P.op('act', lambda e: e.activation(out=e1[:], in_=e1[:], func=AF.Ln, bias=1.0), reads=['e1'], writes=['e1'])
            P.op('act', lambda e: e.activation(out=e1[:], in_=e1[:], func=AF.Exp, scale=-1.0), reads=['e1'], writes=['e1'])
            P.op('pool', lambda e: e.tensor_tensor(out=dst, in0=dst, in1=e1[:], op=ALU.mult), reads=[dst_key, 'e1'], writes=[dst_key])

        def rotary(src, nh, cos_ap, sin_ap, dstb, skey, dkey):
            x1 = src[:, :, 0:8]; x2 = src[:, :, 8:16]
            cb = bc(cos_ap, 1, [128, nh, 8]); sn = bc(sin_ap, 1, [128, nh, 8])
            r0, r1, r2, r3 = [t[:, 0:nh, :] for t in rt]
            ck_ = ['cosp', 'sinp']
            P.op('pool', lambda e: e.tensor_tensor(out=r0, in0=x1, in1=cb, op=ALU.mult), reads=[skey] + ck_, writes=[RTK[0]])
            P.op('pool', lambda e: e.tensor_tensor(out=r1, in0=x2, in1=sn, op=ALU.mult), reads=[skey] + ck_, writes=[RTK[1]])
            P.op('pool', lambda e: e.tensor_tensor(out=r2, in0=x2, in1=cb, op=ALU.mult), reads=[skey] + ck_, writes=[RTK[2]])
            P.op('pool', lambda e: e.tensor_tensor(out=r3, in0=x1, in1=sn, op=ALU.mult), reads=[skey] + ck_, writes=[RTK[3]])
            P.op('pool', lambda e: e.tensor_tensor(out=x1, in0=r0, in1=r1, op=ALU.subtract), reads=['cl', skey], writes=[skey])
            P.op('pool', lambda e: e.tensor_tensor(out=x2, in0=r2, in1=r3, op=ALU.add), reads=['eq', skey], writes=[skey])
            P.op('pool', lambda e: e.tensor_copy(out=dstb, in_=src), reads=[skey], writes=[dkey])

        def transposes_bf(srcs, src_keys, dst, dst_key, rows, evac='act'):
            n = len(srcs)
            dkeys = dst_key if isinstance(dst_key, list) else [dst_key]
            def f(e):
                ins = None
                for i, s_ in enumerate(srcs):
                    ins = e.transpose(PB[0:rows, i * 128:i * 128 + 128], s_, identb[:])
                return ins
            P.op('pe', f, reads=list(src_keys) + ['identb'], writes=['PB'])
            src_v = PB[0:rows, 0:n * 128].rearrange("p (k t) -> p k t", k=n)
            if evac == 'act':
                P.op('act', lambda e: e.activation(out=dst, in_=src_v, func=AF.Copy), reads=['PB'], writes=dkeys)
            else:
                P.op('dve', lambda e: e.tensor_copy(out=dst, in_=src_v), reads=['PB'], writes=dkeys)

        def F_tile(l, ti, seg, is_smp, par):
            xk = 'x%d' % ti
            rms_stats(xres[:, ti, :], xk, rstd[:], 'rstd', also_neg=nrstd[:])
            for _ in make_hT(ti, l, hT, 'hT'):
                yield
            kTc, vac = kT[l][par], vaug[l][par]
            kTck, vack = 'kT%d_%d' % (l, par), 'vaug%d_%d' % (l, par)
            if is_smp:
                cos_ap, sin_ap = coss[:], sins[:]
            else:
                cos_ap, sin_ap = cosp[:, ti, :], sinp[:, ti, :]
            pj = proj_group(0, 512)
            P.op('act', lambda e: e.activation(out=qa[:].rearrange("p h d -> p (h d)"), in_=PJ[pj][:], func=AF.Copy, scale=rstd[:]),
                 reads=['PJ%d' % pj, 'rstd'], writes=['qa'])
            yield
            rotary(qa[:], 8, cos_ap, sin_ap, qab[:], 'qa', 'qab')
            pj = proj_group(512, 768)
            P.op('act', lambda e: e.activation(out=ka[:].rearrange("p h d -> p (h d)"), in_=PJ[pj][:, 0:128], func=AF.Copy, scale=rstd[:]),
                 reads=['PJ%d' % pj, 'rstd'], writes=['ka'])
            P.op('act', lambda e: e.activation(out=va[:], in_=PJ[pj][:, 128:256], func=AF.Copy, scale=rstd[:]),
                 reads=['PJ%d' % pj, 'rstd'], writes=['va'])
            yield
            rotary(ka[:], 2, cos_ap, sin_ap, kab[:], 'ka', 'kab')
            P.op('pool', lambda e: e.tensor_copy(out=vac[:, :, 0:64], in_=va[:].rearrange("p (g d) -> p g d", g=2)), reads=['va'], writes=[vack])
            if is_smp:
                for t in range(4):
                    P.op('sp', lambda e: e.dma_start(out=nk_s[l, :, 124 + t, :], in_=ka[t * 16:(t + 1) * 16].rearrange("p h d -> p (h d)")), reads=['ka'], dma='o_nk%d' % t)
                    P.op('sp', lambda e: e.dma_start(out=nv_s[l, :, 124 + t, :], in_=va[t * 16:(t + 1) * 16, :]), reads=['va'], dma='o_nv%d' % t)
            elif seg == NSEG - 1 and ti == TPS - 1:
                P.op('sp', lambda e: e.dma_start(out=nk_p[l], in_=ka[:].rearrange("p h d -> p (h d)")), reads=['ka'], dma='o_nkp')
                P.op('sp', lambda e: e.dma_start(out=nv_p[l], in_=va[:]), reads=['va'], dma='o_nvp')
            pj = proj_group(768, 1280)
            silu_from_psum(pj, qh[:], 'qh')
            yield
            pj = proj_group(1280, 1792)
            k = 'PJ%d' % pj
            P.op('act', lambda e: e.activation(out=e1[:], in_=PJ[pj][:], func=AF.Exp, scale=nrstd[:]), reads=[k, 'nrstd'], writes=['e1'])
            P.op('act', lambda e: e.activation(out=e1[:], in_=e1[:], func=AF.Ln, bias=1.0), reads=['e1'], writes=['e1'])
            P.op('act', lambda e: e.activation(out=fg[:], in_=e1[:], func=AF.Exp, scale=-1.0), reads=['e1'], writes=['fg'])
            yield
            transposes_bf([qab[:, h, :] for h in range(8)], ['qab'], qT[par][:], 'qT%d' % par, 64)
            yield
            P.op('pool', lambda e: e.tensor_scalar(out=kk[:], in0=fg[:], scalar1=-1.0, scalar2=1.0, op0=ALU.mult, op1=ALU.add), reads=['fg'], writes=['kk'])
            if l == 1:
                P.op('dve', lambda e: e.tensor_tensor(out=fg[:], in0=kk[:], in1=lb1[:], op=ALU.mult), reads=['kk', 'lb1'], writes=['fg'])
                P.op('pool', lambda e: e.tensor_tensor(out=kk[:], in0=kk[:], in1=fg[:], op=ALU.subtract), reads=['kk', 'fg'], writes=['kk'])
                P.op('pool', lambda e: e.tensor_scalar(out=fg[:], in0=kk[:], scalar1=-1.0, scalar2=1.0, op0=ALU.mult, op1=ALU.add), reads=['kk'], writes=['fg'])
            P.op('act', lambda e: e.activation(out=gl[:], in_=fg[:], func=AF.Ln), reads=['fg'], writes=['gl'])
            pj = proj_group(1792, 2304)
            P.op('act', lambda e: e.activation(out=vh[par][:], in_=PJ[pj][:], func=AF.Copy, scale=rstd[:]), reads=['PJ%d' % pj, 'rstd'], writes=['vh%d' % par])
            yield
            pj = proj_group(2304, 2816)
            silu_from_psum(pj, fg[:], 'fg')
            P.op('pool', lambda e: e.tensor_tensor(out=gate[par][:].rearrange("p (h d) -> p h d", h=4), in0=fg[:].rearrange("p (h d) -> p h d", h=4),
                                                   in1=bc(onb[:, l, :], 1, [128, 4, 128]), op=ALU.mult), reads=['fg', 'onb'], writes=['gate%d' % par])
            yield
            mi = 1 if is_smp else 0
            p1 = nxt('pj'); p2 = nxt('pj')
            P.op('pe', lambda e: e.matmul(PJ[p1][:], lhsT=mrel[:, mi, :], rhs=gl[:], start=True, stop=True), reads=['mrel', 'gl'], writes=['PJ%d' % p1])
            P.op('pe', lambda e: e.matmul(PJ[p2][:], lhsT=mkd[:, mi, :], rhs=gl[:], start=True, stop=True), reads=['mkd', 'gl'], writes=['PJ%d' % p2])
            P.op('dve', lambda e: e.tensor_scalar(out=cl[:], in0=PJ[p1][:], scalar1=-40.0, scalar2=40.0, op0=ALU.max, op1=ALU.min), reads=['PJ%d' % p1], writes=['cl'])
            P.op('act', lambda e: e.activation(out=eq[:], in_=cl[:], func=AF.Exp), reads=['cl'], writes=['eq'])
            P.op('act', lambda e: e.activation(out=ek[:], in_=cl[:], func=AF.Exp, scale=-1.0), reads=['cl'], writes=['e1'])
            P.op('act', lambda e: e.activation(out=cl[:], in_=PJ[p2][:], func=AF.Exp), reads=['PJ%d' % p2, 'cl'], writes=['cl'])
            P.op('dve', lambda e: e.tensor_tensor(out=qe[par][:].rearrange("p h d -> p (h d)"), in0=qh[:], in1=eq[:], op=ALU.mult), reads=['qh', 'eq'], writes=['qe%d' % par])
            P.op('dve', lambda e: e.tensor_tensor(out=ke[par][:].rearrange("p h d -> p (h d)"), in0=kk[:], in1=ek[:], op=ALU.mult), reads=['kk', 'e1'], writes=['ke%d' % par])
            P.op('dve', lambda e: e.tensor_tensor(out=kd[par][:], in0=kk[:], in1=cl[:], op=ALU.mult), reads=['kk', 'cl'], writes=['kd%d' % par])
            yield
            nind = 16 if is_smp else 8
            ind_ap = inds[:] if is_smp else indp[:]
            p3 = nxt('pj')
            def fbg(e):
                ins = None
                for h in range(4):
                    ins = e.matmul(PJ[p3][:, h * 16:h * 16 + nind], lhsT=gl[:, h * 128:(h + 1) * 128], rhs=ind_ap, start=True, stop=True)
                return ins
            P.op('pe', fbg, reads=['gl', 'inds', 'indp'], writes=['PJ%d' % p3])
            P.op('act', lambda e: e.activation(out=bg[par][:, :, 0:nind], in_=PJ[p3][:, 0:64].rearrange("p (h c) -> p h c", h=4)[:, :, 0:nind], func=AF.Exp),
                 reads=['PJ%d' % p3], writes=['bg%d' % par])
            yield

            transposes_bf([kab[:, g, :] for g in range(2)], ['kab'], kTc[:], kTck, 64)
            yield

        def B_tile(l, ti, seg, is_smp, par):
            xk = 'x%d' % ti
            prv = 1 - par
            kTc, vac = kT[l][par], vaug[l][par]
            kTck, vack = 'kT%d_%d' % (l, par), 'vaug%d_%d' % (l, par)
            qTc, qTk = qT[par], 'qT%d' % par
            mi = 1 if is_smp else 0
            if is_smp:
                blocks = [('c', j) for j in range(NSMP)] + [('n', 0)]
            else:
                blocks = ([('p', 0)] if not (seg == 0 and ti == 0) else []) + [('u', 0)]
            nb = len(blocks)
            pv_bank = {0: (PV, 'PV'), 1: ((PT, 'PT') if is_smp else (PV, 'PV'))}

            def finish_g(g):
                bank, bkey = pv_bank[g]
                pv4 = bank[:].rearrange("p (h c) -> p h c", h=4)
                P.op('dve', lambda e: e.tensor_tensor(out=den[:], in0=pv4[:, :, 64], in1=esink[:, l, 4 * g:4 * g + 4], op=ALU.add), reads=[bkey, 'esink'], writes=['den'])
                P.op('dve', lambda e: e.reciprocal(out=rden[:], in_=den[:]), reads=['den'], writes=['rden'])
                P.op('dve', lambda e: e.tensor_tensor(out=mix[:, g * 256:(g + 1) * 256].rearrange("p (h d) -> p h d", h=4), in0=pv4[:, :, 0:64],
                                                      in1=bc(rden[:], 2, [128, 4, 64]), op=ALU.mult), reads=[bkey, 'rden'], writes=['mixa'])

            def block_g(g, bi, bk, j, mask_ap, mask_key, lhs, lk, rv, rvk, neg=None):
                bank, bkey = pv_bank[g]
                psi = nxt('ps')
                pk = 'PS%d' % psi
                def fsc(e):
                    ins = e.matmul(PS[psi][:], lhsT=lhs, rhs=qTc[:, 4 * g:4 * g + 4, :], start=True, stop=(neg is None))
                    if neg is not None:
                        ins = e.matmul(PS[psi][:], lhsT=identb[:], rhs=neg, start=False, stop=True)
                    return ins
                P.op('pe', fsc, reads=[lk, qTk, 'identb', 'negm'], writes=[pk])
                pi = nxt('pt')
                P.op('act', lambda e: e.activation(out=pT[pi][:].rearrange("p h t -> p (h t)"), in_=PS[psi][:], func=AF.Exp, scale=0.125),
                     reads=[pk], writes=['pT%d' % pi])
                if neg is None:
                    P.op('pool', lambda e: e.tensor_tensor(out=pT[pi][:], in0=pT[pi][:], in1=bc(mask_ap, 1, [128, 4, 128]), op=ALU.mult),
                         reads=['pT%d' % pi, mask_key], writes=['pT%d' % pi])
                yield 1
                def fpv(e):
                    ins = None
                    for hh in range(4):
                        ins = e.matmul(bank[:, hh * 128:hh * 128 + 65], lhsT=pT[pi][:, hh, :], rhs=rv,
                                       start=(bi == 0 and hh == 0), stop=(bi == nb - 1), skip_group_check=True)
                    return ins
                P.op('pe', fpv, reads=['pT%d' % pi, rvk], writes=[bkey])
                yield 0

            if not is_smp:
                for g in range(2):
                    for bi, (bk, j) in enumerate(blocks):
                        if bk == 'p':
                            yield from block_g(g, bi, bk, j, None, None, kT[l][prv][:, g, :], 'kT%d_%d' % (l, prv), vaug[l][prv][:, g, :], 'vaug%d_%d' % (l, prv), neg=negm[:, 1, :])
                        else:
                            yield from block_g(g, bi, bk, j, None, None, kTc[:, g, :], kTck, vac[:, g, :], vack, neg=negm[:, 0, :])
                    finish_g(g)
            else:
                P.barrier()
                P.op('pool', lambda e: e.memset(ft[:, 1024:2064], 0.0), writes=['vcsA'])
                P.op('dve', lambda e: e.memset(vcsA[:, :, :, 64], 1.0), reads=['vcsA'], writes=['vcsA'])
                for i_ in range(2):
                    P.op('dve', lambda e: e.memset(pT[i_][:], 0.0), writes=['pT%d' % i_])
                P.op('pool', lambda e: e.dma_start(out=kcsA, in_=ck[l].rearrange("j k d -> k j d")), writes=['kcsA'], dma='l_kc')
                for g in range(2):
                    P.op('pool', lambda e: e.dma_start(out=vcsA[:, :, g, 0:64], in_=cv[l, :, :, g * 64:(g + 1) * 64].rearrange("j k d -> k j d")), reads=['vcsA'], writes=['vcsA'], dma='l_vc%d' % g)

            def att_seq(bi, j):
                kcT = kcT2[j % 2]
                kck = 'kcT%d' % (j % 2)
                transposes_bf([kcsA[:, j, g * 64:(g + 1) * 64] for g in range(2)], ['kcsA'], kcT, kck, 64)
                for g in range(2):
                    bank, bkey = pv_bank[g]
                    psi = nxt('ps')
                    pk = 'PS%d' % psi
                    P.op('pe', lambda e: e.matmul(PS[psi][:, 0:16], lhsT=kcT[:, g, :], rhs=qTc[:, 4 * g:4 * g + 4, j:64:16], start=True, stop=True),
                         reads=[kck, qTk], writes=[pk])
                    pi = nxt('pt')
                    pslice = pT[pi][:, :, j:64:16]
                    P.op('act', lambda e: e.activation(out=pslice, in_=PS[psi][:, 0:16].rearrange("p (h t) -> p h t", h=4), func=AF.Exp, scale=0.125),
                         reads=[pk], writes=['pT%d' % pi])
                    P.op('dve', lambda e: e.tensor_tensor(out=pslice, in0=pslice, in1=bc(mc0[:, j:64:16], 1, [128, 4, 4]), op=ALU.mult),
                         reads=['pT%d' % pi, 'mc0'], writes=['pT%d' % pi])
                    def fpv(e):
                        ins = None
                        for hh in range(4):
                            ins = e.matmul(bank[:, hh * 128:hh * 128 + 65], lhsT=pT[pi][:, hh, :], rhs=vcsA[:, j, g, :],
                                           start=(bi == 0 and hh == 0), stop=False, skip_group_check=True)
                        return ins
                    P.op('pe', fpv, reads=['pT%d' % pi, 'vcsA'], writes=[bkey])
                    P.op('dve', lambda e: e.memset(pslice, 0.0), reads=['pT%d' % pi], writes=['pT%d' % pi])

            def att_epi():
                bi, (bk, j) = nb - 1, blocks[-1]
                for g in range(2):
                    for _ in block_g(g, bi, bk, j, mns[:], 'mns', kTc[:, g, :], kTck, vac[:, g, :], vack):
                        pass
                finish_g(0)
                finish_g(1)

            qec, kec, kdc, bgc, vhc = qe[par], ke[par], kd[par], bg[par], vh[par]
            qek, kek, kdk, bgk, vhk = 'qe%d' % par, 'ke%d' % par, 'kd%d' % par, 'bg%d' % par, 'vh%d' % par
            transposes_bf([qec[:, h, :] for h in range(4)], [qek], qeT[:], 'qeT', 128)
            yield 1
            transposes_bf([kec[:, h, :] for h in range(4)], [kek], keT[:], 'keT', 128, evac='dve')
            yield 1
            p4 = nxt('ps')
            def fa(e):
                ins = None
                for h in range(4):
                    ins = e.matmul(PS[p4][:, h * 128:(h + 1) * 128], lhsT=keT[:, h, :], rhs=qeT[:, h, :], start=True, stop=True)
                return ins
            P.op('pe', fa, reads=['keT', 'qeT'], writes=['PS%d' % p4])
            P.op('dve', lambda e: e.tensor_tensor(out=atm[:], in0=PS[p4][:].rearrange("p (h t) -> p h t", h=4), in1=bc(ma[:, mi, :], 1, [128, 4, 128]), op=ALU.mult),
                 reads=['PS%d' % p4, 'ma'], writes=['atm'])
            yield 0
            def fo(e):
                ins = None
                for h in range(4):
                    ins = e.matmul(PO[:, h * 128:(h + 1) * 128], lhsT=atm[:, h, :], rhs=vhc[:, h * 128:(h + 1) * 128],
                                   start=(h == 0), stop=False, skip_group_check=True)
                return ins
            P.op('pe', fo, reads=['atm', vhk], writes=['PO'])
            yield 0
            Sl = S[l]
            Sk = 'S%d' % l
            if not is_smp:
                qeTs = keT
                P.op('dve', lambda e: e.tensor_tensor(out=qeTs.rearrange("p h (c t) -> p h c t", c=4), in0=qeT.rearrange("p h (c t) -> p h c t", c=4),
                                                      in1=bc(bgc[:, :, 4:8], 3, [128, 4, 4, 32]), op=ALU.mult),
                     reads=['qeT', 'keT', bgk], writes=['keT'])
                P.op('dve', lambda e: e.tensor_copy(out=Sp[0][:], in_=Sl[:]), reads=[Sk], writes=['Sp0'])
                for c in range(4):
                    pu = nxt('ps')
                    def fu(e):
                        ins = None
                        for h in range(4):
                            ins = e.matmul(PS[pu][:, h * 128:(h + 1) * 128], lhsT=kdc[32 * c:32 * c + 32, h * 128:(h + 1) * 128],
                                           rhs=vhc[32 * c:32 * c + 32, h * 128:(h + 1) * 128], start=True, stop=True, tile_position=(32 * c, 0))
                        return ins
                    P.op('pe', fu, reads=[kdk, vhk], writes=['PS%d' % pu])
                    def fupd(e):
                        ins = None
                        for h in range(4):
                            ins = e.scalar_tensor_tensor(out=Sl[:, h, :], in0=Sl[:, h, :], scalar=bgc[:, h, c:c + 1], in1=PS[pu][:, h * 128:(h + 1) * 128],
                                                         op0=ALU.mult, op1=ALU.add)
                        return ins
                    if c < 3:
                        def fupb(e):
                            ins = None
                            for h in range(4):
                                ins = e.scalar_tensor_tensor(out=Sp[(c + 1) % 2][:, h, :], in0=Sl[:, h, :], scalar=bgc[:, h, c:c + 1], in1=PS[pu][:, h * 128:(h + 1) * 128],
                                                             op0=ALU.mult, op1=ALU.add)
                            return ins
                        P.op('dve', fupb, reads=[Sk, bgk, 'PS%d' % pu], writes=['Sp%d' % ((c + 1) % 2)])
                    P.op('dve', fupd, reads=[Sk, bgk, 'PS%d' % pu], writes=[Sk])
                    if c == 3:
                        yield 3
                    def foi(e):
                        ins = None
                        for h in range(4):
                            ins = e.matmul(PO[32 * c:32 * c + 32, h * 128:(h + 1) * 128], lhsT=qeTs[:, h, 32 * c:32 * c + 32], rhs=Sp[c % 2][:, h, :],
                                           start=False, stop=(c == 3 and h == 3), skip_group_check=True, tile_position=(0, 32 * c))
                        return ins
                    P.op('pe', foi, reads=['keT', 'Sp%d' % (c % 2)], writes=['PO'])
                    if c < 3:
                        yield 1
                if seg == NSEG - 1 and ti == TPS - 1:
                    P.op('sp', lambda e: e.dma_start(out=ns_p[l].rearrange("h k v -> k h v"), in_=Sl[:]), reads=[Sk], dma='o_nsp')
            else:
                kdm = atm[:].rearrange("p h d -> p (h d)")
                qeTm = Sp[0][:, :, 0:64]
                sbf2 = [sbf, mix[:, 512:1024].rearrange("p (h d) -> p h d", h=4)]
                sbk = ['sbfA', 'mixh']
                def ld_state(jj):
                    s_ = jj % 2
                    P.op('sp', lambda e: e.dma_start(out=sstA[s_], in_=st_in[l, jj].rearrange("h k v -> k h v")), writes=['sstA%d' % s_], dma='l_st%d' % s_)
                    P.op('pool', lambda e: e.dma_start(out=sbf2[s_], in_=st_in[l, jj].rearrange("h k v -> k h v")), writes=[sbk[s_]], dma='l_sb%d' % s_)
                ld_state(0)
                for j in range(NSMP):
                    si = j % 2
                    if j + 1 < NSMP:
                        ld_state(j + 1)
                    att_seq(j, j)
                    P.op('dve', lambda e: e.tensor_tensor(out=qeTm, in0=qeT[:, :, 0:64], in1=bc(selT[:, j, :], 1, [128, 4, 64]), op=ALU.mult),
                         reads=['qeT', 'selT'], writes=['Sp0'])
                    def foi(e):
                        ins = None
                        for h in range(4):
                            ins = e.matmul(PO[0:64, h * 128:(h + 1) * 128], lhsT=qeTm[:, h, :], rhs=sbf2[si][:, h, :],
                                           start=False, stop=(j == NSMP - 1), skip_group_check=True)
                        return ins
                    P.op('pe', foi, reads=['Sp0', sbk[si]], writes=['PO'])
                    P.op('act', lambda e: e.activation(out=kdm, in_=kdc[:], func=AF.Copy, scale=msel[:, j:j + 1]), reads=[kdk, 'msel'], writes=['atm'])
                    pu = nxt('ps')
                    def fu(e):
                        ins = None
                        for h in range(4):
                            ins = e.matmul(PS[pu][:, h * 128:(h + 1) * 128], lhsT=kdm[:, h * 128:(h + 1) * 128], rhs=vhc[:, h * 128:(h + 1) * 128], start=True, stop=True)
                        return ins
                    P.op('pe', fu, reads=['atm', vhk], writes=['PS%d' % pu])
                    P.op('dve', lambda e: e.tensor_tensor(out=stmp, in0=sstA[si], in1=bc(bgc[:, :, j], 2, [128, 4, 128]), op=ALU.mult),
                         reads=['sstA%d' % si, bgk], writes=['stmpA'])
                    P.op('dve', lambda e: e.tensor_tensor(out=stmp.rearrange("p h d -> p (h d)"), in0=stmp.rearrange("p h d -> p (h d)"), in1=PS[pu][:], op=ALU.add),
                         reads=['stmpA', 'PS%d' % pu], writes=['stmpA'])
                    P.op('sp', lambda e: e.dma_start(out=ns_s[l, j].rearrange("h k v -> k h v"), in_=stmp), reads=['stmpA'], dma='o_st')
                    yield 1
                att_epi()
            def fsq(e):
                for h in range(4):
                    e.activation(out=atm[:, h, :], in_=PO[:, h * 128:(h + 1) * 128], func=AF.Square, accum_out=sso[:, h:h + 1])
                return e.activation(out=dmy[:], in_=dmy[:], func=AF.Copy)
            P.op('act', fsq, reads=['PO'], writes=['atm', 'sso', 'dmy'])
            P.op('act', lambda e: e.activation(out=sso[:], in_=sso[:], func=AF.Ln, scale=1.0 / 128.0, bias=EPS_AP[:]), reads=['sso', 'eps_ap'], writes=['sso'])
            P.op('act', lambda e: e.activation(out=rso[:], in_=sso[:], func=AF.Exp, scale=-0.5), reads=['sso'], writes=['rso'])
            def fmx(e):
                ins = None
                for h in range(4):
                    ins = e.scalar_tensor_tensor(out=mix[:, 512 + h * 128:512 + (h + 1) * 128], in0=PO[:, h * 128:(h + 1) * 128], scalar=rso[:, h:h + 1],
                                                 in1=gate[par][:, h * 128:(h + 1) * 128], op0=ALU.mult, op1=ALU.mult)
                return ins
            P.op('dve', fmx, reads=['PO', 'rso', 'gate%d' % par], writes=['mixh'])
            transposes_bf([mix[:, i * 128:(i + 1) * 128] for i in range(4)], ['mixa'], mixT[:, 0:4, :], 'qeT', 128)
            wob = [(PV, 'PV'), (PT, 'PT')]
            for half in range(2):
                wbank, wkey = wob[half]
                def fw0(e):
                    ins = None
                    for kc in range(4):
                        ins = e.matmul(wbank[:], lhsT=mixT[:, kc, :], rhs=w_o_sb[:, kc, half * 512:(half + 1) * 512], start=(kc == 0), stop=False)
                    return ins
                P.op('pe', fw0, reads=['qeT'] + WO_KEYS, writes=[wkey])
            yield 99
            transposes_bf([mix[:, i * 128:(i + 1) * 128] for i in range(4, 8)], ['mixh'], mixT[:, 4:8, :], 'keT', 128, evac='dve')
            for half in range(2):
                wbank, wkey = wob[half]
                def fw1(e):
                    ins = None
                    for kc in range(4, 8):
                        ins = e.matmul(wbank[:], lhsT=mixT[:, kc, :], rhs=w_o_sb[:, kc, half * 512:(half + 1) * 512], start=False, stop=(kc == 7))
                    return ins
                P.op('pe', fw1, reads=['keT'] + WO_KEYS, writes=[wkey])
                P.op('dve', lambda e: e.tensor_tensor(out=xres[:, ti, half * 512:(half + 1) * 512], in0=xres[:, ti, half * 512:(half + 1) * 512], in1=wbank[:], op=ALU.add),
                     reads=[xk, wkey], writes=[xk])
                yield 0

        def drain(g):
            for _ in g:
                pass

        def interleave(gb, gf):
            done_f = False
            for nf in gb:
                for _ in range(nf or 0):
                    if done_f:
                        break
                    try:
                        next(gf)
                    except StopIteration:
                        done_f = True
            if not done_f:
                drain(gf)

        for seg in range(NSEG):
            tiles = list(range(TPS)) + ([TPS] if seg == 0 else [])
            for ti in range(TPS):
                if ti == 0 or seg > 0:
                    continue
                P.op('sp', lambda e: e.dma_start(out=xres[:, ti, :], in_=x_seq[(seg * TPS + ti) * 128:(seg * TPS + ti + 1) * 128, :]),
                     writes=['x%d' % ti], dma='lx%d' % ti)
            P.op('sp', lambda e: e.dma_start(out=cosp[:], in_=c_cosp[:, seg * TPS:(seg + 1) * TPS, :]), writes=['cosp'], dma='l_cos')
            P.op('sp', lambda e: e.dma_start(out=sinp[:], in_=c_sinp[:, seg * TPS:(seg + 1) * TPS, :]), writes=['sinp'], dma='l_sin')
            if seg == 0:
                P.op('sp', lambda e: e.dma_start(out=xres[0:64, TPS, :], in_=x_smp), writes=['x%d' % TPS], dma='lx%d' % TPS)
            for l in range(2):
                if not (seg == 0 and l == 0):
                    P.barrier()
                def load_pass(p_):
                    load_pass_l(l, p_)
                if l != 0:
                    load_phase_a(l)
                if seg == 0:
                    P.op('sp', lambda e: e.dma_start(out=nk_s[l, :, 0:124, :], in_=ck[l, :, 4:128, :]), dma='o_ckc')
                    P.op('sp', lambda e: e.dma_start(out=nv_s[l, :, 0:124, :], in_=cv[l, :, 4:128, :]), dma='o_cvc')
                n = len(tiles)
                prepped = set()
                drain(F_tile(l, tiles[0], seg, tiles[0] == TPS, 0))
                load_phase_a2(l)
                for tix, ti in enumerate(tiles):
                    gb = B_tile(l, ti, seg, ti == TPS, tix % 2)
                    if tix + 1 < n:
                        gf = F_tile(l, tiles[tix + 1], seg, tiles[tix + 1] == TPS, (tix + 1) % 2)
                        interleave(gb, gf)
                    elif ti == TPS:
                        def prep_gen():
                            for t_ in range(TPS):
                                rms_stats(xres[:, t_, :], 'x%d' % t_, rstd[:], 'rstd', sq=rr2[:, t_:t_ + 1])
                                for _ in make_hT(t_, 2 + l, hT_all[:, :, t_ * 128:(t_ + 1) * 128], 'hTall', pj_only=True):
                                    pass
                                prepped.add(t_)
                                yield
                        interleave(gb, prep_gen())
                    else:
                        drain(gb)
                P.barrier()
                ntok = n * 128
                for ti in tiles:
                    if ti in prepped:
                        continue
                    rms_stats(xres[:, ti, :], 'x%d' % ti, rstd[:], 'rstd', sq=rr2[:, ti:ti + 1])
                    drain(make_hT(ti, 2 + l, hT_all[:, :, ti * 128:(ti + 1) * 128], 'hTall'))
                groups = [(g0, min(GRP, ntok - g0)) for g0 in range(0, ntok, GRP)]

                def up(p_, g0, gn, ui):
                    slot = p_ % 2
                    wu, wd = ring[slot]
                    uT = uTs[ui]
                    for fc in range(4):
                        psi = nxt('ps')
                        def fup(e):
                            ins = None
                            for kc in range(8):
                                ins = e.matmul(PS[psi][:, 0:gn], lhsT=wu[:, kc, fc * 128:(fc + 1) * 128], rhs=hT_all[:, kc, g0:g0 + gn], start=(kc == 0), stop=(kc == 7))
                            return ins
                        P.op('pe', fup, reads=['ringu%d_%d' % (slot, k_) for k_ in range(8)] + ['hTall'], writes=['PS%d' % psi])
                        rlb = junk[:, (fc % 2) * 512:(fc % 2) * 512 + gn]
                        rlk = 'hT' + 'ab'[fc % 2]
                        P.op('act', lambda e: e.activation(out=rlb, in_=PS[psi][:, 0:gn], func=AF.Relu), reads=['PS%d' % psi], writes=[rlk])
                        P.op('dve', lambda e: e.tensor_tensor(out=uT[:, fc, 0:gn], in0=rlb, in1=rlb, op=ALU.mult), reads=[rlk], writes=['uT%d' % ui])

                def down(p_, g0, gn, ui):
                    slot = p_ % 2
                    wu, wd = ring[slot]
                    uT = uTs[ui]
                    for tt in range(gn // 128):
                        ti = (g0 // 128) + tt
                        for half in range(2):
                            pj = nxt('pj')
                            def fdn(e):
                                ins = None
                                for fc in range(4):
                                    ins = e.matmul(PJ[pj][:], lhsT=uT[:, fc, tt * 128:(tt + 1) * 128], rhs=wd[:, fc, half * 512:(half + 1) * 512], start=(fc == 0), stop=(fc == 3))
                                return ins
                            P.op('pe', fdn, reads=['uT%d' % ui] + ['ringd%d_%d' % (slot, k_) for k_ in range(4)], writes=['PJ%d' % pj])
                            P.op('dve', lambda e: e.scalar_tensor_tensor(
                                out=xres[:, ti, half * 512:(half + 1) * 512], in0=PJ[pj][:], scalar=rr2[:, ti:ti + 1],
                                in1=xres[:, ti, half * 512:(half + 1) * 512], op0=ALU.mult, op1=ALU.add),
                                reads=['PJ%d' % pj, 'rr2', 'x%d' % ti], writes=['x%d' % ti])

                def final_tile(ti):
                    rms_stats(xres[:, ti, :], 'x%d' % ti, rstd[:], 'rstd')
                    yi = nxt('y')
                    P.op('dve', lambda e: e.scalar_tensor_tensor(out=ysb[yi], in0=xres[:, ti, :], scalar=rstd[:], in1=fing, op0=ALU.mult, op1=ALU.mult),
                         reads=['x%d' % ti, 'rstd', 'fing'], writes=['ysb%d' % yi])
                    if ti == TPS:
                        P.op('sp', lambda e: e.dma_start(out=y_smp, in_=ysb[yi][0:64, :]), reads=['ysb%d' % yi], dma='o_y%d' % yi)
                    else:
                        P.op('sp', lambda e: e.dma_start(out=y_seq[(seg * TPS + ti) * 128:(seg * TPS + ti + 1) * 128, :], in_=ysb[yi]),
                             reads=['ysb%d' % yi], dma='o_y%d' % yi)
                        if seg + 1 < NSEG:
                            P.op('sp', lambda e: e.dma_start(out=xres[:, ti, :], in_=x_seq[((seg + 1) * TPS + ti) * 128:((seg + 1) * TPS + ti + 1) * 128, :]),
                                 writes=['x%d' % ti], dma='lx%d' % ti)

                def after_down(pv):
                    if l == 1 and pv[0] == NPASS - 1:
                        for tt in range(pv[2] // 128):
                            final_tile(pv[1] // 128 + tt)

                if l == 1:
                    P.op('sp', lambda e: e.dma_start(out=fing, in_=fin_g.partition_broadcast(128)), writes=['fing'], dma='c10')
                work = [(p_, g0, gn) for p_ in range(NPASS) for (g0, gn) in groups]
                prev = None
                for wi, (p_, g0, gn) in enumerate(work):
                    ui = nxt('u')
                    up(p_, g0, gn, ui)
                    if prev is not None:
                        down(*prev)
                        after_down(prev)
                    if g0 == 0 and p_ + 1 < NPASS:
                        load_pass(p_ + 1)
                    prev = (p_, g0, gn, ui)
                down(*prev)
                after_down(prev)
            P.barrier()
            if seg + 1 < NSEG:
                load_phase_a(0)
            P.barrier()
        P.emit()
    return nc


def _consts():
    c = {}
    c['c_identf'] = np.eye(128, dtype=np.float32)
    c['c_identb'] = np.eye(128, dtype=np.float32)
    half = 8
    inv_freq = np.power(np.float32(500000.0), -np.arange(half, dtype=np.float32) * np.float32(2.0 / 16)).astype(np.float32)
    pos = np.arange(SEQ, dtype=np.float32)
    ang = (pos[:, None] * inv_freq[None, :]).astype(np.float32)
    c['c_cosp'] = np.cos(ang).astype(np.float32).reshape(32, 128, 8).transpose(1, 0, 2).copy()
    c['c_sinp'] = np.sin(ang).astype(np.float32).reshape(32, 128, 8).transpose(1, 0, 2).copy()
    p = np.arange(128)
    tt = p // 16
    jj = p % 16
    valid = p < 64
    pos_s = (PAST + tt).astype(np.float32)
    ang_s = (pos_s[:, None] * inv_freq[None, :]).astype(np.float32)
    c['c_coss'] = np.where(valid[:, None], np.cos(ang_s), 1.0).astype(np.float32)
    c['c_sins'] = np.where(valid[:, None], np.sin(ang_s), 0.0).astype(np.float32)
    s = p[:, None]; t = p[None, :]
    ncur = np.where(s <= t, 0.0, -30000.0).astype(np.float32)
    nprev = np.where(s >= t, 0.0, -30000.0).astype(np.float32)
    c['c_negm'] = np.stack([np.tile(ncur, (1, 4)), np.tile(nprev, (1, 4))], axis=1).astype(np.float32)
    ch = p // 32
    same = ch[:, None] == ch[None, :]
    ref = ch * 32 + 15
    mrel_p = same * ((s <= t).astype(np.float32) - (s <= ref[None, :]).astype(np.float32))
    mkd_p = same * (s > t)
    ma_p = same * (s <= t)
    same_s = (jj[:, None] == jj[None, :]) & valid[:, None] & valid[None, :]
    mrel_s = same_s * (tt[:, None] <= tt[None, :])
    mkd_s = same_s * (tt[:, None] > tt[None, :])
    ma_s = same_s * (tt[:, None] <= tt[None, :])
    c['c_mrel'] = np.stack([mrel_p, mrel_s]).astype(np.float32)
    c['c_mkd'] = np.stack([mkd_p, mkd_s]).astype(np.float32)
    c['c_ma'] = np.stack([ma_p, ma_s]).astype(np.float32)
    indp = np.zeros((128, 8), np.float32)
    for cc in range(4):
        indp[cc * 32:(cc + 1) * 32, cc] = 1.0
        indp[cc * 32:cc * 32 + 16, 4 + cc] = 1.0
    c['c_indp'] = indp
    inds = np.zeros((128, 16), np.float32)
    inds[p[valid], jj[valid]] = 1.0
    c['c_inds'] = inds
    c['c_msel'] = inds.copy()
    c['c_mc0'] = ((p[:, None] >= tt[None, :]) & valid[None, :]).astype(np.float32)
    mns = (same_s & (tt[:, None] <= tt[None, :])).astype(np.float32)
    c['c_mns'] = mns
    selT = np.zeros((128, 16, 64), np.float32)
    for j in range(16):
        selT[:, j, :] = ((jj == j) & valid)[None, 0:64]
    c['c_selT'] = selT
    return c


_CACHE = {}
PCORES = [0, 1, 4, 5]


def kernel(x_prompt, x_sample, cache_k, cache_v, state_hgrn, attn_norm, w_in, att_sinks,
           hgrn_lower_bounds, hgrn_out_norm, w_o, mlp_norm, w_up, w_down, final_norm):
    f = lambda a: np.ascontiguousarray(np.asarray(a, dtype=np.float32))
    x_prompt, x_sample, cache_k, cache_v, state_hgrn = map(f, (x_prompt, x_sample, cache_k, cache_v, state_hgrn))
    if 'nc' not in _CACHE:
        _CACHE['nc'] = build_program()
        _CACHE['consts'] = _consts()
    nc = _CACHE['nc']
    consts = _CACHE['consts']
    gains = np.concatenate([f(attn_norm).reshape(2, 8, 128), f(mlp_norm).reshape(2, 8, 128), f(final_norm).reshape(1, 8, 128)], axis=0).reshape(40, 128)
    shared = dict(gains=np.ascontiguousarray(gains), fin_g=f(final_norm).reshape(1, D), w_in=f(w_in), sinks=f(att_sinks),
                  lbraw=f(hgrn_lower_bounds), onorm=f(hgrn_out_norm), w_o=f(w_o), w_up=f(w_up), w_dn=f(w_down))
    shared.update(consts)
    zero_seq = np.zeros((SEQ, D), np.float32)
    in_maps = []
    for c in range(8):
        sl = slice(c * NSMP, (c + 1) * NSMP)
        m = dict(shared)
        m['x_seq'] = x_prompt[PCORES.index(c)] if c in PCORES else zero_seq
        m['x_smp'] = np.ascontiguousarray(x_sample[sl].transpose(1, 0, 2).reshape(64, D))
        m['ck'] = np.ascontiguousarray(cache_k[:, sl].reshape(2, NSMP, 128, 128))
        m['cv'] = np.ascontiguousarray(cache_v[:, sl].reshape(2, NSMP, 128, 128))
        m['st_in'] = np.ascontiguousarray(state_hgrn[:, sl])
        in_maps.append(m)
    res = run_bass_kernel_spmd(nc, in_maps, core_ids=list(range(8)))
    R = res.results
    y_prompt = np.stack([R[b]['y_seq'] for b in PCORES]).astype(np.float32)
    y_sample = np.concatenate([R[c]['y_smp'].reshape(4, NSMP, D).transpose(1, 0, 2) for c in range(8)], axis=0).astype(np.float32)
    nk_p = np.stack([R[b]['nk_p'] for b in PCORES], axis=1).reshape(2, 4, 128, 2, 64)
    nv_p = np.stack([R[b]['nv_p'] for b in PCORES], axis=1).reshape(2, 4, 128, 2, 64)
    ns_p = np.stack([R[b]['ns_p'] for b in PCORES], axis=1)
    nk_s = np.concatenate([R[c]['nk_s'] for c in range(8)], axis=1).reshape(2, 128, 128, 2, 64)
    nv_s = np.concatenate([R[c]['nv_s'] for c in range(8)], axis=1).reshape(2, 128, 128, 2, 64)
    ns_s = np.concatenate([R[c]['ns_s'] for c in range(8)], axis=1)
    return (y_prompt, y_sample, np.ascontiguousarray(nk_p), np.ascontiguousarray(nv_p), np.ascontiguousarray(ns_p),
            np.ascontiguousarray(nk_s), np.ascontiguousarray(nv_s), np.ascontiguousarray(ns_s))
```
